# Optimizing a Trainium2 kernel written in Bass

```python
import math
import jax, jax.numpy as jnp
from jax import lax
import numpy as np

D_MODEL = 2048
BATCH = 4
SEQ = 4096
DEPTH = 4

CHUNK = 64
Q_BLOCK = 128
N_MIXERS = 4
EPS = 1e-6
NEG_INF = -1e30

A_WIDTH = D_MODEL
A_HEADS = 16
A_HEAD_DIM = A_WIDTH // (2 * A_HEADS)
T5_BUCKETS = 32
T5_MAX_DIST = 128

B_HEADS = 4
B_DK = D_MODEL // 2 // B_HEADS
B_DV = D_MODEL // B_HEADS
B_GATE_RANK = 16
B_GATE_TAU = 16.0

C_WIDTH = D_MODEL
C_BLOCKS = 8
C_BLOCK_DIM = C_WIDTH // C_BLOCKS
C_CONV = 4
C_C = 8.0

D_HEADS = 16
D_Q_RANK = 512
D_KV_RANK = 512
D_NOPE = 128
D_ROPE = 64
D_V = 128
ROPE_THETA = 10000.0


def _n_layers_of(m):
    return (DEPTH - m + N_MIXERS - 1) // N_MIXERS


N_A = _n_layers_of(0)
N_B = _n_layers_of(1)
N_C = _n_layers_of(2)
N_D = _n_layers_of(3)

kernel_name = "hybrid_chunk_causal_interleaved_trunk"


def _rmsnorm(x, g):
    xf = x.astype(jnp.float32)
    y = xf * lax.rsqrt(jnp.mean(xf * xf, axis=-1, keepdims=True) + EPS)
    return (y * g.astype(jnp.float32)).astype(x.dtype)


def _chunk_mask(qpos, kpos):
    return (kpos[None, :] // CHUNK) <= (qpos[:, None] // CHUNK)


def _t5_bucket(rel):
    nb = T5_BUCKETS // 2
    max_exact = nb // 2
    ret = jnp.where(rel > 0, nb, 0)
    n = jnp.abs(rel)
    nf = jnp.maximum(n, 1).astype(jnp.float32)
    large = max_exact + (jnp.log(nf / max_exact) / math.log(T5_MAX_DIST / max_exact)
                         * (nb - max_exact)).astype(jnp.int32)
    large = jnp.minimum(large, nb - 1)
    return ret + jnp.where(n < max_exact, n, large)


def _rope(x, pos):
    half = x.shape[-1] // 2
    inv = ROPE_THETA ** (-jnp.arange(half, dtype=jnp.float32) / half)
    ang = pos[:, None] * inv[None, :]
    cos = jnp.cos(ang)[None, :, None, :]
    sin = jnp.sin(ang)[None, :, None, :]
    xf = x.astype(jnp.float32)
    x1, x2 = xf[..., :half], xf[..., half:]
    return jnp.concatenate([x1 * cos - x2 * sin, x2 * cos + x1 * sin], axis=-1).astype(x.dtype)


def _diff_attention(h, w_in, qk_g, lam_vecs, subln_g, w_out, rel_bias, layer_idx):
    B, S, _ = h.shape
    H, d = A_HEADS, A_HEAD_DIM
    q, k, v, g = jnp.split(h @ w_in, 4, axis=-1)
    q = _rmsnorm(q.reshape(B, S, H, 2, d), qk_g[0]) * (d ** -0.5)
    k = _rmsnorm(k.reshape(B, S, H, 2, d), qk_g[1])
    v = v.reshape(B, S, H, 2 * d)
    lf = lam_vecs.astype(jnp.float32)
    lam_init = 0.8 - 0.6 * math.exp(-0.3 * layer_idx)
    lam = jnp.exp(jnp.sum(lf[0] * lf[1])) - jnp.exp(jnp.sum(lf[2] * lf[3])) + lam_init
    kpos = jnp.arange(S)
    nq = S // Q_BLOCK
    qb = q.reshape(B, nq, Q_BLOCK, H, 2, d).transpose(1, 0, 2, 3, 4, 5)

    def block(args):
        qblk, j = args
        qpos = j * Q_BLOCK + jnp.arange(Q_BLOCK)
        bias = rel_bias[_t5_bucket(kpos[None, :] - qpos[:, None])]
        s = (jnp.einsum('bqhtd,bkhtd->bthqk', qblk, k).astype(jnp.float32)
             + jnp.transpose(bias, (2, 0, 1)).astype(jnp.float32))
        s = jnp.where(_chunk_mask(qpos, kpos), s, NEG_INF)
        p = jax.nn.softmax(s, axis=-1)
        attn = p[:, 0] - lam * p[:, 1]
        return jnp.einsum('bhqk,bkhe->bqhe', attn.astype(v.dtype), v)

    o = lax.map(block, (qb, jnp.arange(nq)))
    o = o.transpose(1, 0, 2, 3, 4).reshape(B, S, H, 2 * d)
    o = _rmsnorm(o, subln_g) * (1.0 - lam_init)
    y = o.reshape(B, S, A_WIDTH) * jax.nn.silu(g)
    return y @ w_out


def _gla(h, w_in, w_gate, gate_bias, out_g, w_out):
    B, S, _ = h.shape
    H, dk, dv = B_HEADS, B_DK, B_DV
    nc = S // CHUNK
    q, k, v, g, lr = jnp.split(
        h @ w_in, [H * dk, 2 * H * dk, 2 * H * dk + H * dv, 2 * H * dk + 2 * H * dv], axis=-1)
    log_alpha = jax.nn.log_sigmoid((lr @ w_gate + gate_bias).astype(jnp.float32)) / B_GATE_TAU

    def chunks(t, e):
        return t.astype(jnp.float32).reshape(B, nc, CHUNK, H, e).transpose(1, 0, 2, 3, 4)

    qc = chunks(q, dk) * (dk ** -0.5)
    kc = chunks(k, dk)
    vc = chunks(v, dv)
    cum = jnp.cumsum(chunks(log_alpha, dk), axis=2)
    total = cum[:, :, -1]
    kc = kc * jnp.exp(total[:, :, None] - cum)

    def step(state, xs):
        qi, ki, vi, ti = xs
        state = jnp.exp(ti)[..., None] * state + jnp.einsum('bchk,bchv->bhkv', ki, vi)
        return state, jnp.einsum('bchk,bhkv->bchv', qi, state)

    s0 = jnp.zeros((B, H, dk, dv), jnp.float32)
    _, o = lax.scan(step, s0, (qc, kc, vc, total))
    o = o.transpose(1, 0, 2, 3, 4).reshape(B, S, H, dv)
    o = _rmsnorm(o, out_g).reshape(B, S, H * dv).astype(h.dtype)
    return (o * jax.nn.silu(g)) @ w_out


def _rglru(h, w_in, conv_w, conv_b, w_rg, b_rg, w_ig, b_ig, lam, w_out):
    B, S, _ = h.shape
    u, g = jnp.split(h @ w_in, 2, axis=-1)
    up = jnp.pad(u, ((0, 0), (C_CONV - 1, 0), (0, 0)))
    xc = conv_b + up[:, 0:S] * conv_w[0]
    for t in range(1, C_CONV):
        xc = xc + up[:, t:t + S] * conv_w[t]
    xb = xc.reshape(B, S, C_BLOCKS, C_BLOCK_DIM)
    r = jax.nn.sigmoid((jnp.einsum('bsnd,nde->bsne', xb, w_rg).reshape(B, S, C_WIDTH)
                        + b_rg).astype(jnp.float32))
    i = jax.nn.sigmoid((jnp.einsum('bsnd,nde->bsne', xb, w_ig).reshape(B, S, C_WIDTH)
                        + b_ig).astype(jnp.float32))
    log_a = -C_C * r * jax.nn.softplus(-lam.astype(jnp.float32))
    a = jnp.exp(log_a)
    xin = jnp.sqrt(-jnp.expm1(2.0 * log_a)) * (i * xc.astype(jnp.float32))

    def combine(left, right):
        a_l, b_l = left
        a_r, b_r = right
        return a_l * a_r, a_r * b_l + b_r

    _, hs = lax.associative_scan(combine, (a, xin), axis=1)
    y = hs.astype(h.dtype) * jax.nn.silu(g)
    return y @ w_out


def _chunk_causal_attention(q, k, v):
    B, S, H, dq = q.shape
    dv = v.shape[-1]
    nq = S // Q_BLOCK
    kpos = jnp.arange(S)
    qb = q.reshape(B, nq, Q_BLOCK, H, dq).transpose(1, 0, 2, 3, 4)

    def block(args):
        qblk, j = args
        qpos = j * Q_BLOCK + jnp.arange(Q_BLOCK)
        s = jnp.einsum('bqhd,bkhd->bhqk', qblk, k).astype(jnp.float32)
        s = jnp.where(_chunk_mask(qpos, kpos), s, NEG_INF)
        p = jax.nn.softmax(s, axis=-1)
        return jnp.einsum('bhqk,bkhe->bqhe', p.astype(v.dtype), v)

    o = lax.map(block, (qb, jnp.arange(nq)))
    return o.transpose(1, 0, 2, 3, 4).reshape(B, S, H, dv)


def _mla(h, w_in, q_lat_g, kv_lat_g, w_uq, w_ukv, qk_g, w_out):
    B, S, _ = h.shape
    H = D_HEADS
    dqk = D_NOPE + D_ROPE
    cq, ckv, k_pe, g = jnp.split(
        h @ w_in, [D_Q_RANK, D_Q_RANK + D_KV_RANK, D_Q_RANK + D_KV_RANK + D_ROPE], axis=-1)
    q = (_rmsnorm(cq, q_lat_g) @ w_uq).reshape(B, S, H, dqk)
    kv = (_rmsnorm(ckv, kv_lat_g) @ w_ukv).reshape(B, S, H, D_NOPE + D_V)
    k_nope, v = kv[..., :D_NOPE], kv[..., D_NOPE:]
    pos = jnp.arange(S, dtype=jnp.float32)
    q_nope = _rmsnorm(q[..., :D_NOPE], qk_g[0, :D_NOPE])
    q_pe = _rope(_rmsnorm(q[..., D_NOPE:], qk_g[0, D_NOPE:]), pos)
    k_nope = _rmsnorm(k_nope, qk_g[1, :D_NOPE])
    k_pe = _rope(_rmsnorm(k_pe, qk_g[1, D_NOPE:])[:, :, None, :], pos)
    q = jnp.concatenate([q_nope, q_pe], axis=-1) * (dqk ** -0.5)
    k = jnp.concatenate([k_nope, jnp.broadcast_to(k_pe, (B, S, H, D_ROPE))], axis=-1)
    o = _chunk_causal_attention(q, k, v)
    y = o.reshape(B, S, H * D_V) * jax.nn.silu(g)
    return y @ w_out


def setup_inputs(seed: int = 0) -> dict:
    key = jax.random.key(seed)
    ks = jax.random.split(key, 29)
    f32 = jnp.float32

    def nrm(k, shape, scale):
        return jax.random.normal(k, shape, f32) * scale

    def gain(k, shape):
        return 1.0 + 0.02 * jax.random.normal(k, shape, f32)

    a_in = 4 * A_WIDTH
    b_in = 2 * B_HEADS * B_DK + 2 * B_HEADS * B_DV + B_GATE_RANK
    c_in = 2 * C_WIDTH
    d_in = D_Q_RANK + D_KV_RANK + D_ROPE + D_HEADS * D_V
    u = jax.random.uniform(ks[20], (N_C, C_WIDTH), f32, minval=0.9, maxval=0.999)
    a0 = u ** (1.0 / C_C)
    c_lambda = jnp.log(a0) - jnp.log1p(-a0)
    return {
        "x": nrm(ks[0], (BATCH, SEQ, D_MODEL), 1.0),
        "norm_g": gain(ks[1], (DEPTH, D_MODEL)),
        "rel_bias": nrm(ks[2], (T5_BUCKETS, A_HEADS), 0.2),
        "a_w_in": nrm(ks[3], (N_A, D_MODEL, a_in), D_MODEL ** -0.5),
        "a_qk_g": gain(ks[4], (N_A, 2, A_HEAD_DIM)),
        "a_lambda": nrm(ks[5], (N_A, 4, A_HEAD_DIM), 0.1),
        "a_subln_g": gain(ks[6], (N_A, 2 * A_HEAD_DIM)),
        "a_w_out": nrm(ks[7], (N_A, A_WIDTH, D_MODEL), A_WIDTH ** -0.5),
        "b_w_in": nrm(ks[8], (N_B, D_MODEL, b_in), D_MODEL ** -0.5),
        "b_w_gate": nrm(ks[9], (N_B, B_GATE_RANK, B_HEADS * B_DK), B_GATE_RANK ** -0.5),
        "b_gate_bias": nrm(ks[10], (N_B, B_HEADS * B_DK), 0.1),
        "b_out_g": gain(ks[11], (N_B, B_DV)),
        "b_w_out": nrm(ks[12], (N_B, B_HEADS * B_DV, D_MODEL), (B_HEADS * B_DV) ** -0.5),
        "c_w_in": nrm(ks[13], (N_C, D_MODEL, c_in), D_MODEL ** -0.5),
        "c_conv_w": nrm(ks[14], (N_C, C_CONV, C_WIDTH), C_CONV ** -0.5),
        "c_conv_b": nrm(ks[15], (N_C, C_WIDTH), 0.02),
        "c_w_rgate": nrm(ks[16], (N_C, C_BLOCKS, C_BLOCK_DIM, C_BLOCK_DIM), C_BLOCK_DIM ** -0.5),
        "c_b_rgate": nrm(ks[17], (N_C, C_WIDTH), 0.02),
        "c_w_igate": nrm(ks[18], (N_C, C_BLOCKS, C_BLOCK_DIM, C_BLOCK_DIM), C_BLOCK_DIM ** -0.5),
        "c_b_igate": nrm(ks[19], (N_C, C_WIDTH), 0.02),
        "c_lambda": c_lambda,
        "c_w_out": nrm(ks[21], (N_C, C_WIDTH, D_MODEL), C_WIDTH ** -0.5),
        "d_w_in": nrm(ks[22], (N_D, D_MODEL, d_in), D_MODEL ** -0.5),
        "d_q_lat_g": gain(ks[23], (N_D, D_Q_RANK)),
        "d_kv_lat_g": gain(ks[24], (N_D, D_KV_RANK)),
        "d_w_uq": nrm(ks[25], (N_D, D_Q_RANK, D_HEADS * (D_NOPE + D_ROPE)), D_Q_RANK ** -0.5),
        "d_w_ukv": nrm(ks[26], (N_D, D_KV_RANK, D_HEADS * (D_NOPE + D_V)), D_KV_RANK ** -0.5),
        "d_qk_g": gain(ks[27], (N_D, 2, D_NOPE + D_ROPE)),
        "d_w_out": nrm(ks[28], (N_D, D_HEADS * D_V, D_MODEL), (D_HEADS * D_V) ** -0.5),
    }


def reference(x, norm_g, rel_bias,
              a_w_in, a_qk_g, a_lambda, a_subln_g, a_w_out,
              b_w_in, b_w_gate, b_gate_bias, b_out_g, b_w_out,
              c_w_in, c_conv_w, c_conv_b, c_w_rgate, c_b_rgate, c_w_igate, c_b_igate,
              c_lambda, c_w_out,
              d_w_in, d_q_lat_g, d_kv_lat_g, d_w_uq, d_w_ukv, d_qk_g, d_w_out):
    for i in range(DEPTH):
        m, j = i % N_MIXERS, i // N_MIXERS
        h = _rmsnorm(x, norm_g[i])
        if m == 0:
            y = _diff_attention(h, a_w_in[j], a_qk_g[j], a_lambda[j], a_subln_g[j],
                                a_w_out[j], rel_bias, i)
        elif m == 1:
            y = _gla(h, b_w_in[j], b_w_gate[j], b_gate_bias[j], b_out_g[j], b_w_out[j])
        elif m == 2:
            y = _rglru(h, c_w_in[j], c_conv_w[j], c_conv_b[j], c_w_rgate[j], c_b_rgate[j],
                       c_w_igate[j], c_b_igate[j], c_lambda[j], c_w_out[j])
        else:
            y = _mla(h, d_w_in[j], d_q_lat_g[j], d_kv_lat_g[j], d_w_uq[j], d_w_ukv[j],
                     d_qk_g[j], d_w_out[j])
        x = x + y.astype(x.dtype)
    return x
```

```python
import math
from contextlib import ExitStack
import numpy as np
import concourse.bass as bass
import concourse.mybir as mybir
from concourse.bass_utils import run_bass_kernel_spmd

F32 = mybir.dt.float32
BF16 = mybir.dt.bfloat16
AF = mybir.ActivationFunctionType
ALU = mybir.AluOpType
AX = mybir.AxisListType

ENGS = ("sync", "act", "pool", "dve", "pe")


class Tok:
    __slots__ = ("name", "w", "r", "pool")

    def __init__(self, name):
        self.name = name
        self.w = {}
        self.r = {}
        self.pool = None


class Prog:
    def __init__(self, nc, ctx, same_engine_sync=("act", "dve", "pool")):
        self.nc = nc
        self.ctx = ctx
        self.q = {e: [] for e in ENGS}
        self.cnt = {e: 0 for e in ENGS}
        self.waited = {e: {} for e in ENGS}
        self.pend = {e: {} for e in ENGS}
        self.needed = set()
        self.same_sync = set(same_engine_sync)
        self.pool_val = []
        self.pool_free = []
        self.live = []
        self.all_toks = []

    def tok(self, name="t"):
        t = Tok(name)
        self.all_toks.append(t)
        return t

    def toks(self, n, name="t"):
        return [self.tok(f"{name}{i}") for i in range(n)]

    def op(self, eng, fn, reads=(), writes=(), dma=False, join=False):
        deps = dict(self.pend[eng])
        self.pend[eng] = {}

        def add(k, v):
            if deps.get(k, 0) < v:
                deps[k] = v

        for t in reads:
            for k, v in t.w.items():
                add(k, v)
        for t in writes:
            if not (join and not t.r):
                for k, v in t.w.items():
                    add(k, v)
            for k, v in t.r.items():
                add(k, v)
        waits = []
        wd = self.waited[eng]
        for k, v in deps.items():
            if k == ("e", eng) and eng not in self.same_sync:
                continue
            if wd.get(k, 0) < v:
                wd[k] = v
                waits.append((k, v))
                self.needed.add((k, v))
        if dma:
            assert len(writes) == 1
            t = writes[0]
            if t.pool is None:
                if self.pool_free:
                    t.pool = self.pool_free.pop()
                else:
                    self.pool_val.append(0)
                    t.pool = len(self.pool_val) - 1
                self.live.append(t)
            self.pool_val[t.pool] += 16
            h = (("s", t.pool), self.pool_val[t.pool])
        else:
            self.cnt[eng] += 1
            h = (("e", eng), self.cnt[eng])
        k, v = h
        for t in reads:
            if t.r.get(k, 0) < v:
                t.r[k] = v
        for t in writes:
            if join and not t.r:
                t.w[k] = v
            else:
                t.w = {k: v}
                t.r = {}
        self.q[eng].append((waits, fn, h, dma))
        return h

    def barrier(self):
        hs = {}
        for e in ENGS:
            if self.cnt[e] > 0:
                hs[("e", e)] = self.cnt[e]
        for i, v in enumerate(self.pool_val):
            if v > 0:
                hs[("s", i)] = v
        for e in ENGS:
            for k, v in hs.items():
                if self.pend[e].get(k, 0) < v:
                    self.pend[e][k] = v
        for t in self.live:
            self.pool_free.append(t.pool)
            t.pool = None
        self.live = []
        for t in self.all_toks:
            t.w = {}
            t.r = {}
            t.pool = None

    def emit(self):
        nc = self.nc
        self.barrier()
        fw = []
        wd = self.waited["sync"]
        for k, v in self.pend["sync"].items():
            if wd.get(k, 0) < v:
                fw.append((k, v))
                self.needed.add((k, v))
        esem = {e: self.ctx.enter_context(nc.semaphore(f"sem_{e}")) for e in ENGS}
        psem = [self.ctx.enter_context(nc.semaphore(f"dsem{i}")) for i in range(len(self.pool_val))]
        rank = {}
        for e in ENGS:
            r = 0
            for (_w, _f, h, dma) in self.q[e]:
                if not dma and h in self.needed:
                    r += 1
                    rank[h] = r

        def resolve(h):
            k, v = h
            if k[0] == "e":
                return esem[k[1]], rank[h]
            return psem[k[1]], v

        block = self.ctx.enter_context(nc.Block())
        names = {"sync": "sync", "act": "scalar", "pool": "gpsimd", "dve": "vector", "pe": "tensor"}
        for e in ENGS:
            ops = self.q[e]
            if not ops and e != "sync":
                continue

            def body(eng, ops=ops, e=e):
                for (waits, fn, h, dma) in ops:
                    for w in waits:
                        s, v = resolve(w)
                        eng.wait_ge(s, v)
                    ins = fn(eng)
                    if dma:
                        s, v = resolve(h)
                        ins.then_inc(s, 16)
                    elif h in self.needed:
                        s, v = resolve(h)
                        ins.then_inc(s, 1)
                if e == "sync":
                    for w in fw:
                        s, v = resolve(w)
                        eng.wait_ge(s, v)

            getattr(block, names[e])(body)
        self.n_sems = len(psem) + 5


class Arena:
    def __init__(self, nc, ctx, kib):
        self.n32 = kib * 256
        self.t = ctx.enter_context(nc.sbuf_tensor("arena", [128, self.n32], F32))
        self.off = 0

    def mark(self):
        return self.off

    def release(self, m):
        self.off = m

    def alloc(self, shape, dt, parts=128):
        n = int(np.prod(shape))
        n32 = n if dt == F32 else (n + 1) // 2
        n32 = (n32 + 7) // 8 * 8
        assert self.off + n32 <= self.n32, f"arena overflow {self.off + n32} > {self.n32}"
        v = self.t[0:parts, self.off:self.off + n32]
        self.off += n32
        if dt != F32:
            v = v.bitcast(dt)
        v = v[:, 0:n]
        if len(shape) == 2:
            v = v.rearrange("p (a b) -> p a b", a=shape[0])
        elif len(shape) == 3:
            v = v.rearrange("p (a b c) -> p a b c", a=shape[0], b=shape[1])
        return v


S = 4096
D = 2048
EPS = 1e-6
NT = S // 128
TB = 2048
NTB = S // TB
TPB = TB // 128


class Ctx:
    pass


def dram(nc, name, shape, dt, kind="Internal"):
    return nc.dram_tensor(name, list(shape), dt, kind=kind).ap()


def load_w_block(C, dst, dtok, wsrc, c0, ncols, kc=16, rows0=0):
    P = C.P
    wv = wsrc[rows0:rows0 + kc * 128, :].rearrange("(k p) c -> p k c", p=128)
    step = 4 if kc >= 4 else kc
    for k0 in range(0, kc, step):
        k1 = min(kc, k0 + step)
        P.op("pool", lambda e, k0=k0, k1=k1: e.dma_start(out=dst[:, k0:k1, 0:ncols], in_=wv[:, k0:k1, c0:c0 + ncols]),
             writes=[dtok], dma=True, join=True)


def phase_norm_T(C, x_src, g_row, tok0, hT, hT_toks, tbsz=TB):
    P, A = C.P, C.A
    m = A.mark()
    xin = [A.alloc([D], F32) for _ in range(2)]
    hb = [A.alloc([D], BF16) for _ in range(2)]
    gb = A.alloc([D], F32)
    ssq = [A.alloc([1], F32) for _ in range(2)]
    t_xin = P.toks(2, "xin"); t_hb = P.toks(2, "hb"); t_gb = P.tok("gb"); t_ssq = P.toks(2, "ssq")
    P.op("sync", lambda e: e.dma_start(out=gb, in_=g_row.partition_broadcast(128)), writes=[t_gb], dma=True)
    for i in range(tbsz // 128):
        s = i % 2
        r0 = tok0 + i * 128
        P.op("sync", lambda e, s=s, r0=r0: e.dma_start(out=xin[s], in_=x_src[r0:r0 + 128, :]), writes=[t_xin[s]], dma=True)
        P.op("act", lambda e, s=s: e.activation(out=hb[s], in_=xin[s], func=AF.Square, accum_out=ssq[s]),
             reads=[t_xin[s]], writes=[t_hb[s], t_ssq[s]])
        P.op("act", lambda e, s=s: e.activation(out=ssq[s], in_=ssq[s], func=AF.Sqrt, bias=C.epscol, scale=1.0 / D),
             reads=[t_ssq[s], C.t_const], writes=[t_ssq[s]])
        P.op("dve", lambda e, s=s: e.reciprocal(out=ssq[s], in_=ssq[s]), reads=[t_ssq[s]], writes=[t_ssq[s]])
        P.op("dve", lambda e, s=s: e.scalar_tensor_tensor(out=hb[s], in0=xin[s], scalar=ssq[s], in1=gb, op0=ALU.mult, op1=ALU.mult),
             reads=[t_xin[s], t_ssq[s], t_gb], writes=[t_hb[s]])
        for half in range(2):
            tp, ttp = C.tpb[half], C.t_tpb[half]
            for j in range(8):
                k = half * 8 + j
                P.op("pe", lambda e, s=s, j=j, k=k, tp=tp: e.transpose(out=tp[:, j * 128:(j + 1) * 128], in_=hb[s][:, k * 128:(k + 1) * 128], identity=C.ident),
                     reads=[t_hb[s], C.t_const], writes=[ttp], join=True)
            eng = "act" if half == 0 else "dve"
            dst = hT[:, half * 8:(half + 1) * 8, i * 128:(i + 1) * 128]
            srcv = tp.rearrange("p (a b) -> p a b", a=8)
            if eng == "act":
                P.op("act", lambda e, dst=dst, srcv=srcv: e.activation(out=dst, in_=srcv, func=AF.Copy), reads=[ttp], writes=[hT_toks[i]], join=True)
            else:
                P.op("dve", lambda e, dst=dst, srcv=srcv: e.tensor_copy(out=dst, in_=srcv), reads=[ttp], writes=[hT_toks[i]], join=True)
    A.release(m)


def phase_outproj(C, yT_d, w_out, x_src, x_dst):
    P, A = C.P, C.A
    m = A.mark()
    wo = A.alloc([16, D], BF16)
    yT = A.alloc([16, TB], BF16)
    xin = [A.alloc([D], F32) for _ in range(2)]
    xo = [A.alloc([D], F32) for _ in range(2)]
    t_wo = P.tok("wo"); t_yT = P.toks(16, "yT"); t_xin = P.toks(2, "xin"); t_xo = P.toks(2, "xo"); t_dst = P.tok("xdst")
    for n in range(4):
        load_w_block(C, wo[:, :, n * 512:(n + 1) * 512], t_wo, w_out, n * 512, 512)
    cnt = 0
    for tb in range(NTB):
        tok0 = tb * TB
        for k in range(16):
            P.op("sync", lambda e, k=k, tok0=tok0: e.dma_start(out=yT[:, k, :], in_=yT_d[k, :, tok0:tok0 + TB]), writes=[t_yT[k]], dma=True)
        for i in range(TPB):
            s = i % 2
            r0 = tok0 + i * 128
            P.op("sync", lambda e, s=s, r0=r0: e.dma_start(out=xin[s], in_=x_src[r0:r0 + 128, :]), writes=[t_xin[s]], dma=True)
            for n in range(4):
                pb = cnt % 4; cnt += 1
                ps, tps = C.psf[pb], C.t_psf[pb]
                for k in range(16):
                    P.op("pe", lambda e, ps=ps, k=k, i=i, n=n: e.matmul(ps, lhsT=yT[:, k, i * 128:(i + 1) * 128], rhs=wo[:, k, n * 512:(n + 1) * 512], start=(k == 0), stop=(k == 15)),
                         reads=[t_yT[k], t_wo], writes=[tps], join=(k > 0))
                P.op("dve", lambda e, ps=ps, s=s, n=n: e.tensor_tensor(out=xo[s][:, n * 512:(n + 1) * 512], in0=ps, in1=xin[s][:, n * 512:(n + 1) * 512], op=ALU.add),
                     reads=[tps, t_xin[s]], writes=[t_xo[s]], join=(n > 0))
            P.op("sync", lambda e, s=s, r0=r0: e.dma_start(out=x_dst[r0:r0 + 128, :], in_=xo[s]), reads=[t_xo[s]], writes=[t_dst], dma=True)
    A.release(m)
    P.barrier()


def qk_norm_epilogue(C, ps, tps, gcol, dst, t_dst, tmp, t_tmp, grp):
    P = C.P
    qf, sq, rs = tmp
    ones = C.ones64 if grp == 64 else C.ones128
    P.op("act", lambda e: e.activation(out=qf, in_=ps, func=AF.Copy), reads=[tps], writes=[t_tmp[0]])
    P.op("act", lambda e: e.activation(out=sq, in_=ps, func=AF.Square), reads=[tps], writes=[t_tmp[1]])
    pss, tpss = C.psf[4 + C.auxcnt % 2], C.t_psf[4 + C.auxcnt % 2]
    C.auxcnt += 1
    P.op("pe", lambda e: e.matmul(pss, lhsT=ones, rhs=sq, start=True, stop=True), reads=[t_tmp[1], C.t_const], writes=[tpss])
    P.op("act", lambda e: e.activation(out=rs, in_=pss, func=AF.Sqrt, bias=C.epscol, scale=1.0 / grp), reads=[tpss, C.t_const], writes=[t_tmp[2]])
    P.op("dve", lambda e: e.reciprocal(out=rs, in_=rs), reads=[t_tmp[2]], writes=[t_tmp[2]])
    P.op("dve", lambda e: e.scalar_tensor_tensor(out=dst, in0=qf, scalar=gcol, in1=rs, op0=ALU.mult, op1=ALU.mult),
         reads=[t_tmp[0], t_tmp[2], C.t_lconst], writes=[t_dst], join=True)


def layer0_proj(C, x_src, W):
    P, A = C.P, C.A
    m = A.mark()
    hT = A.alloc([16, TB], BF16)
    hT_toks = P.toks(TPB, "hT")
    wt = [A.alloc([16, 512], BF16) for _ in range(2)]
    t_wt = P.toks(2, "wt")
    qst = [A.alloc([TB], BF16) for _ in range(2)]
    t_qst = P.toks(2, "qst")
    tmp = [[A.alloc([512], F32) for _ in range(3)] for _ in range(2)]
    t_tmp = [P.toks(3, "tmp") for _ in range(2)]
    vst = [A.alloc([8, 512], BF16) for _ in range(2)]
    t_vst = P.toks(2, "vst")
    gst = [A.alloc([4, 512], F32) for _ in range(2)]
    t_gst = P.toks(2, "gst")
    t_qd = P.tok("QTd"); t_kd = P.tok("KTd"); t_vd = P.tok("Vd"); t_gd = P.tok("Gd")
    gq = A.alloc([1], F32); gk = A.alloc([1], F32)
    for half in range(2):
        P.op("sync", lambda e, half=half: e.dma_start(out=gq[half * 64:(half + 1) * 64, :], in_=W["a_qk_g"][0, 0, :].rearrange("(d o) -> d o", o=1)), writes=[C.t_lconst], dma=True, join=True)
        P.op("sync", lambda e, half=half: e.dma_start(out=gk[half * 64:(half + 1) * 64, :], in_=W["a_qk_g"][0, 1, :].rearrange("(d o) -> d o", o=1)), writes=[C.t_lconst], dma=True, join=True)
    P.op("dve", lambda e: e.tensor_scalar(out=gq, in0=gq, scalar1=0.125, scalar2=None, op0=ALU.mult), reads=[C.t_lconst], writes=[C.t_lconst])
    wcnt = 0; qcnt = 0; tcnt = 0; pcnt = 0; vcnt = 0; gcnt = 0
    for tb in range(NTB):
        tok0 = tb * TB
        phase_norm_T(C, x_src, W["norm_g"][C.layer, :], tok0, hT, hT_toks)
        for cb in range(16):
            ws = wcnt % 2; wcnt += 1
            load_w_block(C, wt[ws], t_wt[ws], W["a_w_in"][0], cb * 512, 512)
            kind = cb // 4
            if kind < 2:
                for mm_ in range(4):
                    h = (cb % 4) * 4 + mm_
                    qs = qcnt % 2; qcnt += 1
                    for tq in range(TB // 512):
                        pb = pcnt % 4; pcnt += 1
                        ps, tps = C.psf[pb], C.t_psf[pb]
                        for k in range(16):
                            P.op("pe", lambda e, ps=ps, k=k, ws=ws, mm_=mm_, tq=tq: e.matmul(ps, lhsT=wt[ws][:, k, mm_ * 128:(mm_ + 1) * 128], rhs=hT[:, k, tq * 512:(tq + 1) * 512], start=(k == 0), stop=(k == 15)),
                                 reads=[t_wt[ws]] + hT_toks[tq * 4:(tq + 1) * 4], writes=[tps], join=(k > 0))
                        ts_ = tcnt % 2; tcnt += 1
                        qk_norm_epilogue(C, ps, tps, gq if kind == 0 else gk, qst[qs][:, tq * 512:(tq + 1) * 512], t_qst[qs], tmp[ts_], t_tmp[ts_], 64)
                    dd, td = (C.QT_d, t_qd) if kind == 0 else (C.KT_d, t_kd)
                    P.op("sync", lambda e, dd=dd, h=h, qs=qs, tok0=tok0: e.dma_start(out=dd[h, :, tok0:tok0 + TB], in_=qst[qs]), reads=[t_qst[qs]], writes=[td], dma=True, join=True)
            else:
                c0 = (cb % 4) * 512
                for i in range(TPB):
                    pb = pcnt % 4; pcnt += 1
                    ps, tps = C.psf[pb], C.t_psf[pb]
                    for k in range(16):
                        P.op("pe", lambda e, ps=ps, k=k, ws=ws, i=i: e.matmul(ps, lhsT=hT[:, k, i * 128:(i + 1) * 128], rhs=wt[ws][:, k, :], start=(k == 0), stop=(k == 15)),
                             reads=[t_wt[ws], hT_toks[i]], writes=[tps], join=(k > 0))
                    r0 = tok0 + i * 128
                    if kind == 2:
                        g8 = i % 8
                        if g8 == 0:
                            vs = vcnt % 2; vcnt += 1
                        P.op("dve", lambda e, ps=ps, vs=vs, g8=g8: e.tensor_copy(out=vst[vs][:, g8, :], in_=ps), reads=[tps], writes=[t_vst[vs]], join=True)
                        if g8 == 7:
                            rr = r0 - 7 * 128
                            P.op("sync", lambda e, vs=vs, rr=rr, c0=c0: e.dma_start(out=C.V_d[rr:rr + 1024, c0:c0 + 512].rearrange("(a p) c -> p a c", p=128), in_=vst[vs]),
                                 reads=[t_vst[vs]], writes=[t_vd], dma=True, join=True)
                    else:
                        g4 = i % 4
                        if g4 == 0:
                            gs = gcnt % 2; gcnt += 1
                        P.op("act", lambda e, ps=ps, gs=gs, g4=g4: e.activation(out=gst[gs][:, g4, :], in_=ps, func=AF.Silu), reads=[tps], writes=[t_gst[gs]], join=True)
                        if g4 == 3:
                            rr = r0 - 3 * 128
                            P.op("sync", lambda e, gs=gs, rr=rr, c0=c0: e.dma_start(out=C.G_d[rr:rr + 512, c0:c0 + 512].rearrange("(a p) c -> p a c", p=128), in_=gst[gs]),
                                 reads=[t_gst[gs]], writes=[t_gd], dma=True, join=True)
    A.release(m)
    P.barrier()


def attn_phase(C, W, mode):
    P, A = C.P, C.A
    m = A.mark()
    H = 16
    diff = (mode == "diff")
    nsub = 2 if diff else 1
    lam_init = 0.8 - 0.6 * math.exp(-0.3 * C.layer)
    QT = [A.alloc([S], BF16) for _ in range(2)]
    KT = [A.alloc([S], BF16) for _ in range(2)]
    Vh = [A.alloc([NT, 129], BF16) for _ in range(2)]
    Gh = [A.alloc([NT, 128], F32) for _ in range(2)]
    yT = [A.alloc([S], BF16) for _ in range(2)]
    PT = [A.alloc([512], BF16) for _ in range(3)]
    t_QT = P.toks(2, "QT"); t_KT = P.toks(2, "KT"); t_Vh = P.toks(2, "Vh"); t_Gh = P.toks(2, "Gh"); t_yT = P.toks(2, "yT"); t_PT = P.toks(3, "PT")
    t_yd = P.tok("YTd")
    o1 = A.alloc([4, 128], F32); oo = A.alloc([4, 128], F32); yb = A.alloc([4, 128], BF16)
    rden = A.alloc([2, 4], F32); ssq = A.alloc([4], F32); junk = A.alloc([128], F32)
    t_ep = P.tok("ep")
    tl = C.t_lconst
    if diff:
        biasT = A.alloc([H, 2, 128], F32)
        b15 = A.alloc([H], F32)
        lam4 = A.alloc([4, 64], F32)
        lamc = A.alloc([4], F32)
        sgb = A.alloc([128], F32)
        P.op("sync", lambda e: e.dma_start(out=biasT, in_=C.biasT_in), writes=[tl], dma=True, join=True)
        P.op("sync", lambda e: e.dma_start(out=b15, in_=W["rel_bias"][15, :].partition_broadcast(128)), writes=[tl], dma=True, join=True)
        P.op("sync", lambda e: e.dma_start(out=lam4, in_=W["a_lambda"][0].rearrange("a d -> (a d)").partition_broadcast(128).rearrange("p (a d) -> p a d", a=4)), writes=[tl], dma=True, join=True)
        P.op("sync", lambda e: e.dma_start(out=sgb, in_=W["a_subln_g"][0, :].partition_broadcast(128)), writes=[tl], dma=True, join=True)
        for hh in range(H):
            P.op("dve", lambda e, hh=hh: e.tensor_scalar(out=biasT[:, hh], in0=biasT[:, hh], scalar1=b15[:, hh:hh + 1], scalar2=None, op0=ALU.subtract), reads=[tl], writes=[tl])
        P.op("dve", lambda e: e.tensor_tensor(out=lam4[:, 0, :], in0=lam4[:, 0, :], in1=lam4[:, 1, :], op=ALU.mult), reads=[tl], writes=[tl])
        P.op("dve", lambda e: e.tensor_tensor(out=lam4[:, 2, :], in0=lam4[:, 2, :], in1=lam4[:, 3, :], op=ALU.mult), reads=[tl], writes=[tl])
        P.op("dve", lambda e: e.reduce_sum(out=lamc[:, 0:1], in_=lam4[:, 0, :], axis=AX.X), reads=[tl], writes=[tl])
        P.op("dve", lambda e: e.reduce_sum(out=lamc[:, 1:2], in_=lam4[:, 2, :], axis=AX.X), reads=[tl], writes=[tl])
        P.op("act", lambda e: e.activation(out=lamc[:, 0:2], in_=lamc[:, 0:2], func=AF.Exp), reads=[tl], writes=[tl])
        P.op("dve", lambda e: e.scalar_tensor_tensor(out=lamc[:, 0:1], in0=lamc[:, 1:2], scalar=-lam_init, in1=lamc[:, 0:1], op0=ALU.add, op1=ALU.subtract), reads=[tl], writes=[tl])
        P.op("dve", lambda e: e.tensor_scalar(out=sgb, in0=sgb, scalar1=1.0 - lam_init, scalar2=None, op0=ALU.mult), reads=[tl], writes=[tl])
    else:
        maskT = A.alloc([128], F32)
        QP = [A.alloc([S], BF16) for _ in range(2)]
        KP = A.alloc([S], BF16)
        t_QP = P.toks(2, "QP"); t_KP = P.tok("KP")
        P.op("sync", lambda e: e.dma_start(out=maskT, in_=C.maskT_d), writes=[tl], dma=True, join=True)
        P.op("sync", lambda e: e.dma_start(out=KP[0:64, :], in_=C.KPE_d), writes=[t_KP], dma=True)
    for s in range(2):
        P.op("pool", lambda e, s=s: e.memset(Vh[s][:, :, 128:129], 1.0), writes=[t_Vh[s]])
    scnt = 0; pcnt = 0
    for h in range(H):
        hs = h % 2
        P.op("sync", lambda e, h=h, hs=hs: e.dma_start(out=QT[hs], in_=C.QT_d[h]), writes=[t_QT[hs]], dma=True)
        P.op("sync", lambda e, h=h, hs=hs: e.dma_start(out=KT[hs], in_=C.KT_d[h]), writes=[t_KT[hs]], dma=True)
        if not diff:
            P.op("sync", lambda e, h=h, hs=hs: e.dma_start(out=QP[hs][0:64, :], in_=C.QPE_d[h]), writes=[t_QP[hs]], dma=True)
        P.op("sync", lambda e, h=h, hs=hs: e.dma_start(out=Vh[hs][:, :, 0:128], in_=C.V_d[:, h * 128:(h + 1) * 128].rearrange("(a p) c -> p a c", p=128)), writes=[t_Vh[hs]], dma=True, join=True)
        P.op("sync", lambda e, h=h, hs=hs: e.dma_start(out=Gh[hs], in_=C.G_d[:, h * 128:(h + 1) * 128].rearrange("(a p) c -> p a c", p=128)), writes=[t_Gh[hs]], dma=True)
        for qg in range(NT // 4):
            for t in range(nsub):
                ob0 = 2 + 2 * t if diff else 2 + 2 * (qg % 2)
                ob = [C.psf[ob0], C.psf[ob0 + 1]]
                tob = [C.t_psf[ob0], C.t_psf[ob0 + 1]]
                O = lambda jj, ob=ob: ob[jj // 2][:, (jj % 2) * 129:(jj % 2) * 129 + 129]
                nkb = 4 * qg + 4
                for i in range(nkb):
                    jmin = max(0, i - 4 * qg)
                    sb_ = scnt % 2; scnt += 1
                    Sp, tSp = C.psf[sb_], C.t_psf[sb_]
                    q0 = (4 * qg + jmin) * 128
                    ncol = (4 - jmin) * 128
                    c0 = jmin * 128
                    if diff:
                        P.op("pe", lambda e, Sp=Sp, hs=hs, t=t, i=i, q0=q0, ncol=ncol, c0=c0: e.matmul(Sp[:, c0:c0 + ncol], lhsT=KT[hs][t * 64:(t + 1) * 64, i * 128:(i + 1) * 128], rhs=QT[hs][t * 64:(t + 1) * 64, q0:q0 + ncol], start=True, stop=True),
                             reads=[t_KT[hs], t_QT[hs]], writes=[tSp])
                    else:
                        P.op("pe", lambda e, Sp=Sp, hs=hs, i=i, q0=q0, ncol=ncol, c0=c0: e.matmul(Sp[:, c0:c0 + ncol], lhsT=KT[hs][:, i * 128:(i + 1) * 128], rhs=QT[hs][:, q0:q0 + ncol], start=True, stop=False),
                             reads=[t_KT[hs], t_QT[hs]], writes=[tSp])
                        P.op("pe", lambda e, Sp=Sp, hs=hs, i=i, q0=q0, ncol=ncol, c0=c0: e.matmul(Sp[:, c0:c0 + ncol], lhsT=KP[0:64, i * 128:(i + 1) * 128], rhs=QP[hs][0:64, q0:q0 + ncol], start=False, stop=True),
                             reads=[t_KP, t_QP[hs]], writes=[tSp], join=True)
                    for rel in ((0, 1) if diff else (0,)):
                        jj = i - 4 * qg + rel
                        if 0 <= jj <= 3 and jj >= jmin:
                            btile = biasT[:, h, rel, :] if diff else maskT
                            P.op("dve", lambda e, Sp=Sp, jj=jj, btile=btile: e.tensor_tensor(out=Sp[:, jj * 128:(jj + 1) * 128], in0=Sp[:, jj * 128:(jj + 1) * 128], in1=btile, op=ALU.add),
                                 reads=[tSp, tl], writes=[tSp])
                    pp = pcnt % 3; pcnt += 1
                    ebias = b15[:, h:h + 1] if diff else 0.0
                    P.op("act", lambda e, Sp=Sp, pp=pp, c0=c0, ncol=ncol, ebias=ebias: e.activation(out=PT[pp][:, c0:c0 + ncol], in_=Sp[:, c0:c0 + ncol], func=AF.Exp, bias=ebias, scale=1.0),
                         reads=[tSp, tl], writes=[t_PT[pp]])
                    for jj in range(jmin, 4):
                        P.op("pe", lambda e, jj=jj, pp=pp, hs=hs, i=i, O=O, qg=qg: e.matmul(O(jj), lhsT=PT[pp][:, jj * 128:(jj + 1) * 128], rhs=Vh[hs][:, i, :], start=(i == 0 and jj % 2 == 0), stop=(i == 4 * qg + jj), skip_group_check=True),
                             reads=[t_PT[pp], t_Vh[hs]], writes=[tob[jj // 2]], join=True)
            if diff:
                for t in range(2):
                    for b in range(2):
                        bank = C.psf[2 + 2 * t + b]
                        P.op("dve", lambda e, t=t, b=b, bank=bank: e.reciprocal(out=rden[:, t, 2 * b:2 * b + 2], in_=bank[:, 0:258].rearrange("p (a c) -> p a c", a=2)[:, :, 128]),
                             reads=[C.t_psf[2 + 2 * t + b]], writes=[t_ep])
                for jj in range(4):
                    src1 = C.psf[4 + jj // 2][:, (jj % 2) * 129:(jj % 2) * 129 + 128]
                    src0 = C.psf[2 + jj // 2][:, (jj % 2) * 129:(jj % 2) * 129 + 128]
                    P.op("dve", lambda e, jj=jj, src1=src1: e.tensor_scalar(out=o1[:, jj, :], in0=src1, scalar1=rden[:, 1, jj:jj + 1], scalar2=lamc[:, 0:1], op0=ALU.mult, op1=ALU.mult),
                         reads=[C.t_psf[4 + jj // 2], t_ep, tl], writes=[t_ep])
                    P.op("dve", lambda e, jj=jj, src0=src0: e.scalar_tensor_tensor(out=oo[:, jj, :], in0=src0, scalar=rden[:, 0, jj:jj + 1], in1=o1[:, jj, :], op0=ALU.mult, op1=ALU.add),
                         reads=[C.t_psf[2 + jj // 2], t_ep], writes=[t_ep])
                    P.op("act", lambda e, jj=jj: e.activation(out=junk, in_=oo[:, jj, :], func=AF.Square, accum_out=ssq[:, jj:jj + 1]), reads=[t_ep], writes=[t_ep])
                P.op("act", lambda e: e.activation(out=ssq, in_=ssq, func=AF.Sqrt, bias=C.epscol, scale=1.0 / 128), reads=[t_ep, C.t_const], writes=[t_ep])
                P.op("dve", lambda e: e.reciprocal(out=ssq, in_=ssq), reads=[t_ep], writes=[t_ep])
                for jj in range(4):
                    P.op("dve", lambda e, jj=jj: e.scalar_tensor_tensor(out=oo[:, jj, :], in0=oo[:, jj, :], scalar=ssq[:, jj:jj + 1], in1=sgb, op0=ALU.mult, op1=ALU.mult),
                         reads=[t_ep, tl], writes=[t_ep])
            else:
                for b in range(2):
                    bank = C.psf[ob0 + b]
                    P.op("dve", lambda e, b=b, bank=bank: e.reciprocal(out=rden[:, 0, 2 * b:2 * b + 2], in_=bank[:, 0:258].rearrange("p (a c) -> p a c", a=2)[:, :, 128]),
                         reads=[C.t_psf[ob0 + b]], writes=[t_ep])
                for jj in range(4):
                    src0 = C.psf[ob0 + jj // 2][:, (jj % 2) * 129:(jj % 2) * 129 + 128]
                    P.op("dve", lambda e, jj=jj, src0=src0: e.tensor_scalar(out=oo[:, jj, :], in0=src0, scalar1=rden[:, 0, jj:jj + 1], scalar2=None, op0=ALU.mult),
                         reads=[C.t_psf[ob0 + jj // 2], t_ep], writes=[t_ep])
            P.op("pool", lambda e, qg=qg, hs=hs: e.tensor_tensor(out=yb, in0=oo, in1=Gh[hs][:, 4 * qg:4 * qg + 4, :], op=ALU.mult), reads=[t_ep, t_Gh[hs]], writes=[t_ep])
            tp, ttp = C.tpb[qg % 2], C.t_tpb[qg % 2]
            for jj in range(4):
                P.op("pe", lambda e, jj=jj, tp=tp: e.transpose(out=tp[:, jj * 128:(jj + 1) * 128], in_=yb[:, jj, :], identity=C.ident), reads=[t_ep, C.t_const], writes=[ttp], join=True)
            P.op("act", lambda e, tp=tp, hs=hs, qg=qg: e.activation(out=yT[hs][:, qg * 512:(qg + 1) * 512], in_=tp[:, 0:512], func=AF.Copy), reads=[ttp], writes=[t_yT[hs]], join=True)
        P.op("sync", lambda e, h=h, hs=hs: e.dma_start(out=C.YT_d[h], in_=yT[hs]), reads=[t_yT[hs]], writes=[t_yd], dma=True, join=True)
    A.release(m)
    P.barrier()


def load_col(C, dst, src_vec, n, scale=None):
    P = C.P
    P.op("sync", lambda e: e.dma_start(out=dst[0:n, :], in_=src_vec.rearrange("(d o) -> d o", o=1)), writes=[C.t_lconst], dma=True, join=True)
    if scale is not None:
        P.op("dve", lambda e: e.tensor_scalar(out=dst[0:n, :], in0=dst[0:n, :], scalar1=float(scale), scalar2=None, op0=ALU.mult), reads=[C.t_lconst], writes=[C.t_lconst])


def rope_epilogue(C, ps, tps, gcol, cs, sn, t_cs, dst, t_dst, tmp, t_tmp):
    P = C.P
    xf, sq, rs = tmp
    P.op("act", lambda e: e.activation(out=xf[0:64, :], in_=ps[0:64, :], func=AF.Copy), reads=[tps], writes=[t_tmp[0]])
    P.op("act", lambda e: e.activation(out=sq[0:64, :], in_=ps[0:64, :], func=AF.Square), reads=[tps], writes=[t_tmp[1]])
    pss, tpss = C.psf[4 + C.auxcnt % 2], C.t_psf[4 + C.auxcnt % 2]
    C.auxcnt += 1
    P.op("pe", lambda e: e.matmul(pss[0:64, :], lhsT=C.ones64[0:64, 0:64], rhs=sq[0:64, :], start=True, stop=True), reads=[t_tmp[1], C.t_const], writes=[tpss])
    P.op("act", lambda e: e.activation(out=rs[0:64, :], in_=pss[0:64, :], func=AF.Sqrt, bias=C.epscol[0:64, :], scale=1.0 / 64), reads=[tpss, C.t_const], writes=[t_tmp[2]])
    P.op("dve", lambda e: e.reciprocal(out=rs[0:64, :], in_=rs[0:64, :]), reads=[t_tmp[2]], writes=[t_tmp[2]])
    P.op("dve", lambda e: e.scalar_tensor_tensor(out=xf[0:64, :], in0=xf[0:64, :], scalar=gcol[0:64, :], in1=rs[0:64, :], op0=ALU.mult, op1=ALU.mult),
         reads=[t_tmp[0], t_tmp[2], C.t_lconst], writes=[t_tmp[0]])
    pr, tpr = C.psf[4 + C.auxcnt % 2], C.t_psf[4 + C.auxcnt % 2]
    C.auxcnt += 1
    P.op("pe", lambda e: e.matmul(pr[0:64, :], lhsT=C.rotm[0:64, 0:64], rhs=xf[0:64, :], start=True, stop=True), reads=[t_tmp[0], C.t_const], writes=[tpr])
    P.op("dve", lambda e: e.tensor_tensor(out=sq[0:64, :], in0=pr[0:64, :], in1=sn, op=ALU.mult), reads=[tpr, t_cs], writes=[t_tmp[1]])
    P.op("pool", lambda e: e.tensor_tensor(out=xf[0:64, :], in0=xf[0:64, :], in1=cs, op=ALU.mult), reads=[t_tmp[0], t_cs], writes=[t_tmp[0]])
    P.op("pool", lambda e: e.tensor_tensor(out=dst, in0=xf[0:64, :], in1=sq[0:64, :], op=ALU.add), reads=[t_tmp[0], t_tmp[1]], writes=[t_dst], join=True)


def g_block(C, hT, hT_toks, wt_s, t_wt_s, tok0, c0, pcnt, gst, t_gst, gcnt, tpb=TPB):
    P = C.P
    t_gd = C.t_gd
    for i in range(tpb):
        pb = pcnt % 4; pcnt += 1
        ps, tps = C.psf[pb], C.t_psf[pb]
        for k in range(16):
            P.op("pe", lambda e, ps=ps, k=k, i=i: e.matmul(ps, lhsT=hT[:, k, i * 128:(i + 1) * 128], rhs=wt_s[:, k, :], start=(k == 0), stop=(k == 15)),
                 reads=[t_wt_s, hT_toks[i]], writes=[tps], join=(k > 0))
        r0 = tok0 + i * 128
        g4 = i % 4
        if g4 == 0:
            gs = gcnt % 2; gcnt += 1
        P.op("act", lambda e, ps=ps, gs=gs, g4=g4: e.activation(out=gst[gs][:, g4, :], in_=ps, func=AF.Silu), reads=[tps], writes=[t_gst[gs]], join=True)
        if g4 == 3:
            rr = r0 - 3 * 128
            P.op("sync", lambda e, gs=gs, rr=rr, c0=c0: e.dma_start(out=C.G_d[rr:rr + 512, c0:c0 + 512].rearrange("(a p) c -> p a c", p=128), in_=gst[gs]),
                 reads=[t_gst[gs]], writes=[t_gd], dma=True, join=True)
    return pcnt, gcnt


def layer3_proj1(C, x_src, W):
    P, A = C.P, C.A
    m = A.mark()
    TB = 1024; TPB = TB // 128; NTB = S // TB
    hT = A.alloc([16, TB], BF16); hT_toks = P.toks(TPB, "hT")
    wt = [A.alloc([16, 512], BF16) for _ in range(2)]; t_wt = P.toks(2, "wt")
    wkp = A.alloc([16, 64], BF16); t_wkp = P.tok("wkp")
    cst = [A.alloc([4, TB], BF16) for _ in range(2)]; t_cst = P.toks(2, "cst")
    cf = [A.alloc([512], F32) for _ in range(4)]; t_cf = P.toks(4, "cf")
    sq = [A.alloc([512], F32) for _ in range(2)]; t_sq = P.toks(2, "sq")
    rs = A.alloc([512], F32); t_rs = P.tok("rs")
    tmp = [A.alloc([512], F32) for _ in range(3)]; t_tmp = P.toks(3, "tmp")
    kpst = A.alloc([TB], BF16); t_kpst = P.tok("kpst")
    cs = A.alloc([TB], F32); sn = A.alloc([TB], F32); t_cs = P.tok("cs")
    gst = [A.alloc([4, 512], F32) for _ in range(2)]; t_gst = P.toks(2, "gst")
    glat = A.alloc([2, 4], F32); gkp = A.alloc([1], F32)
    C.t_gd = P.tok("Gd"); t_cd = P.tok("CQd"); t_kd = P.tok("KPEd")
    tl = C.t_lconst
    for mm_ in range(4):
        load_col(C, glat[:, 0, mm_:mm_ + 1], W["d_q_lat_g"][0, mm_ * 128:(mm_ + 1) * 128], 128)
        load_col(C, glat[:, 1, mm_:mm_ + 1], W["d_kv_lat_g"][0, mm_ * 128:(mm_ + 1) * 128], 128)
    load_col(C, gkp, W["d_qk_g"][0, 1, 128:192], 64)
    wcnt = 0; pcnt = 0; gcnt = 0; scnt = 0
    for tb in range(NTB):
        tok0 = tb * TB
        phase_norm_T(C, x_src, W["norm_g"][C.layer, :], tok0, hT, hT_toks, TB)
        P.op("sync", lambda e, tok0=tok0: e.dma_start(out=cs[0:64, :], in_=C.rope_in[0, :, tok0:tok0 + TB]), writes=[t_cs], dma=True)
        P.op("sync", lambda e, tok0=tok0: e.dma_start(out=sn[0:64, :], in_=C.rope_in[1, :, tok0:tok0 + TB]), writes=[t_cs], dma=True, join=True)
        for kind in range(2):
            ws = wcnt % 2; wcnt += 1
            load_w_block(C, wt[ws], t_wt[ws], W["d_w_in"][0], kind * 512, 512)
            for tq in range(TB // 512):
                pss, tpss = C.psf[4 + C.auxcnt % 2], C.t_psf[4 + C.auxcnt % 2]
                C.auxcnt += 1
                for mm_ in range(4):
                    pb = pcnt % 4; pcnt += 1
                    ps, tps = C.psf[pb], C.t_psf[pb]
                    for k in range(16):
                        P.op("pe", lambda e, ps=ps, k=k, ws=ws, mm_=mm_, tq=tq: e.matmul(ps, lhsT=wt[ws][:, k, mm_ * 128:(mm_ + 1) * 128], rhs=hT[:, k, tq * 512:(tq + 1) * 512], start=(k == 0), stop=(k == 15)),
                             reads=[t_wt[ws]] + hT_toks[tq * 4:(tq + 1) * 4], writes=[tps], join=(k > 0))
                    P.op("act", lambda e, ps=ps, mm_=mm_: e.activation(out=cf[mm_], in_=ps, func=AF.Copy), reads=[tps], writes=[t_cf[mm_]])
                    ss = scnt % 2; scnt += 1
                    P.op("act", lambda e, ps=ps, ss=ss: e.activation(out=sq[ss], in_=ps, func=AF.Square), reads=[tps], writes=[t_sq[ss]])
                    P.op("pe", lambda e, pss=pss, ss=ss, mm_=mm_: e.matmul(pss, lhsT=C.ones128, rhs=sq[ss], start=(mm_ == 0), stop=(mm_ == 3)), reads=[t_sq[ss], C.t_const], writes=[tpss], join=(mm_ > 0))
                P.op("act", lambda e, pss=pss: e.activation(out=rs, in_=pss, func=AF.Sqrt, bias=C.epscol, scale=1.0 / 512), reads=[tpss, C.t_const], writes=[t_rs])
                P.op("dve", lambda e: e.reciprocal(out=rs, in_=rs), reads=[t_rs], writes=[t_rs])
                for mm_ in range(4):
                    P.op("dve", lambda e, mm_=mm_, kind=kind, tq=tq: e.scalar_tensor_tensor(out=cst[kind][:, mm_, tq * 512:(tq + 1) * 512], in0=cf[mm_], scalar=glat[:, kind, mm_:mm_ + 1], in1=rs, op0=ALU.mult, op1=ALU.mult),
                         reads=[t_cf[mm_], t_rs, tl], writes=[t_cst[kind]], join=True)
            dd = C.CQ_d if kind == 0 else C.CKV_d
            for mm_ in range(4):
                P.op("sync", lambda e, dd=dd, mm_=mm_, kind=kind, tok0=tok0: e.dma_start(out=dd[mm_, :, tok0:tok0 + TB], in_=cst[kind][:, mm_, :]), reads=[t_cst[kind]], writes=[t_cd], dma=True, join=True)
        load_w_block(C, wkp, t_wkp, W["d_w_in"][0], 1024, 64)
        for tq in range(TB // 512):
            pb = pcnt % 4; pcnt += 1
            ps, tps = C.psf[pb], C.t_psf[pb]
            for k in range(16):
                P.op("pe", lambda e, ps=ps, k=k, tq=tq: e.matmul(ps[0:64, :], lhsT=wkp[:, k, 0:64], rhs=hT[:, k, tq * 512:(tq + 1) * 512], start=(k == 0), stop=(k == 15)),
                     reads=[t_wkp] + hT_toks[tq * 4:(tq + 1) * 4], writes=[tps], join=(k > 0))
            rope_epilogue(C, ps, tps, gkp, cs[0:64, tq * 512:(tq + 1) * 512], sn[0:64, tq * 512:(tq + 1) * 512], t_cs, kpst[0:64, tq * 512:(tq + 1) * 512], t_kpst, tmp, t_tmp)
        P.op("sync", lambda e, tok0=tok0: e.dma_start(out=C.KPE_d[:, tok0:tok0 + TB], in_=kpst[0:64, :]), reads=[t_kpst], writes=[t_kd], dma=True, join=True)
        for cb in range(4):
            ws = wcnt % 2; wcnt += 1
            load_w_block(C, wt[ws], t_wt[ws], W["d_w_in"][0], 1088 + cb * 512, 512)
            pcnt, gcnt = g_block(C, hT, hT_toks, wt[ws], t_wt[ws], tok0, cb * 512, pcnt, gst, t_gst, gcnt, TPB)
    A.release(m)
    P.barrier()


def layer3_proj2(C, W):
    P, A = C.P, C.A
    m = A.mark()
    H = 16
    cq = A.alloc([4, TB], BF16); ckv = A.alloc([4, TB], BF16); t_cq = P.tok("cq"); t_ckv = P.tok("ckv")
    wuq = A.alloc([4, 3072], BF16); wkn = A.alloc([4, 2048], BF16); wv = A.alloc([4, 2048], BF16); t_w = P.tok("wup")
    qst = [A.alloc([TB], BF16) for _ in range(2)]; t_qst = P.toks(2, "qst")
    kst = [A.alloc([TB], BF16) for _ in range(2)]; t_kst = P.toks(2, "kst")
    qpst = [A.alloc([TB], BF16) for _ in range(2)]; t_qpst = P.toks(2, "qpst")
    tmp = [[A.alloc([512], F32) for _ in range(3)] for _ in range(2)]; t_tmp = [P.toks(3, "tmp") for _ in range(2)]
    cs = A.alloc([TB], F32); sn = A.alloc([TB], F32); t_cs = P.tok("cs")
    vst = [A.alloc([8, 512], BF16) for _ in range(2)]; t_vst = P.toks(2, "vst")
    gqn = A.alloc([1], F32); gkn = A.alloc([1], F32); gqp = A.alloc([1], F32)
    t_qd = P.tok("QTd"); t_kd = P.tok("KTd"); t_qpd = P.tok("QPEd"); t_vd = P.tok("Vd")
    sc = 192.0 ** -0.5
    load_col(C, gqn, W["d_qk_g"][0, 0, 0:128], 128, sc)
    load_col(C, gkn, W["d_qk_g"][0, 1, 0:128], 128)
    load_col(C, gqp, W["d_qk_g"][0, 0, 128:192], 64, sc)
    wq_v = W["d_w_uq"][0].rearrange("(k p) c -> p k c", p=128)
    wkv_v = W["d_w_ukv"][0].rearrange("(k p) (h c) -> p k h c", p=128, c=256)
    for k in range(4):
        P.op("pool", lambda e, k=k: e.dma_start(out=wuq[:, k, :], in_=wq_v[:, k, :]), writes=[t_w], dma=True, join=True)
        P.op("pool", lambda e, k=k: e.dma_start(out=wkn[:, k, :].rearrange("p (h c) -> p h c", c=128), in_=wkv_v[:, k, :, 0:128]), writes=[t_w], dma=True, join=True)
        P.op("pool", lambda e, k=k: e.dma_start(out=wv[:, k, :].rearrange("p (h c) -> p h c", c=128), in_=wkv_v[:, k, :, 128:256]), writes=[t_w], dma=True, join=True)
    pcnt = 0; tcnt = 0; vcnt = 0
    for tb in range(NTB):
        tok0 = tb * TB
        for k in range(4):
            P.op("sync", lambda e, k=k, tok0=tok0: e.dma_start(out=cq[:, k, :], in_=C.CQ_d[k, :, tok0:tok0 + TB]), writes=[t_cq], dma=True, join=(k > 0))
            P.op("sync", lambda e, k=k, tok0=tok0: e.dma_start(out=ckv[:, k, :], in_=C.CKV_d[k, :, tok0:tok0 + TB]), writes=[t_ckv], dma=True, join=(k > 0))
        P.op("sync", lambda e, tok0=tok0: e.dma_start(out=cs[0:64, :], in_=C.rope_in[0, :, tok0:tok0 + TB]), writes=[t_cs], dma=True)
        P.op("sync", lambda e, tok0=tok0: e.dma_start(out=sn[0:64, :], in_=C.rope_in[1, :, tok0:tok0 + TB]), writes=[t_cs], dma=True, join=True)
        for h in range(H):
            hs = h % 2
            for which in range(3):
                for tq in range(TB // 512):
                    pb = pcnt % 4; pcnt += 1
                    ps, tps = C.psf[pb], C.t_psf[pb]
                    for k in range(4):
                        if which == 0:
                            lhs, rhs_, rt, M = wuq[:, k, h * 192:h * 192 + 128], cq[:, k, tq * 512:(tq + 1) * 512], t_cq, 128
                        elif which == 1:
                            lhs, rhs_, rt, M = wkn[:, k, h * 128:(h + 1) * 128], ckv[:, k, tq * 512:(tq + 1) * 512], t_ckv, 128
                        else:
                            lhs, rhs_, rt, M = wuq[:, k, h * 192 + 128:h * 192 + 192], cq[:, k, tq * 512:(tq + 1) * 512], t_cq, 64
                        P.op("pe", lambda e, ps=ps, k=k, lhs=lhs, rhs_=rhs_, M=M: e.matmul(ps[0:M, :], lhsT=lhs, rhs=rhs_, start=(k == 0), stop=(k == 3)),
                             reads=[t_w, rt], writes=[tps], join=(k > 0))
                    ts_ = tcnt % 2; tcnt += 1
                    if which == 0:
                        qk_norm_epilogue(C, ps, tps, gqn, qst[hs][:, tq * 512:(tq + 1) * 512], t_qst[hs], tmp[ts_], t_tmp[ts_], 128)
                    elif which == 1:
                        qk_norm_epilogue(C, ps, tps, gkn, kst[hs][:, tq * 512:(tq + 1) * 512], t_kst[hs], tmp[ts_], t_tmp[ts_], 128)
                    else:
                        rope_epilogue(C, ps, tps, gqp, cs[0:64, tq * 512:(tq + 1) * 512], sn[0:64, tq * 512:(tq + 1) * 512], t_cs, qpst[hs][0:64, tq * 512:(tq + 1) * 512], t_qpst[hs], tmp[ts_], t_tmp[ts_])
            P.op("sync", lambda e, h=h, hs=hs, tok0=tok0: e.dma_start(out=C.QT_d[h, :, tok0:tok0 + TB], in_=qst[hs]), reads=[t_qst[hs]], writes=[t_qd], dma=True, join=True)
            P.op("sync", lambda e, h=h, hs=hs, tok0=tok0: e.dma_start(out=C.KT_d[h, :, tok0:tok0 + TB], in_=kst[hs]), reads=[t_kst[hs]], writes=[t_kd], dma=True, join=True)
            P.op("sync", lambda e, h=h, hs=hs, tok0=tok0: e.dma_start(out=C.QPE_d[h, :, tok0:tok0 + TB], in_=qpst[hs][0:64, :]), reads=[t_qpst[hs]], writes=[t_qpd], dma=True, join=True)
        for n in range(4):
            for i in range(TPB):
                pb = pcnt % 4; pcnt += 1
                ps, tps = C.psf[pb], C.t_psf[pb]
                for k in range(4):
                    P.op("pe", lambda e, ps=ps, k=k, i=i, n=n: e.matmul(ps, lhsT=ckv[:, k, i * 128:(i + 1) * 128], rhs=wv[:, k, n * 512:(n + 1) * 512], start=(k == 0), stop=(k == 3)),
                         reads=[t_w, t_ckv], writes=[tps], join=(k > 0))
                g8 = i % 8
                if g8 == 0:
                    vs = vcnt % 2; vcnt += 1
                P.op("dve", lambda e, ps=ps, vs=vs, g8=g8: e.tensor_copy(out=vst[vs][:, g8, :], in_=ps), reads=[tps], writes=[t_vst[vs]], join=True)
                if g8 == 7:
                    rr = tok0 + (i - 7) * 128
                    P.op("sync", lambda e, vs=vs, rr=rr, n=n: e.dma_start(out=C.V_d[rr:rr + 1024, n * 512:(n + 1) * 512].rearrange("(a p) c -> p a c", p=128), in_=vst[vs]),
                         reads=[t_vst[vs]], writes=[t_vd], dma=True, join=True)
    A.release(m)
    P.barrier()


def layer2_all(C, x_src, W):
    P, A = C.P, C.A
    m = A.mark()
    TB = 1024; TPB = TB // 128; NTB = S // TB
    hT = A.alloc([16, TB], BF16); hT_toks = P.toks(TPB, "hT")
    wt = [A.alloc([16, 512], BF16) for _ in range(2)]; t_wt = P.toks(2, "wt")
    wrg = A.alloc([8, 2, 256], BF16); wig = A.alloc([8, 2, 256], BF16); t_wg = P.tok("wg")
    stage = A.alloc([128], F32); cols = A.alloc([8, 16], F32)
    halo = A.alloc([16, 3], F32); hprev = A.alloc([16], F32); t_halo = P.tok("halo"); t_hprev = P.tok("hprev")
    ubuf = [A.alloc([TB + 8], F32) for _ in range(2)]; t_ubuf = P.toks(2, "ubuf")
    xc = [A.alloc([TB], F32) for _ in range(2)]; t_xc = P.toks(2, "xc")
    xcb = [A.alloc([TB], BF16) for _ in range(2)]; t_xcb = P.toks(2, "xcb")
    sg = [A.alloc([TB], F32) for _ in range(2)]; t_sg = P.toks(2, "sg")
    rg = [A.alloc([TB], F32) for _ in range(2)]; t_rg = P.toks(2, "rg")
    ig = [A.alloc([TB], F32) for _ in range(2)]; t_ig = P.toks(2, "ig")
    abuf = A.alloc([TB], F32); a2buf = A.alloc([TB], F32); xinb = A.alloc([TB], F32); hh = A.alloc([TB], F32)
    t_a = P.tok("a"); t_a2 = P.tok("a2"); t_xin = P.tok("xin"); t_hh = P.tok("hh")
    yst = [A.alloc([TB], BF16) for _ in range(2)]; t_yst = P.toks(2, "yst")
    t_yd = P.tok("YTd")
    tl = C.t_lconst
    vecs = [W["c_conv_w"][0, 0], W["c_conv_w"][0, 1], W["c_conv_w"][0, 2], W["c_conv_w"][0, 3], W["c_conv_b"][0], W["c_b_rgate"][0], W["c_b_igate"][0], W["c_lambda"][0]]
    for v, vec in enumerate(vecs):
        P.op("sync", lambda e, v=v, vec=vec: e.dma_start(out=stage[v * 16:(v + 1) * 16, :], in_=vec.rearrange("(t p) -> t p", p=128)), writes=[tl], dma=True, join=True)
    ps0, tps0 = C.psf[4], C.t_psf[4]
    P.op("pe", lambda e: e.matmul(ps0[:, 0:128], lhsT=stage, rhs=C.identf, start=True, stop=True), reads=[tl, C.t_const], writes=[tps0])
    P.op("dve", lambda e: e.tensor_copy(out=cols, in_=ps0[:, 0:128].rearrange("p (v t) -> p v t", v=8)), reads=[tps0], writes=[tl])
    P.op("act", lambda e: e.activation(out=cols[:, 7, :], in_=cols[:, 7, :], func=AF.Exp, scale=-1.0), reads=[tl], writes=[tl])
    P.op("act", lambda e: e.activation(out=cols[:, 7, :], in_=cols[:, 7, :], func=AF.Ln, bias=1.0, scale=1.0), reads=[tl], writes=[tl])
    P.op("dve", lambda e: e.tensor_scalar(out=cols[:, 7, :], in0=cols[:, 7, :], scalar1=-8.0, scalar2=None, op0=ALU.mult), reads=[tl], writes=[tl])
    for n in range(8):
        P.op("pool", lambda e, n=n: e.dma_start(out=wrg[:, n], in_=W["c_w_rgate"][0, n].rearrange("(c p) e -> p c e", p=128)), writes=[t_wg], dma=True, join=True)
        P.op("pool", lambda e, n=n: e.dma_start(out=wig[:, n], in_=W["c_w_igate"][0, n].rearrange("(c p) e -> p c e", p=128)), writes=[t_wg], dma=True, join=True)
    wcnt = 0; pcnt = 0
    for tb in range(NTB):
        tok0 = tb * TB
        phase_norm_T(C, x_src, W["norm_g"][C.layer, :], tok0, hT, hT_toks, TB)
        for n in range(8):
            ws = wcnt % 2; wcnt += 1
            load_w_block(C, wt[ws][:, :, 0:256], t_wt[ws], W["c_w_in"][0], n * 256, 256)
            load_w_block(C, wt[ws][:, :, 256:512], t_wt[ws], W["c_w_in"][0], 2048 + n * 256, 256)
            for c in range(2):
                tile = n * 2 + c
                if tb == 0:
                    P.op("pool", lambda e, c=c: e.memset(ubuf[c][:, 0:3], 0.0), writes=[t_ubuf[c]])
                else:
                    P.op("pool", lambda e, c=c, tile=tile: e.tensor_copy(out=ubuf[c][:, 0:3], in_=halo[:, tile, :]), reads=[t_halo], writes=[t_ubuf[c]])
                for tq in range(TB // 512):
                    pb = pcnt % 4; pcnt += 1
                    ps, tps = C.psf[pb], C.t_psf[pb]
                    for k in range(16):
                        P.op("pe", lambda e, ps=ps, k=k, ws=ws, c=c, tq=tq: e.matmul(ps, lhsT=wt[ws][:, k, c * 128:(c + 1) * 128], rhs=hT[:, k, tq * 512:(tq + 1) * 512], start=(k == 0), stop=(k == 15)),
                             reads=[t_wt[ws]] + hT_toks[tq * 4:(tq + 1) * 4], writes=[tps], join=(k > 0))
                    P.op("act", lambda e, ps=ps, c=c, tq=tq: e.activation(out=ubuf[c][:, 3 + tq * 512:3 + (tq + 1) * 512], in_=ps, func=AF.Copy), reads=[tps], writes=[t_ubuf[c]], join=True)
                P.op("pool", lambda e, c=c, tile=tile: e.tensor_copy(out=halo[:, tile, :], in_=ubuf[c][:, TB:TB + 3]), reads=[t_ubuf[c]], writes=[t_halo], join=True)
                P.op("dve", lambda e, c=c, tile=tile: e.tensor_scalar(out=xc[c], in0=ubuf[c][:, 3:3 + TB], scalar1=cols[:, 3, tile:tile + 1], scalar2=cols[:, 4, tile:tile + 1], op0=ALU.mult, op1=ALU.add),
                     reads=[t_ubuf[c], tl], writes=[t_xc[c]])
                for tau in (2, 1, 0):
                    P.op("dve", lambda e, c=c, tile=tile, tau=tau: e.scalar_tensor_tensor(out=xc[c], in0=ubuf[c][:, tau:tau + TB], scalar=cols[:, tau, tile:tile + 1], in1=xc[c], op0=ALU.mult, op1=ALU.add),
                         reads=[t_ubuf[c], tl, t_xc[c]], writes=[t_xc[c]])
                P.op("pool", lambda e, c=c: e.tensor_copy(out=xcb[c], in_=xc[c]), reads=[t_xc[c]], writes=[t_xcb[c]])
                for tq in range(TB // 512):
                    pb = pcnt % 4; pcnt += 1
                    ps, tps = C.psf[pb], C.t_psf[pb]
                    for k in range(16):
                        P.op("pe", lambda e, ps=ps, k=k, ws=ws, c=c, tq=tq: e.matmul(ps, lhsT=wt[ws][:, k, 256 + c * 128:256 + (c + 1) * 128], rhs=hT[:, k, tq * 512:(tq + 1) * 512], start=(k == 0), stop=(k == 15)),
                             reads=[t_wt[ws]] + hT_toks[tq * 4:(tq + 1) * 4], writes=[tps], join=(k > 0))
                    P.op("act", lambda e, ps=ps, c=c, tq=tq: e.activation(out=sg[c][:, tq * 512:(tq + 1) * 512], in_=ps, func=AF.Silu), reads=[tps], writes=[t_sg[c]], join=True)
            for ce in range(2):
                tile = n * 2 + ce
                for (wg, bidx, dstb, tdst) in ((wrg, 5, rg, t_rg), (wig, 6, ig, t_ig)):
                    for tq in range(TB // 512):
                        pb = pcnt % 4; pcnt += 1
                        ps, tps = C.psf[pb], C.t_psf[pb]
                        for cc in range(2):
                            P.op("pe", lambda e, ps=ps, wg=wg, cc=cc, ce=ce, tq=tq, n=n: e.matmul(ps, lhsT=wg[:, n, cc, ce * 128:(ce + 1) * 128], rhs=xcb[cc][:, tq * 512:(tq + 1) * 512], start=(cc == 0), stop=(cc == 1)),
                                 reads=[t_wg, t_xcb[cc]], writes=[tps], join=(cc > 0))
                        P.op("act", lambda e, ps=ps, dstb=dstb, ce=ce, tq=tq, bidx=bidx, tile=tile: e.activation(out=dstb[ce][:, tq * 512:(tq + 1) * 512], in_=ps, func=AF.Sigmoid, bias=cols[:, bidx, tile:tile + 1], scale=1.0),
                             reads=[tps, tl], writes=[tdst[ce]], join=True)
                P.op("act", lambda e, ce=ce, tile=tile: e.activation(out=abuf, in_=rg[ce], func=AF.Exp, scale=cols[:, 7, tile:tile + 1]), reads=[t_rg[ce], tl], writes=[t_a])
                P.op("pool", lambda e: e.tensor_tensor(out=a2buf, in0=abuf, in1=abuf, op=ALU.mult), reads=[t_a], writes=[t_a2])
                P.op("act", lambda e: e.activation(out=a2buf, in_=a2buf, func=AF.Sqrt, bias=1.0, scale=-1.0), reads=[t_a2], writes=[t_a2])
                P.op("pool", lambda e, ce=ce: e.tensor_tensor(out=xinb, in0=ig[ce], in1=xc[ce], op=ALU.mult), reads=[t_ig[ce], t_xc[ce]], writes=[t_xin])
                P.op("dve", lambda e: e.tensor_tensor(out=xinb, in0=xinb, in1=a2buf, op=ALU.mult), reads=[t_xin, t_a2], writes=[t_xin])
                init = 0.0 if tb == 0 else hprev[:, tile:tile + 1]
                P.op("dve", lambda e, init=init: e.tensor_tensor_scan(out=hh, data0=abuf, data1=xinb, initial=init, op0=ALU.mult, op1=ALU.add), reads=[t_a, t_xin, t_hprev], writes=[t_hh])
                P.op("pool", lambda e, tile=tile: e.tensor_copy(out=hprev[:, tile:tile + 1], in_=hh[:, TB - 1:TB]), reads=[t_hh], writes=[t_hprev])
                P.op("pool", lambda e, ce=ce: e.tensor_tensor(out=yst[ce], in0=hh, in1=sg[ce], op=ALU.mult), reads=[t_hh, t_sg[ce]], writes=[t_yst[ce]])
                P.op("sync", lambda e, ce=ce, tile=tile, tok0=tok0: e.dma_start(out=C.YT_d[tile, :, tok0:tok0 + TB], in_=yst[ce]), reads=[t_yst[ce]], writes=[t_yd], dma=True, join=True)
    A.release(m)
    P.barrier()


def layer1_all(C, x_src, W):
    P, A = C.P, C.A
    m0 = A.mark()
    dec = A.alloc([8, 64], F32); t_dec = P.tok("dec")
    m = A.mark()
    TB = 1024; TPB = TB // 128; NTB = S // TB
    hT = A.alloc([16, TB], BF16); hT_toks = P.toks(TPB, "hT")
    wt = [A.alloc([16, 512], BF16) for _ in range(2)]; t_wt = P.toks(2, "wt")
    wlr = A.alloc([16, 16], BF16); t_wlr = P.tok("wlr")
    lrT = A.alloc([TB], F32); t_lrT = P.tok("lrT")
    wga = A.alloc([1024], F32)
    TU = A.alloc([128], F32); CI = A.alloc([2], F32)
    qst = [A.alloc([TB], BF16) for _ in range(2)]; t_qst = P.toks(2, "qst")
    ebuf = [A.alloc([512], F32) for _ in range(2)]; t_eb = P.toks(2, "ebuf")
    wbuf = [A.alloc([512], F32) for _ in range(2)]; t_wb = P.toks(2, "wbuf")
    kst = [A.alloc([8, 512], BF16) for _ in range(2)]; t_kst = P.toks(2, "kst")
    vst = [A.alloc([8, 512], BF16) for _ in range(2)]; t_vst = P.toks(2, "vst")
    gst = [A.alloc([4, 512], F32) for _ in range(2)]; t_gst = P.toks(2, "gst")
    C.t_gd = P.tok("Gd"); t_qd = P.tok("QTd"); t_kd = P.tok("KPd"); t_vd = P.tok("Vd")
    tl = C.t_lconst
    Wi = W["b_w_in"][0]
    P.op("pool", lambda e: e.memset(wga[0:32, :], 0.0), writes=[tl])
    P.op("sync", lambda e: e.dma_start(out=wga[0:16, :], in_=W["b_w_gate"][0]), writes=[tl], dma=True)
    P.op("sync", lambda e: e.dma_start(out=wga[16:17, :], in_=W["b_gate_bias"][0:1, :]), writes=[tl], dma=True, join=True)
    P.op("sync", lambda e: e.dma_start(out=TU, in_=C.TU_d), writes=[tl], dma=True, join=True)
    P.op("sync", lambda e: e.dma_start(out=CI, in_=C.CI_d), writes=[tl], dma=True, join=True)
    P.op("pool", lambda e: e.memset(lrT[0:32, :], 1.0), writes=[t_lrT])
    wcnt = 0; pcnt = 0; gcnt = 0; vcnt = 0; qcnt = 0; ecnt = 0; kcnt = 0
    for tb in range(NTB):
        tok0 = tb * TB
        phase_norm_T(C, x_src, W["norm_g"][C.layer, :], tok0, hT, hT_toks, TB)
        load_w_block(C, wlr, t_wlr, Wi, 6144, 16)
        for tq in range(TB // 512):
            pb = pcnt % 4; pcnt += 1
            ps, tps = C.psf[pb], C.t_psf[pb]
            for k in range(16):
                P.op("pe", lambda e, ps=ps, k=k, tq=tq: e.matmul(ps[0:16, :], lhsT=wlr[:, k, 0:16], rhs=hT[:, k, tq * 512:(tq + 1) * 512], start=(k == 0), stop=(k == 15)),
                     reads=[t_wlr] + hT_toks[tq * 4:(tq + 1) * 4], writes=[tps], join=(k > 0))
            P.op("act", lambda e, ps=ps, tq=tq: e.activation(out=lrT[0:16, tq * 512:(tq + 1) * 512], in_=ps[0:16, :], func=AF.Copy), reads=[tps], writes=[t_lrT], join=True)
        for cb in range(12):
            ws = wcnt % 2; wcnt += 1
            load_w_block(C, wt[ws], t_wt[ws], Wi, cb * 512, 512)
            if cb < 2:
                for mm_ in range(4):
                    qt = cb * 4 + mm_
                    qs = qcnt % 2; qcnt += 1
                    for tq in range(TB // 512):
                        pb = pcnt % 4; pcnt += 1
                        ps, tps = C.psf[pb], C.t_psf[pb]
                        for k in range(16):
                            P.op("pe", lambda e, ps=ps, k=k, ws=ws, mm_=mm_, tq=tq: e.matmul(ps, lhsT=wt[ws][:, k, mm_ * 128:(mm_ + 1) * 128], rhs=hT[:, k, tq * 512:(tq + 1) * 512], start=(k == 0), stop=(k == 15)),
                                 reads=[t_wt[ws]] + hT_toks[tq * 4:(tq + 1) * 4], writes=[tps], join=(k > 0))
                        P.op("act", lambda e, ps=ps, qs=qs, tq=tq: e.activation(out=qst[qs][:, tq * 512:(tq + 1) * 512], in_=ps, func=AF.Copy, scale=1.0 / 16.0), reads=[tps], writes=[t_qst[qs]], join=True)
                    P.op("sync", lambda e, qt=qt, qs=qs, tok0=tok0: e.dma_start(out=C.QT_d[qt, :, tok0:tok0 + TB], in_=qst[qs]), reads=[t_qst[qs]], writes=[t_qd], dma=True, join=True)
            elif cb < 4:
                kb = cb - 2
                ks = kcnt % 2; kcnt += 1
                for i in range(TPB):
                    pb = pcnt % 4; pcnt += 1
                    ps, tps = C.psf[pb], C.t_psf[pb]
                    for k in range(16):
                        P.op("pe", lambda e, ps=ps, k=k, ws=ws, i=i: e.matmul(ps, lhsT=hT[:, k, i * 128:(i + 1) * 128], rhs=wt[ws][:, k, :], start=(k == 0), stop=(k == 15)),
                             reads=[t_wt[ws], hT_toks[i]], writes=[tps], join=(k > 0))
                    es = ecnt % 2; ecnt += 1
                    pz, tpz = C.psf[4], C.t_psf[4]
                    P.op("pe", lambda e, pz=pz, i=i, kb=kb: e.matmul(pz, lhsT=lrT[0:32, i * 128:(i + 1) * 128], rhs=wga[0:32, kb * 512:(kb + 1) * 512], start=True, stop=True),
                         reads=[t_lrT, tl], writes=[tpz])
                    P.op("act", lambda e, pz=pz, es=es: e.activation(out=ebuf[es], in_=pz, func=AF.Exp, scale=-1.0), reads=[tpz], writes=[t_eb[es]])
                    P.op("act", lambda e, es=es: e.activation(out=ebuf[es], in_=ebuf[es], func=AF.Ln, bias=1.0, scale=1.0), reads=[t_eb[es]], writes=[t_eb[es]])
                    pr_, tpr = C.psf[5], C.t_psf[5]
                    P.op("pe", lambda e, pr_=pr_, es=es: e.matmul(pr_, lhsT=TU, rhs=ebuf[es], start=True, stop=True), reads=[t_eb[es], tl], writes=[tpr])
                    P.op("act", lambda e, pr_=pr_, es=es: e.activation(out=wbuf[es], in_=pr_, func=AF.Exp), reads=[tpr], writes=[t_wb[es]])
                    P.op("dve", lambda e, ps=ps, es=es, ks=ks, i=i: e.tensor_tensor(out=kst[ks][:, i, :], in0=ps, in1=wbuf[es], op=ALU.mult), reads=[tps, t_wb[es]], writes=[t_kst[ks]], join=True)
                    for dt in range(4):
                        P.op("pe", lambda e, pz=pz, es=es, dt=dt: e.matmul(pz[:, dt * 2:dt * 2 + 2], lhsT=ebuf[es][:, dt * 128:(dt + 1) * 128], rhs=CI, start=(dt == 0), stop=(dt == 3), skip_group_check=True),
                             reads=[t_eb[es], tl], writes=[tpz], join=(dt > 0))
                    ch0 = (tok0 + i * 128) // 64
                    P.op("act", lambda e, pz=pz, kb=kb, ch0=ch0: e.activation(out=dec[:, kb * 4:(kb + 1) * 4, ch0:ch0 + 2], in_=pz[:, 0:8].rearrange("p (a b) -> p a b", a=4), func=AF.Exp), reads=[tpz], writes=[t_dec], join=True)
                P.op("sync", lambda e, ks=ks, tok0=tok0, kb=kb: e.dma_start(out=C.KP_d[tok0:tok0 + TB, kb * 512:(kb + 1) * 512].rearrange("(a p) c -> p a c", p=128), in_=kst[ks]),
                     reads=[t_kst[ks]], writes=[t_kd], dma=True, join=True)
            elif cb < 8:
                c0 = (cb - 4) * 512
                vs = vcnt % 2; vcnt += 1
                for i in range(TPB):
                    pb = pcnt % 4; pcnt += 1
                    ps, tps = C.psf[pb], C.t_psf[pb]
                    for k in range(16):
                        P.op("pe", lambda e, ps=ps, k=k, ws=ws, i=i: e.matmul(ps, lhsT=hT[:, k, i * 128:(i + 1) * 128], rhs=wt[ws][:, k, :], start=(k == 0), stop=(k == 15)),
                             reads=[t_wt[ws], hT_toks[i]], writes=[tps], join=(k > 0))
                    P.op("dve", lambda e, ps=ps, vs=vs, i=i: e.tensor_copy(out=vst[vs][:, i, :], in_=ps), reads=[tps], writes=[t_vst[vs]], join=True)
                P.op("sync", lambda e, vs=vs, tok0=tok0, c0=c0: e.dma_start(out=C.V_d[tok0:tok0 + TB, c0:c0 + 512].rearrange("(a p) c -> p a c", p=128), in_=vst[vs]),
                     reads=[t_vst[vs]], writes=[t_vd], dma=True, join=True)
            else:
                pcnt, gcnt = g_block(C, hT, hT_toks, wt[ws], t_wt[ws], tok0, (cb - 8) * 512, pcnt, gst, t_gst, gcnt, TPB)
    A.release(m)
    P.barrier()
    m = A.mark()
    Kp = A.alloc([NT, 256], BF16); t_Kp = P.tok("Kp")
    Vh = A.alloc([NT, 512], BF16); t_Vh = P.tok("Vh")
    QTt = [A.alloc([S], BF16) for _ in range(2)]; t_QTt = P.tok("QTt")
    Sf = A.alloc([2, 512], F32); t_Sf = P.tok("Sf")
    Sb = [A.alloc([2, 512], BF16) for _ in range(2)]; t_Sb = P.toks(2, "Sb")
    Gt = [A.alloc([2, 512], F32) for _ in range(2)]; t_Gt = P.toks(2, "Gt")
    yT = A.alloc([4, S], BF16); t_yT = P.tok("yT")
    ogb = A.alloc([512], F32)
    of = A.alloc([512], F32); yb = A.alloc([512], BF16); ssq = A.alloc([1], F32); junk = A.alloc([512], BF16)
    t_ep = P.tok("ep"); t_yd = P.tok("YTd")
    P.op("sync", lambda e: e.dma_start(out=ogb, in_=W["b_out_g"][0, :].partition_broadcast(128)), writes=[tl], dma=True)

    def emit_kv(hh, c):
        i, par = c // 2, c % 2
        pr0 = par * 64
        for dh in range(2):
            kv, tkv = C.psf[2 * (c % 2) + dh], C.t_psf[2 * (c % 2) + dh]
            P.op("pe", lambda e, kv=kv, i=i, pr0=pr0, dh=dh: e.matmul(kv, lhsT=Kp[pr0:pr0 + 64, i, dh * 128:(dh + 1) * 128], rhs=Vh[pr0:pr0 + 64, i, :], start=True, stop=True),
                 reads=[t_Kp, t_Vh], writes=[tkv])

    for hh in range(4):
        P.op("sync", lambda e, hh=hh: e.dma_start(out=Kp, in_=C.KP_d[:, hh * 256:(hh + 1) * 256].rearrange("(a p) c -> p a c", p=128)), writes=[t_Kp], dma=True)
        P.op("sync", lambda e, hh=hh: e.dma_start(out=Vh, in_=C.V_d[:, hh * 512:(hh + 1) * 512].rearrange("(a p) c -> p a c", p=128)), writes=[t_Vh], dma=True)
        for dh in range(2):
            P.op("sync", lambda e, hh=hh, dh=dh: e.dma_start(out=QTt[dh], in_=C.QT_d[2 * hh + dh]), writes=[t_QTt], dma=True, join=(dh > 0))
        P.op("pool", lambda e: e.memset(Sf, 0.0), writes=[t_Sf])
        emit_kv(hh, 0)
        for c in range(S // 64):
            i, par = c // 2, c % 2
            if par == 0:
                gs = i % 2
                P.op("sync", lambda e, gs=gs, i=i, hh=hh: e.dma_start(out=Gt[gs][0:64, :, :], in_=C.G_d[i * 128:(i + 1) * 128, hh * 512:(hh + 1) * 512].rearrange("(par p) c -> p par c", p=64)), writes=[t_Gt[gs]], dma=True)
            sbs = c % 2
            for dh in range(2):
                kv, tkv = C.psf[2 * (c % 2) + dh], C.t_psf[2 * (c % 2) + dh]
                P.op("dve", lambda e, kv=kv, dh=dh, hh=hh, c=c: e.scalar_tensor_tensor(out=Sf[:, dh, :], in0=Sf[:, dh, :], scalar=dec[:, 2 * hh + dh, c:c + 1], in1=kv, op0=ALU.mult, op1=ALU.add),
                     reads=[t_Sf, t_dec, tkv], writes=[t_Sf])
                P.op("act", lambda e, dh=dh, sbs=sbs: e.activation(out=Sb[sbs][:, dh, :], in_=Sf[:, dh, :], func=AF.Copy), reads=[t_Sf], writes=[t_Sb[sbs]], join=(dh > 0))
            if c + 1 < S // 64:
                emit_kv(hh, c + 1)
            po, tpo = C.psf[4 + c % 2], C.t_psf[4 + c % 2]
            for dh in range(2):
                P.op("pe", lambda e, po=po, dh=dh, c=c, sbs=sbs: e.matmul(po[0:64, :], lhsT=QTt[dh][:, c * 64:(c + 1) * 64], rhs=Sb[sbs][:, dh, :], start=(dh == 0), stop=(dh == 1)),
                     reads=[t_QTt, t_Sb[sbs]], writes=[tpo], join=(dh > 0))
            P.op("act", lambda e, po=po: e.activation(out=junk[0:64, :], in_=po[0:64, :], func=AF.Square, accum_out=ssq[0:64, :]), reads=[tpo], writes=[t_ep])
            P.op("act", lambda e: e.activation(out=ssq[0:64, :], in_=ssq[0:64, :], func=AF.Sqrt, bias=C.epscol[0:64, :], scale=1.0 / 512), reads=[t_ep, C.t_const], writes=[t_ep])
            P.op("dve", lambda e: e.reciprocal(out=ssq[0:64, :], in_=ssq[0:64, :]), reads=[t_ep], writes=[t_ep])
            P.op("dve", lambda e, po=po: e.scalar_tensor_tensor(out=of[0:64, :], in0=po[0:64, :], scalar=ssq[0:64, :], in1=ogb[0:64, :], op0=ALU.mult, op1=ALU.mult), reads=[tpo, t_ep, tl], writes=[t_ep])
            P.op("pool", lambda e, gs=gs, par=par: e.tensor_tensor(out=yb[0:64, :], in0=of[0:64, :], in1=Gt[gs][0:64, par, :], op=ALU.mult), reads=[t_ep, t_Gt[gs]], writes=[t_ep])
            tp, ttp = C.tpb[c % 2], C.t_tpb[c % 2]
            for j in range(4):
                P.op("pe", lambda e, tp=tp, j=j: e.transpose(out=tp[:, j * 64:(j + 1) * 64], in_=yb[0:64, j * 128:(j + 1) * 128], identity=C.ident[0:64, 0:64]), reads=[t_ep, C.t_const], writes=[ttp], join=True)
            P.op("dve", lambda e, tp=tp, c=c: e.tensor_copy(out=yT[:, :, c * 64:(c + 1) * 64], in_=tp[:, 0:256].rearrange("p (a b) -> p a b", a=4)), reads=[ttp], writes=[t_yT], join=True)
        for j in range(4):
            P.op("sync", lambda e, hh=hh, j=j: e.dma_start(out=C.YT_d[hh * 4 + j], in_=yT[:, j, :]), reads=[t_yT], writes=[t_yd], dma=True, join=True)
    A.release(m0)
    P.barrier()


WSHAPES = {
    "norm_g": (4, 2048), "rel_bias": (32, 16),
    "a_w_in": (1, 2048, 8192), "a_qk_g": (1, 2, 64), "a_lambda": (1, 4, 64), "a_subln_g": (1, 128), "a_w_out": (1, 2048, 2048),
    "b_w_in": (1, 2048, 6160), "b_w_gate": (1, 16, 1024), "b_gate_bias": (1, 1024), "b_out_g": (1, 512), "b_w_out": (1, 2048, 2048),
    "c_w_in": (1, 2048, 4096), "c_conv_w": (1, 4, 2048), "c_conv_b": (1, 2048), "c_w_rgate": (1, 8, 256, 256), "c_b_rgate": (1, 2048),
    "c_w_igate": (1, 8, 256, 256), "c_b_igate": (1, 2048), "c_lambda": (1, 2048), "c_w_out": (1, 2048, 2048),
    "d_w_in": (1, 2048, 3136), "d_q_lat_g": (1, 512), "d_kv_lat_g": (1, 512), "d_w_uq": (1, 512, 3072), "d_w_ukv": (1, 512, 4096),
    "d_qk_g": (1, 2, 192), "d_w_out": (1, 2048, 2048),
}
LAYER_W = {
    0: ["norm_g", "rel_bias", "a_w_in", "a_qk_g", "a_lambda", "a_subln_g", "a_w_out"],
    1: ["norm_g", "b_w_in", "b_w_gate", "b_gate_bias", "b_out_g", "b_w_out"],
    2: ["norm_g", "c_w_in", "c_conv_w", "c_conv_b", "c_w_rgate", "c_b_rgate", "c_w_igate", "c_b_igate", "c_lambda", "c_w_out"],
    3: ["norm_g", "d_w_in", "d_q_lat_g", "d_kv_lat_g", "d_w_uq", "d_w_ukv", "d_qk_g", "d_w_out"],
}


def t5_bucket_np(rel):
    nb = 16; max_exact = 8
    ret = np.where(rel > 0, nb, 0)
    n = np.abs(rel)
    nf = np.maximum(n, 1).astype(np.float32)
    large = max_exact + (np.log(nf / max_exact) / math.log(128 / max_exact) * (nb - max_exact)).astype(np.int32)
    large = np.minimum(large, nb - 1)
    return ret + np.where(n < max_exact, n, large)


def bias_index_tiles():
    k = np.arange(128)[:, None]; q = np.arange(128)[None, :]
    idx = np.zeros((128, 2, 128), np.int64); msk = np.zeros((128, 2, 128), bool)
    idx[:, 0, :] = t5_bucket_np(k - q)
    msk[:, 0, :] = (k // 64) > (q // 64)
    idx[:, 1, :] = t5_bucket_np(k - q - 128)
    return idx, msk


def rope_tables():
    half = 32
    inv = (np.float32(10000.0) ** (-np.arange(half, dtype=np.float32) / np.float32(half))).astype(np.float32)
    ang = (np.arange(S, dtype=np.float32)[:, None] * inv[None, :]).astype(np.float32)
    c = np.cos(ang).astype(np.float32).T; s_ = np.sin(ang).astype(np.float32).T
    return np.ascontiguousarray(np.stack([np.concatenate([c, c], 0), np.concatenate([s_, s_], 0)], 0))


def build_program(layers, debug=False):
    nc = bass.Bass("TRN2", target_bir_lowering=False)
    C = Ctx()
    C.nc = nc
    x_in = dram(nc, "x", [S, D], F32, "ExternalInput")
    out = dram(nc, "out", [S, D], F32, "ExternalOutput")
    names = []
    for l in layers:
        for n in LAYER_W[l]:
            if n not in names:
                names.append(n)
    W = {n: dram(nc, n, WSHAPES[n], F32, "ExternalInput") for n in names}
    if 0 in layers:
        C.biasT_in = dram(nc, "biasT", [128, 16, 2, 128], F32, "ExternalInput")
    ident_d = nc.inline_tensor(np.eye(128, dtype=np.float32), "ident_c").ap()
    o64 = np.zeros((128, 128), np.float32); o64[:64, :64] = 1; o64[64:, 64:] = 1
    ones64_d = nc.inline_tensor(o64, "ones64_c").ap()
    ones128_d = nc.inline_tensor(np.ones((128, 128), np.float32), "ones128_c").ap()
    xs = [dram(nc, f"xs{i}", [S, D], F32) for i in range(2)]
    sk = "ExternalOutput" if debug else "Internal"
    C.QT_d = dram(nc, "QT_d", [16, 128, S], BF16, sk)
    C.KT_d = dram(nc, "KT_d", [16, 128, S], BF16, sk)
    C.V_d = dram(nc, "V_d", [S, D], BF16, sk)
    C.G_d = dram(nc, "G_d", [S, D], F32, sk)
    C.YT_d = dram(nc, "YT_d", [16, 128, S], BF16, sk)
    if 3 in layers:
        C.QPE_d = dram(nc, "QPE_d", [16, 64, S], BF16, sk)
        C.KPE_d = dram(nc, "KPE_d", [64, S], BF16, sk)
        C.CQ_d = dram(nc, "CQ_d", [4, 128, S], BF16, sk)
        C.CKV_d = dram(nc, "CKV_d", [4, 128, S], BF16, sk)
        C.rope_in = dram(nc, "rope_cs", [2, 64, S], F32, "ExternalInput")
        kk = np.arange(128)[:, None]; qq = np.arange(128)[None, :]
        C.maskT_d = nc.inline_tensor(np.where((kk // 64) > (qq // 64), -30000.0, 0.0).astype(np.float32), "maskT_c").ap()
    if 1 in layers:
        C.KP_d = dram(nc, "KP_d", [S, 1024], BF16, sk)
        tt = np.arange(128)
        tu = np.where((tt[:, None] > tt[None, :]) & (tt[:, None] // 64 == tt[None, :] // 64), -1.0 / 16.0, 0.0).astype(np.float32)
        ci = np.where(tt[:, None] // 64 == np.arange(2)[None, :], -1.0 / 16.0, 0.0).astype(np.float32)
        C.TU_d = nc.inline_tensor(tu, "TU_c").ap()
        C.CI_d = nc.inline_tensor(ci, "CI_c").ap()
    rot = np.zeros((128, 128), np.float32)
    for i_ in range(32):
        rot[32 + i_, i_] = -1.0; rot[i_, 32 + i_] = 1.0
    rotm_d = nc.inline_tensor(rot, "rotm_c").ap()
    with ExitStack() as ctx:
        P = Prog(nc, ctx)
        C.P = P
        A = Arena(nc, ctx, 200)
        C.A = A
        C.psf = [ctx.enter_context(nc.psum_tensor(f"psf{i}", [128, 512], F32)) for i in range(6)]
        C.tpb = [ctx.enter_context(nc.psum_tensor(f"tpb{i}", [128, 1024], BF16)) for i in range(2)]
        C.psf = [p[:] for p in C.psf]; C.tpb = [p[:] for p in C.tpb]
        C.t_psf = P.toks(6, "psf"); C.t_tpb = P.toks(2, "tpb")
        C.auxcnt = 0
        C.t_const = P.tok("const"); C.t_lconst = P.tok("lconst")
        C.ident = A.alloc([128], BF16); C.ones64 = A.alloc([128], F32); C.ones128 = A.alloc([128], F32); C.epscol = A.alloc([1], F32)
        C.rotm = A.alloc([128], F32); C.identf = A.alloc([128], F32)
        P.op("sync", lambda e: e.dma_start(out=C.identf, in_=ident_d), writes=[C.t_const], dma=True, join=True)
        P.op("sync", lambda e: e.dma_start(out=C.rotm, in_=rotm_d), writes=[C.t_const], dma=True, join=True)
        P.op("pool", lambda e: e.dma_start(out=C.ident, in_=ident_d), writes=[C.t_const], dma=True, join=True)
        P.op("sync", lambda e: e.dma_start(out=C.ones64, in_=ones64_d), writes=[C.t_const], dma=True, join=True)
        P.op("sync", lambda e: e.dma_start(out=C.ones128, in_=ones128_d), writes=[C.t_const], dma=True, join=True)
        P.op("dve", lambda e: e.memset(C.epscol, EPS), writes=[C.t_const], join=True)
        P.barrier()
        cur = x_in
        for li, l in enumerate(layers):
            C.layer = l
            dst = out if li == len(layers) - 1 else xs[li % 2]
            if l == 0:
                layer0_proj(C, cur, W)
                attn_phase(C, W, "diff")
                phase_outproj(C, C.YT_d, W["a_w_out"][0], cur, dst)
            elif l == 1:
                layer1_all(C, cur, W)
                phase_outproj(C, C.YT_d, W["b_w_out"][0], cur, dst)
            elif l == 2:
                layer2_all(C, cur, W)
                phase_outproj(C, C.YT_d, W["c_w_out"][0], cur, dst)
            elif l == 3:
                layer3_proj1(C, cur, W)
                layer3_proj2(C, W)
                attn_phase(C, W, "mla")
                phase_outproj(C, C.YT_d, W["d_w_out"][0], cur, dst)
            else:
                raise NotImplementedError
            cur = dst
        P.emit()
    C.names = names
    return nc, C


_CACHE = {}


def run_layers(layers, x, inputs, n_cores=4):
    key = tuple(layers)
    if key not in _CACHE:
        _CACHE[key] = build_program(layers)
    nc, C = _CACHE[key]
    shared = {n: np.ascontiguousarray(inputs[n], dtype=np.float32) for n in C.names}
    if 0 in layers:
        idx, msk = bias_index_tiles()
        rb = np.asarray(inputs["rel_bias"], np.float32)
        bt = rb[idx]
        bt = np.where(msk[..., None], np.float32(-30000.0), bt)
        shared["biasT"] = np.ascontiguousarray(bt.transpose(0, 3, 1, 2))
    if 3 in layers:
        shared["rope_cs"] = rope_tables()
    in_maps = [dict(shared, x=np.ascontiguousarray(x[b])) for b in range(n_cores)]
    res = run_bass_kernel_spmd(nc, in_maps, core_ids=list(range(n_cores)))
    C.last_res = res
    return np.stack([r["out"] for r in res.results], axis=0)


def kernel(**inputs):
    x = np.asarray(inputs["x"], np.float32)
    return run_layers([0, 1, 2, 3], x, inputs)
```

```python
import math
from contextlib import ExitStack
import numpy as np
import concourse.bass as bass
import concourse.mybir as mybir
from concourse.bass_utils import run_bass_kernel_spmd

F32 = mybir.dt.float32
BF16 = mybir.dt.bfloat16
AF = mybir.ActivationFunctionType
ALU = mybir.AluOpType
AX = mybir.AxisListType

ENGS = ("sync", "act", "pool", "dve", "pe")


class Tok:
    __slots__ = ("name", "w", "r", "pool")

    def __init__(self, name):
        self.name = name
        self.w = {}
        self.r = {}
        self.pool = None


class Prog:
    def __init__(self, nc, ctx, same_engine_sync=("act", "dve", "pool")):
        self.nc = nc
        self.ctx = ctx
        self.q = {e: [] for e in ENGS}
        self.cnt = {e: 0 for e in ENGS}
        self.waited = {e: {} for e in ENGS}
        self.pend = {e: {} for e in ENGS}
        self.needed = set()
        self.same_sync = set(same_engine_sync)
        self.pool_val = []
        self.pool_free = []
        self.live = []
        self.all_toks = []

    def tok(self, name="t"):
        t = Tok(name)
        self.all_toks.append(t)
        return t

    def toks(self, n, name="t"):
        return [self.tok(f"{name}{i}") for i in range(n)]

    def op(self, eng, fn, reads=(), writes=(), dma=False, join=False):
        deps = dict(self.pend[eng])
        self.pend[eng] = {}

        def add(k, v):
            if deps.get(k, 0) < v:
                deps[k] = v

        for t in reads:
            for k, v in t.w.items():
                add(k, v)
        for t in writes:
            if not (join and not t.r):
                for k, v in t.w.items():
                    add(k, v)
            for k, v in t.r.items():
                add(k, v)
        waits = []
        wd = self.waited[eng]
        for k, v in deps.items():
            if k == ("e", eng) and (eng not in self.same_sync or self.cnt[eng] + 1 - v >= 3):
                continue
            if wd.get(k, 0) < v:
                wd[k] = v
                waits.append((k, v))
                self.needed.add((k, v))
        if dma:
            assert len(writes) == 1
            t = writes[0]
            if t.pool is None:
                if self.pool_free:
                    t.pool = self.pool_free.pop()
                else:
                    self.pool_val.append(0)
                    t.pool = len(self.pool_val) - 1
                self.live.append(t)
            self.pool_val[t.pool] += 16
            h = (("s", t.pool), self.pool_val[t.pool])
        else:
            self.cnt[eng] += 1
            h = (("e", eng), self.cnt[eng])
        k, v = h
        for t in reads:
            if t.r.get(k, 0) < v:
                t.r[k] = v
        for t in writes:
            if join and not t.r:
                t.w[k] = v
            else:
                t.w = {k: v}
                t.r = {}
        self.q[eng].append((waits, fn, h, dma))
        return h

    def barrier(self):
        hs = {}
        for e in ENGS:
            if self.cnt[e] > 0:
                hs[("e", e)] = self.cnt[e]
        for i, v in enumerate(self.pool_val):
            if v > 0:
                hs[("s", i)] = v
        for e in ENGS:
            for k, v in hs.items():
                if self.pend[e].get(k, 0) < v:
                    self.pend[e][k] = v
        for t in self.live:
            self.pool_free.append(t.pool)
            t.pool = None
        self.live = []
        for t in self.all_toks:
            t.w = {}
            t.r = {}
            t.pool = None

    def emit(self):
        nc = self.nc
        self.barrier()
        fw = []
        wd = self.waited["sync"]
        for k, v in self.pend["sync"].items():
            if wd.get(k, 0) < v:
                fw.append((k, v))
                self.needed.add((k, v))
        esem = {e: self.ctx.enter_context(nc.semaphore(f"sem_{e}")) for e in ENGS}
        psem = [self.ctx.enter_context(nc.semaphore(f"dsem{i}")) for i in range(len(self.pool_val))]
        rank = {}
        for e in ENGS:
            r = 0
            for (_w, _f, h, dma) in self.q[e]:
                if not dma and h in self.needed:
                    r += 1
                    rank[h] = r

        def resolve(h):
            k, v = h
            if k[0] == "e":
                return esem[k[1]], rank[h]
            return psem[k[1]], v

        block = self.ctx.enter_context(nc.Block())
        names = {"sync": "sync", "act": "scalar", "pool": "gpsimd", "dve": "vector", "pe": "tensor"}
        for e in ENGS:
            ops = self.q[e]
            if not ops and e != "sync":
                continue

            def body(eng, ops=ops, e=e):
                for (waits, fn, h, dma) in ops:
                    for w in waits:
                        s, v = resolve(w)
                        eng.wait_ge(s, v)
                    ins = fn(eng)
                    if dma:
                        s, v = resolve(h)
                        ins.then_inc(s, 16)
                    elif h in self.needed:
                        s, v = resolve(h)
                        ins.then_inc(s, 1)
                if e == "sync":
                    for w in fw:
                        s, v = resolve(w)
                        eng.wait_ge(s, v)

            getattr(block, names[e])(body)
        self.n_sems = len(psem) + 5


class Arena:
    def __init__(self, nc, ctx, kib):
        self.n32 = kib * 256
        self.t = ctx.enter_context(nc.sbuf_tensor("arena", [128, self.n32], F32))
        self.off = 0

    def mark(self):
        return self.off

    def release(self, m):
        self.off = m

    def alloc(self, shape, dt, parts=128):
        n = int(np.prod(shape))
        n32 = n if dt == F32 else (n + 1) // 2
        n32 = (n32 + 7) // 8 * 8
        assert self.off + n32 <= self.n32, f"arena overflow {self.off + n32} > {self.n32}"
        v = self.t[0:parts, self.off:self.off + n32]
        self.off += n32
        if dt != F32:
            v = v.bitcast(dt)
        v = v[:, 0:n]
        if len(shape) == 2:
            v = v.rearrange("p (a b) -> p a b", a=shape[0])
        elif len(shape) == 3:
            v = v.rearrange("p (a b c) -> p a b c", a=shape[0], b=shape[1])
        return v


S = 4096
D = 2048
EPS = 1e-6
NT = S // 128
TB = 2048
NTB = S // TB
TPB = TB // 128


class Ctx:
    pass


def dram(nc, name, shape, dt, kind="Internal"):
    return nc.dram_tensor(name, list(shape), dt, kind=kind).ap()


def load_w_block(C, dst, dtok, wsrc, c0, ncols, kc=16, rows0=0):
    P = C.P
    wv = wsrc[rows0:rows0 + kc * 128, :].rearrange("(k p) c -> p k c", p=128)
    step = 4 if kc >= 4 else kc
    for k0 in range(0, kc, step):
        k1 = min(kc, k0 + step)
        P.op("pool", lambda e, k0=k0, k1=k1: e.dma_start(out=dst[:, k0:k1, 0:ncols], in_=wv[:, k0:k1, c0:c0 + ncols]),
             writes=[dtok], dma=True, join=True)


def phase_norm_T(C, x_src, g_row, tok0, hT, hT_toks, tbsz=TB):
    P, A = C.P, C.A
    m = A.mark()
    xin = [A.alloc([D], F32) for _ in range(2)]
    hb = [A.alloc([D], BF16) for _ in range(2)]
    gb = A.alloc([D], F32)
    ssq = [A.alloc([1], F32) for _ in range(2)]
    t_xin = P.toks(2, "xin"); t_hb = P.toks(2, "hb"); t_gb = P.tok("gb"); t_ssq = P.toks(2, "ssq")
    P.op("sync", lambda e: e.dma_start(out=gb, in_=g_row.partition_broadcast(128)), writes=[t_gb], dma=True)
    for i in range(tbsz // 128):
        s = i % 2
        r0 = tok0 + i * 128
        P.op("sync", lambda e, s=s, r0=r0: e.dma_start(out=xin[s], in_=x_src[r0:r0 + 128, :]), writes=[t_xin[s]], dma=True)
        P.op("act", lambda e, s=s: e.activation(out=hb[s], in_=xin[s], func=AF.Square, accum_out=ssq[s]),
             reads=[t_xin[s]], writes=[t_hb[s], t_ssq[s]])
        P.op("act", lambda e, s=s: e.activation(out=ssq[s], in_=ssq[s], func=AF.Ln, bias=C.epscol, scale=1.0 / D),
             reads=[t_ssq[s], C.t_const], writes=[t_ssq[s]])
        P.op("act", lambda e, s=s: e.activation(out=ssq[s], in_=ssq[s], func=AF.Exp, scale=-0.5), reads=[t_ssq[s]], writes=[t_ssq[s]])
        P.op("dve", lambda e, s=s: e.scalar_tensor_tensor(out=hb[s], in0=xin[s], scalar=ssq[s], in1=gb, op0=ALU.mult, op1=ALU.mult),
             reads=[t_xin[s], t_ssq[s], t_gb], writes=[t_hb[s]])
        for half in range(2):
            tp, ttp = C.tpb[half], C.t_tpb[half]
            for j in range(8):
                k = half * 8 + j
                P.op("pe", lambda e, s=s, j=j, k=k, tp=tp: e.transpose(out=tp[:, j * 128:(j + 1) * 128], in_=hb[s][:, k * 128:(k + 1) * 128], identity=C.ident),
                     reads=[t_hb[s], C.t_const], writes=[ttp], join=True)
            eng = "act" if half == 0 else "dve"
            dst = hT[:, half * 8:(half + 1) * 8, i * 128:(i + 1) * 128]
            srcv = tp.rearrange("p (a b) -> p a b", a=8)
            if eng == "act":
                P.op("act", lambda e, dst=dst, srcv=srcv: e.activation(out=dst, in_=srcv, func=AF.Copy), reads=[ttp], writes=[hT_toks[i]], join=True)
            else:
                P.op("dve", lambda e, dst=dst, srcv=srcv: e.tensor_copy(out=dst, in_=srcv), reads=[ttp], writes=[hT_toks[i]], join=True)
    A.release(m)


def phase_outproj(C, yT_d, w_out, x_src, x_dst):
    P, A = C.P, C.A
    m = A.mark()
    wo = A.alloc([16, D], BF16)
    yT = A.alloc([16, TB], BF16)
    xin = [A.alloc([D], F32) for _ in range(2)]
    xo = [A.alloc([D], F32) for _ in range(2)]
    t_wo = P.tok("wo"); t_yT = P.toks(16, "yT"); t_xin = P.toks(2, "xin"); t_xo = P.toks(2, "xo"); t_dst = P.tok("xdst")
    for n in range(4):
        load_w_block(C, wo[:, :, n * 512:(n + 1) * 512], t_wo, w_out, n * 512, 512)
    cnt = 0
    for tb in range(NTB):
        tok0 = tb * TB
        for k in range(16):
            P.op("sync", lambda e, k=k, tok0=tok0: e.dma_start(out=yT[:, k, :], in_=yT_d[k, :, tok0:tok0 + TB]), writes=[t_yT[k]], dma=True)
        for i in range(TPB):
            s = i % 2
            r0 = tok0 + i * 128
            P.op("sync", lambda e, s=s, r0=r0: e.dma_start(out=xin[s], in_=x_src[r0:r0 + 128, :]), writes=[t_xin[s]], dma=True)
            for n in range(4):
                pb = cnt % 4; cnt += 1
                ps, tps = C.psf[pb], C.t_psf[pb]
                for k in range(16):
                    P.op("pe", lambda e, ps=ps, k=k, i=i, n=n: e.matmul(ps, lhsT=yT[:, k, i * 128:(i + 1) * 128], rhs=wo[:, k, n * 512:(n + 1) * 512], start=(k == 0), stop=(k == 15)),
                         reads=[t_yT[k], t_wo], writes=[tps], join=(k > 0))
                P.op("dve", lambda e, ps=ps, s=s, n=n: e.tensor_tensor(out=xo[s][:, n * 512:(n + 1) * 512], in0=ps, in1=xin[s][:, n * 512:(n + 1) * 512], op=ALU.add),
                     reads=[tps, t_xin[s]], writes=[t_xo[s]], join=(n > 0))
            P.op("sync", lambda e, s=s, r0=r0: e.dma_start(out=x_dst[r0:r0 + 128, :], in_=xo[s]), reads=[t_xo[s]], writes=[t_dst], dma=True)
    A.release(m)
    P.barrier()


def qk_norm_epilogue(C, ps, tps, gcol, dst, t_dst, tmp, t_tmp, grp):
    P = C.P
    qf, sq, rs = tmp
    ones = C.ones64 if grp == 64 else C.ones128
    P.op("act", lambda e: e.activation(out=qf, in_=ps, func=AF.Copy), reads=[tps], writes=[t_tmp[0]])
    P.op("act", lambda e: e.activation(out=sq, in_=ps, func=AF.Square), reads=[tps], writes=[t_tmp[1]])
    pss, tpss = C.psf[4 + C.auxcnt % 2], C.t_psf[4 + C.auxcnt % 2]
    C.auxcnt += 1
    P.op("pe", lambda e: e.matmul(pss, lhsT=ones, rhs=sq, start=True, stop=True), reads=[t_tmp[1], C.t_const], writes=[tpss])
    P.op("act", lambda e: e.activation(out=rs, in_=pss, func=AF.Ln, bias=C.epscol, scale=1.0 / grp), reads=[tpss, C.t_const], writes=[t_tmp[2]])
    P.op("act", lambda e: e.activation(out=rs, in_=rs, func=AF.Exp, scale=-0.5), reads=[t_tmp[2]], writes=[t_tmp[2]])
    P.op("dve", lambda e: e.scalar_tensor_tensor(out=dst, in0=qf, scalar=gcol, in1=rs, op0=ALU.mult, op1=ALU.mult),
         reads=[t_tmp[0], t_tmp[2], C.t_lconst], writes=[t_dst], join=True)


def gT_block(C, hT, hT_toks, wt_s, t_wt_s, tok0, h0, tbsz, pcnt, gst, t_gst, gcnt):
    P = C.P
    for mm_ in range(4):
        gs = gcnt % 2; gcnt += 1
        for tq in range(tbsz // 512):
            pb = pcnt % 4; pcnt += 1
            ps, tps = C.psf[pb], C.t_psf[pb]
            for k in range(16):
                P.op("pe", lambda e, ps=ps, k=k, mm_=mm_, tq=tq: e.matmul(ps, lhsT=wt_s[:, k, mm_ * 128:(mm_ + 1) * 128], rhs=hT[:, k, tq * 512:(tq + 1) * 512], start=(k == 0), stop=(k == 15)),
                     reads=[t_wt_s] + hT_toks[tq * 4:(tq + 1) * 4], writes=[tps], join=(k > 0))
            P.op("act", lambda e, ps=ps, gs=gs, tq=tq: e.activation(out=gst[gs][:, tq * 512:(tq + 1) * 512], in_=ps, func=AF.Silu), reads=[tps], writes=[t_gst[gs]], join=True)
        P.op("sync", lambda e, gs=gs, mm_=mm_: e.dma_start(out=C.GT_d[h0 + mm_, :, tok0:tok0 + tbsz], in_=gst[gs][:, 0:tbsz]), reads=[t_gst[gs]], writes=[C.t_gd], dma=True, join=True)
    return pcnt, gcnt


def layer0_proj(C, x_src, W):
    P, A = C.P, C.A
    m = A.mark()
    hT = A.alloc([16, TB], BF16)
    hT_toks = P.toks(TPB, "hT")
    wt = [A.alloc([16, 512], BF16) for _ in range(2)]
    t_wt = P.toks(2, "wt")
    qst = [A.alloc([TB], BF16) for _ in range(2)]
    t_qst = P.toks(2, "qst")
    tmp = [[A.alloc([512], F32) for _ in range(3)] for _ in range(2)]
    t_tmp = [P.toks(3, "tmp") for _ in range(2)]
    vst = [A.alloc([8, 512], BF16) for _ in range(2)]
    t_vst = P.toks(2, "vst")
    gst = [A.alloc([TB], F32) for _ in range(2)]
    t_gst = P.toks(2, "gst")
    t_qd = P.tok("QTd"); t_kd = P.tok("KTd"); t_vd = P.tok("Vd"); C.t_gd = P.tok("Gd")
    gq = A.alloc([1], F32); gk = A.alloc([1], F32)
    for half in range(2):
        P.op("sync", lambda e, half=half: e.dma_start(out=gq[half * 64:(half + 1) * 64, :], in_=W["a_qk_g"][0, 0, :].rearrange("(d o) -> d o", o=1)), writes=[C.t_lconst], dma=True, join=True)
        P.op("sync", lambda e, half=half: e.dma_start(out=gk[half * 64:(half + 1) * 64, :], in_=W["a_qk_g"][0, 1, :].rearrange("(d o) -> d o", o=1)), writes=[C.t_lconst], dma=True, join=True)
    P.op("dve", lambda e: e.tensor_scalar(out=gq, in0=gq, scalar1=0.125, scalar2=None, op0=ALU.mult), reads=[C.t_lconst], writes=[C.t_lconst])
    wcnt = 0; qcnt = 0; tcnt = 0; pcnt = 0; vcnt = 0; gcnt = 0
    for tb in range(NTB):
        tok0 = tb * TB
        phase_norm_T(C, x_src, W["norm_g"][C.layer, :], tok0, hT, hT_toks)
        for cb in range(16):
            ws = wcnt % 2; wcnt += 1
            load_w_block(C, wt[ws], t_wt[ws], W["a_w_in"][0], cb * 512, 512)
            kind = cb // 4
            if kind < 2:
                for mm_ in range(4):
                    h = (cb % 4) * 4 + mm_
                    qs = qcnt % 2; qcnt += 1
                    for tq in range(TB // 512):
                        pb = pcnt % 4; pcnt += 1
                        ps, tps = C.psf[pb], C.t_psf[pb]
                        for k in range(16):
                            P.op("pe", lambda e, ps=ps, k=k, ws=ws, mm_=mm_, tq=tq: e.matmul(ps, lhsT=wt[ws][:, k, mm_ * 128:(mm_ + 1) * 128], rhs=hT[:, k, tq * 512:(tq + 1) * 512], start=(k == 0), stop=(k == 15)),
                                 reads=[t_wt[ws]] + hT_toks[tq * 4:(tq + 1) * 4], writes=[tps], join=(k > 0))
                        ts_ = tcnt % 2; tcnt += 1
                        qk_norm_epilogue(C, ps, tps, gq if kind == 0 else gk, qst[qs][:, tq * 512:(tq + 1) * 512], t_qst[qs], tmp[ts_], t_tmp[ts_], 64)
                    dd, td = (C.QT_d, t_qd) if kind == 0 else (C.KT_d, t_kd)
                    P.op("sync", lambda e, dd=dd, h=h, qs=qs, tok0=tok0: e.dma_start(out=dd[h, :, tok0:tok0 + TB], in_=qst[qs]), reads=[t_qst[qs]], writes=[td], dma=True, join=True)
            elif kind == 3:
                pcnt, gcnt = gT_block(C, hT, hT_toks, wt[ws], t_wt[ws], tok0, (cb % 4) * 4, TB, pcnt, gst, t_gst, gcnt)
            else:
                c0 = (cb % 4) * 512
                for i in range(TPB):
                    pb = pcnt % 4; pcnt += 1
                    ps, tps = C.psf[pb], C.t_psf[pb]
                    for k in range(16):
                        P.op("pe", lambda e, ps=ps, k=k, ws=ws, i=i: e.matmul(ps, lhsT=hT[:, k, i * 128:(i + 1) * 128], rhs=wt[ws][:, k, :], start=(k == 0), stop=(k == 15)),
                             reads=[t_wt[ws], hT_toks[i]], writes=[tps], join=(k > 0))
                    r0 = tok0 + i * 128
                    if True:
                        g8 = i % 8
                        if g8 == 0:
                            vs = vcnt % 2; vcnt += 1
                        P.op("dve", lambda e, ps=ps, vs=vs, g8=g8: e.tensor_copy(out=vst[vs][:, g8, :], in_=ps), reads=[tps], writes=[t_vst[vs]], join=True)
                        if g8 == 7:
                            rr = r0 - 7 * 128
                            P.op("sync", lambda e, vs=vs, rr=rr, c0=c0: e.dma_start(out=C.V_d[rr:rr + 1024, c0:c0 + 512].rearrange("(a p) c -> p a c", p=128), in_=vst[vs]),
                                 reads=[t_vst[vs]], writes=[t_vd], dma=True, join=True)
    A.release(m)
    P.barrier()


def attn_phase(C, W, mode):
    P, A = C.P, C.A
    m = A.mark()
    H = 16
    diff = (mode == "diff")
    nsub = 2 if diff else 1
    lam_init = 0.8 - 0.6 * math.exp(-0.3 * C.layer)
    QT = [A.alloc([S], BF16) for _ in range(2)]
    KT = [A.alloc([S], BF16) for _ in range(2)]
    Vh = [A.alloc([NT, 128], BF16) for _ in range(2)]
    Gh = [A.alloc([S], F32) for _ in range(2)]
    yT = [A.alloc([S], BF16) for _ in range(2)]
    PT = [A.alloc([512], BF16) for _ in range(4)]
    onesb = A.alloc([128], BF16)
    t_QT = P.toks(2, "QT"); t_KT = P.toks(2, "KT"); t_Vh = P.toks(2, "Vh"); t_Gh = P.toks(2, "Gh"); t_yT = P.toks(2, "yT"); t_PT = P.toks(4, "PT")
    t_yd = P.tok("YTd")
    Osb = [A.alloc([512], F32) for _ in range(2)]; Dsb = [A.alloc([512], F32) for _ in range(2)]
    sqb = A.alloc([512], F32); rsb = A.alloc([512], F32)
    t_Osb = P.toks(2, "Osb"); t_Dsb = P.toks(2, "Dsb"); t_sqb = P.tok("sqb"); t_rsb = P.tok("rsb")
    tl = C.t_lconst
    P.op("pool", lambda e: e.memset(onesb, 1.0), writes=[tl])
    if diff:
        biasT = A.alloc([H, 2, 128], F32)
        b15 = A.alloc([H], F32)
        lam4 = A.alloc([4, 64], F32)
        lamc = A.alloc([4], F32)
        sgcol = A.alloc([1], F32)
        P.op("sync", lambda e: e.dma_start(out=biasT, in_=C.biasT_in), writes=[tl], dma=True, join=True)
        P.op("sync", lambda e: e.dma_start(out=b15, in_=W["rel_bias"][15, :].partition_broadcast(128)), writes=[tl], dma=True, join=True)
        P.op("sync", lambda e: e.dma_start(out=lam4, in_=W["a_lambda"][0].rearrange("a d -> (a d)").partition_broadcast(128).rearrange("p (a d) -> p a d", a=4)), writes=[tl], dma=True, join=True)
        load_col(C, sgcol, W["a_subln_g"][0, :], 128, 1.0 - lam_init)
        for hh in range(H):
            P.op("dve", lambda e, hh=hh: e.tensor_scalar(out=biasT[:, hh], in0=biasT[:, hh], scalar1=b15[:, hh:hh + 1], scalar2=None, op0=ALU.subtract), reads=[tl], writes=[tl])
        P.op("dve", lambda e: e.tensor_tensor(out=lam4[:, 0, :], in0=lam4[:, 0, :], in1=lam4[:, 1, :], op=ALU.mult), reads=[tl], writes=[tl])
        P.op("dve", lambda e: e.tensor_tensor(out=lam4[:, 2, :], in0=lam4[:, 2, :], in1=lam4[:, 3, :], op=ALU.mult), reads=[tl], writes=[tl])
        P.op("dve", lambda e: e.reduce_sum(out=lamc[:, 0:1], in_=lam4[:, 0, :], axis=AX.X), reads=[tl], writes=[tl])
        P.op("dve", lambda e: e.reduce_sum(out=lamc[:, 1:2], in_=lam4[:, 2, :], axis=AX.X), reads=[tl], writes=[tl])
        P.op("act", lambda e: e.activation(out=lamc[:, 0:2], in_=lamc[:, 0:2], func=AF.Exp), reads=[tl], writes=[tl])
        P.op("dve", lambda e: e.scalar_tensor_tensor(out=lamc[:, 0:1], in0=lamc[:, 1:2], scalar=-lam_init, in1=lamc[:, 0:1], op0=ALU.add, op1=ALU.subtract), reads=[tl], writes=[tl])
    else:
        maskT = A.alloc([128], F32)
        QP = [A.alloc([S], BF16) for _ in range(2)]
        KP = A.alloc([S], BF16)
        t_QP = P.toks(2, "QP"); t_KP = P.tok("KP")
        P.op("sync", lambda e: e.dma_start(out=maskT, in_=C.maskT_d), writes=[tl], dma=True, join=True)
        P.op("sync", lambda e: e.dma_start(out=KP[0:64, :], in_=C.KPE_d), writes=[t_KP], dma=True)
    SB = [0, 1, 7]

    def banks(qg, t):
        b0 = 2 + 2 * t if diff else 2 + 2 * (qg % 2)
        return b0, b0 + 1

    def emit_loads(h):
        hs = h % 2
        P.op("sync", lambda e: e.dma_start(out=QT[hs], in_=C.QT_d[h]), writes=[t_QT[hs]], dma=True)
        P.op("sync", lambda e: e.dma_start(out=KT[hs], in_=C.KT_d[h]), writes=[t_KT[hs]], dma=True)
        if not diff:
            P.op("sync", lambda e: e.dma_start(out=QP[hs][0:64, :], in_=C.QPE_d[h]), writes=[t_QP[hs]], dma=True)
        P.op("sync", lambda e: e.dma_start(out=Vh[hs], in_=C.V_d[:, h * 128:(h + 1) * 128].rearrange("(a p) c -> p a c", p=128)), writes=[t_Vh[hs]], dma=True)
        P.op("sync", lambda e: e.dma_start(out=Gh[hs], in_=C.GT_d[h]), writes=[t_Gh[hs]], dma=True)

    def emit_S(n, h, qg, t, i):
        hs = h % 2
        jmin = max(0, i - 4 * qg)
        Sp, tSp = C.psf[SB[n % 3]], C.t_psf[SB[n % 3]]
        q0 = (4 * qg + jmin) * 128
        ncol = (4 - jmin) * 128
        c0 = jmin * 128
        if diff:
            P.op("pe", lambda e: e.matmul(Sp[:, c0:c0 + ncol], lhsT=KT[hs][t * 64:(t + 1) * 64, i * 128:(i + 1) * 128], rhs=QT[hs][t * 64:(t + 1) * 64, q0:q0 + ncol], start=True, stop=True),
                 reads=[t_KT[hs], t_QT[hs]], writes=[tSp])
        else:
            P.op("pe", lambda e: e.matmul(Sp[:, c0:c0 + ncol], lhsT=KT[hs][:, i * 128:(i + 1) * 128], rhs=QT[hs][:, q0:q0 + ncol], start=True, stop=False),
                 reads=[t_KT[hs], t_QT[hs]], writes=[tSp])
            P.op("pe", lambda e: e.matmul(Sp[:, c0:c0 + ncol], lhsT=KP[0:64, i * 128:(i + 1) * 128], rhs=QP[hs][0:64, q0:q0 + ncol], start=False, stop=True),
                 reads=[t_KP, t_QP[hs]], writes=[tSp], join=True)
        for rel in ((0, 1) if diff else (0,)):
            jj = i - 4 * qg + rel
            if 0 <= jj <= 3 and jj >= jmin:
                btile = biasT[:, h, rel, :] if diff else maskT
                P.op("dve", lambda e, jj=jj, btile=btile: e.tensor_tensor(out=Sp[:, jj * 128:(jj + 1) * 128], in0=Sp[:, jj * 128:(jj + 1) * 128], in1=btile, op=ALU.add),
                     reads=[tSp, tl], writes=[tSp])
        pp = n % 4
        ebias = b15[:, h:h + 1] if diff else 0.0
        P.op("act", lambda e: e.activation(out=PT[pp][:, c0:c0 + ncol], in_=Sp[:, c0:c0 + ncol], func=AF.Exp, bias=ebias, scale=1.0),
             reads=[tSp, tl], writes=[t_PT[pp]])

    def emit_PV(n, h, qg, t, i):
        hs = h % 2
        jmin = max(0, i - 4 * qg)
        c0 = jmin * 128
        pp = n % 4
        bo, bd = banks(qg, t)
        last = (i == 4 * qg + 3)
        P.op("pe", lambda e: e.matmul(C.psf[bo][:, c0:512], lhsT=Vh[hs][:, i, :], rhs=PT[pp][:, c0:512], start=(i == 0), stop=last),
             reads=[t_PT[pp], t_Vh[hs]], writes=[C.t_psf[bo]], join=(i > 0))
        P.op("pe", lambda e: e.matmul(C.psf[bd][:, c0:512], lhsT=onesb, rhs=PT[pp][:, c0:512], start=(i == 0), stop=last),
             reads=[t_PT[pp], tl], writes=[C.t_psf[bd]], join=(i > 0))

    def emit_epilogue(h, qg):
        hs = h % 2
        ysl = yT[hs][:, qg * 512:(qg + 1) * 512]
        gsl = Gh[hs][:, qg * 512:(qg + 1) * 512]
        if diff:
            for t in range(2):
                bo, bd = banks(qg, t)
                P.op("dve", lambda e, t=t, bo=bo: e.tensor_copy(out=Osb[t], in_=C.psf[bo]), reads=[C.t_psf[bo]], writes=[t_Osb[t]])
            for t in range(2):
                bo, bd = banks(qg, t)
                P.op("dve", lambda e, t=t, bd=bd: e.reciprocal(out=Dsb[t], in_=C.psf[bd]), reads=[C.t_psf[bd]], writes=[t_Dsb[t]])
            for t in (1, 0):
                P.op("dve", lambda e, t=t: e.tensor_tensor(out=Osb[t], in0=Osb[t], in1=Dsb[t], op=ALU.mult), reads=[t_Osb[t], t_Dsb[t]], writes=[t_Osb[t]])
            P.op("dve", lambda e: e.scalar_tensor_tensor(out=Osb[0], in0=Osb[1], scalar=lamc[:, 0:1], in1=Osb[0], op0=ALU.mult, op1=ALU.add), reads=[t_Osb[0], t_Osb[1], tl], writes=[t_Osb[0]])
            P.op("pool", lambda e: e.tensor_tensor(out=sqb, in0=Osb[0], in1=Osb[0], op=ALU.mult), reads=[t_Osb[0]], writes=[t_sqb])
            P.op("pe", lambda e: e.matmul(C.psf[6], lhsT=C.ones128, rhs=sqb, start=True, stop=True), reads=[t_sqb, C.t_const], writes=[C.t_psf[6]])
            P.op("act", lambda e: e.activation(out=rsb, in_=C.psf[6], func=AF.Ln, bias=C.epscol, scale=1.0 / 128), reads=[C.t_psf[6], C.t_const], writes=[t_rsb])
            P.op("act", lambda e: e.activation(out=rsb, in_=rsb, func=AF.Exp, scale=-0.5), reads=[t_rsb], writes=[t_rsb])
            P.op("dve", lambda e: e.scalar_tensor_tensor(out=Osb[0], in0=Osb[0], scalar=sgcol, in1=rsb, op0=ALU.mult, op1=ALU.mult), reads=[t_Osb[0], t_rsb, tl], writes=[t_Osb[0]])
            P.op("pool", lambda e: e.tensor_tensor(out=ysl, in0=Osb[0], in1=gsl, op=ALU.mult), reads=[t_Osb[0], t_Gh[hs]], writes=[t_yT[hs]], join=True)
        else:
            bo, bd = banks(qg, 0)
            s_ = qg % 2
            P.op("dve", lambda e: e.reciprocal(out=Dsb[s_], in_=C.psf[bd]), reads=[C.t_psf[bd]], writes=[t_Dsb[s_]])
            P.op("dve", lambda e: e.tensor_tensor(out=Osb[s_], in0=C.psf[bo], in1=Dsb[s_], op=ALU.mult), reads=[C.t_psf[bo], t_Dsb[s_]], writes=[t_Osb[s_]])
            P.op("pool", lambda e: e.tensor_tensor(out=ysl, in0=Osb[s_], in1=gsl, op=ALU.mult), reads=[t_Osb[s_], t_Gh[hs]], writes=[t_yT[hs]], join=True)
        if qg == NT // 4 - 1:
            P.op("sync", lambda e: e.dma_start(out=C.YT_d[h], in_=yT[hs]), reads=[t_yT[hs]], writes=[t_yd], dma=True, join=True)

    tiles = [(h, qg, t, i) for h in range(H) for qg in range(NT // 4) for t in range(nsub) for i in range(4 * qg + 4)]
    LA = 2

    def emit_S_at(n):
        h_, qg_, t_, i_ = tiles[n]
        if qg_ == 0 and t_ == 0 and i_ == 0:
            emit_loads(h_)
        emit_S(n, h_, qg_, t_, i_)

    for n in range(min(LA, len(tiles))):
        emit_S_at(n)
    for n, (h, qg, t, i) in enumerate(tiles):
        if n + LA < len(tiles):
            emit_S_at(n + LA)
        emit_PV(n, h, qg, t, i)
        if t == nsub - 1 and i == 4 * qg + 3:
            emit_epilogue(h, qg)
    A.release(m)
    P.barrier()


def load_col(C, dst, src_vec, n, scale=None):
    P = C.P
    P.op("sync", lambda e: e.dma_start(out=dst[0:n, :], in_=src_vec.rearrange("(d o) -> d o", o=1)), writes=[C.t_lconst], dma=True, join=True)
    if scale is not None:
        P.op("dve", lambda e: e.tensor_scalar(out=dst[0:n, :], in0=dst[0:n, :], scalar1=float(scale), scalar2=None, op0=ALU.mult), reads=[C.t_lconst], writes=[C.t_lconst])


def rope_epilogue(C, ps, tps, gcol, cs, sn, t_cs, dst, t_dst, tmp, t_tmp):
    P = C.P
    xf, sq, rs = tmp
    P.op("act", lambda e: e.activation(out=xf[0:64, :], in_=ps[0:64, :], func=AF.Copy), reads=[tps], writes=[t_tmp[0]])
    P.op("act", lambda e: e.activation(out=sq[0:64, :], in_=ps[0:64, :], func=AF.Square), reads=[tps], writes=[t_tmp[1]])
    pss, tpss = C.psf[4 + C.auxcnt % 2], C.t_psf[4 + C.auxcnt % 2]
    C.auxcnt += 1
    P.op("pe", lambda e: e.matmul(pss[0:64, :], lhsT=C.ones64[0:64, 0:64], rhs=sq[0:64, :], start=True, stop=True), reads=[t_tmp[1], C.t_const], writes=[tpss])
    P.op("act", lambda e: e.activation(out=rs[0:64, :], in_=pss[0:64, :], func=AF.Ln, bias=C.epscol[0:64, :], scale=1.0 / 64), reads=[tpss, C.t_const], writes=[t_tmp[2]])
    P.op("act", lambda e: e.activation(out=rs[0:64, :], in_=rs[0:64, :], func=AF.Exp, scale=-0.5), reads=[t_tmp[2]], writes=[t_tmp[2]])
    P.op("dve", lambda e: e.scalar_tensor_tensor(out=xf[0:64, :], in0=xf[0:64, :], scalar=gcol[0:64, :], in1=rs[0:64, :], op0=ALU.mult, op1=ALU.mult),
         reads=[t_tmp[0], t_tmp[2], C.t_lconst], writes=[t_tmp[0]])
    pr, tpr = C.psf[4 + C.auxcnt % 2], C.t_psf[4 + C.auxcnt % 2]
    C.auxcnt += 1
    P.op("pe", lambda e: e.matmul(pr[0:64, :], lhsT=C.rotm[0:64, 0:64], rhs=xf[0:64, :], start=True, stop=True), reads=[t_tmp[0], C.t_const], writes=[tpr])
    P.op("dve", lambda e: e.tensor_tensor(out=sq[0:64, :], in0=pr[0:64, :], in1=sn, op=ALU.mult), reads=[tpr, t_cs], writes=[t_tmp[1]])
    P.op("pool", lambda e: e.tensor_tensor(out=xf[0:64, :], in0=xf[0:64, :], in1=cs, op=ALU.mult), reads=[t_tmp[0], t_cs], writes=[t_tmp[0]])
    P.op("pool", lambda e: e.tensor_tensor(out=dst, in0=xf[0:64, :], in1=sq[0:64, :], op=ALU.add), reads=[t_tmp[0], t_tmp[1]], writes=[t_dst], join=True)


def g_block(C, hT, hT_toks, wt_s, t_wt_s, tok0, c0, pcnt, gst, t_gst, gcnt, tpb=TPB):
    P = C.P
    t_gd = C.t_gd
    for i in range(tpb):
        pb = pcnt % 4; pcnt += 1
        ps, tps = C.psf[pb], C.t_psf[pb]
        for k in range(16):
            P.op("pe", lambda e, ps=ps, k=k, i=i: e.matmul(ps, lhsT=hT[:, k, i * 128:(i + 1) * 128], rhs=wt_s[:, k, :], start=(k == 0), stop=(k == 15)),
                 reads=[t_wt_s, hT_toks[i]], writes=[tps], join=(k > 0))
        r0 = tok0 + i * 128
        g4 = i % 4
        if g4 == 0:
            gs = gcnt % 2; gcnt += 1
        P.op("act", lambda e, ps=ps, gs=gs, g4=g4: e.activation(out=gst[gs][:, g4, :], in_=ps, func=AF.Silu), reads=[tps], writes=[t_gst[gs]], join=True)
        if g4 == 3:
            rr = r0 - 3 * 128
            P.op("sync", lambda e, gs=gs, rr=rr, c0=c0: e.dma_start(out=C.G_d[rr:rr + 512, c0:c0 + 512].rearrange("(a p) c -> p a c", p=128), in_=gst[gs]),
                 reads=[t_gst[gs]], writes=[t_gd], dma=True, join=True)
    return pcnt, gcnt


def layer3_proj1(C, x_src, W):
    P, A = C.P, C.A
    m = A.mark()
    TB = 1024; TPB = TB // 128; NTB = S // TB
    hT = A.alloc([16, TB], BF16); hT_toks = P.toks(TPB, "hT")
    wt = [A.alloc([16, 512], BF16) for _ in range(2)]; t_wt = P.toks(2, "wt")
    wkp = A.alloc([16, 64], BF16); t_wkp = P.tok("wkp")
    cst = [A.alloc([4, TB], BF16) for _ in range(2)]; t_cst = P.toks(2, "cst")
    cf = [A.alloc([512], F32) for _ in range(4)]; t_cf = P.toks(4, "cf")
    sq = [A.alloc([512], F32) for _ in range(2)]; t_sq = P.toks(2, "sq")
    rs = A.alloc([512], F32); t_rs = P.tok("rs")
    tmp = [A.alloc([512], F32) for _ in range(3)]; t_tmp = P.toks(3, "tmp")
    kpst = A.alloc([TB], BF16); t_kpst = P.tok("kpst")
    cs = A.alloc([TB], F32); sn = A.alloc([TB], F32); t_cs = P.tok("cs")
    gst = [A.alloc([TB], F32) for _ in range(2)]; t_gst = P.toks(2, "gst")
    glat = A.alloc([2, 4], F32); gkp = A.alloc([1], F32)
    C.t_gd = P.tok("Gd"); t_cd = P.tok("CQd"); t_kd = P.tok("KPEd")
    tl = C.t_lconst
    for mm_ in range(4):
        load_col(C, glat[:, 0, mm_:mm_ + 1], W["d_q_lat_g"][0, mm_ * 128:(mm_ + 1) * 128], 128)
        load_col(C, glat[:, 1, mm_:mm_ + 1], W["d_kv_lat_g"][0, mm_ * 128:(mm_ + 1) * 128], 128)
    load_col(C, gkp, W["d_qk_g"][0, 1, 128:192], 64)
    wcnt = 0; pcnt = 0; gcnt = 0; scnt = 0
    for tb in range(NTB):
        tok0 = tb * TB
        phase_norm_T(C, x_src, W["norm_g"][C.layer, :], tok0, hT, hT_toks, TB)
        P.op("sync", lambda e, tok0=tok0: e.dma_start(out=cs[0:64, :], in_=C.rope_in[0, :, tok0:tok0 + TB]), writes=[t_cs], dma=True)
        P.op("sync", lambda e, tok0=tok0: e.dma_start(out=sn[0:64, :], in_=C.rope_in[1, :, tok0:tok0 + TB]), writes=[t_cs], dma=True, join=True)
        for kind in range(2):
            ws = wcnt % 2; wcnt += 1
            load_w_block(C, wt[ws], t_wt[ws], W["d_w_in"][0], kind * 512, 512)
            for tq in range(TB // 512):
                pss, tpss = C.psf[4 + C.auxcnt % 2], C.t_psf[4 + C.auxcnt % 2]
                C.auxcnt += 1
                for mm_ in range(4):
                    pb = pcnt % 4; pcnt += 1
                    ps, tps = C.psf[pb], C.t_psf[pb]
                    for k in range(16):
                        P.op("pe", lambda e, ps=ps, k=k, ws=ws, mm_=mm_, tq=tq: e.matmul(ps, lhsT=wt[ws][:, k, mm_ * 128:(mm_ + 1) * 128], rhs=hT[:, k, tq * 512:(tq + 1) * 512], start=(k == 0), stop=(k == 15)),
                             reads=[t_wt[ws]] + hT_toks[tq * 4:(tq + 1) * 4], writes=[tps], join=(k > 0))
                    P.op("act", lambda e, ps=ps, mm_=mm_: e.activation(out=cf[mm_], in_=ps, func=AF.Copy), reads=[tps], writes=[t_cf[mm_]])
                    ss = scnt % 2; scnt += 1
                    P.op("act", lambda e, ps=ps, ss=ss: e.activation(out=sq[ss], in_=ps, func=AF.Square), reads=[tps], writes=[t_sq[ss]])
                    P.op("pe", lambda e, pss=pss, ss=ss, mm_=mm_: e.matmul(pss, lhsT=C.ones128, rhs=sq[ss], start=(mm_ == 0), stop=(mm_ == 3)), reads=[t_sq[ss], C.t_const], writes=[tpss], join=(mm_ > 0))
                P.op("act", lambda e, pss=pss: e.activation(out=rs, in_=pss, func=AF.Ln, bias=C.epscol, scale=1.0 / 512), reads=[tpss, C.t_const], writes=[t_rs])
                P.op("act", lambda e: e.activation(out=rs, in_=rs, func=AF.Exp, scale=-0.5), reads=[t_rs], writes=[t_rs])
                for mm_ in range(4):
                    P.op("dve", lambda e, mm_=mm_, kind=kind, tq=tq: e.scalar_tensor_tensor(out=cst[kind][:, mm_, tq * 512:(tq + 1) * 512], in0=cf[mm_], scalar=glat[:, kind, mm_:mm_ + 1], in1=rs, op0=ALU.mult, op1=ALU.mult),
                         reads=[t_cf[mm_], t_rs, tl], writes=[t_cst[kind]], join=True)
            dd = C.CQ_d if kind == 0 else C.CKV_d
            for mm_ in range(4):
                P.op("sync", lambda e, dd=dd, mm_=mm_, kind=kind, tok0=tok0: e.dma_start(out=dd[mm_, :, tok0:tok0 + TB], in_=cst[kind][:, mm_, :]), reads=[t_cst[kind]], writes=[t_cd], dma=True, join=True)
        load_w_block(C, wkp, t_wkp, W["d_w_in"][0], 1024, 64)
        for tq in range(TB // 512):
            pb = pcnt % 4; pcnt += 1
            ps, tps = C.psf[pb], C.t_psf[pb]
            for k in range(16):
                P.op("pe", lambda e, ps=ps, k=k, tq=tq: e.matmul(ps[0:64, :], lhsT=wkp[:, k, 0:64], rhs=hT[:, k, tq * 512:(tq + 1) * 512], start=(k == 0), stop=(k == 15)),
                     reads=[t_wkp] + hT_toks[tq * 4:(tq + 1) * 4], writes=[tps], join=(k > 0))
            rope_epilogue(C, ps, tps, gkp, cs[0:64, tq * 512:(tq + 1) * 512], sn[0:64, tq * 512:(tq + 1) * 512], t_cs, kpst[0:64, tq * 512:(tq + 1) * 512], t_kpst, tmp, t_tmp)
        P.op("sync", lambda e, tok0=tok0: e.dma_start(out=C.KPE_d[:, tok0:tok0 + TB], in_=kpst[0:64, :]), reads=[t_kpst], writes=[t_kd], dma=True, join=True)
        for cb in range(4):
            ws = wcnt % 2; wcnt += 1
            load_w_block(C, wt[ws], t_wt[ws], W["d_w_in"][0], 1088 + cb * 512, 512)
            pcnt, gcnt = gT_block(C, hT, hT_toks, wt[ws], t_wt[ws], tok0, cb * 4, TB, pcnt, gst, t_gst, gcnt)
    A.release(m)
    P.barrier()


def layer3_proj2(C, W):
    P, A = C.P, C.A
    m = A.mark()
    H = 16
    cq = A.alloc([4, TB], BF16); ckv = A.alloc([4, TB], BF16); t_cq = P.tok("cq"); t_ckv = P.tok("ckv")
    wuq = A.alloc([4, 3072], BF16); wkn = A.alloc([4, 2048], BF16); wv = A.alloc([4, 2048], BF16); t_w = P.tok("wup")
    qst = [A.alloc([TB], BF16) for _ in range(2)]; t_qst = P.toks(2, "qst")
    kst = [A.alloc([TB], BF16) for _ in range(2)]; t_kst = P.toks(2, "kst")
    qpst = [A.alloc([TB], BF16) for _ in range(2)]; t_qpst = P.toks(2, "qpst")
    tmp = [[A.alloc([512], F32) for _ in range(3)] for _ in range(2)]; t_tmp = [P.toks(3, "tmp") for _ in range(2)]
    cs = A.alloc([TB], F32); sn = A.alloc([TB], F32); t_cs = P.tok("cs")
    vst = [A.alloc([8, 512], BF16) for _ in range(2)]; t_vst = P.toks(2, "vst")
    gqn = A.alloc([1], F32); gkn = A.alloc([1], F32); gqp = A.alloc([1], F32)
    t_qd = P.tok("QTd"); t_kd = P.tok("KTd"); t_qpd = P.tok("QPEd"); t_vd = P.tok("Vd")
    sc = 192.0 ** -0.5
    load_col(C, gqn, W["d_qk_g"][0, 0, 0:128], 128, sc)
    load_col(C, gkn, W["d_qk_g"][0, 1, 0:128], 128)
    load_col(C, gqp, W["d_qk_g"][0, 0, 128:192], 64, sc)
    wq_v = W["d_w_uq"][0].rearrange("(k p) c -> p k c", p=128)
    wkv_v = W["d_w_ukv"][0].rearrange("(k p) (h c) -> p k h c", p=128, c=256)
    for k in range(4):
        P.op("pool", lambda e, k=k: e.dma_start(out=wuq[:, k, :], in_=wq_v[:, k, :]), writes=[t_w], dma=True, join=True)
        P.op("pool", lambda e, k=k: e.dma_start(out=wkn[:, k, :].rearrange("p (h c) -> p h c", c=128), in_=wkv_v[:, k, :, 0:128]), writes=[t_w], dma=True, join=True)
        P.op("pool", lambda e, k=k: e.dma_start(out=wv[:, k, :].rearrange("p (h c) -> p h c", c=128), in_=wkv_v[:, k, :, 128:256]), writes=[t_w], dma=True, join=True)
    pcnt = 0; tcnt = 0; vcnt = 0
    for tb in range(NTB):
        tok0 = tb * TB
        for k in range(4):
            P.op("sync", lambda e, k=k, tok0=tok0: e.dma_start(out=cq[:, k, :], in_=C.CQ_d[k, :, tok0:tok0 + TB]), writes=[t_cq], dma=True, join=(k > 0))
            P.op("sync", lambda e, k=k, tok0=tok0: e.dma_start(out=ckv[:, k, :], in_=C.CKV_d[k, :, tok0:tok0 + TB]), writes=[t_ckv], dma=True, join=(k > 0))
        P.op("sync", lambda e, tok0=tok0: e.dma_start(out=cs[0:64, :], in_=C.rope_in[0, :, tok0:tok0 + TB]), writes=[t_cs], dma=True)
        P.op("sync", lambda e, tok0=tok0: e.dma_start(out=sn[0:64, :], in_=C.rope_in[1, :, tok0:tok0 + TB]), writes=[t_cs], dma=True, join=True)
        for h in range(H):
            hs = h % 2
            for which in range(3):
                for tq in range(TB // 512):
                    pb = pcnt % 4; pcnt += 1
                    ps, tps = C.psf[pb], C.t_psf[pb]
                    for k in range(4):
                        if which == 0:
                            lhs, rhs_, rt, M = wuq[:, k, h * 192:h * 192 + 128], cq[:, k, tq * 512:(tq + 1) * 512], t_cq, 128
                        elif which == 1:
                            lhs, rhs_, rt, M = wkn[:, k, h * 128:(h + 1) * 128], ckv[:, k, tq * 512:(tq + 1) * 512], t_ckv, 128
                        else:
                            lhs, rhs_, rt, M = wuq[:, k, h * 192 + 128:h * 192 + 192], cq[:, k, tq * 512:(tq + 1) * 512], t_cq, 64
                        P.op("pe", lambda e, ps=ps, k=k, lhs=lhs, rhs_=rhs_, M=M: e.matmul(ps[0:M, :], lhsT=lhs, rhs=rhs_, start=(k == 0), stop=(k == 3)),
                             reads=[t_w, rt], writes=[tps], join=(k > 0))
                    ts_ = tcnt % 2; tcnt += 1
                    if which == 0:
                        qk_norm_epilogue(C, ps, tps, gqn, qst[hs][:, tq * 512:(tq + 1) * 512], t_qst[hs], tmp[ts_], t_tmp[ts_], 128)
                    elif which == 1:
                        qk_norm_epilogue(C, ps, tps, gkn, kst[hs][:, tq * 512:(tq + 1) * 512], t_kst[hs], tmp[ts_], t_tmp[ts_], 128)
                    else:
                        rope_epilogue(C, ps, tps, gqp, cs[0:64, tq * 512:(tq + 1) * 512], sn[0:64, tq * 512:(tq + 1) * 512], t_cs, qpst[hs][0:64, tq * 512:(tq + 1) * 512], t_qpst[hs], tmp[ts_], t_tmp[ts_])
            P.op("sync", lambda e, h=h, hs=hs, tok0=tok0: e.dma_start(out=C.QT_d[h, :, tok0:tok0 + TB], in_=qst[hs]), reads=[t_qst[hs]], writes=[t_qd], dma=True, join=True)
            P.op("sync", lambda e, h=h, hs=hs, tok0=tok0: e.dma_start(out=C.KT_d[h, :, tok0:tok0 + TB], in_=kst[hs]), reads=[t_kst[hs]], writes=[t_kd], dma=True, join=True)
            P.op("sync", lambda e, h=h, hs=hs, tok0=tok0: e.dma_start(out=C.QPE_d[h, :, tok0:tok0 + TB], in_=qpst[hs][0:64, :]), reads=[t_qpst[hs]], writes=[t_qpd], dma=True, join=True)
        for n in range(4):
            for i in range(TPB):
                pb = pcnt % 4; pcnt += 1
                ps, tps = C.psf[pb], C.t_psf[pb]
                for k in range(4):
                    P.op("pe", lambda e, ps=ps, k=k, i=i, n=n: e.matmul(ps, lhsT=ckv[:, k, i * 128:(i + 1) * 128], rhs=wv[:, k, n * 512:(n + 1) * 512], start=(k == 0), stop=(k == 3)),
                         reads=[t_w, t_ckv], writes=[tps], join=(k > 0))
                g8 = i % 8
                if g8 == 0:
                    vs = vcnt % 2; vcnt += 1
                P.op("dve", lambda e, ps=ps, vs=vs, g8=g8: e.tensor_copy(out=vst[vs][:, g8, :], in_=ps), reads=[tps], writes=[t_vst[vs]], join=True)
                if g8 == 7:
                    rr = tok0 + (i - 7) * 128
                    P.op("sync", lambda e, vs=vs, rr=rr, n=n: e.dma_start(out=C.V_d[rr:rr + 1024, n * 512:(n + 1) * 512].rearrange("(a p) c -> p a c", p=128), in_=vst[vs]),
                         reads=[t_vst[vs]], writes=[t_vd], dma=True, join=True)
    A.release(m)
    P.barrier()


def layer2_all(C, x_src, W):
    P, A = C.P, C.A
    m = A.mark()
    TB = 1024; TPB = TB // 128; NTB = S // TB
    hT = A.alloc([16, TB], BF16); hT_toks = P.toks(TPB, "hT")
    wt = [A.alloc([16, 512], BF16) for _ in range(2)]; t_wt = P.toks(2, "wt")
    wrg = A.alloc([8, 2, 256], BF16); wig = A.alloc([8, 2, 256], BF16); t_wg = P.tok("wg")
    stage = A.alloc([128], F32); cols = A.alloc([8, 16], F32)
    halo = A.alloc([16, 3], F32); hprev = A.alloc([16], F32); t_halo = P.tok("halo"); t_hprev = P.tok("hprev")
    ubuf = [A.alloc([TB + 8], F32) for _ in range(2)]; t_ubuf = P.toks(2, "ubuf")
    xc = [A.alloc([TB], F32) for _ in range(2)]; t_xc = P.toks(2, "xc")
    xcb = [A.alloc([TB], BF16) for _ in range(2)]; t_xcb = P.toks(2, "xcb")
    sg = [A.alloc([TB], F32) for _ in range(2)]; t_sg = P.toks(2, "sg")
    rg = [A.alloc([TB], F32) for _ in range(2)]; t_rg = P.toks(2, "rg")
    ig = [A.alloc([TB], F32) for _ in range(2)]; t_ig = P.toks(2, "ig")
    abuf = A.alloc([TB], F32); a2buf = A.alloc([TB], F32); xinb = A.alloc([TB], F32); hh = A.alloc([TB], F32)
    t_a = P.tok("a"); t_a2 = P.tok("a2"); t_xin = P.tok("xin"); t_hh = P.tok("hh")
    yst = [A.alloc([TB], BF16) for _ in range(2)]; t_yst = P.toks(2, "yst")
    t_yd = P.tok("YTd")
    tl = C.t_lconst
    vecs = [W["c_conv_w"][0, 0], W["c_conv_w"][0, 1], W["c_conv_w"][0, 2], W["c_conv_w"][0, 3], W["c_conv_b"][0], W["c_b_rgate"][0], W["c_b_igate"][0], W["c_lambda"][0]]
    for v, vec in enumerate(vecs):
        P.op("sync", lambda e, v=v, vec=vec: e.dma_start(out=stage[v * 16:(v + 1) * 16, :], in_=vec.rearrange("(t p) -> t p", p=128)), writes=[tl], dma=True, join=True)
    ps0, tps0 = C.psf[4], C.t_psf[4]
    P.op("pe", lambda e: e.matmul(ps0[:, 0:128], lhsT=stage, rhs=C.identf, start=True, stop=True), reads=[tl, C.t_const], writes=[tps0])
    P.op("dve", lambda e: e.tensor_copy(out=cols, in_=ps0[:, 0:128].rearrange("p (v t) -> p v t", v=8)), reads=[tps0], writes=[tl])
    P.op("act", lambda e: e.activation(out=cols[:, 7, :], in_=cols[:, 7, :], func=AF.Exp, scale=-1.0), reads=[tl], writes=[tl])
    P.op("act", lambda e: e.activation(out=cols[:, 7, :], in_=cols[:, 7, :], func=AF.Ln, bias=1.0, scale=1.0), reads=[tl], writes=[tl])
    P.op("dve", lambda e: e.tensor_scalar(out=cols[:, 7, :], in0=cols[:, 7, :], scalar1=-8.0, scalar2=None, op0=ALU.mult), reads=[tl], writes=[tl])
    for n in range(8):
        P.op("pool", lambda e, n=n: e.dma_start(out=wrg[:, n], in_=W["c_w_rgate"][0, n].rearrange("(c p) e -> p c e", p=128)), writes=[t_wg], dma=True, join=True)
        P.op("pool", lambda e, n=n: e.dma_start(out=wig[:, n], in_=W["c_w_igate"][0, n].rearrange("(c p) e -> p c e", p=128)), writes=[t_wg], dma=True, join=True)
    wcnt = 0; pcnt = 0
    for tb in range(NTB):
        tok0 = tb * TB
        phase_norm_T(C, x_src, W["norm_g"][C.layer, :], tok0, hT, hT_toks, TB)
        for n in range(8):
            ws = wcnt % 2; wcnt += 1
            load_w_block(C, wt[ws][:, :, 0:256], t_wt[ws], W["c_w_in"][0], n * 256, 256)
            load_w_block(C, wt[ws][:, :, 256:512], t_wt[ws], W["c_w_in"][0], 2048 + n * 256, 256)
            for c in range(2):
                tile = n * 2 + c
                if tb == 0:
                    P.op("pool", lambda e, c=c: e.memset(ubuf[c][:, 0:3], 0.0), writes=[t_ubuf[c]])
                else:
                    P.op("pool", lambda e, c=c, tile=tile: e.tensor_copy(out=ubuf[c][:, 0:3], in_=halo[:, tile, :]), reads=[t_halo], writes=[t_ubuf[c]])
                for tq in range(TB // 512):
                    pb = pcnt % 4; pcnt += 1
                    ps, tps = C.psf[pb], C.t_psf[pb]
                    for k in range(16):
                        P.op("pe", lambda e, ps=ps, k=k, ws=ws, c=c, tq=tq: e.matmul(ps, lhsT=wt[ws][:, k, c * 128:(c + 1) * 128], rhs=hT[:, k, tq * 512:(tq + 1) * 512], start=(k == 0), stop=(k == 15)),
                             reads=[t_wt[ws]] + hT_toks[tq * 4:(tq + 1) * 4], writes=[tps], join=(k > 0))
                    P.op("act", lambda e, ps=ps, c=c, tq=tq: e.activation(out=ubuf[c][:, 3 + tq * 512:3 + (tq + 1) * 512], in_=ps, func=AF.Copy), reads=[tps], writes=[t_ubuf[c]], join=True)
                P.op("pool", lambda e, c=c, tile=tile: e.tensor_copy(out=halo[:, tile, :], in_=ubuf[c][:, TB:TB + 3]), reads=[t_ubuf[c]], writes=[t_halo], join=True)
                P.op("dve", lambda e, c=c, tile=tile: e.tensor_scalar(out=xc[c], in0=ubuf[c][:, 3:3 + TB], scalar1=cols[:, 3, tile:tile + 1], scalar2=cols[:, 4, tile:tile + 1], op0=ALU.mult, op1=ALU.add),
                     reads=[t_ubuf[c], tl], writes=[t_xc[c]])
                for tau in (2, 1, 0):
                    P.op("dve", lambda e, c=c, tile=tile, tau=tau: e.scalar_tensor_tensor(out=xc[c], in0=ubuf[c][:, tau:tau + TB], scalar=cols[:, tau, tile:tile + 1], in1=xc[c], op0=ALU.mult, op1=ALU.add),
                         reads=[t_ubuf[c], tl, t_xc[c]], writes=[t_xc[c]])
                P.op("pool", lambda e, c=c: e.tensor_copy(out=xcb[c], in_=xc[c]), reads=[t_xc[c]], writes=[t_xcb[c]])
                for tq in range(TB // 512):
                    pb = pcnt % 4; pcnt += 1
                    ps, tps = C.psf[pb], C.t_psf[pb]
                    for k in range(16):
                        P.op("pe", lambda e, ps=ps, k=k, ws=ws, c=c, tq=tq: e.matmul(ps, lhsT=wt[ws][:, k, 256 + c * 128:256 + (c + 1) * 128], rhs=hT[:, k, tq * 512:(tq + 1) * 512], start=(k == 0), stop=(k == 15)),
                             reads=[t_wt[ws]] + hT_toks[tq * 4:(tq + 1) * 4], writes=[tps], join=(k > 0))
                    P.op("act", lambda e, ps=ps, c=c, tq=tq: e.activation(out=sg[c][:, tq * 512:(tq + 1) * 512], in_=ps, func=AF.Silu), reads=[tps], writes=[t_sg[c]], join=True)
            for ce in range(2):
                tile = n * 2 + ce
                for (wg, bidx, dstb, tdst) in ((wrg, 5, rg, t_rg), (wig, 6, ig, t_ig)):
                    for tq in range(TB // 512):
                        pb = pcnt % 4; pcnt += 1
                        ps, tps = C.psf[pb], C.t_psf[pb]
                        for cc in range(2):
                            P.op("pe", lambda e, ps=ps, wg=wg, cc=cc, ce=ce, tq=tq, n=n: e.matmul(ps, lhsT=wg[:, n, cc, ce * 128:(ce + 1) * 128], rhs=xcb[cc][:, tq * 512:(tq + 1) * 512], start=(cc == 0), stop=(cc == 1)),
                                 reads=[t_wg, t_xcb[cc]], writes=[tps], join=(cc > 0))
                        P.op("act", lambda e, ps=ps, dstb=dstb, ce=ce, tq=tq, bidx=bidx, tile=tile: e.activation(out=dstb[ce][:, tq * 512:(tq + 1) * 512], in_=ps, func=AF.Sigmoid, bias=cols[:, bidx, tile:tile + 1], scale=1.0),
                             reads=[tps, tl], writes=[tdst[ce]], join=True)
                P.op("act", lambda e, ce=ce, tile=tile: e.activation(out=abuf, in_=rg[ce], func=AF.Exp, scale=cols[:, 7, tile:tile + 1]), reads=[t_rg[ce], tl], writes=[t_a])
                P.op("pool", lambda e: e.tensor_tensor(out=a2buf, in0=abuf, in1=abuf, op=ALU.mult), reads=[t_a], writes=[t_a2])
                P.op("act", lambda e: e.activation(out=a2buf, in_=a2buf, func=AF.Sqrt, bias=1.0, scale=-1.0), reads=[t_a2], writes=[t_a2])
                P.op("pool", lambda e, ce=ce: e.tensor_tensor(out=xinb, in0=ig[ce], in1=xc[ce], op=ALU.mult), reads=[t_ig[ce], t_xc[ce]], writes=[t_xin])
                P.op("dve", lambda e: e.tensor_tensor(out=xinb, in0=xinb, in1=a2buf, op=ALU.mult), reads=[t_xin, t_a2], writes=[t_xin])
                init = 0.0 if tb == 0 else hprev[:, tile:tile + 1]
                P.op("dve", lambda e, init=init: e.tensor_tensor_scan(out=hh, data0=abuf, data1=xinb, initial=init, op0=ALU.mult, op1=ALU.add), reads=[t_a, t_xin, t_hprev], writes=[t_hh])
                P.op("pool", lambda e, tile=tile: e.tensor_copy(out=hprev[:, tile:tile + 1], in_=hh[:, TB - 1:TB]), reads=[t_hh], writes=[t_hprev])
                P.op("pool", lambda e, ce=ce: e.tensor_tensor(out=yst[ce], in0=hh, in1=sg[ce], op=ALU.mult), reads=[t_hh, t_sg[ce]], writes=[t_yst[ce]])
                P.op("sync", lambda e, ce=ce, tile=tile, tok0=tok0: e.dma_start(out=C.YT_d[tile, :, tok0:tok0 + TB], in_=yst[ce]), reads=[t_yst[ce]], writes=[t_yd], dma=True, join=True)
    A.release(m)
    P.barrier()


def layer1_all(C, x_src, W):
    P, A = C.P, C.A
    m0 = A.mark()
    dec = A.alloc([8, 64], F32); t_dec = P.tok("dec")
    m = A.mark()
    TB = 1024; TPB = TB // 128; NTB = S // TB
    hT = A.alloc([16, TB], BF16); hT_toks = P.toks(TPB, "hT")
    wt = [A.alloc([16, 512], BF16) for _ in range(2)]; t_wt = P.toks(2, "wt")
    wlr = A.alloc([16, 16], BF16); t_wlr = P.tok("wlr")
    lrT = A.alloc([TB], F32); t_lrT = P.tok("lrT")
    wga = A.alloc([1024], F32)
    TU = A.alloc([128], F32); CI = A.alloc([2], F32)
    qst = [A.alloc([TB], BF16) for _ in range(2)]; t_qst = P.toks(2, "qst")
    ebuf = [A.alloc([512], F32) for _ in range(2)]; t_eb = P.toks(2, "ebuf")
    wbuf = [A.alloc([512], F32) for _ in range(2)]; t_wb = P.toks(2, "wbuf")
    kst = [A.alloc([8, 512], BF16) for _ in range(2)]; t_kst = P.toks(2, "kst")
    vst = [A.alloc([8, 512], BF16) for _ in range(2)]; t_vst = P.toks(2, "vst")
    gst = [A.alloc([4, 512], F32) for _ in range(2)]; t_gst = P.toks(2, "gst")
    C.t_gd = P.tok("Gd"); t_qd = P.tok("QTd"); t_kd = P.tok("KPd"); t_vd = P.tok("Vd")
    tl = C.t_lconst
    Wi = W["b_w_in"][0]
    P.op("pool", lambda e: e.memset(wga[0:32, :], 0.0), writes=[tl])
    P.op("sync", lambda e: e.dma_start(out=wga[0:16, :], in_=W["b_w_gate"][0]), writes=[tl], dma=True)
    P.op("sync", lambda e: e.dma_start(out=wga[16:17, :], in_=W["b_gate_bias"][0:1, :]), writes=[tl], dma=True, join=True)
    P.op("sync", lambda e: e.dma_start(out=TU, in_=C.TU_d), writes=[tl], dma=True, join=True)
    P.op("sync", lambda e: e.dma_start(out=CI, in_=C.CI_d), writes=[tl], dma=True, join=True)
    P.op("pool", lambda e: e.memset(lrT[0:32, :], 1.0), writes=[t_lrT])
    wcnt = 0; pcnt = 0; gcnt = 0; vcnt = 0; qcnt = 0; ecnt = 0; kcnt = 0
    for tb in range(NTB):
        tok0 = tb * TB
        phase_norm_T(C, x_src, W["norm_g"][C.layer, :], tok0, hT, hT_toks, TB)
        load_w_block(C, wlr, t_wlr, Wi, 6144, 16)
        for tq in range(TB // 512):
            pb = pcnt % 4; pcnt += 1
            ps, tps = C.psf[pb], C.t_psf[pb]
            for k in range(16):
                P.op("pe", lambda e, ps=ps, k=k, tq=tq: e.matmul(ps[0:16, :], lhsT=wlr[:, k, 0:16], rhs=hT[:, k, tq * 512:(tq + 1) * 512], start=(k == 0), stop=(k == 15)),
                     reads=[t_wlr] + hT_toks[tq * 4:(tq + 1) * 4], writes=[tps], join=(k > 0))
            P.op("act", lambda e, ps=ps, tq=tq: e.activation(out=lrT[0:16, tq * 512:(tq + 1) * 512], in_=ps[0:16, :], func=AF.Copy), reads=[tps], writes=[t_lrT], join=True)
        for cb in range(12):
            ws = wcnt % 2; wcnt += 1
            load_w_block(C, wt[ws], t_wt[ws], Wi, cb * 512, 512)
            if cb < 2:
                for mm_ in range(4):
                    qt = cb * 4 + mm_
                    qs = qcnt % 2; qcnt += 1
                    for tq in range(TB // 512):
                        pb = pcnt % 4; pcnt += 1
                        ps, tps = C.psf[pb], C.t_psf[pb]
                        for k in range(16):
                            P.op("pe", lambda e, ps=ps, k=k, ws=ws, mm_=mm_, tq=tq: e.matmul(ps, lhsT=wt[ws][:, k, mm_ * 128:(mm_ + 1) * 128], rhs=hT[:, k, tq * 512:(tq + 1) * 512], start=(k == 0), stop=(k == 15)),
                                 reads=[t_wt[ws]] + hT_toks[tq * 4:(tq + 1) * 4], writes=[tps], join=(k > 0))
                        P.op("act", lambda e, ps=ps, qs=qs, tq=tq: e.activation(out=qst[qs][:, tq * 512:(tq + 1) * 512], in_=ps, func=AF.Copy, scale=1.0 / 16.0), reads=[tps], writes=[t_qst[qs]], join=True)
                    P.op("sync", lambda e, qt=qt, qs=qs, tok0=tok0: e.dma_start(out=C.QT_d[qt, :, tok0:tok0 + TB], in_=qst[qs]), reads=[t_qst[qs]], writes=[t_qd], dma=True, join=True)
            elif cb < 4:
                kb = cb - 2
                ks = kcnt % 2; kcnt += 1
                for i in range(TPB):
                    pb = pcnt % 4; pcnt += 1
                    ps, tps = C.psf[pb], C.t_psf[pb]
                    for k in range(16):
                        P.op("pe", lambda e, ps=ps, k=k, ws=ws, i=i: e.matmul(ps, lhsT=hT[:, k, i * 128:(i + 1) * 128], rhs=wt[ws][:, k, :], start=(k == 0), stop=(k == 15)),
                             reads=[t_wt[ws], hT_toks[i]], writes=[tps], join=(k > 0))
                    es = ecnt % 2; ecnt += 1
                    pz, tpz = C.psf[4], C.t_psf[4]
                    P.op("pe", lambda e, pz=pz, i=i, kb=kb: e.matmul(pz, lhsT=lrT[0:32, i * 128:(i + 1) * 128], rhs=wga[0:32, kb * 512:(kb + 1) * 512], start=True, stop=True),
                         reads=[t_lrT, tl], writes=[tpz])
                    P.op("act", lambda e, pz=pz, es=es: e.activation(out=ebuf[es], in_=pz, func=AF.Exp, scale=-1.0), reads=[tpz], writes=[t_eb[es]])
                    P.op("act", lambda e, es=es: e.activation(out=ebuf[es], in_=ebuf[es], func=AF.Ln, bias=1.0, scale=1.0), reads=[t_eb[es]], writes=[t_eb[es]])
                    pr_, tpr = C.psf[5], C.t_psf[5]
                    P.op("pe", lambda e, pr_=pr_, es=es: e.matmul(pr_, lhsT=TU, rhs=ebuf[es], start=True, stop=True), reads=[t_eb[es], tl], writes=[tpr])
                    P.op("act", lambda e, pr_=pr_, es=es: e.activation(out=wbuf[es], in_=pr_, func=AF.Exp), reads=[tpr], writes=[t_wb[es]])
                    P.op("dve", lambda e, ps=ps, es=es, ks=ks, i=i: e.tensor_tensor(out=kst[ks][:, i, :], in0=ps, in1=wbuf[es], op=ALU.mult), reads=[tps, t_wb[es]], writes=[t_kst[ks]], join=True)
                    for dt in range(4):
                        P.op("pe", lambda e, pz=pz, es=es, dt=dt: e.matmul(pz[:, dt * 2:dt * 2 + 2], lhsT=ebuf[es][:, dt * 128:(dt + 1) * 128], rhs=CI, start=(dt == 0), stop=(dt == 3), skip_group_check=True),
                             reads=[t_eb[es], tl], writes=[tpz], join=(dt > 0))
                    ch0 = (tok0 + i * 128) // 64
                    P.op("act", lambda e, pz=pz, kb=kb, ch0=ch0: e.activation(out=dec[:, kb * 4:(kb + 1) * 4, ch0:ch0 + 2], in_=pz[:, 0:8].rearrange("p (a b) -> p a b", a=4), func=AF.Exp), reads=[tpz], writes=[t_dec], join=True)
                P.op("sync", lambda e, ks=ks, tok0=tok0, kb=kb: e.dma_start(out=C.KP_d[tok0:tok0 + TB, kb * 512:(kb + 1) * 512].rearrange("(a p) c -> p a c", p=128), in_=kst[ks]),
                     reads=[t_kst[ks]], writes=[t_kd], dma=True, join=True)
            elif cb < 8:
                c0 = (cb - 4) * 512
                vs = vcnt % 2; vcnt += 1
                for i in range(TPB):
                    pb = pcnt % 4; pcnt += 1
                    ps, tps = C.psf[pb], C.t_psf[pb]
                    for k in range(16):
                        P.op("pe", lambda e, ps=ps, k=k, ws=ws, i=i: e.matmul(ps, lhsT=hT[:, k, i * 128:(i + 1) * 128], rhs=wt[ws][:, k, :], start=(k == 0), stop=(k == 15)),
                             reads=[t_wt[ws], hT_toks[i]], writes=[tps], join=(k > 0))
                    P.op("dve", lambda e, ps=ps, vs=vs, i=i: e.tensor_copy(out=vst[vs][:, i, :], in_=ps), reads=[tps], writes=[t_vst[vs]], join=True)
                P.op("sync", lambda e, vs=vs, tok0=tok0, c0=c0: e.dma_start(out=C.V_d[tok0:tok0 + TB, c0:c0 + 512].rearrange("(a p) c -> p a c", p=128), in_=vst[vs]),
                     reads=[t_vst[vs]], writes=[t_vd], dma=True, join=True)
            else:
                pcnt, gcnt = g_block(C, hT, hT_toks, wt[ws], t_wt[ws], tok0, (cb - 8) * 512, pcnt, gst, t_gst, gcnt, TPB)
    A.release(m)
    P.barrier()
    m = A.mark()
    Kp = A.alloc([NT, 256], BF16); t_Kp = P.tok("Kp")
    Vh = A.alloc([NT, 512], BF16); t_Vh = P.tok("Vh")
    QTt = [A.alloc([S], BF16) for _ in range(2)]; t_QTt = P.tok("QTt")
    Sf = A.alloc([2, 512], F32); t_Sf = P.tok("Sf")
    Sb = [A.alloc([2, 512], BF16) for _ in range(2)]; t_Sb = P.toks(2, "Sb")
    Gt = [A.alloc([2, 512], F32) for _ in range(2)]; t_Gt = P.toks(2, "Gt")
    yT = A.alloc([4, S], BF16); t_yT = P.tok("yT")
    ogb = A.alloc([512], F32)
    of = A.alloc([512], F32); yb = A.alloc([512], BF16); ssq = A.alloc([1], F32); junk = A.alloc([512], BF16)
    t_ep = P.tok("ep"); t_yd = P.tok("YTd")
    P.op("sync", lambda e: e.dma_start(out=ogb, in_=W["b_out_g"][0, :].partition_broadcast(128)), writes=[tl], dma=True)

    def emit_kv(hh, c):
        i, par = c // 2, c % 2
        pr0 = par * 64
        for dh in range(2):
            kv, tkv = C.psf[2 * (c % 2) + dh], C.t_psf[2 * (c % 2) + dh]
            P.op("pe", lambda e, kv=kv, i=i, pr0=pr0, dh=dh: e.matmul(kv, lhsT=Kp[pr0:pr0 + 64, i, dh * 128:(dh + 1) * 128], rhs=Vh[pr0:pr0 + 64, i, :], start=True, stop=True),
                 reads=[t_Kp, t_Vh], writes=[tkv])

    for hh in range(4):
        P.op("sync", lambda e, hh=hh: e.dma_start(out=Kp, in_=C.KP_d[:, hh * 256:(hh + 1) * 256].rearrange("(a p) c -> p a c", p=128)), writes=[t_Kp], dma=True)
        P.op("sync", lambda e, hh=hh: e.dma_start(out=Vh, in_=C.V_d[:, hh * 512:(hh + 1) * 512].rearrange("(a p) c -> p a c", p=128)), writes=[t_Vh], dma=True)
        for dh in range(2):
            P.op("sync", lambda e, hh=hh, dh=dh: e.dma_start(out=QTt[dh], in_=C.QT_d[2 * hh + dh]), writes=[t_QTt], dma=True, join=(dh > 0))
        P.op("pool", lambda e: e.memset(Sf, 0.0), writes=[t_Sf])
        emit_kv(hh, 0)
        for c in range(S // 64):
            i, par = c // 2, c % 2
            if par == 0:
                gs = i % 2
                P.op("sync", lambda e, gs=gs, i=i, hh=hh: e.dma_start(out=Gt[gs][0:64, :, :], in_=C.G_d[i * 128:(i + 1) * 128, hh * 512:(hh + 1) * 512].rearrange("(par p) c -> p par c", p=64)), writes=[t_Gt[gs]], dma=True)
            sbs = c % 2
            for dh in range(2):
                kv, tkv = C.psf[2 * (c % 2) + dh], C.t_psf[2 * (c % 2) + dh]
                P.op("dve", lambda e, kv=kv, dh=dh, hh=hh, c=c: e.scalar_tensor_tensor(out=Sf[:, dh, :], in0=Sf[:, dh, :], scalar=dec[:, 2 * hh + dh, c:c + 1], in1=kv, op0=ALU.mult, op1=ALU.add),
                     reads=[t_Sf, t_dec, tkv], writes=[t_Sf])
                P.op("act", lambda e, dh=dh, sbs=sbs: e.activation(out=Sb[sbs][:, dh, :], in_=Sf[:, dh, :], func=AF.Copy), reads=[t_Sf], writes=[t_Sb[sbs]], join=(dh > 0))
            if c + 1 < S // 64:
                emit_kv(hh, c + 1)
            po, tpo = C.psf[4 + c % 2], C.t_psf[4 + c % 2]
            for dh in range(2):
                P.op("pe", lambda e, po=po, dh=dh, c=c, sbs=sbs: e.matmul(po[0:64, :], lhsT=QTt[dh][:, c * 64:(c + 1) * 64], rhs=Sb[sbs][:, dh, :], start=(dh == 0), stop=(dh == 1)),
                     reads=[t_QTt, t_Sb[sbs]], writes=[tpo], join=(dh > 0))
            P.op("act", lambda e, po=po: e.activation(out=junk[0:64, :], in_=po[0:64, :], func=AF.Square, accum_out=ssq[0:64, :]), reads=[tpo], writes=[t_ep])
            P.op("act", lambda e: e.activation(out=ssq[0:64, :], in_=ssq[0:64, :], func=AF.Sqrt, bias=C.epscol[0:64, :], scale=1.0 / 512), reads=[t_ep, C.t_const], writes=[t_ep])
            P.op("dve", lambda e: e.reciprocal(out=ssq[0:64, :], in_=ssq[0:64, :]), reads=[t_ep], writes=[t_ep])
            P.op("dve", lambda e, po=po: e.scalar_tensor_tensor(out=of[0:64, :], in0=po[0:64, :], scalar=ssq[0:64, :], in1=ogb[0:64, :], op0=ALU.mult, op1=ALU.mult), reads=[tpo, t_ep, tl], writes=[t_ep])
            P.op("pool", lambda e, gs=gs, par=par: e.tensor_tensor(out=yb[0:64, :], in0=of[0:64, :], in1=Gt[gs][0:64, par, :], op=ALU.mult), reads=[t_ep, t_Gt[gs]], writes=[t_ep])
            tp, ttp = C.tpb[c % 2], C.t_tpb[c % 2]
            for j in range(4):
                P.op("pe", lambda e, tp=tp, j=j: e.transpose(out=tp[:, j * 64:(j + 1) * 64], in_=yb[0:64, j * 128:(j + 1) * 128], identity=C.ident[0:64, 0:64]), reads=[t_ep, C.t_const], writes=[ttp], join=True)
            P.op("dve", lambda e, tp=tp, c=c: e.tensor_copy(out=yT[:, :, c * 64:(c + 1) * 64], in_=tp[:, 0:256].rearrange("p (a b) -> p a b", a=4)), reads=[ttp], writes=[t_yT], join=True)
        for j in range(4):
            P.op("sync", lambda e, hh=hh, j=j: e.dma_start(out=C.YT_d[hh * 4 + j], in_=yT[:, j, :]), reads=[t_yT], writes=[t_yd], dma=True, join=True)
    A.release(m0)
    P.barrier()


WSHAPES = {
    "norm_g": (4, 2048), "rel_bias": (32, 16),
    "a_w_in": (1, 2048, 8192), "a_qk_g": (1, 2, 64), "a_lambda": (1, 4, 64), "a_subln_g": (1, 128), "a_w_out": (1, 2048, 2048),
    "b_w_in": (1, 2048, 6160), "b_w_gate": (1, 16, 1024), "b_gate_bias": (1, 1024), "b_out_g": (1, 512), "b_w_out": (1, 2048, 2048),
    "c_w_in": (1, 2048, 4096), "c_conv_w": (1, 4, 2048), "c_conv_b": (1, 2048), "c_w_rgate": (1, 8, 256, 256), "c_b_rgate": (1, 2048),
    "c_w_igate": (1, 8, 256, 256), "c_b_igate": (1, 2048), "c_lambda": (1, 2048), "c_w_out": (1, 2048, 2048),
    "d_w_in": (1, 2048, 3136), "d_q_lat_g": (1, 512), "d_kv_lat_g": (1, 512), "d_w_uq": (1, 512, 3072), "d_w_ukv": (1, 512, 4096),
    "d_qk_g": (1, 2, 192), "d_w_out": (1, 2048, 2048),
}
LAYER_W = {
    0: ["norm_g", "rel_bias", "a_w_in", "a_qk_g", "a_lambda", "a_subln_g", "a_w_out"],
    1: ["norm_g", "b_w_in", "b_w_gate", "b_gate_bias", "b_out_g", "b_w_out"],
    2: ["norm_g", "c_w_in", "c_conv_w", "c_conv_b", "c_w_rgate", "c_b_rgate", "c_w_igate", "c_b_igate", "c_lambda", "c_w_out"],
    3: ["norm_g", "d_w_in", "d_q_lat_g", "d_kv_lat_g", "d_w_uq", "d_w_ukv", "d_qk_g", "d_w_out"],
}


def t5_bucket_np(rel):
    nb = 16; max_exact = 8
    ret = np.where(rel > 0, nb, 0)
    n = np.abs(rel)
    nf = np.maximum(n, 1).astype(np.float32)
    large = max_exact + (np.log(nf / max_exact) / math.log(128 / max_exact) * (nb - max_exact)).astype(np.int32)
    large = np.minimum(large, nb - 1)
    return ret + np.where(n < max_exact, n, large)


def bias_index_tiles():
    k = np.arange(128)[:, None]; q = np.arange(128)[None, :]
    idx = np.zeros((128, 2, 128), np.int64); msk = np.zeros((128, 2, 128), bool)
    idx[:, 0, :] = t5_bucket_np(k - q)
    msk[:, 0, :] = (k // 64) > (q // 64)
    idx[:, 1, :] = t5_bucket_np(k - q - 128)
    return idx, msk


def rope_tables():
    half = 32
    inv = (np.float32(10000.0) ** (-np.arange(half, dtype=np.float32) / np.float32(half))).astype(np.float32)
    ang = (np.arange(S, dtype=np.float32)[:, None] * inv[None, :]).astype(np.float32)
    c = np.cos(ang).astype(np.float32).T; s_ = np.sin(ang).astype(np.float32).T
    return np.ascontiguousarray(np.stack([np.concatenate([c, c], 0), np.concatenate([s_, s_], 0)], 0))


def build_program(layers, debug=False):
    nc = bass.Bass("TRN2", target_bir_lowering=False)
    C = Ctx()
    C.nc = nc
    x_in = dram(nc, "x", [S, D], F32, "ExternalInput")
    out = dram(nc, "out", [S, D], F32, "ExternalOutput")
    names = []
    for l in layers:
        for n in LAYER_W[l]:
            if n not in names:
                names.append(n)
    W = {n: dram(nc, n, WSHAPES[n], F32, "ExternalInput") for n in names}
    if 0 in layers:
        C.biasT_in = dram(nc, "biasT", [128, 16, 2, 128], F32, "ExternalInput")
    ident_d = nc.inline_tensor(np.eye(128, dtype=np.float32), "ident_c").ap()
    o64 = np.zeros((128, 128), np.float32); o64[:64, :64] = 1; o64[64:, 64:] = 1
    ones64_d = nc.inline_tensor(o64, "ones64_c").ap()
    ones128_d = nc.inline_tensor(np.ones((128, 128), np.float32), "ones128_c").ap()
    xs = [dram(nc, f"xs{i}", [S, D], F32) for i in range(2)]
    sk = "ExternalOutput" if debug else "Internal"
    C.QT_d = dram(nc, "QT_d", [16, 128, S], BF16, sk)
    C.KT_d = dram(nc, "KT_d", [16, 128, S], BF16, sk)
    C.V_d = dram(nc, "V_d", [S, D], BF16, sk)
    C.G_d = dram(nc, "G_d", [S, D], F32, sk)
    C.YT_d = dram(nc, "YT_d", [16, 128, S], BF16, sk)
    C.GT_d = dram(nc, "GT_d", [16, 128, S], F32, sk)
    if 3 in layers:
        C.QPE_d = dram(nc, "QPE_d", [16, 64, S], BF16, sk)
        C.KPE_d = dram(nc, "KPE_d", [64, S], BF16, sk)
        C.CQ_d = dram(nc, "CQ_d", [4, 128, S], BF16, sk)
        C.CKV_d = dram(nc, "CKV_d", [4, 128, S], BF16, sk)
        C.rope_in = dram(nc, "rope_cs", [2, 64, S], F32, "ExternalInput")
        kk = np.arange(128)[:, None]; qq = np.arange(128)[None, :]
        C.maskT_d = nc.inline_tensor(np.where((kk // 64) > (qq // 64), -30000.0, 0.0).astype(np.float32), "maskT_c").ap()
    if 1 in layers:
        C.KP_d = dram(nc, "KP_d", [S, 1024], BF16, sk)
        tt = np.arange(128)
        tu = np.where((tt[:, None] > tt[None, :]) & (tt[:, None] // 64 == tt[None, :] // 64), -1.0 / 16.0, 0.0).astype(np.float32)
        ci = np.where(tt[:, None] // 64 == np.arange(2)[None, :], -1.0 / 16.0, 0.0).astype(np.float32)
        C.TU_d = nc.inline_tensor(tu, "TU_c").ap()
        C.CI_d = nc.inline_tensor(ci, "CI_c").ap()
    rot = np.zeros((128, 128), np.float32)
    for i_ in range(32):
        rot[32 + i_, i_] = -1.0; rot[i_, 32 + i_] = 1.0
    rotm_d = nc.inline_tensor(rot, "rotm_c").ap()
    with ExitStack() as ctx:
        P = Prog(nc, ctx)
        C.P = P
        A = Arena(nc, ctx, 200)
        C.A = A
        C.psf = [ctx.enter_context(nc.psum_tensor(f"psf{i}", [128, 512], F32))[:] for i in range(8)]
        C.t_psf = P.toks(8, "psf")
        C.tpb = [C.psf[6].bitcast(BF16), C.psf[7].bitcast(BF16)]
        C.t_tpb = [C.t_psf[6], C.t_psf[7]]
        C.auxcnt = 0
        C.t_const = P.tok("const"); C.t_lconst = P.tok("lconst")
        C.ident = A.alloc([128], BF16); C.ones64 = A.alloc([128], F32); C.ones128 = A.alloc([128], F32); C.epscol = A.alloc([1], F32)
        C.rotm = A.alloc([128], F32); C.identf = A.alloc([128], F32)
        P.op("sync", lambda e: e.dma_start(out=C.identf, in_=ident_d), writes=[C.t_const], dma=True, join=True)
        P.op("sync", lambda e: e.dma_start(out=C.rotm, in_=rotm_d), writes=[C.t_const], dma=True, join=True)
        P.op("pool", lambda e: e.dma_start(out=C.ident, in_=ident_d), writes=[C.t_const], dma=True, join=True)
        P.op("sync", lambda e: e.dma_start(out=C.ones64, in_=ones64_d), writes=[C.t_const], dma=True, join=True)
        P.op("sync", lambda e: e.dma_start(out=C.ones128, in_=ones128_d), writes=[C.t_const], dma=True, join=True)
        P.op("dve", lambda e: e.memset(C.epscol, EPS), writes=[C.t_const], join=True)
        P.barrier()
        cur = x_in
        for li, l in enumerate(layers):
            C.layer = l
            dst = out if li == len(layers) - 1 else xs[li % 2]
            if l == 0:
                layer0_proj(C, cur, W)
                attn_phase(C, W, "diff")
                phase_outproj(C, C.YT_d, W["a_w_out"][0], cur, dst)
            elif l == 1:
                layer1_all(C, cur, W)
                phase_outproj(C, C.YT_d, W["b_w_out"][0], cur, dst)
            elif l == 2:
                layer2_all(C, cur, W)
                phase_outproj(C, C.YT_d, W["c_w_out"][0], cur, dst)
            elif l == 3:
                layer3_proj1(C, cur, W)
                layer3_proj2(C, W)
                attn_phase(C, W, "mla")
                phase_outproj(C, C.YT_d, W["d_w_out"][0], cur, dst)
            else:
                raise NotImplementedError
            cur = dst
        P.emit()
    C.names = names
    return nc, C


_CACHE = {}


def run_layers(layers, x, inputs, n_cores=4):
    key = tuple(layers)
    if key not in _CACHE:
        _CACHE[key] = build_program(layers)
    nc, C = _CACHE[key]
    shared = {n: np.ascontiguousarray(inputs[n], dtype=np.float32) for n in C.names}
    if 0 in layers:
        idx, msk = bias_index_tiles()
        rb = np.asarray(inputs["rel_bias"], np.float32)
        bt = rb[idx]
        bt = np.where(msk[..., None], np.float32(-30000.0), bt)
        shared["biasT"] = np.ascontiguousarray(bt.transpose(0, 3, 1, 2))
    if 3 in layers:
        shared["rope_cs"] = rope_tables()
    in_maps = [dict(shared, x=np.ascontiguousarray(x[b])) for b in range(n_cores)]
    res = run_bass_kernel_spmd(nc, in_maps, core_ids=list(range(n_cores)))
    C.last_res = res
    return np.stack([r["out"] for r in res.results], axis=0)


def kernel(**inputs):
    x = np.asarray(inputs["x"], np.float32)
    return run_layers([0, 1, 2, 3], x, inputs)
```

```python
import math
from contextlib import ExitStack
import numpy as np
import concourse.bass as bass
import concourse.mybir as mybir
from concourse.bass_utils import run_bass_kernel_spmd

F32 = mybir.dt.float32
BF16 = mybir.dt.bfloat16
AF = mybir.ActivationFunctionType
ALU = mybir.AluOpType
AX = mybir.AxisListType

ENGS = ("sync", "act", "pool", "dve", "pe")


class Tok:
    __slots__ = ("name", "w", "r", "pool")

    def __init__(self, name):
        self.name = name
        self.w = {}
        self.r = {}
        self.pool = None


class Prog:
    def __init__(self, nc, ctx, same_engine_sync=("act", "dve", "pool")):
        self.nc = nc
        self.ctx = ctx
        self.q = {e: [] for e in ENGS}
        self.cnt = {e: 0 for e in ENGS}
        self.waited = {e: {} for e in ENGS}
        self.pend = {e: {} for e in ENGS}
        self.needed = set()
        self.same_sync = set(same_engine_sync)
        self.pool_val = []
        self.pool_free = []
        self.live = []
        self.all_toks = []

    def tok(self, name="t"):
        t = Tok(name)
        self.all_toks.append(t)
        return t

    def toks(self, n, name="t"):
        return [self.tok(f"{name}{i}") for i in range(n)]

    def op(self, eng, fn, reads=(), writes=(), dma=False, join=False):
        deps = dict(self.pend[eng])
        self.pend[eng] = {}

        def add(k, v):
            if deps.get(k, 0) < v:
                deps[k] = v

        for t in reads:
            for k, v in t.w.items():
                add(k, v)
        for t in writes:
            if not (join and not t.r):
                for k, v in t.w.items():
                    add(k, v)
            for k, v in t.r.items():
                add(k, v)
        waits = []
        wd = self.waited[eng]
        for k, v in deps.items():
            if k == ("e", eng) and (eng not in self.same_sync or self.cnt[eng] + 1 - v >= 3):
                continue
            if wd.get(k, 0) < v:
                wd[k] = v
                waits.append((k, v))
                self.needed.add((k, v))
        if dma:
            assert len(writes) == 1
            t = writes[0]
            if t.pool is None:
                if self.pool_free:
                    t.pool = self.pool_free.pop()
                else:
                    self.pool_val.append(0)
                    t.pool = len(self.pool_val) - 1
                self.live.append(t)
            self.pool_val[t.pool] += 16
            h = (("s", t.pool), self.pool_val[t.pool])
        else:
            self.cnt[eng] += 1
            h = (("e", eng), self.cnt[eng])
        k, v = h
        for t in reads:
            if t.r.get(k, 0) < v:
                t.r[k] = v
        for t in writes:
            if join and not t.r:
                t.w[k] = v
            else:
                t.w = {k: v}
                t.r = {}
        self.q[eng].append((waits, fn, h, dma))
        return h

    def barrier(self):
        hs = {}
        for e in ENGS:
            if self.cnt[e] > 0:
                hs[("e", e)] = self.cnt[e]
        for i, v in enumerate(self.pool_val):
            if v > 0:
                hs[("s", i)] = v
        for e in ENGS:
            for k, v in hs.items():
                if self.pend[e].get(k, 0) < v:
                    self.pend[e][k] = v
        for t in self.live:
            self.pool_free.append(t.pool)
            t.pool = None
        self.live = []
        for t in self.all_toks:
            t.w = {}
            t.r = {}
            t.pool = None

    def emit(self):
        nc = self.nc
        self.barrier()
        fw = []
        wd = self.waited["sync"]
        for k, v in self.pend["sync"].items():
            if wd.get(k, 0) < v:
                fw.append((k, v))
                self.needed.add((k, v))
        esem = {e: self.ctx.enter_context(nc.semaphore(f"sem_{e}")) for e in ENGS}
        psem = [self.ctx.enter_context(nc.semaphore(f"dsem{i}")) for i in range(len(self.pool_val))]
        rank = {}
        for e in ENGS:
            r = 0
            for (_w, _f, h, dma) in self.q[e]:
                if not dma and h in self.needed:
                    r += 1
                    rank[h] = r

        def resolve(h):
            k, v = h
            if k[0] == "e":
                return esem[k[1]], rank[h]
            return psem[k[1]], v

        block = self.ctx.enter_context(nc.Block())
        names = {"sync": "sync", "act": "scalar", "pool": "gpsimd", "dve": "vector", "pe": "tensor"}
        for e in ENGS:
            ops = self.q[e]
            if not ops and e != "sync":
                continue

            def body(eng, ops=ops, e=e):
                for (waits, fn, h, dma) in ops:
                    for w in waits:
                        s, v = resolve(w)
                        eng.wait_ge(s, v)
                    ins = fn(eng)
                    if dma:
                        s, v = resolve(h)
                        ins.then_inc(s, 16)
                    elif h in self.needed:
                        s, v = resolve(h)
                        ins.then_inc(s, 1)
                if e == "sync":
                    for w in fw:
                        s, v = resolve(w)
                        eng.wait_ge(s, v)

            getattr(block, names[e])(body)
        self.n_sems = len(psem) + 5


class Arena:
    def __init__(self, nc, ctx, kib):
        self.n32 = kib * 256
        self.t = ctx.enter_context(nc.sbuf_tensor("arena", [128, self.n32], F32))
        self.off = 0

    def mark(self):
        return self.off

    def release(self, m):
        self.off = m

    def alloc(self, shape, dt, parts=128):
        n = int(np.prod(shape))
        n32 = n if dt == F32 else (n + 1) // 2
        n32 = (n32 + 7) // 8 * 8
        assert self.off + n32 <= self.n32, f"arena overflow {self.off + n32} > {self.n32}"
        v = self.t[0:parts, self.off:self.off + n32]
        self.off += n32
        if dt != F32:
            v = v.bitcast(dt)
        v = v[:, 0:n]
        if len(shape) == 2:
            v = v.rearrange("p (a b) -> p a b", a=shape[0])
        elif len(shape) == 3:
            v = v.rearrange("p (a b c) -> p a b c", a=shape[0], b=shape[1])
        return v


S = 4096
D = 2048
EPS = 1e-6
NT = S // 128
TB = 2048
NTB = S // TB
TPB = TB // 128


class Ctx:
    pass


def dram(nc, name, shape, dt, kind="Internal"):
    return nc.dram_tensor(name, list(shape), dt, kind=kind).ap()


def load_w_block(C, dst, dtok, wsrc, c0, ncols, kc=16, rows0=0):
    P = C.P
    wv = wsrc[rows0:rows0 + kc * 128, :].rearrange("(k p) c -> p k c", p=128)
    step = 4 if kc >= 4 else kc
    for k0 in range(0, kc, step):
        k1 = min(kc, k0 + step)
        P.op("pool", lambda e, k0=k0, k1=k1: e.dma_start(out=dst[:, k0:k1, 0:ncols], in_=wv[:, k0:k1, c0:c0 + ncols]),
             writes=[dtok], dma=True, join=True)


def phase_norm_T(C, x_src, g_row, tok0, hT, hT_toks, tbsz=TB):
    P, A = C.P, C.A
    m = A.mark()
    xin = [A.alloc([D], F32) for _ in range(2)]
    hb = [A.alloc([D], BF16) for _ in range(2)]
    gb = A.alloc([D], F32)
    ssq = [A.alloc([1], F32) for _ in range(2)]
    t_xin = P.toks(2, "xin"); t_hb = P.toks(2, "hb"); t_gb = P.tok("gb"); t_ssq = P.toks(2, "ssq")
    P.op("sync", lambda e: e.dma_start(out=gb, in_=g_row.partition_broadcast(128)), writes=[t_gb], dma=True)
    for i in range(tbsz // 128):
        s = i % 2
        r0 = tok0 + i * 128
        P.op("sync", lambda e, s=s, r0=r0: e.dma_start(out=xin[s], in_=x_src[r0:r0 + 128, :]), writes=[t_xin[s]], dma=True)
        P.op("act", lambda e, s=s: e.activation(out=hb[s], in_=xin[s], func=AF.Square, accum_out=ssq[s]),
             reads=[t_xin[s]], writes=[t_hb[s], t_ssq[s]])
        P.op("act", lambda e, s=s: e.activation(out=ssq[s], in_=ssq[s], func=AF.Ln, bias=C.epscol, scale=1.0 / D),
             reads=[t_ssq[s], C.t_const], writes=[t_ssq[s]])
        P.op("act", lambda e, s=s: e.activation(out=ssq[s], in_=ssq[s], func=AF.Exp, scale=-0.5), reads=[t_ssq[s]], writes=[t_ssq[s]])
        P.op("dve", lambda e, s=s: e.scalar_tensor_tensor(out=hb[s], in0=xin[s], scalar=ssq[s], in1=gb, op0=ALU.mult, op1=ALU.mult),
             reads=[t_xin[s], t_ssq[s], t_gb], writes=[t_hb[s]])
        for half in range(2):
            tp, ttp = C.tpb[half], C.t_tpb[half]
            for j in range(8):
                k = half * 8 + j
                P.op("pe", lambda e, s=s, j=j, k=k, tp=tp: e.transpose(out=tp[:, j * 128:(j + 1) * 128], in_=hb[s][:, k * 128:(k + 1) * 128], identity=C.ident),
                     reads=[t_hb[s], C.t_const], writes=[ttp], join=True)
            eng = "act" if half == 0 else "dve"
            dst = hT[:, half * 8:(half + 1) * 8, i * 128:(i + 1) * 128]
            srcv = tp.rearrange("p (a b) -> p a b", a=8)
            if eng == "act":
                P.op("act", lambda e, dst=dst, srcv=srcv: e.activation(out=dst, in_=srcv, func=AF.Copy), reads=[ttp], writes=[hT_toks[i]], join=True)
            else:
                P.op("dve", lambda e, dst=dst, srcv=srcv: e.tensor_copy(out=dst, in_=srcv), reads=[ttp], writes=[hT_toks[i]], join=True)
    A.release(m)


def phase_outproj(C, yT_d, w_out, x_src, x_dst):
    P, A = C.P, C.A
    m = A.mark()
    wo = A.alloc([16, D], BF16)
    yT = A.alloc([16, TB], BF16)
    xin = [A.alloc([D], F32) for _ in range(2)]
    xo = [A.alloc([D], F32) for _ in range(2)]
    t_wo = P.tok("wo"); t_yT = P.toks(16, "yT"); t_xin = P.toks(2, "xin"); t_xo = P.toks(2, "xo"); t_dst = P.tok("xdst")
    for n in range(4):
        load_w_block(C, wo[:, :, n * 512:(n + 1) * 512], t_wo, w_out, n * 512, 512)
    cnt = 0
    for tb in range(NTB):
        tok0 = tb * TB
        for k in range(16):
            P.op("sync", lambda e, k=k, tok0=tok0: e.dma_start(out=yT[:, k, :], in_=yT_d[k, :, tok0:tok0 + TB]), writes=[t_yT[k]], dma=True)
        for i in range(TPB):
            s = i % 2
            r0 = tok0 + i * 128
            P.op("sync", lambda e, s=s, r0=r0: e.dma_start(out=xin[s], in_=x_src[r0:r0 + 128, :]), writes=[t_xin[s]], dma=True)
            for n in range(4):
                pb = cnt % 4; cnt += 1
                ps, tps = C.psf[pb], C.t_psf[pb]
                for k in range(16):
                    P.op("pe", lambda e, ps=ps, k=k, i=i, n=n: e.matmul(ps, lhsT=yT[:, k, i * 128:(i + 1) * 128], rhs=wo[:, k, n * 512:(n + 1) * 512], start=(k == 0), stop=(k == 15)),
                         reads=[t_yT[k], t_wo], writes=[tps], join=(k > 0))
                P.op("dve", lambda e, ps=ps, s=s, n=n: e.tensor_tensor(out=xo[s][:, n * 512:(n + 1) * 512], in0=ps, in1=xin[s][:, n * 512:(n + 1) * 512], op=ALU.add),
                     reads=[tps, t_xin[s]], writes=[t_xo[s]], join=(n > 0))
            P.op("sync", lambda e, s=s, r0=r0: e.dma_start(out=x_dst[r0:r0 + 128, :], in_=xo[s]), reads=[t_xo[s]], writes=[t_dst], dma=True)
    A.release(m)
    P.barrier()


def qk_norm_epilogue(C, ps, tps, gcol, dst, t_dst, tmp, t_tmp, grp):
    P = C.P
    qf, sq, rs = tmp
    ones = C.ones64 if grp == 64 else C.ones128
    P.op("act", lambda e: e.activation(out=qf, in_=ps, func=AF.Copy), reads=[tps], writes=[t_tmp[0]])
    P.op("act", lambda e: e.activation(out=sq, in_=ps, func=AF.Square), reads=[tps], writes=[t_tmp[1]])
    pss, tpss = C.psf[4 + C.auxcnt % 2], C.t_psf[4 + C.auxcnt % 2]
    C.auxcnt += 1
    P.op("pe", lambda e: e.matmul(pss, lhsT=ones, rhs=sq, start=True, stop=True), reads=[t_tmp[1], C.t_const], writes=[tpss])
    P.op("act", lambda e: e.activation(out=rs, in_=pss, func=AF.Ln, bias=C.epscol, scale=1.0 / grp), reads=[tpss, C.t_const], writes=[t_tmp[2]])
    P.op("act", lambda e: e.activation(out=rs, in_=rs, func=AF.Exp, scale=-0.5), reads=[t_tmp[2]], writes=[t_tmp[2]])
    P.op("dve", lambda e: e.scalar_tensor_tensor(out=dst, in0=qf, scalar=gcol, in1=rs, op0=ALU.mult, op1=ALU.mult),
         reads=[t_tmp[0], t_tmp[2], C.t_lconst], writes=[t_dst], join=True)


def gT_block(C, hT, hT_toks, wt_s, t_wt_s, tok0, h0, tbsz, pcnt, gst, t_gst, gcnt):
    P = C.P
    for mm_ in range(4):
        gs = gcnt % 2; gcnt += 1
        for tq in range(tbsz // 512):
            pb = pcnt % 4; pcnt += 1
            ps, tps = C.psf[pb], C.t_psf[pb]
            for k in range(16):
                P.op("pe", lambda e, ps=ps, k=k, mm_=mm_, tq=tq: e.matmul(ps, lhsT=wt_s[:, k, mm_ * 128:(mm_ + 1) * 128], rhs=hT[:, k, tq * 512:(tq + 1) * 512], start=(k == 0), stop=(k == 15)),
                     reads=[t_wt_s] + hT_toks[tq * 4:(tq + 1) * 4], writes=[tps], join=(k > 0))
            P.op("act", lambda e, ps=ps, gs=gs, tq=tq: e.activation(out=gst[gs][:, tq * 512:(tq + 1) * 512], in_=ps, func=AF.Silu), reads=[tps], writes=[t_gst[gs]], join=True)
        P.op("sync", lambda e, gs=gs, mm_=mm_: e.dma_start(out=C.GT_d[h0 + mm_, :, tok0:tok0 + tbsz], in_=gst[gs][:, 0:tbsz]), reads=[t_gst[gs]], writes=[C.t_gd], dma=True, join=True)
    return pcnt, gcnt


def layer0_proj(C, x_src, W):
    P, A = C.P, C.A
    m = A.mark()
    hT = A.alloc([16, TB], BF16)
    hT_toks = P.toks(TPB, "hT")
    wt = [A.alloc([16, 512], BF16) for _ in range(2)]
    t_wt = P.toks(2, "wt")
    qst = [A.alloc([TB], BF16) for _ in range(2)]
    t_qst = P.toks(2, "qst")
    tmp = [[A.alloc([512], F32) for _ in range(3)] for _ in range(2)]
    t_tmp = [P.toks(3, "tmp") for _ in range(2)]
    vst = [A.alloc([8, 512], BF16) for _ in range(2)]
    t_vst = P.toks(2, "vst")
    gst = [A.alloc([TB], F32) for _ in range(2)]
    t_gst = P.toks(2, "gst")
    t_qd = P.tok("QTd"); t_kd = P.tok("KTd"); t_vd = P.tok("Vd"); C.t_gd = P.tok("Gd")
    gq = A.alloc([1], F32); gk = A.alloc([1], F32)
    for half in range(2):
        P.op("sync", lambda e, half=half: e.dma_start(out=gq[half * 64:(half + 1) * 64, :], in_=W["a_qk_g"][0, 0, :].rearrange("(d o) -> d o", o=1)), writes=[C.t_lconst], dma=True, join=True)
        P.op("sync", lambda e, half=half: e.dma_start(out=gk[half * 64:(half + 1) * 64, :], in_=W["a_qk_g"][0, 1, :].rearrange("(d o) -> d o", o=1)), writes=[C.t_lconst], dma=True, join=True)
    P.op("dve", lambda e: e.tensor_scalar(out=gq, in0=gq, scalar1=0.125, scalar2=None, op0=ALU.mult), reads=[C.t_lconst], writes=[C.t_lconst])
    wcnt = 0; qcnt = 0; tcnt = 0; pcnt = 0; vcnt = 0; gcnt = 0
    for tb in range(NTB):
        tok0 = tb * TB
        phase_norm_T(C, x_src, W["norm_g"][C.layer, :], tok0, hT, hT_toks)
        for cb in range(16):
            ws = wcnt % 2; wcnt += 1
            load_w_block(C, wt[ws], t_wt[ws], W["a_w_in"][0], cb * 512, 512)
            kind = cb // 4
            if kind < 2:
                for mm_ in range(4):
                    h = (cb % 4) * 4 + mm_
                    qs = qcnt % 2; qcnt += 1
                    for tq in range(TB // 512):
                        pb = pcnt % 4; pcnt += 1
                        ps, tps = C.psf[pb], C.t_psf[pb]
                        for k in range(16):
                            P.op("pe", lambda e, ps=ps, k=k, ws=ws, mm_=mm_, tq=tq: e.matmul(ps, lhsT=wt[ws][:, k, mm_ * 128:(mm_ + 1) * 128], rhs=hT[:, k, tq * 512:(tq + 1) * 512], start=(k == 0), stop=(k == 15)),
                                 reads=[t_wt[ws]] + hT_toks[tq * 4:(tq + 1) * 4], writes=[tps], join=(k > 0))
                        ts_ = tcnt % 2; tcnt += 1
                        qk_norm_epilogue(C, ps, tps, gq if kind == 0 else gk, qst[qs][:, tq * 512:(tq + 1) * 512], t_qst[qs], tmp[ts_], t_tmp[ts_], 64)
                    dd, td = (C.QT_d, t_qd) if kind == 0 else (C.KT_d, t_kd)
                    P.op("sync", lambda e, dd=dd, h=h, qs=qs, tok0=tok0: e.dma_start(out=dd[h, :, tok0:tok0 + TB], in_=qst[qs]), reads=[t_qst[qs]], writes=[td], dma=True, join=True)
            elif kind == 3:
                pcnt, gcnt = gT_block(C, hT, hT_toks, wt[ws], t_wt[ws], tok0, (cb % 4) * 4, TB, pcnt, gst, t_gst, gcnt)
            else:
                c0 = (cb % 4) * 512
                for i in range(TPB):
                    pb = pcnt % 4; pcnt += 1
                    ps, tps = C.psf[pb], C.t_psf[pb]
                    for k in range(16):
                        P.op("pe", lambda e, ps=ps, k=k, ws=ws, i=i: e.matmul(ps, lhsT=hT[:, k, i * 128:(i + 1) * 128], rhs=wt[ws][:, k, :], start=(k == 0), stop=(k == 15)),
                             reads=[t_wt[ws], hT_toks[i]], writes=[tps], join=(k > 0))
                    r0 = tok0 + i * 128
                    if True:
                        g8 = i % 8
                        if g8 == 0:
                            vs = vcnt % 2; vcnt += 1
                        P.op("dve", lambda e, ps=ps, vs=vs, g8=g8: e.tensor_copy(out=vst[vs][:, g8, :], in_=ps), reads=[tps], writes=[t_vst[vs]], join=True)
                        if g8 == 7:
                            rr = r0 - 7 * 128
                            P.op("sync", lambda e, vs=vs, rr=rr, c0=c0: e.dma_start(out=C.V_d[rr:rr + 1024, c0:c0 + 512].rearrange("(a p) c -> p a c", p=128), in_=vst[vs]),
                                 reads=[t_vst[vs]], writes=[t_vd], dma=True, join=True)
    A.release(m)
    P.barrier()


def attn_phase(C, W, mode):
    P, A = C.P, C.A
    m = A.mark()
    H = 16
    diff = (mode == "diff")
    nsub = 2 if diff else 1
    lam_init = 0.8 - 0.6 * math.exp(-0.3 * C.layer)
    QT = [A.alloc([S], BF16) for _ in range(2)]
    KT = [A.alloc([S], BF16) for _ in range(2)]
    Vh = [A.alloc([NT, 128], BF16) for _ in range(2)]
    Gh = [A.alloc([S], F32) for _ in range(2)]
    yT = [A.alloc([S], BF16) for _ in range(2)]
    LA = 4 if diff else 3
    NPT = LA + 2
    PT = [A.alloc([512], BF16) for _ in range(NPT)]
    onesb = A.alloc([128], BF16)
    t_QT = P.toks(2, "QT"); t_KT = P.toks(2, "KT"); t_Vh = P.toks(2, "Vh"); t_Gh = P.toks(2, "Gh"); t_yT = P.toks(2, "yT"); t_PT = P.toks(NPT, "PT")
    t_yd = P.tok("YTd")
    Osb = [[A.alloc([512], F32) for _ in range(2)] for _ in range(2)]; Dsb = [[A.alloc([512], F32) for _ in range(2)] for _ in range(2)]
    sqb = [A.alloc([512], F32) for _ in range(2)]; rsb = [A.alloc([512], F32) for _ in range(2)]
    t_Osb = [P.toks(2, "Osb") for _ in range(2)]; t_Dsb = [P.toks(2, "Dsb") for _ in range(2)]; t_sqb = P.toks(2, "sqb"); t_rsb = P.toks(2, "rsb")
    tl = C.t_lconst
    P.op("pool", lambda e: e.memset(onesb, 1.0), writes=[tl])
    if diff:
        biasT = A.alloc([H, 2, 128], F32)
        b15 = A.alloc([H], F32)
        lam4 = A.alloc([4, 64], F32)
        lamc = A.alloc([4], F32)
        sgcol = A.alloc([1], F32)
        P.op("sync", lambda e: e.dma_start(out=biasT, in_=C.biasT_in), writes=[tl], dma=True, join=True)
        P.op("sync", lambda e: e.dma_start(out=b15, in_=W["rel_bias"][15, :].partition_broadcast(128)), writes=[tl], dma=True, join=True)
        P.op("sync", lambda e: e.dma_start(out=lam4, in_=W["a_lambda"][0].rearrange("a d -> (a d)").partition_broadcast(128).rearrange("p (a d) -> p a d", a=4)), writes=[tl], dma=True, join=True)
        load_col(C, sgcol, W["a_subln_g"][0, :], 128, 1.0 - lam_init)
        biasB = A.alloc([H, 2, 128], BF16)
        for hh in range(H):
            P.op("dve", lambda e, hh=hh: e.tensor_scalar(out=biasT[:, hh], in0=biasT[:, hh], scalar1=b15[:, hh:hh + 1], scalar2=None, op0=ALU.subtract), reads=[tl], writes=[tl])
        P.op("dve", lambda e: e.tensor_copy(out=biasB, in_=biasT), reads=[tl], writes=[tl])
        P.op("dve", lambda e: e.tensor_tensor(out=lam4[:, 0, :], in0=lam4[:, 0, :], in1=lam4[:, 1, :], op=ALU.mult), reads=[tl], writes=[tl])
        P.op("dve", lambda e: e.tensor_tensor(out=lam4[:, 2, :], in0=lam4[:, 2, :], in1=lam4[:, 3, :], op=ALU.mult), reads=[tl], writes=[tl])
        P.op("dve", lambda e: e.reduce_sum(out=lamc[:, 0:1], in_=lam4[:, 0, :], axis=AX.X), reads=[tl], writes=[tl])
        P.op("dve", lambda e: e.reduce_sum(out=lamc[:, 1:2], in_=lam4[:, 2, :], axis=AX.X), reads=[tl], writes=[tl])
        P.op("act", lambda e: e.activation(out=lamc[:, 0:2], in_=lamc[:, 0:2], func=AF.Exp), reads=[tl], writes=[tl])
        P.op("dve", lambda e: e.scalar_tensor_tensor(out=lamc[:, 0:1], in0=lamc[:, 1:2], scalar=-lam_init, in1=lamc[:, 0:1], op0=ALU.add, op1=ALU.subtract), reads=[tl], writes=[tl])
    else:
        maskT = A.alloc([128], F32)
        QP = [A.alloc([S], BF16) for _ in range(2)]
        KP = A.alloc([S], BF16)
        t_QP = P.toks(2, "QP"); t_KP = P.tok("KP")
        P.op("sync", lambda e: e.dma_start(out=maskT, in_=C.maskT_d), writes=[tl], dma=True)
        maskB = A.alloc([128], BF16)
        P.op("dve", lambda e: e.tensor_copy(out=maskB, in_=maskT), reads=[tl], writes=[tl])
        P.op("sync", lambda e: e.dma_start(out=KP[0:64, :], in_=C.KPE_d), writes=[t_KP], dma=True)
    SB = [0, 1, 2, 3, 7] if diff else [0, 1, 6, 7]
    NSB = len(SB)

    def banks(qg, t):
        b0 = 4 if diff else 2 + 2 * (qg % 2)
        return b0, b0 + 1

    def emit_loads(h):
        hs = h % 2
        P.op("sync", lambda e: e.dma_start(out=QT[hs], in_=C.QT_d[h]), writes=[t_QT[hs]], dma=True)
        P.op("sync", lambda e: e.dma_start(out=KT[hs], in_=C.KT_d[h]), writes=[t_KT[hs]], dma=True)
        if not diff:
            P.op("sync", lambda e: e.dma_start(out=QP[hs][0:64, :], in_=C.QPE_d[h]), writes=[t_QP[hs]], dma=True)
        P.op("sync", lambda e: e.dma_start(out=Vh[hs], in_=C.V_d[:, h * 128:(h + 1) * 128].rearrange("(a p) c -> p a c", p=128)), writes=[t_Vh[hs]], dma=True)
        P.op("sync", lambda e: e.dma_start(out=Gh[hs], in_=C.GT_d[h]), writes=[t_Gh[hs]], dma=True)

    def emit_S(n, h, qg, t, i):
        hs = h % 2
        jmin = max(0, i - 4 * qg)
        Sp, tSp = C.psf[SB[n % NSB]], C.t_psf[SB[n % NSB]]
        q0 = (4 * qg + jmin) * 128
        ncol = (4 - jmin) * 128
        c0 = jmin * 128
        near = []
        for rel in ((0, 1) if diff else (0,)):
            jj = i - 4 * qg + rel
            if 0 <= jj <= 3 and jj >= jmin:
                near.append((jj, biasB[:, h, rel, :] if diff else maskB))
        nn = len(near)
        if diff:
            P.op("pe", lambda e: e.matmul(Sp[:, c0:c0 + ncol], lhsT=KT[hs][t * 64:(t + 1) * 64, i * 128:(i + 1) * 128], rhs=QT[hs][t * 64:(t + 1) * 64, q0:q0 + ncol], start=True, stop=(nn == 0), skip_group_check=True),
                 reads=[t_KT[hs], t_QT[hs]], writes=[tSp])
        else:
            P.op("pe", lambda e: e.matmul(Sp[:, c0:c0 + ncol], lhsT=KT[hs][:, i * 128:(i + 1) * 128], rhs=QT[hs][:, q0:q0 + ncol], start=True, stop=False, skip_group_check=True),
                 reads=[t_KT[hs], t_QT[hs]], writes=[tSp])
            P.op("pe", lambda e: e.matmul(Sp[:, c0:c0 + ncol], lhsT=KP[0:64, i * 128:(i + 1) * 128], rhs=QP[hs][0:64, q0:q0 + ncol], start=False, stop=(nn == 0), skip_group_check=True),
                 reads=[t_KP, t_QP[hs]], writes=[tSp], join=True)
        for bi, (jj, btile) in enumerate(near):
            P.op("pe", lambda e, jj=jj, btile=btile, bi=bi: e.matmul(Sp[:, jj * 128:(jj + 1) * 128], lhsT=C.ident, rhs=btile, start=False, stop=(bi == nn - 1), skip_group_check=True),
                 reads=[tl, C.t_const], writes=[tSp], join=True)
        pp = n % NPT
        ebias = 0.0
        P.op("act", lambda e: e.activation(out=PT[pp][:, c0:c0 + ncol], in_=Sp[:, c0:c0 + ncol], func=AF.Exp, bias=ebias, scale=1.0),
             reads=[tSp, tl], writes=[t_PT[pp]])

    def emit_PV(n, h, qg, t, i):
        hs = h % 2
        jmin = max(0, i - 4 * qg)
        c0 = jmin * 128
        pp = n % NPT
        bo, bd = banks(qg, t)
        last = (i == 4 * qg + 3)
        P.op("pe", lambda e: e.matmul(C.psf[bo][:, c0:512], lhsT=Vh[hs][:, i, :], rhs=PT[pp][:, c0:512], start=(i == 0), stop=last),
             reads=[t_PT[pp], t_Vh[hs]], writes=[C.t_psf[bo]], join=(i > 0))
        P.op("pe", lambda e: e.matmul(C.psf[bd][:, c0:512], lhsT=onesb, rhs=PT[pp][:, c0:512], start=(i == 0), stop=last),
             reads=[t_PT[pp], tl], writes=[C.t_psf[bd]], join=(i > 0))

    pending = []
    deferred = []
    cur_n = [0]
    gser = [0]

    def emit_evac(qg, t):
        par = qg % 2
        bo, bd = banks(qg, t)
        P.op("dve", lambda e: e.tensor_copy(out=Osb[par][t], in_=C.psf[bo]), reads=[C.t_psf[bo]], writes=[t_Osb[par][t]])
        P.op("dve", lambda e: e.tensor_copy(out=Dsb[par][t], in_=C.psf[bd]), reads=[C.t_psf[bd]], writes=[t_Dsb[par][t]])

    def emit_epilogue(n, h, qg):
        hs = h % 2
        par = qg % 2
        ysl = yT[hs][:, qg * 512:(qg + 1) * 512]
        gsl = Gh[hs][:, qg * 512:(qg + 1) * 512]
        O_, D_, tO, tD = Osb[par], Dsb[par], t_Osb[par], t_Dsb[par]
        ts = (0, 1) if diff else (0,)
        last = (qg == NT // 4 - 1)
        deferred.append(lambda: emit_epilogue2(n, h, qg))

    def emit_epilogue2(n, h, qg):
        hs = h % 2
        par = qg % 2
        ysl = yT[hs][:, qg * 512:(qg + 1) * 512]
        gsl = Gh[hs][:, qg * 512:(qg + 1) * 512]
        O_, D_, tO, tD = Osb[par], Dsb[par], t_Osb[par], t_Dsb[par]
        ts = (0, 1) if diff else (0,)
        last = (qg == NT // 4 - 1)
        for t in ts:
            P.op("dve", lambda e, t=t: e.reciprocal(out=D_[t], in_=D_[t]), reads=[tD[t]], writes=[tD[t]])
        if diff:
            for t in (1, 0):
                P.op("dve", lambda e, t=t: e.tensor_tensor(out=O_[t], in0=O_[t], in1=D_[t], op=ALU.mult), reads=[tO[t], tD[t]], writes=[tO[t]])
            P.op("dve", lambda e: e.scalar_tensor_tensor(out=O_[0], in0=O_[1], scalar=lamc[:, 0:1], in1=O_[0], op0=ALU.mult, op1=ALU.add), reads=[tO[0], tO[1], tl], writes=[tO[0]])
            P.op("pool", lambda e: e.tensor_tensor(out=sqb[par], in0=O_[0], in1=O_[0], op=ALU.mult), reads=[tO[0]], writes=[t_sqb[par]])

            def stage2():
                P.op("pe", lambda e: e.matmul(C.psf[6], lhsT=C.ones128, rhs=sqb[par], start=True, stop=True), reads=[t_sqb[par], C.t_const], writes=[C.t_psf[6]])
                P.op("act", lambda e: e.activation(out=rsb[par], in_=C.psf[6], func=AF.Ln, bias=C.epscol, scale=1.0 / 128), reads=[C.t_psf[6], C.t_const], writes=[t_rsb[par]])
                P.op("act", lambda e: e.activation(out=rsb[par], in_=rsb[par], func=AF.Exp, scale=-0.5), reads=[t_rsb[par]], writes=[t_rsb[par]])
                P.op("dve", lambda e: e.scalar_tensor_tensor(out=O_[0], in0=O_[0], scalar=sgcol, in1=rsb[par], op0=ALU.mult, op1=ALU.mult), reads=[tO[0], t_rsb[par], tl], writes=[tO[0]])
                P.op("pool", lambda e: e.tensor_tensor(out=ysl, in0=O_[0], in1=gsl, op=ALU.mult), reads=[tO[0], t_Gh[hs]], writes=[t_yT[hs]], join=True)
                if last:
                    P.op("sync", lambda e: e.dma_start(out=C.YT_d[h], in_=yT[hs]), reads=[t_yT[hs]], writes=[t_yd], dma=True, join=True)
            pending.append((cur_n[0] + 16, stage2, gser[0] - 1))
        else:
            P.op("pool", lambda e: e.tensor_tensor(out=O_[0], in0=O_[0], in1=D_[0], op=ALU.mult), reads=[tO[0], tD[0]], writes=[tO[0]])
            P.op("pool", lambda e: e.tensor_tensor(out=ysl, in0=O_[0], in1=gsl, op=ALU.mult), reads=[tO[0], t_Gh[hs]], writes=[t_yT[hs]], join=True)
            if last:
                P.op("sync", lambda e: e.dma_start(out=C.YT_d[h], in_=yT[hs]), reads=[t_yT[hs]], writes=[t_yd], dma=True, join=True)

    tiles = [(h, qg, t, i) for h in range(H) for qg in range(NT // 4) for t in range(nsub) for i in range(4 * qg + 4)]

    def emit_S_at(n):
        h_, qg_, t_, i_ = tiles[n]
        if qg_ == 0 and t_ == 0 and i_ == 0:
            emit_loads(h_)
        emit_S(n, h_, qg_, t_, i_)

    for n in range(min(LA, len(tiles))):
        emit_S_at(n)
    for n, (h, qg, t, i) in enumerate(tiles):
        if n + LA < len(tiles):
            emit_S_at(n + LA)
        emit_PV(n, h, qg, t, i)
        while pending and pending[0][0] <= n:
            pending.pop(0)[1]()
        cur_n[0] = n
        if i == 4 * qg + 3:
            if t == 0:
                while pending and pending[0][2] <= gser[0] - 2:
                    pending.pop(0)[1]()
            emit_evac(qg, t)
            while deferred:
                deferred.pop(0)()
            if t == nsub - 1:
                emit_epilogue(n, h, qg)
                gser[0] += 1
    while deferred:
        deferred.pop(0)()
    while pending:
        pending.pop(0)[1]()
    A.release(m)
    P.barrier()


def load_col(C, dst, src_vec, n, scale=None):
    P = C.P
    P.op("sync", lambda e: e.dma_start(out=dst[0:n, :], in_=src_vec.rearrange("(d o) -> d o", o=1)), writes=[C.t_lconst], dma=True, join=True)
    if scale is not None:
        P.op("dve", lambda e: e.tensor_scalar(out=dst[0:n, :], in0=dst[0:n, :], scalar1=float(scale), scalar2=None, op0=ALU.mult), reads=[C.t_lconst], writes=[C.t_lconst])


def rope_epilogue(C, ps, tps, gcol, cs, sn, t_cs, dst, t_dst, tmp, t_tmp):
    P = C.P
    xf, sq, rs = tmp
    P.op("act", lambda e: e.activation(out=xf[0:64, :], in_=ps[0:64, :], func=AF.Copy), reads=[tps], writes=[t_tmp[0]])
    P.op("act", lambda e: e.activation(out=sq[0:64, :], in_=ps[0:64, :], func=AF.Square), reads=[tps], writes=[t_tmp[1]])
    pss, tpss = C.psf[4 + C.auxcnt % 2], C.t_psf[4 + C.auxcnt % 2]
    C.auxcnt += 1
    P.op("pe", lambda e: e.matmul(pss[0:64, :], lhsT=C.ones64[0:64, 0:64], rhs=sq[0:64, :], start=True, stop=True), reads=[t_tmp[1], C.t_const], writes=[tpss])
    P.op("act", lambda e: e.activation(out=rs[0:64, :], in_=pss[0:64, :], func=AF.Ln, bias=C.epscol[0:64, :], scale=1.0 / 64), reads=[tpss, C.t_const], writes=[t_tmp[2]])
    P.op("act", lambda e: e.activation(out=rs[0:64, :], in_=rs[0:64, :], func=AF.Exp, scale=-0.5), reads=[t_tmp[2]], writes=[t_tmp[2]])
    P.op("dve", lambda e: e.scalar_tensor_tensor(out=xf[0:64, :], in0=xf[0:64, :], scalar=gcol[0:64, :], in1=rs[0:64, :], op0=ALU.mult, op1=ALU.mult),
         reads=[t_tmp[0], t_tmp[2], C.t_lconst], writes=[t_tmp[0]])
    pr, tpr = C.psf[4 + C.auxcnt % 2], C.t_psf[4 + C.auxcnt % 2]
    C.auxcnt += 1
    P.op("pe", lambda e: e.matmul(pr[0:64, :], lhsT=C.rotm[0:64, 0:64], rhs=xf[0:64, :], start=True, stop=True), reads=[t_tmp[0], C.t_const], writes=[tpr])
    P.op("dve", lambda e: e.tensor_tensor(out=sq[0:64, :], in0=pr[0:64, :], in1=sn, op=ALU.mult), reads=[tpr, t_cs], writes=[t_tmp[1]])
    P.op("pool", lambda e: e.tensor_tensor(out=xf[0:64, :], in0=xf[0:64, :], in1=cs, op=ALU.mult), reads=[t_tmp[0], t_cs], writes=[t_tmp[0]])
    P.op("pool", lambda e: e.tensor_tensor(out=dst, in0=xf[0:64, :], in1=sq[0:64, :], op=ALU.add), reads=[t_tmp[0], t_tmp[1]], writes=[t_dst], join=True)


def g_block(C, hT, hT_toks, wt_s, t_wt_s, tok0, c0, pcnt, gst, t_gst, gcnt, tpb=TPB):
    P = C.P
    t_gd = C.t_gd
    for i in range(tpb):
        pb = pcnt % 4; pcnt += 1
        ps, tps = C.psf[pb], C.t_psf[pb]
        for k in range(16):
            P.op("pe", lambda e, ps=ps, k=k, i=i: e.matmul(ps, lhsT=hT[:, k, i * 128:(i + 1) * 128], rhs=wt_s[:, k, :], start=(k == 0), stop=(k == 15)),
                 reads=[t_wt_s, hT_toks[i]], writes=[tps], join=(k > 0))
        r0 = tok0 + i * 128
        g4 = i % 4
        if g4 == 0:
            gs = gcnt % 2; gcnt += 1
        P.op("act", lambda e, ps=ps, gs=gs, g4=g4: e.activation(out=gst[gs][:, g4, :], in_=ps, func=AF.Silu), reads=[tps], writes=[t_gst[gs]], join=True)
        if g4 == 3:
            rr = r0 - 3 * 128
            P.op("sync", lambda e, gs=gs, rr=rr, c0=c0: e.dma_start(out=C.G_d[rr:rr + 512, c0:c0 + 512].rearrange("(a p) c -> p a c", p=128), in_=gst[gs]),
                 reads=[t_gst[gs]], writes=[t_gd], dma=True, join=True)
    return pcnt, gcnt


def layer3_proj1(C, x_src, W):
    P, A = C.P, C.A
    m = A.mark()
    TB = 1024; TPB = TB // 128; NTB = S // TB
    hT = A.alloc([16, TB], BF16); hT_toks = P.toks(TPB, "hT")
    wt = [A.alloc([16, 512], BF16) for _ in range(2)]; t_wt = P.toks(2, "wt")
    wkp = A.alloc([16, 64], BF16); t_wkp = P.tok("wkp")
    cst = [A.alloc([4, TB], BF16) for _ in range(2)]; t_cst = P.toks(2, "cst")
    cf = [A.alloc([512], F32) for _ in range(4)]; t_cf = P.toks(4, "cf")
    sq = [A.alloc([512], F32) for _ in range(2)]; t_sq = P.toks(2, "sq")
    rs = A.alloc([512], F32); t_rs = P.tok("rs")
    tmp = [A.alloc([512], F32) for _ in range(3)]; t_tmp = P.toks(3, "tmp")
    kpst = A.alloc([TB], BF16); t_kpst = P.tok("kpst")
    cs = A.alloc([TB], F32); sn = A.alloc([TB], F32); t_cs = P.tok("cs")
    gst = [A.alloc([TB], F32) for _ in range(2)]; t_gst = P.toks(2, "gst")
    glat = A.alloc([2, 4], F32); gkp = A.alloc([1], F32)
    C.t_gd = P.tok("Gd"); t_cd = P.tok("CQd"); t_kd = P.tok("KPEd")
    tl = C.t_lconst
    for mm_ in range(4):
        load_col(C, glat[:, 0, mm_:mm_ + 1], W["d_q_lat_g"][0, mm_ * 128:(mm_ + 1) * 128], 128)
        load_col(C, glat[:, 1, mm_:mm_ + 1], W["d_kv_lat_g"][0, mm_ * 128:(mm_ + 1) * 128], 128)
    load_col(C, gkp, W["d_qk_g"][0, 1, 128:192], 64)
    wcnt = 0; pcnt = 0; gcnt = 0; scnt = 0
    for tb in range(NTB):
        tok0 = tb * TB
        phase_norm_T(C, x_src, W["norm_g"][C.layer, :], tok0, hT, hT_toks, TB)
        P.op("sync", lambda e, tok0=tok0: e.dma_start(out=cs[0:64, :], in_=C.rope_in[0, :, tok0:tok0 + TB]), writes=[t_cs], dma=True)
        P.op("sync", lambda e, tok0=tok0: e.dma_start(out=sn[0:64, :], in_=C.rope_in[1, :, tok0:tok0 + TB]), writes=[t_cs], dma=True, join=True)
        for kind in range(2):
            ws = wcnt % 2; wcnt += 1
            load_w_block(C, wt[ws], t_wt[ws], W["d_w_in"][0], kind * 512, 512)
            for tq in range(TB // 512):
                pss, tpss = C.psf[4 + C.auxcnt % 2], C.t_psf[4 + C.auxcnt % 2]
                C.auxcnt += 1
                for mm_ in range(4):
                    pb = pcnt % 4; pcnt += 1
                    ps, tps = C.psf[pb], C.t_psf[pb]
                    for k in range(16):
                        P.op("pe", lambda e, ps=ps, k=k, ws=ws, mm_=mm_, tq=tq: e.matmul(ps, lhsT=wt[ws][:, k, mm_ * 128:(mm_ + 1) * 128], rhs=hT[:, k, tq * 512:(tq + 1) * 512], start=(k == 0), stop=(k == 15)),
                             reads=[t_wt[ws]] + hT_toks[tq * 4:(tq + 1) * 4], writes=[tps], join=(k > 0))
                    P.op("act", lambda e, ps=ps, mm_=mm_: e.activation(out=cf[mm_], in_=ps, func=AF.Copy), reads=[tps], writes=[t_cf[mm_]])
                    ss = scnt % 2; scnt += 1
                    P.op("act", lambda e, ps=ps, ss=ss: e.activation(out=sq[ss], in_=ps, func=AF.Square), reads=[tps], writes=[t_sq[ss]])
                    P.op("pe", lambda e, pss=pss, ss=ss, mm_=mm_: e.matmul(pss, lhsT=C.ones128, rhs=sq[ss], start=(mm_ == 0), stop=(mm_ == 3)), reads=[t_sq[ss], C.t_const], writes=[tpss], join=(mm_ > 0))
                P.op("act", lambda e, pss=pss: e.activation(out=rs, in_=pss, func=AF.Ln, bias=C.epscol, scale=1.0 / 512), reads=[tpss, C.t_const], writes=[t_rs])
                P.op("act", lambda e: e.activation(out=rs, in_=rs, func=AF.Exp, scale=-0.5), reads=[t_rs], writes=[t_rs])
                for mm_ in range(4):
                    P.op("dve", lambda e, mm_=mm_, kind=kind, tq=tq: e.scalar_tensor_tensor(out=cst[kind][:, mm_, tq * 512:(tq + 1) * 512], in0=cf[mm_], scalar=glat[:, kind, mm_:mm_ + 1], in1=rs, op0=ALU.mult, op1=ALU.mult),
                         reads=[t_cf[mm_], t_rs, tl], writes=[t_cst[kind]], join=True)
            dd = C.CQ_d if kind == 0 else C.CKV_d
            for mm_ in range(4):
                P.op("sync", lambda e, dd=dd, mm_=mm_, kind=kind, tok0=tok0: e.dma_start(out=dd[mm_, :, tok0:tok0 + TB], in_=cst[kind][:, mm_, :]), reads=[t_cst[kind]], writes=[t_cd], dma=True, join=True)
        load_w_block(C, wkp, t_wkp, W["d_w_in"][0], 1024, 64)
        for tq in range(TB // 512):
            pb = pcnt % 4; pcnt += 1
            ps, tps = C.psf[pb], C.t_psf[pb]
            for k in range(16):
                P.op("pe", lambda e, ps=ps, k=k, tq=tq: e.matmul(ps[0:64, :], lhsT=wkp[:, k, 0:64], rhs=hT[:, k, tq * 512:(tq + 1) * 512], start=(k == 0), stop=(k == 15)),
                     reads=[t_wkp] + hT_toks[tq * 4:(tq + 1) * 4], writes=[tps], join=(k > 0))
            rope_epilogue(C, ps, tps, gkp, cs[0:64, tq * 512:(tq + 1) * 512], sn[0:64, tq * 512:(tq + 1) * 512], t_cs, kpst[0:64, tq * 512:(tq + 1) * 512], t_kpst, tmp, t_tmp)
        P.op("sync", lambda e, tok0=tok0: e.dma_start(out=C.KPE_d[:, tok0:tok0 + TB], in_=kpst[0:64, :]), reads=[t_kpst], writes=[t_kd], dma=True, join=True)
        for cb in range(4):
            ws = wcnt % 2; wcnt += 1
            load_w_block(C, wt[ws], t_wt[ws], W["d_w_in"][0], 1088 + cb * 512, 512)
            pcnt, gcnt = gT_block(C, hT, hT_toks, wt[ws], t_wt[ws], tok0, cb * 4, TB, pcnt, gst, t_gst, gcnt)
    A.release(m)
    P.barrier()


def layer3_proj2(C, W):
    P, A = C.P, C.A
    m = A.mark()
    H = 16
    cq = A.alloc([4, TB], BF16); ckv = A.alloc([4, TB], BF16); t_cq = P.tok("cq"); t_ckv = P.tok("ckv")
    wuq = A.alloc([4, 3072], BF16); wkn = A.alloc([4, 2048], BF16); wv = A.alloc([4, 2048], BF16); t_w = P.tok("wup")
    qst = [A.alloc([TB], BF16) for _ in range(2)]; t_qst = P.toks(2, "qst")
    kst = [A.alloc([TB], BF16) for _ in range(2)]; t_kst = P.toks(2, "kst")
    qpst = [A.alloc([TB], BF16) for _ in range(2)]; t_qpst = P.toks(2, "qpst")
    tmp = [[A.alloc([512], F32) for _ in range(3)] for _ in range(2)]; t_tmp = [P.toks(3, "tmp") for _ in range(2)]
    cs = A.alloc([TB], F32); sn = A.alloc([TB], F32); t_cs = P.tok("cs")
    vst = [A.alloc([8, 512], BF16) for _ in range(2)]; t_vst = P.toks(2, "vst")
    gqn = A.alloc([1], F32); gkn = A.alloc([1], F32); gqp = A.alloc([1], F32)
    t_qd = P.tok("QTd"); t_kd = P.tok("KTd"); t_qpd = P.tok("QPEd"); t_vd = P.tok("Vd")
    sc = 192.0 ** -0.5
    load_col(C, gqn, W["d_qk_g"][0, 0, 0:128], 128, sc)
    load_col(C, gkn, W["d_qk_g"][0, 1, 0:128], 128)
    load_col(C, gqp, W["d_qk_g"][0, 0, 128:192], 64, sc)
    wq_v = W["d_w_uq"][0].rearrange("(k p) c -> p k c", p=128)
    wkv_v = W["d_w_ukv"][0].rearrange("(k p) (h c) -> p k h c", p=128, c=256)
    for k in range(4):
        P.op("pool", lambda e, k=k: e.dma_start(out=wuq[:, k, :], in_=wq_v[:, k, :]), writes=[t_w], dma=True, join=True)
        P.op("pool", lambda e, k=k: e.dma_start(out=wkn[:, k, :].rearrange("p (h c) -> p h c", c=128), in_=wkv_v[:, k, :, 0:128]), writes=[t_w], dma=True, join=True)
        P.op("pool", lambda e, k=k: e.dma_start(out=wv[:, k, :].rearrange("p (h c) -> p h c", c=128), in_=wkv_v[:, k, :, 128:256]), writes=[t_w], dma=True, join=True)
    pcnt = 0; tcnt = 0; vcnt = 0
    for tb in range(NTB):
        tok0 = tb * TB
        for k in range(4):
            P.op("sync", lambda e, k=k, tok0=tok0: e.dma_start(out=cq[:, k, :], in_=C.CQ_d[k, :, tok0:tok0 + TB]), writes=[t_cq], dma=True, join=(k > 0))
            P.op("sync", lambda e, k=k, tok0=tok0: e.dma_start(out=ckv[:, k, :], in_=C.CKV_d[k, :, tok0:tok0 + TB]), writes=[t_ckv], dma=True, join=(k > 0))
        P.op("sync", lambda e, tok0=tok0: e.dma_start(out=cs[0:64, :], in_=C.rope_in[0, :, tok0:tok0 + TB]), writes=[t_cs], dma=True)
        P.op("sync", lambda e, tok0=tok0: e.dma_start(out=sn[0:64, :], in_=C.rope_in[1, :, tok0:tok0 + TB]), writes=[t_cs], dma=True, join=True)
        for h in range(H):
            hs = h % 2
            for which in range(3):
                for tq in range(TB // 512):
                    pb = pcnt % 4; pcnt += 1
                    ps, tps = C.psf[pb], C.t_psf[pb]
                    for k in range(4):
                        if which == 0:
                            lhs, rhs_, rt, M = wuq[:, k, h * 192:h * 192 + 128], cq[:, k, tq * 512:(tq + 1) * 512], t_cq, 128
                        elif which == 1:
                            lhs, rhs_, rt, M = wkn[:, k, h * 128:(h + 1) * 128], ckv[:, k, tq * 512:(tq + 1) * 512], t_ckv, 128
                        else:
                            lhs, rhs_, rt, M = wuq[:, k, h * 192 + 128:h * 192 + 192], cq[:, k, tq * 512:(tq + 1) * 512], t_cq, 64
                        P.op("pe", lambda e, ps=ps, k=k, lhs=lhs, rhs_=rhs_, M=M: e.matmul(ps[0:M, :], lhsT=lhs, rhs=rhs_, start=(k == 0), stop=(k == 3)),
                             reads=[t_w, rt], writes=[tps], join=(k > 0))
                    ts_ = tcnt % 2; tcnt += 1
                    if which == 0:
                        qk_norm_epilogue(C, ps, tps, gqn, qst[hs][:, tq * 512:(tq + 1) * 512], t_qst[hs], tmp[ts_], t_tmp[ts_], 128)
                    elif which == 1:
                        qk_norm_epilogue(C, ps, tps, gkn, kst[hs][:, tq * 512:(tq + 1) * 512], t_kst[hs], tmp[ts_], t_tmp[ts_], 128)
                    else:
                        rope_epilogue(C, ps, tps, gqp, cs[0:64, tq * 512:(tq + 1) * 512], sn[0:64, tq * 512:(tq + 1) * 512], t_cs, qpst[hs][0:64, tq * 512:(tq + 1) * 512], t_qpst[hs], tmp[ts_], t_tmp[ts_])
            P.op("sync", lambda e, h=h, hs=hs, tok0=tok0: e.dma_start(out=C.QT_d[h, :, tok0:tok0 + TB], in_=qst[hs]), reads=[t_qst[hs]], writes=[t_qd], dma=True, join=True)
            P.op("sync", lambda e, h=h, hs=hs, tok0=tok0: e.dma_start(out=C.KT_d[h, :, tok0:tok0 + TB], in_=kst[hs]), reads=[t_kst[hs]], writes=[t_kd], dma=True, join=True)
            P.op("sync", lambda e, h=h, hs=hs, tok0=tok0: e.dma_start(out=C.QPE_d[h, :, tok0:tok0 + TB], in_=qpst[hs][0:64, :]), reads=[t_qpst[hs]], writes=[t_qpd], dma=True, join=True)
        for n in range(4):
            for i in range(TPB):
                pb = pcnt % 4; pcnt += 1
                ps, tps = C.psf[pb], C.t_psf[pb]
                for k in range(4):
                    P.op("pe", lambda e, ps=ps, k=k, i=i, n=n: e.matmul(ps, lhsT=ckv[:, k, i * 128:(i + 1) * 128], rhs=wv[:, k, n * 512:(n + 1) * 512], start=(k == 0), stop=(k == 3)),
                         reads=[t_w, t_ckv], writes=[tps], join=(k > 0))
                g8 = i % 8
                if g8 == 0:
                    vs = vcnt % 2; vcnt += 1
                P.op("dve", lambda e, ps=ps, vs=vs, g8=g8: e.tensor_copy(out=vst[vs][:, g8, :], in_=ps), reads=[tps], writes=[t_vst[vs]], join=True)
                if g8 == 7:
                    rr = tok0 + (i - 7) * 128
                    P.op("sync", lambda e, vs=vs, rr=rr, n=n: e.dma_start(out=C.V_d[rr:rr + 1024, n * 512:(n + 1) * 512].rearrange("(a p) c -> p a c", p=128), in_=vst[vs]),
                         reads=[t_vst[vs]], writes=[t_vd], dma=True, join=True)
    A.release(m)
    P.barrier()


def layer2_all(C, x_src, W):
    P, A = C.P, C.A
    m = A.mark()
    TB = 1024; TPB = TB // 128; NTB = S // TB
    hT = A.alloc([16, TB], BF16); hT_toks = P.toks(TPB, "hT")
    wt = [A.alloc([16, 512], BF16) for _ in range(2)]; t_wt = P.toks(2, "wt")
    wrg = A.alloc([8, 2, 256], BF16); wig = A.alloc([8, 2, 256], BF16); t_wg = P.tok("wg")
    stage = A.alloc([128], F32); cols = A.alloc([8, 16], F32)
    halo = A.alloc([16, 3], F32); hprev = A.alloc([16], F32); t_halo = P.tok("halo"); t_hprev = P.tok("hprev")
    ubuf = [A.alloc([TB + 8], F32) for _ in range(2)]; t_ubuf = P.toks(2, "ubuf")
    xc = [A.alloc([TB], F32) for _ in range(2)]; t_xc = P.toks(2, "xc")
    xcb = [A.alloc([TB], BF16) for _ in range(2)]; t_xcb = P.toks(2, "xcb")
    sg = [A.alloc([TB], F32) for _ in range(2)]; t_sg = P.toks(2, "sg")
    rg = [A.alloc([TB], F32) for _ in range(2)]; t_rg = P.toks(2, "rg")
    ig = [A.alloc([TB], F32) for _ in range(2)]; t_ig = P.toks(2, "ig")
    abuf = A.alloc([TB], F32); a2buf = A.alloc([TB], F32); xinb = A.alloc([TB], F32); hh = A.alloc([TB], F32)
    t_a = P.tok("a"); t_a2 = P.tok("a2"); t_xin = P.tok("xin"); t_hh = P.tok("hh")
    yst = [A.alloc([TB], BF16) for _ in range(2)]; t_yst = P.toks(2, "yst")
    t_yd = P.tok("YTd")
    tl = C.t_lconst
    vecs = [W["c_conv_w"][0, 0], W["c_conv_w"][0, 1], W["c_conv_w"][0, 2], W["c_conv_w"][0, 3], W["c_conv_b"][0], W["c_b_rgate"][0], W["c_b_igate"][0], W["c_lambda"][0]]
    for v, vec in enumerate(vecs):
        P.op("sync", lambda e, v=v, vec=vec: e.dma_start(out=stage[v * 16:(v + 1) * 16, :], in_=vec.rearrange("(t p) -> t p", p=128)), writes=[tl], dma=True, join=True)
    ps0, tps0 = C.psf[4], C.t_psf[4]
    P.op("pe", lambda e: e.matmul(ps0[:, 0:128], lhsT=stage, rhs=C.identf, start=True, stop=True), reads=[tl, C.t_const], writes=[tps0])
    P.op("dve", lambda e: e.tensor_copy(out=cols, in_=ps0[:, 0:128].rearrange("p (v t) -> p v t", v=8)), reads=[tps0], writes=[tl])
    P.op("act", lambda e: e.activation(out=cols[:, 7, :], in_=cols[:, 7, :], func=AF.Exp, scale=-1.0), reads=[tl], writes=[tl])
    P.op("act", lambda e: e.activation(out=cols[:, 7, :], in_=cols[:, 7, :], func=AF.Ln, bias=1.0, scale=1.0), reads=[tl], writes=[tl])
    P.op("dve", lambda e: e.tensor_scalar(out=cols[:, 7, :], in0=cols[:, 7, :], scalar1=-8.0, scalar2=None, op0=ALU.mult), reads=[tl], writes=[tl])
    for n in range(8):
        P.op("pool", lambda e, n=n: e.dma_start(out=wrg[:, n], in_=W["c_w_rgate"][0, n].rearrange("(c p) e -> p c e", p=128)), writes=[t_wg], dma=True, join=True)
        P.op("pool", lambda e, n=n: e.dma_start(out=wig[:, n], in_=W["c_w_igate"][0, n].rearrange("(c p) e -> p c e", p=128)), writes=[t_wg], dma=True, join=True)
    wcnt = 0; pcnt = 0
    for tb in range(NTB):
        tok0 = tb * TB
        phase_norm_T(C, x_src, W["norm_g"][C.layer, :], tok0, hT, hT_toks, TB)
        for n in range(8):
            ws = wcnt % 2; wcnt += 1
            load_w_block(C, wt[ws][:, :, 0:256], t_wt[ws], W["c_w_in"][0], n * 256, 256)
            load_w_block(C, wt[ws][:, :, 256:512], t_wt[ws], W["c_w_in"][0], 2048 + n * 256, 256)
            for c in range(2):
                tile = n * 2 + c
                if tb == 0:
                    P.op("pool", lambda e, c=c: e.memset(ubuf[c][:, 0:3], 0.0), writes=[t_ubuf[c]])
                else:
                    P.op("pool", lambda e, c=c, tile=tile: e.tensor_copy(out=ubuf[c][:, 0:3], in_=halo[:, tile, :]), reads=[t_halo], writes=[t_ubuf[c]])
                for tq in range(TB // 512):
                    pb = pcnt % 4; pcnt += 1
                    ps, tps = C.psf[pb], C.t_psf[pb]
                    for k in range(16):
                        P.op("pe", lambda e, ps=ps, k=k, ws=ws, c=c, tq=tq: e.matmul(ps, lhsT=wt[ws][:, k, c * 128:(c + 1) * 128], rhs=hT[:, k, tq * 512:(tq + 1) * 512], start=(k == 0), stop=(k == 15)),
                             reads=[t_wt[ws]] + hT_toks[tq * 4:(tq + 1) * 4], writes=[tps], join=(k > 0))
                    P.op("act", lambda e, ps=ps, c=c, tq=tq: e.activation(out=ubuf[c][:, 3 + tq * 512:3 + (tq + 1) * 512], in_=ps, func=AF.Copy), reads=[tps], writes=[t_ubuf[c]], join=True)
                P.op("pool", lambda e, c=c, tile=tile: e.tensor_copy(out=halo[:, tile, :], in_=ubuf[c][:, TB:TB + 3]), reads=[t_ubuf[c]], writes=[t_halo], join=True)
                P.op("dve", lambda e, c=c, tile=tile: e.tensor_scalar(out=xc[c], in0=ubuf[c][:, 3:3 + TB], scalar1=cols[:, 3, tile:tile + 1], scalar2=cols[:, 4, tile:tile + 1], op0=ALU.mult, op1=ALU.add),
                     reads=[t_ubuf[c], tl], writes=[t_xc[c]])
                for tau in (2, 1, 0):
                    P.op("dve", lambda e, c=c, tile=tile, tau=tau: e.scalar_tensor_tensor(out=xc[c], in0=ubuf[c][:, tau:tau + TB], scalar=cols[:, tau, tile:tile + 1], in1=xc[c], op0=ALU.mult, op1=ALU.add),
                         reads=[t_ubuf[c], tl, t_xc[c]], writes=[t_xc[c]])
                P.op("pool", lambda e, c=c: e.tensor_copy(out=xcb[c], in_=xc[c]), reads=[t_xc[c]], writes=[t_xcb[c]])
                for tq in range(TB // 512):
                    pb = pcnt % 4; pcnt += 1
                    ps, tps = C.psf[pb], C.t_psf[pb]
                    for k in range(16):
                        P.op("pe", lambda e, ps=ps, k=k, ws=ws, c=c, tq=tq: e.matmul(ps, lhsT=wt[ws][:, k, 256 + c * 128:256 + (c + 1) * 128], rhs=hT[:, k, tq * 512:(tq + 1) * 512], start=(k == 0), stop=(k == 15)),
                             reads=[t_wt[ws]] + hT_toks[tq * 4:(tq + 1) * 4], writes=[tps], join=(k > 0))
                    P.op("act", lambda e, ps=ps, c=c, tq=tq: e.activation(out=sg[c][:, tq * 512:(tq + 1) * 512], in_=ps, func=AF.Silu), reads=[tps], writes=[t_sg[c]], join=True)
            for ce in range(2):
                tile = n * 2 + ce
                for (wg, bidx, dstb, tdst) in ((wrg, 5, rg, t_rg), (wig, 6, ig, t_ig)):
                    for tq in range(TB // 512):
                        pb = pcnt % 4; pcnt += 1
                        ps, tps = C.psf[pb], C.t_psf[pb]
                        for cc in range(2):
                            P.op("pe", lambda e, ps=ps, wg=wg, cc=cc, ce=ce, tq=tq, n=n: e.matmul(ps, lhsT=wg[:, n, cc, ce * 128:(ce + 1) * 128], rhs=xcb[cc][:, tq * 512:(tq + 1) * 512], start=(cc == 0), stop=(cc == 1)),
                                 reads=[t_wg, t_xcb[cc]], writes=[tps], join=(cc > 0))
                        P.op("act", lambda e, ps=ps, dstb=dstb, ce=ce, tq=tq, bidx=bidx, tile=tile: e.activation(out=dstb[ce][:, tq * 512:(tq + 1) * 512], in_=ps, func=AF.Sigmoid, bias=cols[:, bidx, tile:tile + 1], scale=1.0),
                             reads=[tps, tl], writes=[tdst[ce]], join=True)
                P.op("act", lambda e, ce=ce, tile=tile: e.activation(out=abuf, in_=rg[ce], func=AF.Exp, scale=cols[:, 7, tile:tile + 1]), reads=[t_rg[ce], tl], writes=[t_a])
                P.op("pool", lambda e: e.tensor_tensor(out=a2buf, in0=abuf, in1=abuf, op=ALU.mult), reads=[t_a], writes=[t_a2])
                P.op("act", lambda e: e.activation(out=a2buf, in_=a2buf, func=AF.Sqrt, bias=1.0, scale=-1.0), reads=[t_a2], writes=[t_a2])
                P.op("pool", lambda e, ce=ce: e.tensor_tensor(out=xinb, in0=ig[ce], in1=xc[ce], op=ALU.mult), reads=[t_ig[ce], t_xc[ce]], writes=[t_xin])
                P.op("dve", lambda e: e.tensor_tensor(out=xinb, in0=xinb, in1=a2buf, op=ALU.mult), reads=[t_xin, t_a2], writes=[t_xin])
                init = 0.0 if tb == 0 else hprev[:, tile:tile + 1]
                P.op("dve", lambda e, init=init: e.tensor_tensor_scan(out=hh, data0=abuf, data1=xinb, initial=init, op0=ALU.mult, op1=ALU.add), reads=[t_a, t_xin, t_hprev], writes=[t_hh])
                P.op("pool", lambda e, tile=tile: e.tensor_copy(out=hprev[:, tile:tile + 1], in_=hh[:, TB - 1:TB]), reads=[t_hh], writes=[t_hprev])
                P.op("pool", lambda e, ce=ce: e.tensor_tensor(out=yst[ce], in0=hh, in1=sg[ce], op=ALU.mult), reads=[t_hh, t_sg[ce]], writes=[t_yst[ce]])
                P.op("sync", lambda e, ce=ce, tile=tile, tok0=tok0: e.dma_start(out=C.YT_d[tile, :, tok0:tok0 + TB], in_=yst[ce]), reads=[t_yst[ce]], writes=[t_yd], dma=True, join=True)
    A.release(m)
    P.barrier()


def layer1_all(C, x_src, W):
    P, A = C.P, C.A
    m0 = A.mark()
    dec = A.alloc([8, 64], F32); t_dec = P.tok("dec")
    m = A.mark()
    TB = 1024; TPB = TB // 128; NTB = S // TB
    hT = A.alloc([16, TB], BF16); hT_toks = P.toks(TPB, "hT")
    wt = [A.alloc([16, 512], BF16) for _ in range(2)]; t_wt = P.toks(2, "wt")
    wlr = A.alloc([16, 16], BF16); t_wlr = P.tok("wlr")
    lrT = A.alloc([TB], F32); t_lrT = P.tok("lrT")
    wga = A.alloc([1024], F32)
    TU = A.alloc([128], F32); CI = A.alloc([2], F32)
    qst = [A.alloc([TB], BF16) for _ in range(2)]; t_qst = P.toks(2, "qst")
    ebuf = [A.alloc([512], F32) for _ in range(2)]; t_eb = P.toks(2, "ebuf")
    wbuf = [A.alloc([512], F32) for _ in range(2)]; t_wb = P.toks(2, "wbuf")
    kst = [A.alloc([8, 512], BF16) for _ in range(2)]; t_kst = P.toks(2, "kst")
    vst = [A.alloc([8, 512], BF16) for _ in range(2)]; t_vst = P.toks(2, "vst")
    gst = [A.alloc([4, 512], F32) for _ in range(2)]; t_gst = P.toks(2, "gst")
    C.t_gd = P.tok("Gd"); t_qd = P.tok("QTd"); t_kd = P.tok("KPd"); t_vd = P.tok("Vd")
    tl = C.t_lconst
    Wi = W["b_w_in"][0]
    P.op("pool", lambda e: e.memset(wga[0:32, :], 0.0), writes=[tl])
    P.op("sync", lambda e: e.dma_start(out=wga[0:16, :], in_=W["b_w_gate"][0]), writes=[tl], dma=True)
    P.op("sync", lambda e: e.dma_start(out=wga[16:17, :], in_=W["b_gate_bias"][0:1, :]), writes=[tl], dma=True, join=True)
    P.op("sync", lambda e: e.dma_start(out=TU, in_=C.TU_d), writes=[tl], dma=True, join=True)
    P.op("sync", lambda e: e.dma_start(out=CI, in_=C.CI_d), writes=[tl], dma=True, join=True)
    P.op("pool", lambda e: e.memset(lrT[0:32, :], 1.0), writes=[t_lrT])
    wcnt = 0; pcnt = 0; gcnt = 0; vcnt = 0; qcnt = 0; ecnt = 0; kcnt = 0
    for tb in range(NTB):
        tok0 = tb * TB
        phase_norm_T(C, x_src, W["norm_g"][C.layer, :], tok0, hT, hT_toks, TB)
        load_w_block(C, wlr, t_wlr, Wi, 6144, 16)
        for tq in range(TB // 512):
            pb = pcnt % 4; pcnt += 1
            ps, tps = C.psf[pb], C.t_psf[pb]
            for k in range(16):
                P.op("pe", lambda e, ps=ps, k=k, tq=tq: e.matmul(ps[0:16, :], lhsT=wlr[:, k, 0:16], rhs=hT[:, k, tq * 512:(tq + 1) * 512], start=(k == 0), stop=(k == 15)),
                     reads=[t_wlr] + hT_toks[tq * 4:(tq + 1) * 4], writes=[tps], join=(k > 0))
            P.op("act", lambda e, ps=ps, tq=tq: e.activation(out=lrT[0:16, tq * 512:(tq + 1) * 512], in_=ps[0:16, :], func=AF.Copy), reads=[tps], writes=[t_lrT], join=True)
        for cb in range(12):
            ws = wcnt % 2; wcnt += 1
            load_w_block(C, wt[ws], t_wt[ws], Wi, cb * 512, 512)
            if cb < 2:
                for mm_ in range(4):
                    qt = cb * 4 + mm_
                    qs = qcnt % 2; qcnt += 1
                    for tq in range(TB // 512):
                        pb = pcnt % 4; pcnt += 1
                        ps, tps = C.psf[pb], C.t_psf[pb]
                        for k in range(16):
                            P.op("pe", lambda e, ps=ps, k=k, ws=ws, mm_=mm_, tq=tq: e.matmul(ps, lhsT=wt[ws][:, k, mm_ * 128:(mm_ + 1) * 128], rhs=hT[:, k, tq * 512:(tq + 1) * 512], start=(k == 0), stop=(k == 15)),
                                 reads=[t_wt[ws]] + hT_toks[tq * 4:(tq + 1) * 4], writes=[tps], join=(k > 0))
                        P.op("act", lambda e, ps=ps, qs=qs, tq=tq: e.activation(out=qst[qs][:, tq * 512:(tq + 1) * 512], in_=ps, func=AF.Copy, scale=1.0 / 16.0), reads=[tps], writes=[t_qst[qs]], join=True)
                    P.op("sync", lambda e, qt=qt, qs=qs, tok0=tok0: e.dma_start(out=C.QT_d[qt, :, tok0:tok0 + TB], in_=qst[qs]), reads=[t_qst[qs]], writes=[t_qd], dma=True, join=True)
            elif cb < 4:
                kb = cb - 2
                ks = kcnt % 2; kcnt += 1
                for i in range(TPB):
                    pb = pcnt % 4; pcnt += 1
                    ps, tps = C.psf[pb], C.t_psf[pb]
                    for k in range(16):
                        P.op("pe", lambda e, ps=ps, k=k, ws=ws, i=i: e.matmul(ps, lhsT=hT[:, k, i * 128:(i + 1) * 128], rhs=wt[ws][:, k, :], start=(k == 0), stop=(k == 15)),
                             reads=[t_wt[ws], hT_toks[i]], writes=[tps], join=(k > 0))
                    es = ecnt % 2; ecnt += 1
                    pz, tpz = C.psf[4], C.t_psf[4]
                    P.op("pe", lambda e, pz=pz, i=i, kb=kb: e.matmul(pz, lhsT=lrT[0:32, i * 128:(i + 1) * 128], rhs=wga[0:32, kb * 512:(kb + 1) * 512], start=True, stop=True),
                         reads=[t_lrT, tl], writes=[tpz])
                    P.op("act", lambda e, pz=pz, es=es: e.activation(out=ebuf[es], in_=pz, func=AF.Exp, scale=-1.0), reads=[tpz], writes=[t_eb[es]])
                    P.op("act", lambda e, es=es: e.activation(out=ebuf[es], in_=ebuf[es], func=AF.Ln, bias=1.0, scale=1.0), reads=[t_eb[es]], writes=[t_eb[es]])
                    pr_, tpr = C.psf[5], C.t_psf[5]
                    P.op("pe", lambda e, pr_=pr_, es=es: e.matmul(pr_, lhsT=TU, rhs=ebuf[es], start=True, stop=True), reads=[t_eb[es], tl], writes=[tpr])
                    P.op("act", lambda e, pr_=pr_, es=es: e.activation(out=wbuf[es], in_=pr_, func=AF.Exp), reads=[tpr], writes=[t_wb[es]])
                    P.op("dve", lambda e, ps=ps, es=es, ks=ks, i=i: e.tensor_tensor(out=kst[ks][:, i, :], in0=ps, in1=wbuf[es], op=ALU.mult), reads=[tps, t_wb[es]], writes=[t_kst[ks]], join=True)
                    for dt in range(4):
                        P.op("pe", lambda e, pz=pz, es=es, dt=dt: e.matmul(pz[:, dt * 2:dt * 2 + 2], lhsT=ebuf[es][:, dt * 128:(dt + 1) * 128], rhs=CI, start=(dt == 0), stop=(dt == 3), skip_group_check=True),
                             reads=[t_eb[es], tl], writes=[tpz], join=(dt > 0))
                    ch0 = (tok0 + i * 128) // 64
                    P.op("act", lambda e, pz=pz, kb=kb, ch0=ch0: e.activation(out=dec[:, kb * 4:(kb + 1) * 4, ch0:ch0 + 2], in_=pz[:, 0:8].rearrange("p (a b) -> p a b", a=4), func=AF.Exp), reads=[tpz], writes=[t_dec], join=True)
                P.op("sync", lambda e, ks=ks, tok0=tok0, kb=kb: e.dma_start(out=C.KP_d[tok0:tok0 + TB, kb * 512:(kb + 1) * 512].rearrange("(a p) c -> p a c", p=128), in_=kst[ks]),
                     reads=[t_kst[ks]], writes=[t_kd], dma=True, join=True)
            elif cb < 8:
                c0 = (cb - 4) * 512
                vs = vcnt % 2; vcnt += 1
                for i in range(TPB):
                    pb = pcnt % 4; pcnt += 1
                    ps, tps = C.psf[pb], C.t_psf[pb]
                    for k in range(16):
                        P.op("pe", lambda e, ps=ps, k=k, ws=ws, i=i: e.matmul(ps, lhsT=hT[:, k, i * 128:(i + 1) * 128], rhs=wt[ws][:, k, :], start=(k == 0), stop=(k == 15)),
                             reads=[t_wt[ws], hT_toks[i]], writes=[tps], join=(k > 0))
                    P.op("dve", lambda e, ps=ps, vs=vs, i=i: e.tensor_copy(out=vst[vs][:, i, :], in_=ps), reads=[tps], writes=[t_vst[vs]], join=True)
                P.op("sync", lambda e, vs=vs, tok0=tok0, c0=c0: e.dma_start(out=C.V_d[tok0:tok0 + TB, c0:c0 + 512].rearrange("(a p) c -> p a c", p=128), in_=vst[vs]),
                     reads=[t_vst[vs]], writes=[t_vd], dma=True, join=True)
            else:
                pcnt, gcnt = g_block(C, hT, hT_toks, wt[ws], t_wt[ws], tok0, (cb - 8) * 512, pcnt, gst, t_gst, gcnt, TPB)
    A.release(m)
    P.barrier()
    m = A.mark()
    Kp = A.alloc([NT, 256], BF16); t_Kp = P.tok("Kp")
    Vh = A.alloc([NT, 512], BF16); t_Vh = P.tok("Vh")
    QTt = [A.alloc([S], BF16) for _ in range(2)]; t_QTt = P.tok("QTt")
    Sf = A.alloc([2, 512], F32); t_Sf = P.tok("Sf")
    Sb = [A.alloc([2, 512], BF16) for _ in range(2)]; t_Sb = P.toks(2, "Sb")
    Gt = [A.alloc([2, 512], F32) for _ in range(2)]; t_Gt = P.toks(2, "Gt")
    yT = A.alloc([4, S], BF16); t_yT = P.tok("yT")
    ogb = A.alloc([512], F32)
    of = A.alloc([512], F32); yb = A.alloc([512], BF16); ssq = A.alloc([1], F32); junk = A.alloc([512], BF16)
    t_ep = P.tok("ep"); t_yd = P.tok("YTd")
    P.op("sync", lambda e: e.dma_start(out=ogb, in_=W["b_out_g"][0, :].partition_broadcast(128)), writes=[tl], dma=True)

    def emit_kv(hh, c):
        i, par = c // 2, c % 2
        pr0 = par * 64
        for dh in range(2):
            kv, tkv = C.psf[2 * (c % 2) + dh], C.t_psf[2 * (c % 2) + dh]
            P.op("pe", lambda e, kv=kv, i=i, pr0=pr0, dh=dh: e.matmul(kv, lhsT=Kp[pr0:pr0 + 64, i, dh * 128:(dh + 1) * 128], rhs=Vh[pr0:pr0 + 64, i, :], start=True, stop=True),
                 reads=[t_Kp, t_Vh], writes=[tkv])

    for hh in range(4):
        P.op("sync", lambda e, hh=hh: e.dma_start(out=Kp, in_=C.KP_d[:, hh * 256:(hh + 1) * 256].rearrange("(a p) c -> p a c", p=128)), writes=[t_Kp], dma=True)
        P.op("sync", lambda e, hh=hh: e.dma_start(out=Vh, in_=C.V_d[:, hh * 512:(hh + 1) * 512].rearrange("(a p) c -> p a c", p=128)), writes=[t_Vh], dma=True)
        for dh in range(2):
            P.op("sync", lambda e, hh=hh, dh=dh: e.dma_start(out=QTt[dh], in_=C.QT_d[2 * hh + dh]), writes=[t_QTt], dma=True, join=(dh > 0))
        P.op("pool", lambda e: e.memset(Sf, 0.0), writes=[t_Sf])
        emit_kv(hh, 0)
        for c in range(S // 64):
            i, par = c // 2, c % 2
            if par == 0:
                gs = i % 2
                P.op("sync", lambda e, gs=gs, i=i, hh=hh: e.dma_start(out=Gt[gs][0:64, :, :], in_=C.G_d[i * 128:(i + 1) * 128, hh * 512:(hh + 1) * 512].rearrange("(par p) c -> p par c", p=64)), writes=[t_Gt[gs]], dma=True)
            sbs = c % 2
            for dh in range(2):
                kv, tkv = C.psf[2 * (c % 2) + dh], C.t_psf[2 * (c % 2) + dh]
                P.op("dve", lambda e, kv=kv, dh=dh, hh=hh, c=c: e.scalar_tensor_tensor(out=Sf[:, dh, :], in0=Sf[:, dh, :], scalar=dec[:, 2 * hh + dh, c:c + 1], in1=kv, op0=ALU.mult, op1=ALU.add),
                     reads=[t_Sf, t_dec, tkv], writes=[t_Sf])
                P.op("act", lambda e, dh=dh, sbs=sbs: e.activation(out=Sb[sbs][:, dh, :], in_=Sf[:, dh, :], func=AF.Copy), reads=[t_Sf], writes=[t_Sb[sbs]], join=(dh > 0))
            if c + 1 < S // 64:
                emit_kv(hh, c + 1)
            po, tpo = C.psf[4 + c % 2], C.t_psf[4 + c % 2]
            for dh in range(2):
                P.op("pe", lambda e, po=po, dh=dh, c=c, sbs=sbs: e.matmul(po[0:64, :], lhsT=QTt[dh][:, c * 64:(c + 1) * 64], rhs=Sb[sbs][:, dh, :], start=(dh == 0), stop=(dh == 1)),
                     reads=[t_QTt, t_Sb[sbs]], writes=[tpo], join=(dh > 0))
            P.op("act", lambda e, po=po: e.activation(out=junk[0:64, :], in_=po[0:64, :], func=AF.Square, accum_out=ssq[0:64, :]), reads=[tpo], writes=[t_ep])
            P.op("act", lambda e: e.activation(out=ssq[0:64, :], in_=ssq[0:64, :], func=AF.Sqrt, bias=C.epscol[0:64, :], scale=1.0 / 512), reads=[t_ep, C.t_const], writes=[t_ep])
            P.op("dve", lambda e: e.reciprocal(out=ssq[0:64, :], in_=ssq[0:64, :]), reads=[t_ep], writes=[t_ep])
            P.op("dve", lambda e, po=po: e.scalar_tensor_tensor(out=of[0:64, :], in0=po[0:64, :], scalar=ssq[0:64, :], in1=ogb[0:64, :], op0=ALU.mult, op1=ALU.mult), reads=[tpo, t_ep, tl], writes=[t_ep])
            P.op("pool", lambda e, gs=gs, par=par: e.tensor_tensor(out=yb[0:64, :], in0=of[0:64, :], in1=Gt[gs][0:64, par, :], op=ALU.mult), reads=[t_ep, t_Gt[gs]], writes=[t_ep])
            tp, ttp = C.tpb[c % 2], C.t_tpb[c % 2]
            for j in range(4):
                P.op("pe", lambda e, tp=tp, j=j: e.transpose(out=tp[:, j * 64:(j + 1) * 64], in_=yb[0:64, j * 128:(j + 1) * 128], identity=C.ident[0:64, 0:64]), reads=[t_ep, C.t_const], writes=[ttp], join=True)
            P.op("dve", lambda e, tp=tp, c=c: e.tensor_copy(out=yT[:, :, c * 64:(c + 1) * 64], in_=tp[:, 0:256].rearrange("p (a b) -> p a b", a=4)), reads=[ttp], writes=[t_yT], join=True)
        for j in range(4):
            P.op("sync", lambda e, hh=hh, j=j: e.dma_start(out=C.YT_d[hh * 4 + j], in_=yT[:, j, :]), reads=[t_yT], writes=[t_yd], dma=True, join=True)
    A.release(m0)
    P.barrier()


WSHAPES = {
    "norm_g": (4, 2048), "rel_bias": (32, 16),
    "a_w_in": (1, 2048, 8192), "a_qk_g": (1, 2, 64), "a_lambda": (1, 4, 64), "a_subln_g": (1, 128), "a_w_out": (1, 2048, 2048),
    "b_w_in": (1, 2048, 6160), "b_w_gate": (1, 16, 1024), "b_gate_bias": (1, 1024), "b_out_g": (1, 512), "b_w_out": (1, 2048, 2048),
    "c_w_in": (1, 2048, 4096), "c_conv_w": (1, 4, 2048), "c_conv_b": (1, 2048), "c_w_rgate": (1, 8, 256, 256), "c_b_rgate": (1, 2048),
    "c_w_igate": (1, 8, 256, 256), "c_b_igate": (1, 2048), "c_lambda": (1, 2048), "c_w_out": (1, 2048, 2048),
    "d_w_in": (1, 2048, 3136), "d_q_lat_g": (1, 512), "d_kv_lat_g": (1, 512), "d_w_uq": (1, 512, 3072), "d_w_ukv": (1, 512, 4096),
    "d_qk_g": (1, 2, 192), "d_w_out": (1, 2048, 2048),
}
LAYER_W = {
    0: ["norm_g", "rel_bias", "a_w_in", "a_qk_g", "a_lambda", "a_subln_g", "a_w_out"],
    1: ["norm_g", "b_w_in", "b_w_gate", "b_gate_bias", "b_out_g", "b_w_out"],
    2: ["norm_g", "c_w_in", "c_conv_w", "c_conv_b", "c_w_rgate", "c_b_rgate", "c_w_igate", "c_b_igate", "c_lambda", "c_w_out"],
    3: ["norm_g", "d_w_in", "d_q_lat_g", "d_kv_lat_g", "d_w_uq", "d_w_ukv", "d_qk_g", "d_w_out"],
}


def t5_bucket_np(rel):
    nb = 16; max_exact = 8
    ret = np.where(rel > 0, nb, 0)
    n = np.abs(rel)
    nf = np.maximum(n, 1).astype(np.float32)
    large = max_exact + (np.log(nf / max_exact) / math.log(128 / max_exact) * (nb - max_exact)).astype(np.int32)
    large = np.minimum(large, nb - 1)
    return ret + np.where(n < max_exact, n, large)


def bias_index_tiles():
    k = np.arange(128)[:, None]; q = np.arange(128)[None, :]
    idx = np.zeros((128, 2, 128), np.int64); msk = np.zeros((128, 2, 128), bool)
    idx[:, 0, :] = t5_bucket_np(k - q)
    msk[:, 0, :] = (k // 64) > (q // 64)
    idx[:, 1, :] = t5_bucket_np(k - q - 128)
    return idx, msk


def rope_tables():
    half = 32
    inv = (np.float32(10000.0) ** (-np.arange(half, dtype=np.float32) / np.float32(half))).astype(np.float32)
    ang = (np.arange(S, dtype=np.float32)[:, None] * inv[None, :]).astype(np.float32)
    c = np.cos(ang).astype(np.float32).T; s_ = np.sin(ang).astype(np.float32).T
    return np.ascontiguousarray(np.stack([np.concatenate([c, c], 0), np.concatenate([s_, s_], 0)], 0))


def build_program(layers, debug=False):
    nc = bass.Bass("TRN2", target_bir_lowering=False)
    C = Ctx()
    C.nc = nc
    x_in = dram(nc, "x", [S, D], F32, "ExternalInput")
    out = dram(nc, "out", [S, D], F32, "ExternalOutput")
    names = []
    for l in layers:
        for n in LAYER_W[l]:
            if n not in names:
                names.append(n)
    W = {n: dram(nc, n, WSHAPES[n], F32, "ExternalInput") for n in names}
    if 0 in layers:
        C.biasT_in = dram(nc, "biasT", [128, 16, 2, 128], F32, "ExternalInput")
    ident_d = nc.inline_tensor(np.eye(128, dtype=np.float32), "ident_c").ap()
    o64 = np.zeros((128, 128), np.float32); o64[:64, :64] = 1; o64[64:, 64:] = 1
    ones64_d = nc.inline_tensor(o64, "ones64_c").ap()
    ones128_d = nc.inline_tensor(np.ones((128, 128), np.float32), "ones128_c").ap()
    xs = [dram(nc, f"xs{i}", [S, D], F32) for i in range(2)]
    sk = "ExternalOutput" if debug else "Internal"
    C.QT_d = dram(nc, "QT_d", [16, 128, S], BF16, sk)
    C.KT_d = dram(nc, "KT_d", [16, 128, S], BF16, sk)
    C.V_d = dram(nc, "V_d", [S, D], BF16, sk)
    C.G_d = dram(nc, "G_d", [S, D], F32, sk)
    C.YT_d = dram(nc, "YT_d", [16, 128, S], BF16, sk)
    C.GT_d = dram(nc, "GT_d", [16, 128, S], F32, sk)
    if 3 in layers:
        C.QPE_d = dram(nc, "QPE_d", [16, 64, S], BF16, sk)
        C.KPE_d = dram(nc, "KPE_d", [64, S], BF16, sk)
        C.CQ_d = dram(nc, "CQ_d", [4, 128, S], BF16, sk)
        C.CKV_d = dram(nc, "CKV_d", [4, 128, S], BF16, sk)
        C.rope_in = dram(nc, "rope_cs", [2, 64, S], F32, "ExternalInput")
        kk = np.arange(128)[:, None]; qq = np.arange(128)[None, :]
        C.maskT_d = nc.inline_tensor(np.where((kk // 64) > (qq // 64), -30000.0, 0.0).astype(np.float32), "maskT_c").ap()
    if 1 in layers:
        C.KP_d = dram(nc, "KP_d", [S, 1024], BF16, sk)
        tt = np.arange(128)
        tu = np.where((tt[:, None] > tt[None, :]) & (tt[:, None] // 64 == tt[None, :] // 64), -1.0 / 16.0, 0.0).astype(np.float32)
        ci = np.where(tt[:, None] // 64 == np.arange(2)[None, :], -1.0 / 16.0, 0.0).astype(np.float32)
        C.TU_d = nc.inline_tensor(tu, "TU_c").ap()
        C.CI_d = nc.inline_tensor(ci, "CI_c").ap()
    rot = np.zeros((128, 128), np.float32)
    for i_ in range(32):
        rot[32 + i_, i_] = -1.0; rot[i_, 32 + i_] = 1.0
    rotm_d = nc.inline_tensor(rot, "rotm_c").ap()
    with ExitStack() as ctx:
        P = Prog(nc, ctx)
        C.P = P
        A = Arena(nc, ctx, 200)
        C.A = A
        C.psf = [ctx.enter_context(nc.psum_tensor(f"psf{i}", [128, 512], F32))[:] for i in range(8)]
        C.t_psf = P.toks(8, "psf")
        C.tpb = [C.psf[6].bitcast(BF16), C.psf[7].bitcast(BF16)]
        C.t_tpb = [C.t_psf[6], C.t_psf[7]]
        C.auxcnt = 0
        C.t_const = P.tok("const"); C.t_lconst = P.tok("lconst")
        C.ident = A.alloc([128], BF16); C.ones64 = A.alloc([128], F32); C.ones128 = A.alloc([128], F32); C.epscol = A.alloc([1], F32)
        C.rotm = A.alloc([128], F32); C.identf = A.alloc([128], F32)
        P.op("sync", lambda e: e.dma_start(out=C.identf, in_=ident_d), writes=[C.t_const], dma=True, join=True)
        P.op("sync", lambda e: e.dma_start(out=C.rotm, in_=rotm_d), writes=[C.t_const], dma=True, join=True)
        P.op("pool", lambda e: e.dma_start(out=C.ident, in_=ident_d), writes=[C.t_const], dma=True, join=True)
        P.op("sync", lambda e: e.dma_start(out=C.ones64, in_=ones64_d), writes=[C.t_const], dma=True, join=True)
        P.op("sync", lambda e: e.dma_start(out=C.ones128, in_=ones128_d), writes=[C.t_const], dma=True, join=True)
        P.op("dve", lambda e: e.memset(C.epscol, EPS), writes=[C.t_const], join=True)
        P.barrier()
        cur = x_in
        for li, l in enumerate(layers):
            C.layer = l
            dst = out if li == len(layers) - 1 else xs[li % 2]
            if l == 0:
                layer0_proj(C, cur, W)
                attn_phase(C, W, "diff")
                phase_outproj(C, C.YT_d, W["a_w_out"][0], cur, dst)
            elif l == 1:
                layer1_all(C, cur, W)
                phase_outproj(C, C.YT_d, W["b_w_out"][0], cur, dst)
            elif l == 2:
                layer2_all(C, cur, W)
                phase_outproj(C, C.YT_d, W["c_w_out"][0], cur, dst)
            elif l == 3:
                layer3_proj1(C, cur, W)
                layer3_proj2(C, W)
                attn_phase(C, W, "mla")
                phase_outproj(C, C.YT_d, W["d_w_out"][0], cur, dst)
            else:
                raise NotImplementedError
            cur = dst
        P.emit()
    C.names = names
    return nc, C


_CACHE = {}


def run_layers(layers, x, inputs, n_cores=4):
    key = tuple(layers)
    if key not in _CACHE:
        _CACHE[key] = build_program(layers)
    nc, C = _CACHE[key]
    shared = {n: np.ascontiguousarray(inputs[n], dtype=np.float32) for n in C.names}
    if 0 in layers:
        idx, msk = bias_index_tiles()
        rb = np.asarray(inputs["rel_bias"], np.float32)
        bt = rb[idx]
        bt = np.where(msk[..., None], np.float32(-30000.0), bt)
        shared["biasT"] = np.ascontiguousarray(bt.transpose(0, 3, 1, 2))
    if 3 in layers:
        shared["rope_cs"] = rope_tables()
    in_maps = [dict(shared, x=np.ascontiguousarray(x[b])) for b in range(n_cores)]
    res = run_bass_kernel_spmd(nc, in_maps, core_ids=list(range(n_cores)))
    C.last_res = res
    return np.stack([r["out"] for r in res.results], axis=0)


def kernel(**inputs):
    x = np.asarray(inputs["x"], np.float32)
    return run_layers([0, 1, 2, 3], x, inputs)
```

```python
import math
from contextlib import ExitStack
import numpy as np
import concourse.bass as bass
import concourse.mybir as mybir
from concourse.bass_utils import run_bass_kernel_spmd

F32 = mybir.dt.float32
BF16 = mybir.dt.bfloat16
AF = mybir.ActivationFunctionType
ALU = mybir.AluOpType
AX = mybir.AxisListType

ENGS = ("sync", "act", "pool", "dve", "pe")


class Tok:
    __slots__ = ("name", "w", "r", "pool")

    def __init__(self, name):
        self.name = name
        self.w = {}
        self.r = {}
        self.pool = {}


class Prog:
    def __init__(self, nc, ctx, same_engine_sync=("act", "dve", "pool")):
        self.nc = nc
        self.ctx = ctx
        self.q = {e: [] for e in ENGS}
        self.cnt = {e: 0 for e in ENGS}
        self.waited = {e: {} for e in ENGS}
        self.pend = {e: {} for e in ENGS}
        self.needed = set()
        self.same_sync = set(same_engine_sync)
        self.pool_val = []
        self.pool_free = {"hw": [], "sw": []}
        self.live = []
        self.all_toks = []

    def tok(self, name="t"):
        t = Tok(name)
        self.all_toks.append(t)
        return t

    def toks(self, n, name="t"):
        return [self.tok(f"{name}{i}") for i in range(n)]

    def op(self, eng, fn, reads=(), writes=(), dma=False, join=False):
        deps = dict(self.pend[eng])
        self.pend[eng] = {}

        def add(k, v):
            if deps.get(k, 0) < v:
                deps[k] = v

        for t in reads:
            for k, v in t.w.items():
                add(k, v)
        for t in writes:
            if not (join and not t.r):
                for k, v in t.w.items():
                    add(k, v)
            for k, v in t.r.items():
                add(k, v)
        waits = []
        wd = self.waited[eng]
        for k, v in deps.items():
            if k == ("e", eng) and (eng not in self.same_sync or self.cnt[eng] + 1 - v >= 3):
                continue
            if wd.get(k, 0) < v:
                wd[k] = v
                waits.append((k, v))
                self.needed.add((k, v))
        if dma:
            assert len(writes) == 1
            t = writes[0]
            kind = "sw" if eng == "pool" else "hw"
            if kind not in t.pool:
                if self.pool_free[kind]:
                    t.pool[kind] = self.pool_free[kind].pop()
                else:
                    self.pool_val.append(0)
                    t.pool[kind] = len(self.pool_val) - 1
                self.live.append((t, kind))
            pi = t.pool[kind]
            self.pool_val[pi] += 16
            h = (("s", pi), self.pool_val[pi])
        else:
            self.cnt[eng] += 1
            h = (("e", eng), self.cnt[eng])
        k, v = h
        for t in reads:
            if t.r.get(k, 0) < v:
                t.r[k] = v
        for t in writes:
            if join and not t.r:
                t.w[k] = v
            else:
                t.w = {k: v}
                t.r = {}
        self.q[eng].append((waits, fn, h, dma))
        return h

    def barrier(self):
        hs = {}
        for e in ENGS:
            if self.cnt[e] > 0:
                hs[("e", e)] = self.cnt[e]
        for i, v in enumerate(self.pool_val):
            if v > 0:
                hs[("s", i)] = v
        for e in ENGS:
            for k, v in hs.items():
                if self.pend[e].get(k, 0) < v:
                    self.pend[e][k] = v
        for t, kind in self.live:
            self.pool_free[kind].append(t.pool[kind])
        self.live = []
        for t in self.all_toks:
            t.w = {}
            t.r = {}
            t.pool = {}

    def emit(self):
        nc = self.nc
        self.barrier()
        fw = []
        wd = self.waited["sync"]
        for k, v in self.pend["sync"].items():
            if wd.get(k, 0) < v:
                fw.append((k, v))
                self.needed.add((k, v))
        esem = {e: self.ctx.enter_context(nc.semaphore(f"sem_{e}")) for e in ENGS}
        psem = [self.ctx.enter_context(nc.semaphore(f"dsem{i}")) for i in range(len(self.pool_val))]
        rank = {}
        for e in ENGS:
            r = 0
            for (_w, _f, h, dma) in self.q[e]:
                if not dma and h in self.needed:
                    r += 1
                    rank[h] = r

        def resolve(h):
            k, v = h
            if k[0] == "e":
                return esem[k[1]], rank[h]
            return psem[k[1]], v

        block = self.ctx.enter_context(nc.Block())
        names = {"sync": "sync", "act": "scalar", "pool": "gpsimd", "dve": "vector", "pe": "tensor"}
        for e in ENGS:
            ops = self.q[e]
            if not ops and e != "sync":
                continue

            def body(eng, ops=ops, e=e):
                for (waits, fn, h, dma) in ops:
                    for w in waits:
                        s, v = resolve(w)
                        eng.wait_ge(s, v)
                    ins = fn(eng)
                    if dma:
                        s, v = resolve(h)
                        ins.then_inc(s, 16)
                    elif h in self.needed:
                        s, v = resolve(h)
                        ins.then_inc(s, 1)
                if e == "sync":
                    for w in fw:
                        s, v = resolve(w)
                        eng.wait_ge(s, v)

            getattr(block, names[e])(body)
        self.n_sems = len(psem) + 5


class Arena:
    def __init__(self, nc, ctx, kib):
        self.n32 = kib * 256
        self.t = ctx.enter_context(nc.sbuf_tensor("arena", [128, self.n32], F32))
        self.off = 0

    def mark(self):
        return self.off

    def release(self, m):
        self.off = m

    def alloc(self, shape, dt, parts=128):
        n = int(np.prod(shape))
        n32 = n if dt == F32 else (n + 1) // 2
        n32 = (n32 + 7) // 8 * 8
        assert self.off + n32 <= self.n32, f"arena overflow {self.off + n32} > {self.n32}"
        v = self.t[0:parts, self.off:self.off + n32]
        self.off += n32
        if dt != F32:
            v = v.bitcast(dt)
        v = v[:, 0:n]
        if len(shape) == 2:
            v = v.rearrange("p (a b) -> p a b", a=shape[0])
        elif len(shape) == 3:
            v = v.rearrange("p (a b c) -> p a b c", a=shape[0], b=shape[1])
        return v


S = 4096
D = 2048
EPS = 1e-6
NT = S // 128
TB = 2048
NTB = S // TB
TPB = TB // 128


class Ctx:
    pass


def dram(nc, name, shape, dt, kind="Internal"):
    return nc.dram_tensor(name, list(shape), dt, kind=kind).ap()


def load_w_block(C, dst, dtok, wsrc, c0, ncols, kc=16, rows0=0):
    P = C.P
    wv = wsrc[rows0:rows0 + kc * 128, :].rearrange("(k p) c -> p k c", p=128)
    step = 4 if kc >= 4 else kc
    for k0 in range(0, kc, step):
        k1 = min(kc, k0 + step)
        P.op("pool", lambda e, k0=k0, k1=k1: e.dma_start(out=dst[:, k0:k1, 0:ncols], in_=wv[:, k0:k1, c0:c0 + ncols]),
             writes=[dtok], dma=True, join=True)


def phase_norm_T(C, x_src, g_row, tok0, hT, hT_toks, tbsz=TB):
    P, A = C.P, C.A
    m = A.mark()
    xin = [A.alloc([D], F32) for _ in range(2)]
    hb = [A.alloc([D], BF16) for _ in range(2)]
    gb = A.alloc([D], F32)
    ssq = [A.alloc([1], F32) for _ in range(2)]
    t_xin = P.toks(2, "xin"); t_hb = P.toks(2, "hb"); t_gb = P.tok("gb"); t_ssq = P.toks(2, "ssq")
    P.op("sync", lambda e: e.dma_start(out=gb, in_=g_row.partition_broadcast(128)), writes=[t_gb], dma=True)
    for i in range(tbsz // 128):
        s = i % 2
        r0 = tok0 + i * 128
        P.op("sync", lambda e, s=s, r0=r0: e.dma_start(out=xin[s], in_=x_src[r0:r0 + 128, :]), writes=[t_xin[s]], dma=True)
        P.op("act", lambda e, s=s: e.activation(out=hb[s], in_=xin[s], func=AF.Square, accum_out=ssq[s]),
             reads=[t_xin[s]], writes=[t_hb[s], t_ssq[s]])
        P.op("act", lambda e, s=s: e.activation(out=ssq[s], in_=ssq[s], func=AF.Ln, bias=C.epscol, scale=1.0 / D),
             reads=[t_ssq[s], C.t_const], writes=[t_ssq[s]])
        P.op("act", lambda e, s=s: e.activation(out=ssq[s], in_=ssq[s], func=AF.Exp, scale=-0.5), reads=[t_ssq[s]], writes=[t_ssq[s]])
        P.op("dve", lambda e, s=s: e.scalar_tensor_tensor(out=hb[s], in0=xin[s], scalar=ssq[s], in1=gb, op0=ALU.mult, op1=ALU.mult),
             reads=[t_xin[s], t_ssq[s], t_gb], writes=[t_hb[s]])
        for half in range(2):
            tp, ttp = C.tpb[half], C.t_tpb[half]
            for j in range(8):
                k = half * 8 + j
                P.op("pe", lambda e, s=s, j=j, k=k, tp=tp: e.transpose(out=tp[:, j * 128:(j + 1) * 128], in_=hb[s][:, k * 128:(k + 1) * 128], identity=C.ident),
                     reads=[t_hb[s], C.t_const], writes=[ttp], join=True)
            eng = "act" if half == 0 else "dve"
            dst = hT[:, half * 8:(half + 1) * 8, i * 128:(i + 1) * 128]
            srcv = tp.rearrange("p (a b) -> p a b", a=8)
            if eng == "act":
                P.op("act", lambda e, dst=dst, srcv=srcv: e.activation(out=dst, in_=srcv, func=AF.Copy), reads=[ttp], writes=[hT_toks[i]], join=True)
            else:
                P.op("dve", lambda e, dst=dst, srcv=srcv: e.tensor_copy(out=dst, in_=srcv), reads=[ttp], writes=[hT_toks[i]], join=True)
    A.release(m)


def phase_outproj(C, yT_d, w_out, x_src, x_dst):
    P, A = C.P, C.A
    m = A.mark()
    wo = A.alloc([16, D], BF16)
    yT = A.alloc([16, TB], BF16)
    xin = [A.alloc([D], F32) for _ in range(2)]
    xo = [A.alloc([D], F32) for _ in range(2)]
    t_wo = P.tok("wo"); t_yT = P.toks(16, "yT"); t_xin = P.toks(2, "xin"); t_xo = P.toks(2, "xo"); t_dst = P.tok("xdst")
    for n in range(4):
        load_w_block(C, wo[:, :, n * 512:(n + 1) * 512], t_wo, w_out, n * 512, 512)
    cnt = 0
    for tb in range(NTB):
        tok0 = tb * TB
        for k in range(16):
            P.op("sync", lambda e, k=k, tok0=tok0: e.dma_start(out=yT[:, k, :], in_=yT_d[k, :, tok0:tok0 + TB]), writes=[t_yT[k]], dma=True)
        for i in range(TPB):
            s = i % 2
            r0 = tok0 + i * 128
            P.op("sync", lambda e, s=s, r0=r0: e.dma_start(out=xin[s], in_=x_src[r0:r0 + 128, :]), writes=[t_xin[s]], dma=True)
            for n in range(4):
                pb = cnt % 4; cnt += 1
                ps, tps = C.psf[pb], C.t_psf[pb]
                for k in range(16):
                    P.op("pe", lambda e, ps=ps, k=k, i=i, n=n: e.matmul(ps, lhsT=yT[:, k, i * 128:(i + 1) * 128], rhs=wo[:, k, n * 512:(n + 1) * 512], start=(k == 0), stop=(k == 15)),
                         reads=[t_yT[k], t_wo], writes=[tps], join=(k > 0))
                P.op("dve", lambda e, ps=ps, s=s, n=n: e.tensor_tensor(out=xo[s][:, n * 512:(n + 1) * 512], in0=ps, in1=xin[s][:, n * 512:(n + 1) * 512], op=ALU.add),
                     reads=[tps, t_xin[s]], writes=[t_xo[s]], join=(n > 0))
            P.op("sync", lambda e, s=s, r0=r0: e.dma_start(out=x_dst[r0:r0 + 128, :], in_=xo[s]), reads=[t_xo[s]], writes=[t_dst], dma=True)
    A.release(m)
    P.barrier()


def defer(C, delay, fn):
    due = C.tickn + delay
    if C.dq and C.dq[-1][0] > due:
        due = C.dq[-1][0]
    if delay <= 0 and not C.dq:
        fn()
    else:
        C.dq.append((due, fn))


def tick(C):
    C.tickn += 1
    while C.dq and C.dq[0][0] <= C.tickn:
        C.dq.pop(0)[1]()


def flush(C):
    while C.dq:
        C.dq.pop(0)[1]()


def qk_norm_epilogue(C, ps, tps, gcol, dst, t_dst, tmp, t_tmp, grp, delay=2):
    P = C.P
    qf, sq, rs = tmp
    ones = C.ones64 if grp == 64 else C.ones128
    tick(C)
    P.op("dve", lambda e: e.tensor_copy(out=qf, in_=ps), reads=[tps], writes=[t_tmp[0]])
    P.op("pool", lambda e: e.tensor_tensor(out=sq, in0=qf, in1=qf, op=ALU.mult), reads=[t_tmp[0]], writes=[t_tmp[1]])

    def stage2():
        pss, tpss = C.psf[4 + C.auxcnt % 2], C.t_psf[4 + C.auxcnt % 2]
        C.auxcnt += 1
        P.op("pe", lambda e: e.matmul(pss, lhsT=ones, rhs=sq, start=True, stop=True), reads=[t_tmp[1], C.t_const], writes=[tpss])
        P.op("act", lambda e: e.activation(out=rs, in_=pss, func=AF.Ln, bias=C.epscol, scale=1.0 / grp), reads=[tpss, C.t_const], writes=[t_tmp[2]])
        P.op("act", lambda e: e.activation(out=rs, in_=rs, func=AF.Exp, scale=-0.5), reads=[t_tmp[2]], writes=[t_tmp[2]])
        P.op("dve", lambda e: e.scalar_tensor_tensor(out=dst, in0=qf, scalar=gcol, in1=rs, op0=ALU.mult, op1=ALU.mult),
             reads=[t_tmp[0], t_tmp[2], C.t_lconst], writes=[t_dst], join=True)
    defer(C, delay, stage2)


def gT_block(C, hT, hT_toks, wt_s, t_wt_s, tok0, h0, tbsz, pcnt, gst, t_gst, gcnt):
    P = C.P
    for mm_ in range(4):
        gs = gcnt % 2; gcnt += 1
        for tq in range(tbsz // 512):
            pb = pcnt % 4; pcnt += 1
            ps, tps = C.psf[pb], C.t_psf[pb]
            for k in range(16):
                P.op("pe", lambda e, ps=ps, k=k, mm_=mm_, tq=tq: e.matmul(ps, lhsT=wt_s[:, k, mm_ * 128:(mm_ + 1) * 128], rhs=hT[:, k, tq * 512:(tq + 1) * 512], start=(k == 0), stop=(k == 15)),
                     reads=[t_wt_s] + hT_toks[tq * 4:(tq + 1) * 4], writes=[tps], join=(k > 0))
            P.op("act", lambda e, ps=ps, gs=gs, tq=tq: e.activation(out=gst[gs][:, tq * 512:(tq + 1) * 512], in_=ps, func=AF.Silu), reads=[tps], writes=[t_gst[gs]], join=True)
        P.op("sync", lambda e, gs=gs, mm_=mm_: e.dma_start(out=C.GT_d[h0 + mm_, :, tok0:tok0 + tbsz], in_=gst[gs][:, 0:tbsz]), reads=[t_gst[gs]], writes=[C.t_gd], dma=True, join=True)
    return pcnt, gcnt


def layer0_proj(C, x_src, W):
    P, A = C.P, C.A
    m = A.mark()
    hT = A.alloc([16, TB], BF16)
    hT_toks = P.toks(TPB, "hT")
    wt = [A.alloc([16, 512], BF16) for _ in range(2)]
    t_wt = P.toks(2, "wt")
    qst = [A.alloc([TB], BF16) for _ in range(2)]
    t_qst = P.toks(2, "qst")
    tmp = [[A.alloc([512], F32) for _ in range(3)] for _ in range(2)]
    t_tmp = [P.toks(3, "tmp") for _ in range(2)]
    vst = [A.alloc([8, 512], BF16) for _ in range(2)]
    t_vst = P.toks(2, "vst")
    gst = [A.alloc([TB], F32) for _ in range(2)]
    t_gst = P.toks(2, "gst")
    t_qd = P.tok("QTd"); t_kd = P.tok("KTd"); t_vd = P.tok("Vd"); C.t_gd = P.tok("Gd")
    gq = A.alloc([1], F32); gk = A.alloc([1], F32)
    for half in range(2):
        P.op("sync", lambda e, half=half: e.dma_start(out=gq[half * 64:(half + 1) * 64, :], in_=W["a_qk_g"][0, 0, :].rearrange("(d o) -> d o", o=1)), writes=[C.t_lconst], dma=True, join=True)
        P.op("sync", lambda e, half=half: e.dma_start(out=gk[half * 64:(half + 1) * 64, :], in_=W["a_qk_g"][0, 1, :].rearrange("(d o) -> d o", o=1)), writes=[C.t_lconst], dma=True, join=True)
    P.op("dve", lambda e: e.tensor_scalar(out=gq, in0=gq, scalar1=0.125, scalar2=None, op0=ALU.mult), reads=[C.t_lconst], writes=[C.t_lconst])
    wcnt = 0; qcnt = 0; tcnt = 0; pcnt = 0; vcnt = 0; gcnt = 0
    for tb in range(NTB):
        tok0 = tb * TB
        phase_norm_T(C, x_src, W["norm_g"][C.layer, :], tok0, hT, hT_toks)
        for cb in range(16):
            ws = wcnt % 2; wcnt += 1
            load_w_block(C, wt[ws], t_wt[ws], W["a_w_in"][0], cb * 512, 512)
            kind = cb // 4
            if kind < 2:
                for mm_ in range(4):
                    h = (cb % 4) * 4 + mm_
                    qs = qcnt % 2; qcnt += 1
                    for tq in range(TB // 512):
                        pb = pcnt % 4; pcnt += 1
                        ps, tps = C.psf[pb], C.t_psf[pb]
                        for k in range(16):
                            P.op("pe", lambda e, ps=ps, k=k, ws=ws, mm_=mm_, tq=tq: e.matmul(ps, lhsT=wt[ws][:, k, mm_ * 128:(mm_ + 1) * 128], rhs=hT[:, k, tq * 512:(tq + 1) * 512], start=(k == 0), stop=(k == 15)),
                                 reads=[t_wt[ws]] + hT_toks[tq * 4:(tq + 1) * 4], writes=[tps], join=(k > 0))
                        ts_ = tcnt % 2; tcnt += 1
                        qk_norm_epilogue(C, ps, tps, gq if kind == 0 else gk, qst[qs][:, tq * 512:(tq + 1) * 512], t_qst[qs], tmp[ts_], t_tmp[ts_], 64)
                    dd, td = (C.QT_d, t_qd) if kind == 0 else (C.KT_d, t_kd)
                    defer(C, 2, lambda dd=dd, td=td, h=h, qs=qs, tok0=tok0: P.op("sync", lambda e: e.dma_start(out=dd[h, :, tok0:tok0 + TB], in_=qst[qs]), reads=[t_qst[qs]], writes=[td], dma=True, join=True))
                flush(C)
            elif kind == 3:
                pcnt, gcnt = gT_block(C, hT, hT_toks, wt[ws], t_wt[ws], tok0, (cb % 4) * 4, TB, pcnt, gst, t_gst, gcnt)
            else:
                c0 = (cb % 4) * 512
                for i in range(TPB):
                    pb = pcnt % 4; pcnt += 1
                    ps, tps = C.psf[pb], C.t_psf[pb]
                    for k in range(16):
                        P.op("pe", lambda e, ps=ps, k=k, ws=ws, i=i: e.matmul(ps, lhsT=hT[:, k, i * 128:(i + 1) * 128], rhs=wt[ws][:, k, :], start=(k == 0), stop=(k == 15)),
                             reads=[t_wt[ws], hT_toks[i]], writes=[tps], join=(k > 0))
                    r0 = tok0 + i * 128
                    if True:
                        g8 = i % 8
                        if g8 == 0:
                            vs = vcnt % 2; vcnt += 1
                        P.op("dve", lambda e, ps=ps, vs=vs, g8=g8: e.tensor_copy(out=vst[vs][:, g8, :], in_=ps), reads=[tps], writes=[t_vst[vs]], join=True)
                        if g8 == 7:
                            rr = r0 - 7 * 128
                            P.op("sync", lambda e, vs=vs, rr=rr, c0=c0: e.dma_start(out=C.V_d[rr:rr + 1024, c0:c0 + 512].rearrange("(a p) c -> p a c", p=128), in_=vst[vs]),
                                 reads=[t_vst[vs]], writes=[t_vd], dma=True, join=True)
    A.release(m)
    P.barrier()


def attn_phase(C, W, mode):
    P, A = C.P, C.A
    m = A.mark()
    H = 16
    diff = (mode == "diff")
    nsub = 2 if diff else 1
    lam_init = 0.8 - 0.6 * math.exp(-0.3 * C.layer)
    QT = [A.alloc([S], BF16) for _ in range(2)]
    KT = [A.alloc([S], BF16) for _ in range(2)]
    Vh = [A.alloc([NT, 128], BF16) for _ in range(2)]
    Gh = [A.alloc([S], F32) for _ in range(2)]
    yT = [A.alloc([S], BF16) for _ in range(2)]
    LA = 4 if diff else 3
    NPT = LA + 2
    PT = [A.alloc([512], BF16) for _ in range(NPT)]
    onesb = A.alloc([128], BF16)
    t_QT = P.toks(2, "QT"); t_KT = P.toks(2, "KT"); t_Vh = P.toks(2, "Vh"); t_Gh = P.toks(2, "Gh"); t_yT = P.toks(2, "yT"); t_PT = P.toks(NPT, "PT")
    t_yd = P.tok("YTd")
    Osb = [[A.alloc([512], F32) for _ in range(2)] for _ in range(2)]; Dsb = [[A.alloc([512], F32) for _ in range(2)] for _ in range(2)]
    sqb = [A.alloc([512], F32) for _ in range(2)]; rsb = [A.alloc([512], F32) for _ in range(2)]
    t_Osb = [P.toks(2, "Osb") for _ in range(2)]; t_Dsb = [P.toks(2, "Dsb") for _ in range(2)]; t_sqb = P.toks(2, "sqb"); t_rsb = P.toks(2, "rsb")
    tl = C.t_lconst
    P.op("pool", lambda e: e.memset(onesb, 1.0), writes=[tl])
    if diff:
        biasT = A.alloc([H, 2, 128], F32)
        b15 = A.alloc([H], F32)
        lam4 = A.alloc([4, 64], F32)
        lamc = A.alloc([4], F32)
        sgcol = A.alloc([1], F32)
        P.op("sync", lambda e: e.dma_start(out=biasT, in_=C.biasT_in), writes=[tl], dma=True, join=True)
        P.op("sync", lambda e: e.dma_start(out=b15, in_=W["rel_bias"][15, :].partition_broadcast(128)), writes=[tl], dma=True, join=True)
        P.op("sync", lambda e: e.dma_start(out=lam4, in_=W["a_lambda"][0].rearrange("a d -> (a d)").partition_broadcast(128).rearrange("p (a d) -> p a d", a=4)), writes=[tl], dma=True, join=True)
        load_col(C, sgcol, W["a_subln_g"][0, :], 128, 1.0 - lam_init)
        biasB = A.alloc([H, 2, 128], BF16)
        for hh in range(H):
            P.op("dve", lambda e, hh=hh: e.tensor_scalar(out=biasT[:, hh], in0=biasT[:, hh], scalar1=b15[:, hh:hh + 1], scalar2=None, op0=ALU.subtract), reads=[tl], writes=[tl])
        P.op("dve", lambda e: e.tensor_copy(out=biasB, in_=biasT), reads=[tl], writes=[tl])
        P.op("dve", lambda e: e.tensor_tensor(out=lam4[:, 0, :], in0=lam4[:, 0, :], in1=lam4[:, 1, :], op=ALU.mult), reads=[tl], writes=[tl])
        P.op("dve", lambda e: e.tensor_tensor(out=lam4[:, 2, :], in0=lam4[:, 2, :], in1=lam4[:, 3, :], op=ALU.mult), reads=[tl], writes=[tl])
        P.op("dve", lambda e: e.reduce_sum(out=lamc[:, 0:1], in_=lam4[:, 0, :], axis=AX.X), reads=[tl], writes=[tl])
        P.op("dve", lambda e: e.reduce_sum(out=lamc[:, 1:2], in_=lam4[:, 2, :], axis=AX.X), reads=[tl], writes=[tl])
        P.op("act", lambda e: e.activation(out=lamc[:, 0:2], in_=lamc[:, 0:2], func=AF.Exp), reads=[tl], writes=[tl])
        P.op("dve", lambda e: e.scalar_tensor_tensor(out=lamc[:, 0:1], in0=lamc[:, 1:2], scalar=-lam_init, in1=lamc[:, 0:1], op0=ALU.add, op1=ALU.subtract), reads=[tl], writes=[tl])
    else:
        maskT = A.alloc([128], F32)
        QP = [A.alloc([S], BF16) for _ in range(2)]
        KP = A.alloc([S], BF16)
        t_QP = P.toks(2, "QP"); t_KP = P.tok("KP")
        P.op("sync", lambda e: e.dma_start(out=maskT, in_=C.maskT_d), writes=[tl], dma=True)
        maskB = A.alloc([128], BF16)
        P.op("dve", lambda e: e.tensor_copy(out=maskB, in_=maskT), reads=[tl], writes=[tl])
        P.op("sync", lambda e: e.dma_start(out=KP[0:64, :], in_=C.KPE_d), writes=[t_KP], dma=True)
    SB = [0, 1, 2, 3, 7] if diff else [0, 1, 6, 7]
    NSB = len(SB)

    def banks(qg, t):
        b0 = 4 if diff else 2 + 2 * (qg % 2)
        return b0, b0 + 1

    def emit_loads(h):
        hs = h % 2
        P.op("sync", lambda e: e.dma_start(out=QT[hs], in_=C.QT_d[h]), writes=[t_QT[hs]], dma=True)
        P.op("sync", lambda e: e.dma_start(out=KT[hs], in_=C.KT_d[h]), writes=[t_KT[hs]], dma=True)
        if not diff:
            P.op("sync", lambda e: e.dma_start(out=QP[hs][0:64, :], in_=C.QPE_d[h]), writes=[t_QP[hs]], dma=True)
        P.op("sync", lambda e: e.dma_start(out=Vh[hs], in_=C.V_d[:, h * 128:(h + 1) * 128].rearrange("(a p) c -> p a c", p=128)), writes=[t_Vh[hs]], dma=True)
        P.op("sync", lambda e: e.dma_start(out=Gh[hs], in_=C.GT_d[h]), writes=[t_Gh[hs]], dma=True)

    def emit_S(n, h, qg, t, i):
        hs = h % 2
        jmin = max(0, i - 4 * qg)
        Sp, tSp = C.psf[SB[n % NSB]], C.t_psf[SB[n % NSB]]
        q0 = (4 * qg + jmin) * 128
        ncol = (4 - jmin) * 128
        c0 = jmin * 128
        near = []
        for rel in ((0, 1) if diff else (0,)):
            jj = i - 4 * qg + rel
            if 0 <= jj <= 3 and jj >= jmin:
                near.append((jj, biasB[:, h, rel, :] if diff else maskB))
        nn = len(near)
        if diff:
            P.op("pe", lambda e: e.matmul(Sp[:, c0:c0 + ncol], lhsT=KT[hs][t * 64:(t + 1) * 64, i * 128:(i + 1) * 128], rhs=QT[hs][t * 64:(t + 1) * 64, q0:q0 + ncol], start=True, stop=(nn == 0), skip_group_check=True),
                 reads=[t_KT[hs], t_QT[hs]], writes=[tSp])
        else:
            P.op("pe", lambda e: e.matmul(Sp[:, c0:c0 + ncol], lhsT=KT[hs][:, i * 128:(i + 1) * 128], rhs=QT[hs][:, q0:q0 + ncol], start=True, stop=False, skip_group_check=True),
                 reads=[t_KT[hs], t_QT[hs]], writes=[tSp])
            P.op("pe", lambda e: e.matmul(Sp[:, c0:c0 + ncol], lhsT=KP[0:64, i * 128:(i + 1) * 128], rhs=QP[hs][0:64, q0:q0 + ncol], start=False, stop=(nn == 0), skip_group_check=True),
                 reads=[t_KP, t_QP[hs]], writes=[tSp], join=True)
        for bi, (jj, btile) in enumerate(near):
            P.op("pe", lambda e, jj=jj, btile=btile, bi=bi: e.matmul(Sp[:, jj * 128:(jj + 1) * 128], lhsT=C.ident, rhs=btile, start=False, stop=(bi == nn - 1), skip_group_check=True),
                 reads=[tl, C.t_const], writes=[tSp], join=True)
        pp = n % NPT
        ebias = 0.0
        P.op("act", lambda e: e.activation(out=PT[pp][:, c0:c0 + ncol], in_=Sp[:, c0:c0 + ncol], func=AF.Exp, bias=ebias, scale=1.0),
             reads=[tSp, tl], writes=[t_PT[pp]])

    def emit_PV(n, h, qg, t, i):
        hs = h % 2
        jmin = max(0, i - 4 * qg)
        c0 = jmin * 128
        pp = n % NPT
        bo, bd = banks(qg, t)
        last = (i == 4 * qg + 3)
        P.op("pe", lambda e: e.matmul(C.psf[bo][:, c0:512], lhsT=Vh[hs][:, i, :], rhs=PT[pp][:, c0:512], start=(i == 0), stop=last),
             reads=[t_PT[pp], t_Vh[hs]], writes=[C.t_psf[bo]], join=(i > 0))
        P.op("pe", lambda e: e.matmul(C.psf[bd][:, c0:512], lhsT=onesb, rhs=PT[pp][:, c0:512], start=(i == 0), stop=last),
             reads=[t_PT[pp], tl], writes=[C.t_psf[bd]], join=(i > 0))

    pending = []
    deferred = []
    cur_n = [0]
    gser = [0]

    def emit_evac(qg, t):
        par = qg % 2
        bo, bd = banks(qg, t)
        P.op("dve", lambda e: e.tensor_copy(out=Osb[par][t], in_=C.psf[bo]), reads=[C.t_psf[bo]], writes=[t_Osb[par][t]])
        P.op("dve", lambda e: e.tensor_copy(out=Dsb[par][t], in_=C.psf[bd]), reads=[C.t_psf[bd]], writes=[t_Dsb[par][t]])

    def emit_epilogue(n, h, qg):
        hs = h % 2
        par = qg % 2
        ysl = yT[hs][:, qg * 512:(qg + 1) * 512]
        gsl = Gh[hs][:, qg * 512:(qg + 1) * 512]
        O_, D_, tO, tD = Osb[par], Dsb[par], t_Osb[par], t_Dsb[par]
        ts = (0, 1) if diff else (0,)
        last = (qg == NT // 4 - 1)
        deferred.append(lambda: emit_epilogue2(n, h, qg))

    def emit_epilogue2(n, h, qg):
        hs = h % 2
        par = qg % 2
        ysl = yT[hs][:, qg * 512:(qg + 1) * 512]
        gsl = Gh[hs][:, qg * 512:(qg + 1) * 512]
        O_, D_, tO, tD = Osb[par], Dsb[par], t_Osb[par], t_Dsb[par]
        ts = (0, 1) if diff else (0,)
        last = (qg == NT // 4 - 1)
        for t in ts:
            P.op("dve", lambda e, t=t: e.reciprocal(out=D_[t], in_=D_[t]), reads=[tD[t]], writes=[tD[t]])
        if diff:
            for t in (1, 0):
                P.op("dve", lambda e, t=t: e.tensor_tensor(out=O_[t], in0=O_[t], in1=D_[t], op=ALU.mult), reads=[tO[t], tD[t]], writes=[tO[t]])
            P.op("dve", lambda e: e.scalar_tensor_tensor(out=O_[0], in0=O_[1], scalar=lamc[:, 0:1], in1=O_[0], op0=ALU.mult, op1=ALU.add), reads=[tO[0], tO[1], tl], writes=[tO[0]])
            P.op("pool", lambda e: e.tensor_tensor(out=sqb[par], in0=O_[0], in1=O_[0], op=ALU.mult), reads=[tO[0]], writes=[t_sqb[par]])

            def stage2():
                P.op("pe", lambda e: e.matmul(C.psf[6], lhsT=C.ones128, rhs=sqb[par], start=True, stop=True), reads=[t_sqb[par], C.t_const], writes=[C.t_psf[6]])
                P.op("act", lambda e: e.activation(out=rsb[par], in_=C.psf[6], func=AF.Ln, bias=C.epscol, scale=1.0 / 128), reads=[C.t_psf[6], C.t_const], writes=[t_rsb[par]])
                P.op("act", lambda e: e.activation(out=rsb[par], in_=rsb[par], func=AF.Exp, scale=-0.5), reads=[t_rsb[par]], writes=[t_rsb[par]])
                P.op("dve", lambda e: e.scalar_tensor_tensor(out=O_[0], in0=O_[0], scalar=sgcol, in1=rsb[par], op0=ALU.mult, op1=ALU.mult), reads=[tO[0], t_rsb[par], tl], writes=[tO[0]])
                P.op("pool", lambda e: e.tensor_tensor(out=ysl, in0=O_[0], in1=gsl, op=ALU.mult), reads=[tO[0], t_Gh[hs]], writes=[t_yT[hs]], join=True)
                if last:
                    P.op("sync", lambda e: e.dma_start(out=C.YT_d[h], in_=yT[hs]), reads=[t_yT[hs]], writes=[t_yd], dma=True, join=True)
            pending.append((cur_n[0] + 16, stage2, gser[0] - 1))
        else:
            P.op("pool", lambda e: e.tensor_tensor(out=O_[0], in0=O_[0], in1=D_[0], op=ALU.mult), reads=[tO[0], tD[0]], writes=[tO[0]])
            P.op("pool", lambda e: e.tensor_tensor(out=ysl, in0=O_[0], in1=gsl, op=ALU.mult), reads=[tO[0], t_Gh[hs]], writes=[t_yT[hs]], join=True)
            if last:
                P.op("sync", lambda e: e.dma_start(out=C.YT_d[h], in_=yT[hs]), reads=[t_yT[hs]], writes=[t_yd], dma=True, join=True)

    tiles = [(h, qg, t, i) for h in range(H) for qg in range(NT // 4) for t in range(nsub) for i in range(4 * qg + 4)]

    def emit_S_at(n):
        h_, qg_, t_, i_ = tiles[n]
        if qg_ == 0 and t_ == 0 and i_ == 0:
            emit_loads(h_)
        emit_S(n, h_, qg_, t_, i_)

    for n in range(min(LA, len(tiles))):
        emit_S_at(n)
    for n, (h, qg, t, i) in enumerate(tiles):
        if n + LA < len(tiles):
            emit_S_at(n + LA)
        emit_PV(n, h, qg, t, i)
        while pending and pending[0][0] <= n:
            pending.pop(0)[1]()
        cur_n[0] = n
        if i == 4 * qg + 3:
            if t == 0:
                while pending and pending[0][2] <= gser[0] - 2:
                    pending.pop(0)[1]()
            emit_evac(qg, t)
            while deferred:
                deferred.pop(0)()
            if t == nsub - 1:
                emit_epilogue(n, h, qg)
                gser[0] += 1
    while deferred:
        deferred.pop(0)()
    while pending:
        pending.pop(0)[1]()
    A.release(m)
    P.barrier()


def load_col(C, dst, src_vec, n, scale=None):
    P = C.P
    P.op("sync", lambda e: e.dma_start(out=dst[0:n, :], in_=src_vec.rearrange("(d o) -> d o", o=1)), writes=[C.t_lconst], dma=True, join=True)
    if scale is not None:
        P.op("dve", lambda e: e.tensor_scalar(out=dst[0:n, :], in0=dst[0:n, :], scalar1=float(scale), scalar2=None, op0=ALU.mult), reads=[C.t_lconst], writes=[C.t_lconst])


def rope_epilogue(C, ps, tps, gcol, cs, sn, t_cs, dst, t_dst, tmp, t_tmp, delay=2):
    P = C.P
    xf, sq, rs = tmp
    tick(C)
    P.op("dve", lambda e: e.tensor_copy(out=xf[0:64, :], in_=ps[0:64, :]), reads=[tps], writes=[t_tmp[0]])
    P.op("pool", lambda e: e.tensor_tensor(out=sq[0:64, :], in0=xf[0:64, :], in1=xf[0:64, :], op=ALU.mult), reads=[t_tmp[0]], writes=[t_tmp[1]])

    def stage2():
        pss, tpss = C.psf[4 + C.auxcnt % 2], C.t_psf[4 + C.auxcnt % 2]
        C.auxcnt += 1
        P.op("pe", lambda e: e.matmul(pss[0:64, :], lhsT=C.ones64[0:64, 0:64], rhs=sq[0:64, :], start=True, stop=True), reads=[t_tmp[1], C.t_const], writes=[tpss])
        P.op("act", lambda e: e.activation(out=rs[0:64, :], in_=pss[0:64, :], func=AF.Ln, bias=C.epscol[0:64, :], scale=1.0 / 64), reads=[tpss, C.t_const], writes=[t_tmp[2]])
        P.op("act", lambda e: e.activation(out=rs[0:64, :], in_=rs[0:64, :], func=AF.Exp, scale=-0.5), reads=[t_tmp[2]], writes=[t_tmp[2]])
        P.op("dve", lambda e: e.scalar_tensor_tensor(out=xf[0:64, :], in0=xf[0:64, :], scalar=gcol[0:64, :], in1=rs[0:64, :], op0=ALU.mult, op1=ALU.mult),
             reads=[t_tmp[0], t_tmp[2], C.t_lconst], writes=[t_tmp[0]])

    def stage3():
        pr, tpr = C.psf[4 + C.auxcnt % 2], C.t_psf[4 + C.auxcnt % 2]
        C.auxcnt += 1
        P.op("pe", lambda e: e.matmul(pr[0:64, :], lhsT=C.rotm[0:64, 0:64], rhs=xf[0:64, :], start=True, stop=True), reads=[t_tmp[0], C.t_const], writes=[tpr])
        P.op("dve", lambda e: e.tensor_tensor(out=sq[0:64, :], in0=pr[0:64, :], in1=sn, op=ALU.mult), reads=[tpr, t_cs], writes=[t_tmp[1]])
        P.op("pool", lambda e: e.tensor_tensor(out=xf[0:64, :], in0=xf[0:64, :], in1=cs, op=ALU.mult), reads=[t_tmp[0], t_cs], writes=[t_tmp[0]])
        P.op("pool", lambda e: e.tensor_tensor(out=dst, in0=xf[0:64, :], in1=sq[0:64, :], op=ALU.add), reads=[t_tmp[0], t_tmp[1]], writes=[t_dst], join=True)
    defer(C, delay, stage2)
    defer(C, 2 * delay, stage3)


def g_block(C, hT, hT_toks, wt_s, t_wt_s, tok0, c0, pcnt, gst, t_gst, gcnt, tpb=TPB):
    P = C.P
    t_gd = C.t_gd
    for i in range(tpb):
        pb = pcnt % 4; pcnt += 1
        ps, tps = C.psf[pb], C.t_psf[pb]
        for k in range(16):
            P.op("pe", lambda e, ps=ps, k=k, i=i: e.matmul(ps, lhsT=hT[:, k, i * 128:(i + 1) * 128], rhs=wt_s[:, k, :], start=(k == 0), stop=(k == 15)),
                 reads=[t_wt_s, hT_toks[i]], writes=[tps], join=(k > 0))
        r0 = tok0 + i * 128
        g4 = i % 4
        if g4 == 0:
            gs = gcnt % 2; gcnt += 1
        P.op("act", lambda e, ps=ps, gs=gs, g4=g4: e.activation(out=gst[gs][:, g4, :], in_=ps, func=AF.Silu), reads=[tps], writes=[t_gst[gs]], join=True)
        if g4 == 3:
            rr = r0 - 3 * 128
            P.op("sync", lambda e, gs=gs, rr=rr, c0=c0: e.dma_start(out=C.G_d[rr:rr + 512, c0:c0 + 512].rearrange("(a p) c -> p a c", p=128), in_=gst[gs]),
                 reads=[t_gst[gs]], writes=[t_gd], dma=True, join=True)
    return pcnt, gcnt


def layer3_proj1(C, x_src, W):
    P, A = C.P, C.A
    m = A.mark()
    TB = 1024; TPB = TB // 128; NTB = S // TB
    hT = A.alloc([16, TB], BF16); hT_toks = P.toks(TPB, "hT")
    wt = [A.alloc([16, 512], BF16) for _ in range(2)]; t_wt = P.toks(2, "wt")
    wkp = A.alloc([16, 64], BF16); t_wkp = P.tok("wkp")
    cst = [A.alloc([4, TB], BF16) for _ in range(2)]; t_cst = P.toks(2, "cst")
    cf = [A.alloc([512], F32) for _ in range(4)]; t_cf = P.toks(4, "cf")
    sq = [A.alloc([512], F32) for _ in range(2)]; t_sq = P.toks(2, "sq")
    rs = A.alloc([512], F32); t_rs = P.tok("rs")
    tmp = [A.alloc([512], F32) for _ in range(3)]; t_tmp = P.toks(3, "tmp")
    kpst = A.alloc([TB], BF16); t_kpst = P.tok("kpst")
    cs = A.alloc([TB], F32); sn = A.alloc([TB], F32); t_cs = P.tok("cs")
    gst = [A.alloc([TB], F32) for _ in range(2)]; t_gst = P.toks(2, "gst")
    glat = A.alloc([2, 4], F32); gkp = A.alloc([1], F32)
    C.t_gd = P.tok("Gd"); t_cd = P.tok("CQd"); t_kd = P.tok("KPEd")
    tl = C.t_lconst
    for mm_ in range(4):
        load_col(C, glat[:, 0, mm_:mm_ + 1], W["d_q_lat_g"][0, mm_ * 128:(mm_ + 1) * 128], 128)
        load_col(C, glat[:, 1, mm_:mm_ + 1], W["d_kv_lat_g"][0, mm_ * 128:(mm_ + 1) * 128], 128)
    load_col(C, gkp, W["d_qk_g"][0, 1, 128:192], 64)
    wcnt = 0; pcnt = 0; gcnt = 0; scnt = 0
    for tb in range(NTB):
        tok0 = tb * TB
        phase_norm_T(C, x_src, W["norm_g"][C.layer, :], tok0, hT, hT_toks, TB)
        P.op("sync", lambda e, tok0=tok0: e.dma_start(out=cs[0:64, :], in_=C.rope_in[0, :, tok0:tok0 + TB]), writes=[t_cs], dma=True)
        P.op("sync", lambda e, tok0=tok0: e.dma_start(out=sn[0:64, :], in_=C.rope_in[1, :, tok0:tok0 + TB]), writes=[t_cs], dma=True, join=True)
        for kind in range(2):
            ws = wcnt % 2; wcnt += 1
            load_w_block(C, wt[ws], t_wt[ws], W["d_w_in"][0], kind * 512, 512)
            for tq in range(TB // 512):
                pss, tpss = C.psf[4 + C.auxcnt % 2], C.t_psf[4 + C.auxcnt % 2]
                C.auxcnt += 1
                for mm_ in range(4):
                    pb = pcnt % 4; pcnt += 1
                    ps, tps = C.psf[pb], C.t_psf[pb]
                    for k in range(16):
                        P.op("pe", lambda e, ps=ps, k=k, ws=ws, mm_=mm_, tq=tq: e.matmul(ps, lhsT=wt[ws][:, k, mm_ * 128:(mm_ + 1) * 128], rhs=hT[:, k, tq * 512:(tq + 1) * 512], start=(k == 0), stop=(k == 15)),
                             reads=[t_wt[ws]] + hT_toks[tq * 4:(tq + 1) * 4], writes=[tps], join=(k > 0))
                    P.op("act", lambda e, ps=ps, mm_=mm_: e.activation(out=cf[mm_], in_=ps, func=AF.Copy), reads=[tps], writes=[t_cf[mm_]])
                    ss = scnt % 2; scnt += 1
                    P.op("act", lambda e, ps=ps, ss=ss: e.activation(out=sq[ss], in_=ps, func=AF.Square), reads=[tps], writes=[t_sq[ss]])
                    P.op("pe", lambda e, pss=pss, ss=ss, mm_=mm_: e.matmul(pss, lhsT=C.ones128, rhs=sq[ss], start=(mm_ == 0), stop=(mm_ == 3)), reads=[t_sq[ss], C.t_const], writes=[tpss], join=(mm_ > 0))
                P.op("act", lambda e, pss=pss: e.activation(out=rs, in_=pss, func=AF.Ln, bias=C.epscol, scale=1.0 / 512), reads=[tpss, C.t_const], writes=[t_rs])
                P.op("act", lambda e: e.activation(out=rs, in_=rs, func=AF.Exp, scale=-0.5), reads=[t_rs], writes=[t_rs])
                for mm_ in range(4):
                    P.op("dve", lambda e, mm_=mm_, kind=kind, tq=tq: e.scalar_tensor_tensor(out=cst[kind][:, mm_, tq * 512:(tq + 1) * 512], in0=cf[mm_], scalar=glat[:, kind, mm_:mm_ + 1], in1=rs, op0=ALU.mult, op1=ALU.mult),
                         reads=[t_cf[mm_], t_rs, tl], writes=[t_cst[kind]], join=True)
            dd = C.CQ_d if kind == 0 else C.CKV_d
            for mm_ in range(4):
                P.op("sync", lambda e, dd=dd, mm_=mm_, kind=kind, tok0=tok0: e.dma_start(out=dd[mm_, :, tok0:tok0 + TB], in_=cst[kind][:, mm_, :]), reads=[t_cst[kind]], writes=[t_cd], dma=True, join=True)
        load_w_block(C, wkp, t_wkp, W["d_w_in"][0], 1024, 64)
        for tq in range(TB // 512):
            pb = pcnt % 4; pcnt += 1
            ps, tps = C.psf[pb], C.t_psf[pb]
            for k in range(16):
                P.op("pe", lambda e, ps=ps, k=k, tq=tq: e.matmul(ps[0:64, :], lhsT=wkp[:, k, 0:64], rhs=hT[:, k, tq * 512:(tq + 1) * 512], start=(k == 0), stop=(k == 15)),
                     reads=[t_wkp] + hT_toks[tq * 4:(tq + 1) * 4], writes=[tps], join=(k > 0))
            rope_epilogue(C, ps, tps, gkp, cs[0:64, tq * 512:(tq + 1) * 512], sn[0:64, tq * 512:(tq + 1) * 512], t_cs, kpst[0:64, tq * 512:(tq + 1) * 512], t_kpst, tmp, t_tmp, delay=0)
        P.op("sync", lambda e, tok0=tok0: e.dma_start(out=C.KPE_d[:, tok0:tok0 + TB], in_=kpst[0:64, :]), reads=[t_kpst], writes=[t_kd], dma=True, join=True)
        for cb in range(4):
            ws = wcnt % 2; wcnt += 1
            load_w_block(C, wt[ws], t_wt[ws], W["d_w_in"][0], 1088 + cb * 512, 512)
            pcnt, gcnt = gT_block(C, hT, hT_toks, wt[ws], t_wt[ws], tok0, cb * 4, TB, pcnt, gst, t_gst, gcnt)
    A.release(m)
    P.barrier()


def layer3_proj2(C, W):
    P, A = C.P, C.A
    m = A.mark()
    H = 16
    cq = A.alloc([4, TB], BF16); ckv = A.alloc([4, TB], BF16); t_cq = P.tok("cq"); t_ckv = P.tok("ckv")
    wuq = A.alloc([4, 3072], BF16); wkn = A.alloc([4, 2048], BF16); wv = A.alloc([4, 2048], BF16); t_w = P.tok("wup")
    qst = [A.alloc([TB], BF16) for _ in range(2)]; t_qst = P.toks(2, "qst")
    kst = [A.alloc([TB], BF16) for _ in range(2)]; t_kst = P.toks(2, "kst")
    qpst = [A.alloc([TB], BF16) for _ in range(2)]; t_qpst = P.toks(2, "qpst")
    NTMP = 4
    tmp = [[A.alloc([512], F32) for _ in range(3)] for _ in range(NTMP)]; t_tmp = [P.toks(3, "tmp") for _ in range(NTMP)]
    cs = A.alloc([TB], F32); sn = A.alloc([TB], F32); t_cs = P.tok("cs")
    vst = [A.alloc([8, 512], BF16) for _ in range(2)]; t_vst = P.toks(2, "vst")
    gqn = A.alloc([1], F32); gkn = A.alloc([1], F32); gqp = A.alloc([1], F32)
    t_qd = P.tok("QTd"); t_kd = P.tok("KTd"); t_qpd = P.tok("QPEd"); t_vd = P.tok("Vd")
    sc = 192.0 ** -0.5
    load_col(C, gqn, W["d_qk_g"][0, 0, 0:128], 128, sc)
    load_col(C, gkn, W["d_qk_g"][0, 1, 0:128], 128)
    load_col(C, gqp, W["d_qk_g"][0, 0, 128:192], 64, sc)
    wq_v = W["d_w_uq"][0].rearrange("(k p) c -> p k c", p=128)
    wkv_v = W["d_w_ukv"][0].rearrange("(k p) (h c) -> p k h c", p=128, c=256)
    for k in range(4):
        P.op("pool", lambda e, k=k: e.dma_start(out=wuq[:, k, :], in_=wq_v[:, k, :]), writes=[t_w], dma=True, join=True)
        P.op("pool", lambda e, k=k: e.dma_start(out=wkn[:, k, :].rearrange("p (h c) -> p h c", c=128), in_=wkv_v[:, k, :, 0:128]), writes=[t_w], dma=True, join=True)
        P.op("pool", lambda e, k=k: e.dma_start(out=wv[:, k, :].rearrange("p (h c) -> p h c", c=128), in_=wkv_v[:, k, :, 128:256]), writes=[t_w], dma=True, join=True)
    pcnt = 0; tcnt = 0; vcnt = 0
    for tb in range(NTB):
        tok0 = tb * TB
        for k in range(4):
            P.op("sync", lambda e, k=k, tok0=tok0: e.dma_start(out=cq[:, k, :], in_=C.CQ_d[k, :, tok0:tok0 + TB]), writes=[t_cq], dma=True, join=(k > 0))
            P.op("sync", lambda e, k=k, tok0=tok0: e.dma_start(out=ckv[:, k, :], in_=C.CKV_d[k, :, tok0:tok0 + TB]), writes=[t_ckv], dma=True, join=(k > 0))
        P.op("sync", lambda e, tok0=tok0: e.dma_start(out=cs[0:64, :], in_=C.rope_in[0, :, tok0:tok0 + TB]), writes=[t_cs], dma=True)
        P.op("sync", lambda e, tok0=tok0: e.dma_start(out=sn[0:64, :], in_=C.rope_in[1, :, tok0:tok0 + TB]), writes=[t_cs], dma=True, join=True)
        for h in range(H):
            hs = h % 2
            for which in range(3):
                for tq in range(TB // 512):
                    pb = pcnt % 4; pcnt += 1
                    ps, tps = C.psf[pb], C.t_psf[pb]
                    for k in range(4):
                        if which == 0:
                            lhs, rhs_, rt, M = wuq[:, k, h * 192:h * 192 + 128], cq[:, k, tq * 512:(tq + 1) * 512], t_cq, 128
                        elif which == 1:
                            lhs, rhs_, rt, M = wkn[:, k, h * 128:(h + 1) * 128], ckv[:, k, tq * 512:(tq + 1) * 512], t_ckv, 128
                        else:
                            lhs, rhs_, rt, M = wuq[:, k, h * 192 + 128:h * 192 + 192], cq[:, k, tq * 512:(tq + 1) * 512], t_cq, 64
                        P.op("pe", lambda e, ps=ps, k=k, lhs=lhs, rhs_=rhs_, M=M: e.matmul(ps[0:M, :], lhsT=lhs, rhs=rhs_, start=(k == 0), stop=(k == 3)),
                             reads=[t_w, rt], writes=[tps], join=(k > 0))
                    ts_ = tcnt % NTMP; tcnt += 1
                    if which == 0:
                        qk_norm_epilogue(C, ps, tps, gqn, qst[hs][:, tq * 512:(tq + 1) * 512], t_qst[hs], tmp[ts_], t_tmp[ts_], 128)
                    elif which == 1:
                        qk_norm_epilogue(C, ps, tps, gkn, kst[hs][:, tq * 512:(tq + 1) * 512], t_kst[hs], tmp[ts_], t_tmp[ts_], 128)
                    else:
                        rope_epilogue(C, ps, tps, gqp, cs[0:64, tq * 512:(tq + 1) * 512], sn[0:64, tq * 512:(tq + 1) * 512], t_cs, qpst[hs][0:64, tq * 512:(tq + 1) * 512], t_qpst[hs], tmp[ts_], t_tmp[ts_])
            def outs(h=h, hs=hs, tok0=tok0):
                P.op("sync", lambda e: e.dma_start(out=C.QT_d[h, :, tok0:tok0 + TB], in_=qst[hs]), reads=[t_qst[hs]], writes=[t_qd], dma=True, join=True)
                P.op("sync", lambda e: e.dma_start(out=C.KT_d[h, :, tok0:tok0 + TB], in_=kst[hs]), reads=[t_kst[hs]], writes=[t_kd], dma=True, join=True)
                P.op("sync", lambda e: e.dma_start(out=C.QPE_d[h, :, tok0:tok0 + TB], in_=qpst[hs][0:64, :]), reads=[t_qpst[hs]], writes=[t_qpd], dma=True, join=True)
            defer(C, 4, outs)
        flush(C)
        for n in range(4):
            for i in range(TPB):
                pb = pcnt % 4; pcnt += 1
                ps, tps = C.psf[pb], C.t_psf[pb]
                for k in range(4):
                    P.op("pe", lambda e, ps=ps, k=k, i=i, n=n: e.matmul(ps, lhsT=ckv[:, k, i * 128:(i + 1) * 128], rhs=wv[:, k, n * 512:(n + 1) * 512], start=(k == 0), stop=(k == 3)),
                         reads=[t_w, t_ckv], writes=[tps], join=(k > 0))
                g8 = i % 8
                if g8 == 0:
                    vs = vcnt % 2; vcnt += 1
                P.op("dve", lambda e, ps=ps, vs=vs, g8=g8: e.tensor_copy(out=vst[vs][:, g8, :], in_=ps), reads=[tps], writes=[t_vst[vs]], join=True)
                if g8 == 7:
                    rr = tok0 + (i - 7) * 128
                    P.op("sync", lambda e, vs=vs, rr=rr, n=n: e.dma_start(out=C.V_d[rr:rr + 1024, n * 512:(n + 1) * 512].rearrange("(a p) c -> p a c", p=128), in_=vst[vs]),
                         reads=[t_vst[vs]], writes=[t_vd], dma=True, join=True)
    A.release(m)
    P.barrier()


def layer2_all(C, x_src, W):
    P, A = C.P, C.A
    m = A.mark()
    TB = 1024; TPB = TB // 128; NTB = S // TB
    hT = A.alloc([16, TB], BF16); hT_toks = P.toks(TPB, "hT")
    wt = [A.alloc([16, 512], BF16) for _ in range(2)]; t_wt = P.toks(2, "wt")
    wrg = A.alloc([8, 2, 256], BF16); wig = A.alloc([8, 2, 256], BF16); t_wg = P.tok("wg")
    stage = A.alloc([128], F32); cols = A.alloc([8, 16], F32)
    halo = A.alloc([16, 3], F32); hprev = A.alloc([16], F32); t_halo = P.tok("halo"); t_hprev = P.tok("hprev")
    ubuf = [A.alloc([TB + 8], F32) for _ in range(2)]; t_ubuf = P.toks(2, "ubuf")
    xc = [A.alloc([TB], F32) for _ in range(2)]; t_xc = P.toks(2, "xc")
    xcb = [A.alloc([TB], BF16) for _ in range(2)]; t_xcb = P.toks(2, "xcb")
    sg = [A.alloc([TB], F32) for _ in range(2)]; t_sg = P.toks(2, "sg")
    rg = [A.alloc([TB], F32) for _ in range(2)]; t_rg = P.toks(2, "rg")
    ig = [A.alloc([TB], F32) for _ in range(2)]; t_ig = P.toks(2, "ig")
    abuf = A.alloc([TB], F32); a2buf = A.alloc([TB], F32); xinb = A.alloc([TB], F32); hh = A.alloc([TB], F32)
    t_a = P.tok("a"); t_a2 = P.tok("a2"); t_xin = P.tok("xin"); t_hh = P.tok("hh")
    yst = [A.alloc([TB], BF16) for _ in range(2)]; t_yst = P.toks(2, "yst")
    t_yd = P.tok("YTd")
    tl = C.t_lconst
    vecs = [W["c_conv_w"][0, 0], W["c_conv_w"][0, 1], W["c_conv_w"][0, 2], W["c_conv_w"][0, 3], W["c_conv_b"][0], W["c_b_rgate"][0], W["c_b_igate"][0], W["c_lambda"][0]]
    for v, vec in enumerate(vecs):
        P.op("sync", lambda e, v=v, vec=vec: e.dma_start(out=stage[v * 16:(v + 1) * 16, :], in_=vec.rearrange("(t p) -> t p", p=128)), writes=[tl], dma=True, join=True)
    ps0, tps0 = C.psf[4], C.t_psf[4]
    P.op("pe", lambda e: e.matmul(ps0[:, 0:128], lhsT=stage, rhs=C.identf, start=True, stop=True), reads=[tl, C.t_const], writes=[tps0])
    P.op("dve", lambda e: e.tensor_copy(out=cols, in_=ps0[:, 0:128].rearrange("p (v t) -> p v t", v=8)), reads=[tps0], writes=[tl])
    P.op("act", lambda e: e.activation(out=cols[:, 7, :], in_=cols[:, 7, :], func=AF.Exp, scale=-1.0), reads=[tl], writes=[tl])
    P.op("act", lambda e: e.activation(out=cols[:, 7, :], in_=cols[:, 7, :], func=AF.Ln, bias=1.0, scale=1.0), reads=[tl], writes=[tl])
    P.op("dve", lambda e: e.tensor_scalar(out=cols[:, 7, :], in0=cols[:, 7, :], scalar1=-8.0, scalar2=None, op0=ALU.mult), reads=[tl], writes=[tl])
    for n in range(8):
        P.op("pool", lambda e, n=n: e.dma_start(out=wrg[:, n], in_=W["c_w_rgate"][0, n].rearrange("(c p) e -> p c e", p=128)), writes=[t_wg], dma=True, join=True)
        P.op("pool", lambda e, n=n: e.dma_start(out=wig[:, n], in_=W["c_w_igate"][0, n].rearrange("(c p) e -> p c e", p=128)), writes=[t_wg], dma=True, join=True)
    wcnt = 0; pcnt = 0
    for tb in range(NTB):
        tok0 = tb * TB
        phase_norm_T(C, x_src, W["norm_g"][C.layer, :], tok0, hT, hT_toks, TB)
        for n in range(8):
            ws = wcnt % 2; wcnt += 1
            load_w_block(C, wt[ws][:, :, 0:256], t_wt[ws], W["c_w_in"][0], n * 256, 256)
            load_w_block(C, wt[ws][:, :, 256:512], t_wt[ws], W["c_w_in"][0], 2048 + n * 256, 256)
            for c in range(2):
                tile = n * 2 + c
                if tb == 0:
                    P.op("pool", lambda e, c=c: e.memset(ubuf[c][:, 0:3], 0.0), writes=[t_ubuf[c]])
                else:
                    P.op("pool", lambda e, c=c, tile=tile: e.tensor_copy(out=ubuf[c][:, 0:3], in_=halo[:, tile, :]), reads=[t_halo], writes=[t_ubuf[c]])
                for tq in range(TB // 512):
                    pb = pcnt % 4; pcnt += 1
                    ps, tps = C.psf[pb], C.t_psf[pb]
                    for k in range(16):
                        P.op("pe", lambda e, ps=ps, k=k, ws=ws, c=c, tq=tq: e.matmul(ps, lhsT=wt[ws][:, k, c * 128:(c + 1) * 128], rhs=hT[:, k, tq * 512:(tq + 1) * 512], start=(k == 0), stop=(k == 15)),
                             reads=[t_wt[ws]] + hT_toks[tq * 4:(tq + 1) * 4], writes=[tps], join=(k > 0))
                    P.op("act", lambda e, ps=ps, c=c, tq=tq: e.activation(out=ubuf[c][:, 3 + tq * 512:3 + (tq + 1) * 512], in_=ps, func=AF.Copy), reads=[tps], writes=[t_ubuf[c]], join=True)
                P.op("pool", lambda e, c=c, tile=tile: e.tensor_copy(out=halo[:, tile, :], in_=ubuf[c][:, TB:TB + 3]), reads=[t_ubuf[c]], writes=[t_halo], join=True)
                P.op("dve", lambda e, c=c, tile=tile: e.tensor_scalar(out=xc[c], in0=ubuf[c][:, 3:3 + TB], scalar1=cols[:, 3, tile:tile + 1], scalar2=cols[:, 4, tile:tile + 1], op0=ALU.mult, op1=ALU.add),
                     reads=[t_ubuf[c], tl], writes=[t_xc[c]])
                for tau in (2, 1, 0):
                    P.op("dve", lambda e, c=c, tile=tile, tau=tau: e.scalar_tensor_tensor(out=xc[c], in0=ubuf[c][:, tau:tau + TB], scalar=cols[:, tau, tile:tile + 1], in1=xc[c], op0=ALU.mult, op1=ALU.add),
                         reads=[t_ubuf[c], tl, t_xc[c]], writes=[t_xc[c]])
                P.op("pool", lambda e, c=c: e.tensor_copy(out=xcb[c], in_=xc[c]), reads=[t_xc[c]], writes=[t_xcb[c]])
                for tq in range(TB // 512):
                    pb = pcnt % 4; pcnt += 1
                    ps, tps = C.psf[pb], C.t_psf[pb]
                    for k in range(16):
                        P.op("pe", lambda e, ps=ps, k=k, ws=ws, c=c, tq=tq: e.matmul(ps, lhsT=wt[ws][:, k, 256 + c * 128:256 + (c + 1) * 128], rhs=hT[:, k, tq * 512:(tq + 1) * 512], start=(k == 0), stop=(k == 15)),
                             reads=[t_wt[ws]] + hT_toks[tq * 4:(tq + 1) * 4], writes=[tps], join=(k > 0))
                    P.op("act", lambda e, ps=ps, c=c, tq=tq: e.activation(out=sg[c][:, tq * 512:(tq + 1) * 512], in_=ps, func=AF.Silu), reads=[tps], writes=[t_sg[c]], join=True)
            for ce in range(2):
                tile = n * 2 + ce
                for (wg, bidx, dstb, tdst) in ((wrg, 5, rg, t_rg), (wig, 6, ig, t_ig)):
                    for tq in range(TB // 512):
                        pb = pcnt % 4; pcnt += 1
                        ps, tps = C.psf[pb], C.t_psf[pb]
                        for cc in range(2):
                            P.op("pe", lambda e, ps=ps, wg=wg, cc=cc, ce=ce, tq=tq, n=n: e.matmul(ps, lhsT=wg[:, n, cc, ce * 128:(ce + 1) * 128], rhs=xcb[cc][:, tq * 512:(tq + 1) * 512], start=(cc == 0), stop=(cc == 1)),
                                 reads=[t_wg, t_xcb[cc]], writes=[tps], join=(cc > 0))
                        P.op("act", lambda e, ps=ps, dstb=dstb, ce=ce, tq=tq, bidx=bidx, tile=tile: e.activation(out=dstb[ce][:, tq * 512:(tq + 1) * 512], in_=ps, func=AF.Sigmoid, bias=cols[:, bidx, tile:tile + 1], scale=1.0),
                             reads=[tps, tl], writes=[tdst[ce]], join=True)
                P.op("act", lambda e, ce=ce, tile=tile: e.activation(out=abuf, in_=rg[ce], func=AF.Exp, scale=cols[:, 7, tile:tile + 1]), reads=[t_rg[ce], tl], writes=[t_a])
                P.op("pool", lambda e: e.tensor_tensor(out=a2buf, in0=abuf, in1=abuf, op=ALU.mult), reads=[t_a], writes=[t_a2])
                P.op("act", lambda e: e.activation(out=a2buf, in_=a2buf, func=AF.Sqrt, bias=1.0, scale=-1.0), reads=[t_a2], writes=[t_a2])
                P.op("pool", lambda e, ce=ce: e.tensor_tensor(out=xinb, in0=ig[ce], in1=xc[ce], op=ALU.mult), reads=[t_ig[ce], t_xc[ce]], writes=[t_xin])
                P.op("dve", lambda e: e.tensor_tensor(out=xinb, in0=xinb, in1=a2buf, op=ALU.mult), reads=[t_xin, t_a2], writes=[t_xin])
                init = 0.0 if tb == 0 else hprev[:, tile:tile + 1]
                P.op("dve", lambda e, init=init: e.tensor_tensor_scan(out=hh, data0=abuf, data1=xinb, initial=init, op0=ALU.mult, op1=ALU.add), reads=[t_a, t_xin, t_hprev], writes=[t_hh])
                P.op("pool", lambda e, tile=tile: e.tensor_copy(out=hprev[:, tile:tile + 1], in_=hh[:, TB - 1:TB]), reads=[t_hh], writes=[t_hprev])
                P.op("pool", lambda e, ce=ce: e.tensor_tensor(out=yst[ce], in0=hh, in1=sg[ce], op=ALU.mult), reads=[t_hh, t_sg[ce]], writes=[t_yst[ce]])
                P.op("sync", lambda e, ce=ce, tile=tile, tok0=tok0: e.dma_start(out=C.YT_d[tile, :, tok0:tok0 + TB], in_=yst[ce]), reads=[t_yst[ce]], writes=[t_yd], dma=True, join=True)
    A.release(m)
    P.barrier()


def layer1_all(C, x_src, W):
    P, A = C.P, C.A
    m0 = A.mark()
    dec = A.alloc([8, 64], F32); t_dec = P.tok("dec")
    m = A.mark()
    TB = 1024; TPB = TB // 128; NTB = S // TB
    hT = A.alloc([16, TB], BF16); hT_toks = P.toks(TPB, "hT")
    wt = [A.alloc([16, 512], BF16) for _ in range(2)]; t_wt = P.toks(2, "wt")
    wlr = A.alloc([16, 16], BF16); t_wlr = P.tok("wlr")
    lrT = A.alloc([TB], F32); t_lrT = P.tok("lrT")
    wga = A.alloc([1024], F32)
    TU = A.alloc([128], F32); CI = A.alloc([2], F32)
    qst = [A.alloc([TB], BF16) for _ in range(2)]; t_qst = P.toks(2, "qst")
    ebuf = [A.alloc([512], F32) for _ in range(2)]; t_eb = P.toks(2, "ebuf")
    wbuf = [A.alloc([512], F32) for _ in range(2)]; t_wb = P.toks(2, "wbuf")
    kst = [A.alloc([8, 512], BF16) for _ in range(2)]; t_kst = P.toks(2, "kst")
    vst = [A.alloc([8, 512], BF16) for _ in range(2)]; t_vst = P.toks(2, "vst")
    gst = [A.alloc([4, 512], F32) for _ in range(2)]; t_gst = P.toks(2, "gst")
    C.t_gd = P.tok("Gd"); t_qd = P.tok("QTd"); t_kd = P.tok("KPd"); t_vd = P.tok("Vd")
    tl = C.t_lconst
    Wi = W["b_w_in"][0]
    P.op("pool", lambda e: e.memset(wga[0:32, :], 0.0), writes=[tl])
    P.op("sync", lambda e: e.dma_start(out=wga[0:16, :], in_=W["b_w_gate"][0]), writes=[tl], dma=True)
    P.op("sync", lambda e: e.dma_start(out=wga[16:17, :], in_=W["b_gate_bias"][0:1, :]), writes=[tl], dma=True, join=True)
    P.op("sync", lambda e: e.dma_start(out=TU, in_=C.TU_d), writes=[tl], dma=True, join=True)
    P.op("sync", lambda e: e.dma_start(out=CI, in_=C.CI_d), writes=[tl], dma=True, join=True)
    P.op("pool", lambda e: e.memset(lrT[0:32, :], 1.0), writes=[t_lrT])
    wcnt = 0; pcnt = 0; gcnt = 0; vcnt = 0; qcnt = 0; ecnt = 0; kcnt = 0
    for tb in range(NTB):
        tok0 = tb * TB
        phase_norm_T(C, x_src, W["norm_g"][C.layer, :], tok0, hT, hT_toks, TB)
        load_w_block(C, wlr, t_wlr, Wi, 6144, 16)
        for tq in range(TB // 512):
            pb = pcnt % 4; pcnt += 1
            ps, tps = C.psf[pb], C.t_psf[pb]
            for k in range(16):
                P.op("pe", lambda e, ps=ps, k=k, tq=tq: e.matmul(ps[0:16, :], lhsT=wlr[:, k, 0:16], rhs=hT[:, k, tq * 512:(tq + 1) * 512], start=(k == 0), stop=(k == 15)),
                     reads=[t_wlr] + hT_toks[tq * 4:(tq + 1) * 4], writes=[tps], join=(k > 0))
            P.op("act", lambda e, ps=ps, tq=tq: e.activation(out=lrT[0:16, tq * 512:(tq + 1) * 512], in_=ps[0:16, :], func=AF.Copy), reads=[tps], writes=[t_lrT], join=True)
        for cb in range(12):
            ws = wcnt % 2; wcnt += 1
            load_w_block(C, wt[ws], t_wt[ws], Wi, cb * 512, 512)
            if cb < 2:
                for mm_ in range(4):
                    qt = cb * 4 + mm_
                    qs = qcnt % 2; qcnt += 1
                    for tq in range(TB // 512):
                        pb = pcnt % 4; pcnt += 1
                        ps, tps = C.psf[pb], C.t_psf[pb]
                        for k in range(16):
                            P.op("pe", lambda e, ps=ps, k=k, ws=ws, mm_=mm_, tq=tq: e.matmul(ps, lhsT=wt[ws][:, k, mm_ * 128:(mm_ + 1) * 128], rhs=hT[:, k, tq * 512:(tq + 1) * 512], start=(k == 0), stop=(k == 15)),
                                 reads=[t_wt[ws]] + hT_toks[tq * 4:(tq + 1) * 4], writes=[tps], join=(k > 0))
                        P.op("act", lambda e, ps=ps, qs=qs, tq=tq: e.activation(out=qst[qs][:, tq * 512:(tq + 1) * 512], in_=ps, func=AF.Copy, scale=1.0 / 16.0), reads=[tps], writes=[t_qst[qs]], join=True)
                    P.op("sync", lambda e, qt=qt, qs=qs, tok0=tok0: e.dma_start(out=C.QT_d[qt, :, tok0:tok0 + TB], in_=qst[qs]), reads=[t_qst[qs]], writes=[t_qd], dma=True, join=True)
            elif cb < 4:
                kb = cb - 2
                ks = kcnt % 2; kcnt += 1
                for i in range(TPB):
                    pb = pcnt % 4; pcnt += 1
                    ps, tps = C.psf[pb], C.t_psf[pb]
                    for k in range(16):
                        P.op("pe", lambda e, ps=ps, k=k, ws=ws, i=i: e.matmul(ps, lhsT=hT[:, k, i * 128:(i + 1) * 128], rhs=wt[ws][:, k, :], start=(k == 0), stop=(k == 15)),
                             reads=[t_wt[ws], hT_toks[i]], writes=[tps], join=(k > 0))
                    es = ecnt % 2; ecnt += 1
                    pz, tpz = C.psf[4], C.t_psf[4]
                    P.op("pe", lambda e, pz=pz, i=i, kb=kb: e.matmul(pz, lhsT=lrT[0:32, i * 128:(i + 1) * 128], rhs=wga[0:32, kb * 512:(kb + 1) * 512], start=True, stop=True),
                         reads=[t_lrT, tl], writes=[tpz])
                    P.op("act", lambda e, pz=pz, es=es: e.activation(out=ebuf[es], in_=pz, func=AF.Exp, scale=-1.0), reads=[tpz], writes=[t_eb[es]])
                    P.op("act", lambda e, es=es: e.activation(out=ebuf[es], in_=ebuf[es], func=AF.Ln, bias=1.0, scale=1.0), reads=[t_eb[es]], writes=[t_eb[es]])
                    pr_, tpr = C.psf[5], C.t_psf[5]
                    P.op("pe", lambda e, pr_=pr_, es=es: e.matmul(pr_, lhsT=TU, rhs=ebuf[es], start=True, stop=True), reads=[t_eb[es], tl], writes=[tpr])
                    P.op("act", lambda e, pr_=pr_, es=es: e.activation(out=wbuf[es], in_=pr_, func=AF.Exp), reads=[tpr], writes=[t_wb[es]])
                    P.op("dve", lambda e, ps=ps, es=es, ks=ks, i=i: e.tensor_tensor(out=kst[ks][:, i, :], in0=ps, in1=wbuf[es], op=ALU.mult), reads=[tps, t_wb[es]], writes=[t_kst[ks]], join=True)
                    for dt in range(4):
                        P.op("pe", lambda e, pz=pz, es=es, dt=dt: e.matmul(pz[:, dt * 2:dt * 2 + 2], lhsT=ebuf[es][:, dt * 128:(dt + 1) * 128], rhs=CI, start=(dt == 0), stop=(dt == 3), skip_group_check=True),
                             reads=[t_eb[es], tl], writes=[tpz], join=(dt > 0))
                    ch0 = (tok0 + i * 128) // 64
                    P.op("act", lambda e, pz=pz, kb=kb, ch0=ch0: e.activation(out=dec[:, kb * 4:(kb + 1) * 4, ch0:ch0 + 2], in_=pz[:, 0:8].rearrange("p (a b) -> p a b", a=4), func=AF.Exp), reads=[tpz], writes=[t_dec], join=True)
                P.op("sync", lambda e, ks=ks, tok0=tok0, kb=kb: e.dma_start(out=C.KP_d[tok0:tok0 + TB, kb * 512:(kb + 1) * 512].rearrange("(a p) c -> p a c", p=128), in_=kst[ks]),
                     reads=[t_kst[ks]], writes=[t_kd], dma=True, join=True)
            elif cb < 8:
                c0 = (cb - 4) * 512
                vs = vcnt % 2; vcnt += 1
                for i in range(TPB):
                    pb = pcnt % 4; pcnt += 1
                    ps, tps = C.psf[pb], C.t_psf[pb]
                    for k in range(16):
                        P.op("pe", lambda e, ps=ps, k=k, ws=ws, i=i: e.matmul(ps, lhsT=hT[:, k, i * 128:(i + 1) * 128], rhs=wt[ws][:, k, :], start=(k == 0), stop=(k == 15)),
                             reads=[t_wt[ws], hT_toks[i]], writes=[tps], join=(k > 0))
                    P.op("dve", lambda e, ps=ps, vs=vs, i=i: e.tensor_copy(out=vst[vs][:, i, :], in_=ps), reads=[tps], writes=[t_vst[vs]], join=True)
                P.op("sync", lambda e, vs=vs, tok0=tok0, c0=c0: e.dma_start(out=C.V_d[tok0:tok0 + TB, c0:c0 + 512].rearrange("(a p) c -> p a c", p=128), in_=vst[vs]),
                     reads=[t_vst[vs]], writes=[t_vd], dma=True, join=True)
            else:
                pcnt, gcnt = g_block(C, hT, hT_toks, wt[ws], t_wt[ws], tok0, (cb - 8) * 512, pcnt, gst, t_gst, gcnt, TPB)
    A.release(m)
    P.barrier()
    m = A.mark()
    Kp = A.alloc([NT, 256], BF16); t_Kp = P.tok("Kp")
    Vh = A.alloc([NT, 512], BF16); t_Vh = P.tok("Vh")
    QTt = [A.alloc([S], BF16) for _ in range(2)]; t_QTt = P.tok("QTt")
    Sf = A.alloc([2, 512], F32); t_Sf = P.tok("Sf")
    Sb = [A.alloc([2, 512], BF16) for _ in range(2)]; t_Sb = P.toks(2, "Sb")
    NG = 4
    Gt = [A.alloc([2, 512], F32) for _ in range(NG)]; t_Gt = P.toks(NG, "Gt")
    yT = A.alloc([4, S], BF16); t_yT = P.tok("yT")
    ogb = A.alloc([512], F32)
    NE = 6
    of = [A.alloc([512], F32) for _ in range(NE)]; yb = [A.alloc([512], BF16) for _ in range(NE)]
    ssq = [A.alloc([1], F32) for _ in range(NE)]; junk = A.alloc([512], BF16)
    t_ss = P.toks(NE, "ss"); t_of = P.toks(NE, "of"); t_yb = P.toks(NE, "yb"); t_junk = P.tok("junk"); t_yd = P.tok("YTd")
    P.op("sync", lambda e: e.dma_start(out=ogb, in_=W["b_out_g"][0, :].partition_broadcast(128)), writes=[tl], dma=True)
    NCH = S // 64
    PO = [4, 5, 6]
    tp7, ttp7 = C.tpb[1], C.t_tpb[1]

    def emit_kv(hh, c):
        i, par = c // 2, c % 2
        pr0 = par * 64
        for dh in range(2):
            kv, tkv = C.psf[2 * (c % 2) + dh], C.t_psf[2 * (c % 2) + dh]
            P.op("pe", lambda e, kv=kv, dh=dh: e.matmul(kv, lhsT=Kp[pr0:pr0 + 64, i, dh * 128:(dh + 1) * 128], rhs=Vh[pr0:pr0 + 64, i, :], start=True, stop=True),
                 reads=[t_Kp, t_Vh], writes=[tkv])

    def emit_epiA(hh, c):
        i, par = c // 2, c % 2
        es = c % NE
        gs = i % NG
        po, tpo = C.psf[PO[c % 3]], C.t_psf[PO[c % 3]]
        P.op("act", lambda e: e.activation(out=junk[0:64, :], in_=po[0:64, :], func=AF.Square, accum_out=ssq[es][0:64, :]), reads=[tpo], writes=[t_junk, t_ss[es]])
        P.op("act", lambda e: e.activation(out=ssq[es][0:64, :], in_=ssq[es][0:64, :], func=AF.Ln, bias=C.epscol[0:64, :], scale=1.0 / 512), reads=[t_ss[es], C.t_const], writes=[t_ss[es]])
        P.op("act", lambda e: e.activation(out=ssq[es][0:64, :], in_=ssq[es][0:64, :], func=AF.Exp, scale=-0.5), reads=[t_ss[es]], writes=[t_ss[es]])
        P.op("dve", lambda e: e.scalar_tensor_tensor(out=of[es][0:64, :], in0=po[0:64, :], scalar=ssq[es][0:64, :], in1=ogb[0:64, :], op0=ALU.mult, op1=ALU.mult), reads=[tpo, t_ss[es], tl], writes=[t_of[es]])
        P.op("pool", lambda e: e.tensor_tensor(out=yb[es][0:64, :], in0=of[es][0:64, :], in1=Gt[gs][0:64, par, :], op=ALU.mult), reads=[t_of[es], t_Gt[gs]], writes=[t_yb[es]])

    def emit_epiB(hh, c):
        es = c % NE
        q4 = (c % 4) * 256
        for j in range(4):
            P.op("pe", lambda e, j=j: e.transpose(out=tp7[:, q4 + j * 64:q4 + (j + 1) * 64], in_=yb[es][0:64, j * 128:(j + 1) * 128], identity=C.ident[0:64, 0:64]), reads=[t_yb[es], C.t_const], writes=[ttp7], join=True)
        P.op("dve", lambda e: e.tensor_copy(out=yT[:, :, c * 64:(c + 1) * 64], in_=tp7[:, q4:q4 + 256].rearrange("p (a b) -> p a b", a=4)), reads=[ttp7], writes=[t_yT], join=True)

    DA, DB = 2, 4
    for hh in range(4):
        P.op("sync", lambda e, hh=hh: e.dma_start(out=Kp, in_=C.KP_d[:, hh * 256:(hh + 1) * 256].rearrange("(a p) c -> p a c", p=128)), writes=[t_Kp], dma=True)
        P.op("sync", lambda e, hh=hh: e.dma_start(out=Vh, in_=C.V_d[:, hh * 512:(hh + 1) * 512].rearrange("(a p) c -> p a c", p=128)), writes=[t_Vh], dma=True)
        for dh in range(2):
            P.op("sync", lambda e, hh=hh, dh=dh: e.dma_start(out=QTt[dh], in_=C.QT_d[2 * hh + dh]), writes=[t_QTt], dma=True, join=(dh > 0))
        P.op("pool", lambda e: e.memset(Sf, 0.0), writes=[t_Sf])
        emit_kv(hh, 0)
        for c in range(NCH):
            i, par = c // 2, c % 2
            if par == 0:
                gs = i % NG
                P.op("sync", lambda e, gs=gs, i=i, hh=hh: e.dma_start(out=Gt[gs][0:64, :, :], in_=C.G_d[i * 128:(i + 1) * 128, hh * 512:(hh + 1) * 512].rearrange("(par p) c -> p par c", p=64)), writes=[t_Gt[gs]], dma=True)
            sbs = c % 2
            for dh in range(2):
                kv, tkv = C.psf[2 * (c % 2) + dh], C.t_psf[2 * (c % 2) + dh]
                P.op("dve", lambda e, kv=kv, dh=dh, hh=hh, c=c: e.scalar_tensor_tensor(out=Sf[:, dh, :], in0=Sf[:, dh, :], scalar=dec[:, 2 * hh + dh, c:c + 1], in1=kv, op0=ALU.mult, op1=ALU.add),
                     reads=[t_Sf, t_dec, tkv], writes=[t_Sf])
                P.op("act", lambda e, dh=dh, sbs=sbs: e.activation(out=Sb[sbs][:, dh, :], in_=Sf[:, dh, :], func=AF.Copy), reads=[t_Sf], writes=[t_Sb[sbs]], join=(dh > 0))
            if c + 1 < NCH:
                emit_kv(hh, c + 1)
            po, tpo = C.psf[PO[c % 3]], C.t_psf[PO[c % 3]]
            for dh in range(2):
                P.op("pe", lambda e, po=po, dh=dh, c=c, sbs=sbs: e.matmul(po[0:64, :], lhsT=QTt[dh][:, c * 64:(c + 1) * 64], rhs=Sb[sbs][:, dh, :], start=(dh == 0), stop=(dh == 1)),
                     reads=[t_QTt, t_Sb[sbs]], writes=[tpo], join=(dh > 0))
            if c >= DA:
                emit_epiA(hh, c - DA)
            if c >= DB:
                emit_epiB(hh, c - DB)
        for c in range(NCH - DA, NCH):
            emit_epiA(hh, c)
        for c in range(NCH - DB, NCH):
            emit_epiB(hh, c)
        for j in range(4):
            P.op("sync", lambda e, hh=hh, j=j: e.dma_start(out=C.YT_d[hh * 4 + j], in_=yT[:, j, :]), reads=[t_yT], writes=[t_yd], dma=True, join=True)
    A.release(m0)
    P.barrier()


WSHAPES = {
    "norm_g": (4, 2048), "rel_bias": (32, 16),
    "a_w_in": (1, 2048, 8192), "a_qk_g": (1, 2, 64), "a_lambda": (1, 4, 64), "a_subln_g": (1, 128), "a_w_out": (1, 2048, 2048),
    "b_w_in": (1, 2048, 6160), "b_w_gate": (1, 16, 1024), "b_gate_bias": (1, 1024), "b_out_g": (1, 512), "b_w_out": (1, 2048, 2048),
    "c_w_in": (1, 2048, 4096), "c_conv_w": (1, 4, 2048), "c_conv_b": (1, 2048), "c_w_rgate": (1, 8, 256, 256), "c_b_rgate": (1, 2048),
    "c_w_igate": (1, 8, 256, 256), "c_b_igate": (1, 2048), "c_lambda": (1, 2048), "c_w_out": (1, 2048, 2048),
    "d_w_in": (1, 2048, 3136), "d_q_lat_g": (1, 512), "d_kv_lat_g": (1, 512), "d_w_uq": (1, 512, 3072), "d_w_ukv": (1, 512, 4096),
    "d_qk_g": (1, 2, 192), "d_w_out": (1, 2048, 2048),
}
LAYER_W = {
    0: ["norm_g", "rel_bias", "a_w_in", "a_qk_g", "a_lambda", "a_subln_g", "a_w_out"],
    1: ["norm_g", "b_w_in", "b_w_gate", "b_gate_bias", "b_out_g", "b_w_out"],
    2: ["norm_g", "c_w_in", "c_conv_w", "c_conv_b", "c_w_rgate", "c_b_rgate", "c_w_igate", "c_b_igate", "c_lambda", "c_w_out"],
    3: ["norm_g", "d_w_in", "d_q_lat_g", "d_kv_lat_g", "d_w_uq", "d_w_ukv", "d_qk_g", "d_w_out"],
}


def t5_bucket_np(rel):
    nb = 16; max_exact = 8
    ret = np.where(rel > 0, nb, 0)
    n = np.abs(rel)
    nf = np.maximum(n, 1).astype(np.float32)
    large = max_exact + (np.log(nf / max_exact) / math.log(128 / max_exact) * (nb - max_exact)).astype(np.int32)
    large = np.minimum(large, nb - 1)
    return ret + np.where(n < max_exact, n, large)


def bias_index_tiles():
    k = np.arange(128)[:, None]; q = np.arange(128)[None, :]
    idx = np.zeros((128, 2, 128), np.int64); msk = np.zeros((128, 2, 128), bool)
    idx[:, 0, :] = t5_bucket_np(k - q)
    msk[:, 0, :] = (k // 64) > (q // 64)
    idx[:, 1, :] = t5_bucket_np(k - q - 128)
    return idx, msk


def rope_tables():
    half = 32
    inv = (np.float32(10000.0) ** (-np.arange(half, dtype=np.float32) / np.float32(half))).astype(np.float32)
    ang = (np.arange(S, dtype=np.float32)[:, None] * inv[None, :]).astype(np.float32)
    c = np.cos(ang).astype(np.float32).T; s_ = np.sin(ang).astype(np.float32).T
    return np.ascontiguousarray(np.stack([np.concatenate([c, c], 0), np.concatenate([s_, s_], 0)], 0))


def build_program(layers, debug=False):
    nc = bass.Bass("TRN2", target_bir_lowering=False)
    C = Ctx()
    C.nc = nc
    x_in = dram(nc, "x", [S, D], F32, "ExternalInput")
    out = dram(nc, "out", [S, D], F32, "ExternalOutput")
    names = []
    for l in layers:
        for n in LAYER_W[l]:
            if n not in names:
                names.append(n)
    W = {n: dram(nc, n, WSHAPES[n], F32, "ExternalInput") for n in names}
    if 0 in layers:
        C.biasT_in = dram(nc, "biasT", [128, 16, 2, 128], F32, "ExternalInput")
    ident_d = nc.inline_tensor(np.eye(128, dtype=np.float32), "ident_c").ap()
    o64 = np.zeros((128, 128), np.float32); o64[:64, :64] = 1; o64[64:, 64:] = 1
    ones64_d = nc.inline_tensor(o64, "ones64_c").ap()
    ones128_d = nc.inline_tensor(np.ones((128, 128), np.float32), "ones128_c").ap()
    xs = [dram(nc, f"xs{i}", [S, D], F32) for i in range(2)]
    sk = "ExternalOutput" if debug else "Internal"
    C.QT_d = dram(nc, "QT_d", [16, 128, S], BF16, sk)
    C.KT_d = dram(nc, "KT_d", [16, 128, S], BF16, sk)
    C.V_d = dram(nc, "V_d", [S, D], BF16, sk)
    C.G_d = dram(nc, "G_d", [S, D], F32, sk)
    C.YT_d = dram(nc, "YT_d", [16, 128, S], BF16, sk)
    C.GT_d = dram(nc, "GT_d", [16, 128, S], F32, sk)
    if 3 in layers:
        C.QPE_d = dram(nc, "QPE_d", [16, 64, S], BF16, sk)
        C.KPE_d = dram(nc, "KPE_d", [64, S], BF16, sk)
        C.CQ_d = dram(nc, "CQ_d", [4, 128, S], BF16, sk)
        C.CKV_d = dram(nc, "CKV_d", [4, 128, S], BF16, sk)
        C.rope_in = dram(nc, "rope_cs", [2, 64, S], F32, "ExternalInput")
        kk = np.arange(128)[:, None]; qq = np.arange(128)[None, :]
        C.maskT_d = nc.inline_tensor(np.where((kk // 64) > (qq // 64), -30000.0, 0.0).astype(np.float32), "maskT_c").ap()
    if 1 in layers:
        C.KP_d = dram(nc, "KP_d", [S, 1024], BF16, sk)
        tt = np.arange(128)
        tu = np.where((tt[:, None] > tt[None, :]) & (tt[:, None] // 64 == tt[None, :] // 64), -1.0 / 16.0, 0.0).astype(np.float32)
        ci = np.where(tt[:, None] // 64 == np.arange(2)[None, :], -1.0 / 16.0, 0.0).astype(np.float32)
        C.TU_d = nc.inline_tensor(tu, "TU_c").ap()
        C.CI_d = nc.inline_tensor(ci, "CI_c").ap()
    rot = np.zeros((128, 128), np.float32)
    for i_ in range(32):
        rot[32 + i_, i_] = -1.0; rot[i_, 32 + i_] = 1.0
    rotm_d = nc.inline_tensor(rot, "rotm_c").ap()
    with ExitStack() as ctx:
        P = Prog(nc, ctx)
        C.P = P
        A = Arena(nc, ctx, 200)
        C.A = A
        C.psf = [ctx.enter_context(nc.psum_tensor(f"psf{i}", [128, 512], F32))[:] for i in range(8)]
        C.t_psf = P.toks(8, "psf")
        C.tpb = [C.psf[6].bitcast(BF16), C.psf[7].bitcast(BF16)]
        C.t_tpb = [C.t_psf[6], C.t_psf[7]]
        C.auxcnt = 0
        C.dq = []
        C.tickn = 0
        C.t_const = P.tok("const"); C.t_lconst = P.tok("lconst")
        C.ident = A.alloc([128], BF16); C.ones64 = A.alloc([128], F32); C.ones128 = A.alloc([128], F32); C.epscol = A.alloc([1], F32)
        C.rotm = A.alloc([128], F32); C.identf = A.alloc([128], F32)
        P.op("sync", lambda e: e.dma_start(out=C.identf, in_=ident_d), writes=[C.t_const], dma=True, join=True)
        P.op("sync", lambda e: e.dma_start(out=C.rotm, in_=rotm_d), writes=[C.t_const], dma=True, join=True)
        P.op("pool", lambda e: e.dma_start(out=C.ident, in_=ident_d), writes=[C.t_const], dma=True, join=True)
        P.op("sync", lambda e: e.dma_start(out=C.ones64, in_=ones64_d), writes=[C.t_const], dma=True, join=True)
        P.op("sync", lambda e: e.dma_start(out=C.ones128, in_=ones128_d), writes=[C.t_const], dma=True, join=True)
        P.op("dve", lambda e: e.memset(C.epscol, EPS), writes=[C.t_const], join=True)
        P.barrier()
        cur = x_in
        for li, l in enumerate(layers):
            C.layer = l
            dst = out if li == len(layers) - 1 else xs[li % 2]
            if l == 0:
                layer0_proj(C, cur, W)
                attn_phase(C, W, "diff")
                phase_outproj(C, C.YT_d, W["a_w_out"][0], cur, dst)
            elif l == 1:
                layer1_all(C, cur, W)
                phase_outproj(C, C.YT_d, W["b_w_out"][0], cur, dst)
            elif l == 2:
                layer2_all(C, cur, W)
                phase_outproj(C, C.YT_d, W["c_w_out"][0], cur, dst)
            elif l == 3:
                layer3_proj1(C, cur, W)
                layer3_proj2(C, W)
                attn_phase(C, W, "mla")
                phase_outproj(C, C.YT_d, W["d_w_out"][0], cur, dst)
            else:
                raise NotImplementedError
            cur = dst
        P.emit()
    C.names = names
    return nc, C


_CACHE = {}


def run_layers(layers, x, inputs, n_cores=4):
    key = tuple(layers)
    if key not in _CACHE:
        _CACHE[key] = build_program(layers)
    nc, C = _CACHE[key]
    shared = {n: np.ascontiguousarray(inputs[n], dtype=np.float32) for n in C.names}
    if 0 in layers:
        idx, msk = bias_index_tiles()
        rb = np.asarray(inputs["rel_bias"], np.float32)
        bt = rb[idx]
        bt = np.where(msk[..., None], np.float32(-30000.0), bt)
        shared["biasT"] = np.ascontiguousarray(bt.transpose(0, 3, 1, 2))
    if 3 in layers:
        shared["rope_cs"] = rope_tables()
    in_maps = [dict(shared, x=np.ascontiguousarray(x[b])) for b in range(n_cores)]
    res = run_bass_kernel_spmd(nc, in_maps, core_ids=list(range(n_cores)))
    C.last_res = res
    return np.stack([r["out"] for r in res.results], axis=0)


def kernel(**inputs):
    x = np.asarray(inputs["x"], np.float32)
    return run_layers([0, 1, 2, 3], x, inputs)
```

```python
import math
from contextlib import ExitStack
import numpy as np
import concourse.bass as bass
import concourse.mybir as mybir
from concourse.bass_utils import run_bass_kernel_spmd

F32 = mybir.dt.float32
BF16 = mybir.dt.bfloat16
AF = mybir.ActivationFunctionType
ALU = mybir.AluOpType
AX = mybir.AxisListType

ENGS = ("sync", "act", "pool", "dve", "pe")


class Tok:
    __slots__ = ("name", "w", "r", "pool")

    def __init__(self, name):
        self.name = name
        self.w = {}
        self.r = {}
        self.pool = {}


class Prog:
    def __init__(self, nc, ctx, same_engine_sync=("act", "dve", "pool")):
        self.nc = nc
        self.ctx = ctx
        self.q = {e: [] for e in ENGS}
        self.cnt = {e: 0 for e in ENGS}
        self.waited = {e: {} for e in ENGS}
        self.pend = {e: {} for e in ENGS}
        self.needed = set()
        self.same_sync = set(same_engine_sync)
        self.pool_val = []
        self.pool_free = {"hw": [], "sw": []}
        self.live = []
        self.all_toks = []

    def tok(self, name="t"):
        t = Tok(name)
        self.all_toks.append(t)
        return t

    def toks(self, n, name="t"):
        return [self.tok(f"{name}{i}") for i in range(n)]

    def op(self, eng, fn, reads=(), writes=(), dma=False, join=False):
        deps = dict(self.pend[eng])
        self.pend[eng] = {}

        def add(k, v):
            if deps.get(k, 0) < v:
                deps[k] = v

        for t in reads:
            for k, v in t.w.items():
                add(k, v)
        for t in writes:
            if not (join and not t.r):
                for k, v in t.w.items():
                    add(k, v)
            elif dma:
                for k, v in t.w.items():
                    if k[0] == "e":
                        add(k, v)
            for k, v in t.r.items():
                add(k, v)
        waits = []
        wd = self.waited[eng]
        for k, v in deps.items():
            if k == ("e", eng) and (eng not in self.same_sync or self.cnt[eng] + 1 - v >= 3):
                continue
            if wd.get(k, 0) < v:
                wd[k] = v
                waits.append((k, v))
                self.needed.add((k, v))
        if dma:
            assert len(writes) == 1
            t = writes[0]
            kind = "sw" if eng == "pool" else "hw"
            if kind not in t.pool:
                if self.pool_free[kind]:
                    t.pool[kind] = self.pool_free[kind].pop()
                else:
                    self.pool_val.append(0)
                    t.pool[kind] = len(self.pool_val) - 1
                self.live.append((t, kind))
            pi = t.pool[kind]
            self.pool_val[pi] += 16
            h = (("s", pi), self.pool_val[pi])
        else:
            self.cnt[eng] += 1
            h = (("e", eng), self.cnt[eng])
        k, v = h
        for t in reads:
            if t.r.get(k, 0) < v:
                t.r[k] = v
        for t in writes:
            if join and not t.r:
                t.w[k] = v
            else:
                t.w = {k: v}
                t.r = {}
        self.q[eng].append((waits, fn, h, dma))
        return h

    def barrier(self):
        hs = {}
        for e in ENGS:
            if self.cnt[e] > 0:
                hs[("e", e)] = self.cnt[e]
        for i, v in enumerate(self.pool_val):
            if v > 0:
                hs[("s", i)] = v
        for e in ENGS:
            for k, v in hs.items():
                if self.pend[e].get(k, 0) < v:
                    self.pend[e][k] = v
        for t, kind in self.live:
            self.pool_free[kind].append(t.pool[kind])
        self.live = []
        for t in self.all_toks:
            t.w = {}
            t.r = {}
            t.pool = {}

    def emit(self):
        nc = self.nc
        self.barrier()
        fw = []
        wd = self.waited["sync"]
        for k, v in self.pend["sync"].items():
            if wd.get(k, 0) < v:
                fw.append((k, v))
                self.needed.add((k, v))
        esem = {e: self.ctx.enter_context(nc.semaphore(f"sem_{e}")) for e in ENGS}
        psem = [self.ctx.enter_context(nc.semaphore(f"dsem{i}")) for i in range(len(self.pool_val))]
        rank = {}
        for e in ENGS:
            r = 0
            for (_w, _f, h, dma) in self.q[e]:
                if not dma and h in self.needed:
                    r += 1
                    rank[h] = r

        def resolve(h):
            k, v = h
            if k[0] == "e":
                return esem[k[1]], rank[h]
            return psem[k[1]], v

        block = self.ctx.enter_context(nc.Block())
        names = {"sync": "sync", "act": "scalar", "pool": "gpsimd", "dve": "vector", "pe": "tensor"}
        for e in ENGS:
            ops = self.q[e]
            if not ops and e != "sync":
                continue

            def body(eng, ops=ops, e=e):
                for (waits, fn, h, dma) in ops:
                    for w in waits:
                        s, v = resolve(w)
                        eng.wait_ge(s, v)
                    ins = fn(eng)
                    if dma:
                        s, v = resolve(h)
                        ins.then_inc(s, 16)
                    elif h in self.needed:
                        s, v = resolve(h)
                        ins.then_inc(s, 1)
                if e == "sync":
                    for w in fw:
                        s, v = resolve(w)
                        eng.wait_ge(s, v)

            getattr(block, names[e])(body)
        self.n_sems = len(psem) + 5


class Arena:
    def __init__(self, nc, ctx, kib):
        self.n32 = kib * 256
        self.t = ctx.enter_context(nc.sbuf_tensor("arena", [128, self.n32], F32))
        self.off = 0

    def mark(self):
        return self.off

    def release(self, m):
        self.off = m

    def alloc(self, shape, dt, parts=128):
        n = int(np.prod(shape))
        n32 = n if dt == F32 else (n + 1) // 2
        n32 = (n32 + 7) // 8 * 8
        assert self.off + n32 <= self.n32, f"arena overflow {self.off + n32} > {self.n32}"
        v = self.t[0:parts, self.off:self.off + n32]
        self.off += n32
        if dt != F32:
            v = v.bitcast(dt)
        v = v[:, 0:n]
        if len(shape) == 2:
            v = v.rearrange("p (a b) -> p a b", a=shape[0])
        elif len(shape) == 3:
            v = v.rearrange("p (a b c) -> p a b c", a=shape[0], b=shape[1])
        return v


S = 4096
D = 2048
EPS = 1e-6
NT = S // 128
TB = 2048
NTB = S // TB
TPB = TB // 128


class Ctx:
    pass


def dram(nc, name, shape, dt, kind="Internal"):
    return nc.dram_tensor(name, list(shape), dt, kind=kind).ap()


def load_w_block(C, dst, dtok, wsrc, c0, ncols, kc=16, rows0=0):
    P = C.P
    wv = wsrc[rows0:rows0 + kc * 128, :].rearrange("(k p) c -> p k c", p=128)
    step = 4 if kc >= 4 else kc
    for k0 in range(0, kc, step):
        k1 = min(kc, k0 + step)
        P.op("pool", lambda e, k0=k0, k1=k1: e.dma_start(out=dst[:, k0:k1, 0:ncols], in_=wv[:, k0:k1, c0:c0 + ncols]),
             writes=[dtok], dma=True, join=True)


class WPrefetch:
    def __init__(self, C, wt, t_wt, loaders):
        self.C, self.wt, self.t_wt, self.loaders = C, wt, t_wt, loaders
        self.k = 0
        self._issue(0)

    def _issue(self, k):
        if k < len(self.loaders):
            self.loaders[k](self.wt[k % 2], self.t_wt[k % 2])

    def next(self):
        ws = self.k % 2
        self._issue(self.k + 1)
        self.k += 1
        return ws


def phase_norm_T(C, x_src, g_row, tok0, hT, hT_toks, tbsz=TB):
    P, A = C.P, C.A
    m = A.mark()
    xin = [A.alloc([D], F32) for _ in range(2)]
    hb = [A.alloc([D], BF16) for _ in range(2)]
    gb = A.alloc([D], F32)
    ssq = [A.alloc([1], F32) for _ in range(2)]
    t_xin = P.toks(2, "xin"); t_hb = P.toks(2, "hb"); t_gb = P.tok("gb"); t_ssq = P.toks(2, "ssq")
    P.op("sync", lambda e: e.dma_start(out=gb, in_=g_row.partition_broadcast(128)), writes=[t_gb], dma=True)
    for i in range(tbsz // 128):
        s = i % 2
        r0 = tok0 + i * 128
        P.op("sync", lambda e, s=s, r0=r0: e.dma_start(out=xin[s], in_=x_src[r0:r0 + 128, :]), writes=[t_xin[s]], dma=True)
        P.op("act", lambda e, s=s: e.activation(out=hb[s], in_=xin[s], func=AF.Square, accum_out=ssq[s]),
             reads=[t_xin[s]], writes=[t_hb[s], t_ssq[s]])
        P.op("act", lambda e, s=s: e.activation(out=ssq[s], in_=ssq[s], func=AF.Ln, bias=C.epscol, scale=1.0 / D),
             reads=[t_ssq[s], C.t_const], writes=[t_ssq[s]])
        P.op("act", lambda e, s=s: e.activation(out=ssq[s], in_=ssq[s], func=AF.Exp, scale=-0.5), reads=[t_ssq[s]], writes=[t_ssq[s]])
        P.op("dve", lambda e, s=s: e.scalar_tensor_tensor(out=hb[s], in0=xin[s], scalar=ssq[s], in1=gb, op0=ALU.mult, op1=ALU.mult),
             reads=[t_xin[s], t_ssq[s], t_gb], writes=[t_hb[s]])
        for half in range(2):
            tp, ttp = C.tpb[half], C.t_tpb[half]
            for j in range(8):
                k = half * 8 + j
                P.op("pe", lambda e, s=s, j=j, k=k, tp=tp: e.transpose(out=tp[:, j * 128:(j + 1) * 128], in_=hb[s][:, k * 128:(k + 1) * 128], identity=C.ident),
                     reads=[t_hb[s], C.t_const], writes=[ttp], join=True)
            eng = "act" if half == 0 else "dve"
            dst = hT[:, half * 8:(half + 1) * 8, i * 128:(i + 1) * 128]
            srcv = tp.rearrange("p (a b) -> p a b", a=8)
            if eng == "act":
                P.op("act", lambda e, dst=dst, srcv=srcv: e.activation(out=dst, in_=srcv, func=AF.Copy), reads=[ttp], writes=[hT_toks[i]], join=True)
            else:
                P.op("dve", lambda e, dst=dst, srcv=srcv: e.tensor_copy(out=dst, in_=srcv), reads=[ttp], writes=[hT_toks[i]], join=True)
    A.release(m)


def phase_outproj(C, yT_d, w_out, x_src, x_dst):
    P, A = C.P, C.A
    m = A.mark()
    wo = A.alloc([16, D], BF16)
    yT = A.alloc([16, TB], BF16)
    xin = [A.alloc([D], F32) for _ in range(2)]
    xo = [A.alloc([D], F32) for _ in range(2)]
    t_wo = P.tok("wo"); t_yT = P.toks(16, "yT"); t_xin = P.toks(2, "xin"); t_xo = P.toks(2, "xo"); t_dst = P.tok("xdst")
    for n in range(4):
        load_w_block(C, wo[:, :, n * 512:(n + 1) * 512], t_wo, w_out, n * 512, 512)
    cnt = 0
    for tb in range(NTB):
        tok0 = tb * TB
        for k in range(16):
            P.op("sync", lambda e, k=k, tok0=tok0: e.dma_start(out=yT[:, k, :], in_=yT_d[k, :, tok0:tok0 + TB]), writes=[t_yT[k]], dma=True)
        for i in range(TPB):
            s = i % 2
            r0 = tok0 + i * 128
            P.op("sync", lambda e, s=s, r0=r0: e.dma_start(out=xin[s], in_=x_src[r0:r0 + 128, :]), writes=[t_xin[s]], dma=True)
            for n in range(4):
                pb = cnt % 4; cnt += 1
                ps, tps = C.psf[pb], C.t_psf[pb]
                for k in range(16):
                    P.op("pe", lambda e, ps=ps, k=k, i=i, n=n: e.matmul(ps, lhsT=yT[:, k, i * 128:(i + 1) * 128], rhs=wo[:, k, n * 512:(n + 1) * 512], start=(k == 0), stop=(k == 15)),
                         reads=[t_yT[k], t_wo], writes=[tps], join=(k > 0))
                P.op("dve", lambda e, ps=ps, s=s, n=n: e.tensor_tensor(out=xo[s][:, n * 512:(n + 1) * 512], in0=ps, in1=xin[s][:, n * 512:(n + 1) * 512], op=ALU.add),
                     reads=[tps, t_xin[s]], writes=[t_xo[s]], join=(n > 0))
            P.op("sync", lambda e, s=s, r0=r0: e.dma_start(out=x_dst[r0:r0 + 128, :], in_=xo[s]), reads=[t_xo[s]], writes=[t_dst], dma=True)
    A.release(m)
    P.barrier()


def defer(C, delay, fn):
    due = C.tickn + delay
    if C.dq and C.dq[-1][0] > due:
        due = C.dq[-1][0]
    if delay <= 0 and not C.dq:
        fn()
    else:
        C.dq.append((due, fn))


def tick(C):
    C.tickn += 1
    while C.dq and C.dq[0][0] <= C.tickn:
        C.dq.pop(0)[1]()


def flush(C):
    while C.dq:
        C.dq.pop(0)[1]()


def qk_norm_epilogue(C, ps, tps, gcol, dst, t_dst, tmp, t_tmp, grp, delay=2):
    P = C.P
    qf, sq, rs = tmp
    ones = C.ones64 if grp == 64 else C.ones128
    tick(C)
    P.op("dve", lambda e: e.tensor_copy(out=qf, in_=ps), reads=[tps], writes=[t_tmp[0]])
    P.op("pool", lambda e: e.tensor_tensor(out=sq, in0=qf, in1=qf, op=ALU.mult), reads=[t_tmp[0]], writes=[t_tmp[1]])

    def stage2():
        pss, tpss = C.psf[4 + C.auxcnt % 2], C.t_psf[4 + C.auxcnt % 2]
        C.auxcnt += 1
        P.op("pe", lambda e: e.matmul(pss, lhsT=ones, rhs=sq, start=True, stop=True), reads=[t_tmp[1], C.t_const], writes=[tpss])
        P.op("act", lambda e: e.activation(out=rs, in_=pss, func=AF.Ln, bias=C.epscol, scale=1.0 / grp), reads=[tpss, C.t_const], writes=[t_tmp[2]])
        P.op("act", lambda e: e.activation(out=rs, in_=rs, func=AF.Exp, scale=-0.5), reads=[t_tmp[2]], writes=[t_tmp[2]])
        P.op("dve", lambda e: e.scalar_tensor_tensor(out=dst, in0=qf, scalar=gcol, in1=rs, op0=ALU.mult, op1=ALU.mult),
             reads=[t_tmp[0], t_tmp[2], C.t_lconst], writes=[t_dst], join=True)
    defer(C, delay, stage2)


def gT_block(C, hT, hT_toks, wt_s, t_wt_s, tok0, h0, tbsz, pcnt, gst, t_gst, gcnt):
    P = C.P
    for mm_ in range(4):
        gs = gcnt % 2; gcnt += 1
        for tq in range(tbsz // 512):
            pb = pcnt % 4; pcnt += 1
            ps, tps = C.psf[pb], C.t_psf[pb]
            for k in range(16):
                P.op("pe", lambda e, ps=ps, k=k, mm_=mm_, tq=tq: e.matmul(ps, lhsT=wt_s[:, k, mm_ * 128:(mm_ + 1) * 128], rhs=hT[:, k, tq * 512:(tq + 1) * 512], start=(k == 0), stop=(k == 15)),
                     reads=[t_wt_s] + hT_toks[tq * 4:(tq + 1) * 4], writes=[tps], join=(k > 0))
            P.op("act", lambda e, ps=ps, gs=gs, tq=tq: e.activation(out=gst[gs][:, tq * 512:(tq + 1) * 512], in_=ps, func=AF.Silu), reads=[tps], writes=[t_gst[gs]], join=True)
        P.op("sync", lambda e, gs=gs, mm_=mm_: e.dma_start(out=C.GT_d[h0 + mm_, :, tok0:tok0 + tbsz], in_=gst[gs][:, 0:tbsz]), reads=[t_gst[gs]], writes=[C.t_gd], dma=True, join=True)
    return pcnt, gcnt


def layer0_proj(C, x_src, W):
    P, A = C.P, C.A
    m = A.mark()
    hT = A.alloc([16, TB], BF16)
    hT_toks = P.toks(TPB, "hT")
    wt = [A.alloc([16, 512], BF16) for _ in range(2)]
    t_wt = P.toks(2, "wt")
    qst = [A.alloc([TB], BF16) for _ in range(2)]
    t_qst = P.toks(2, "qst")
    tmp = [[A.alloc([512], F32) for _ in range(3)] for _ in range(2)]
    t_tmp = [P.toks(3, "tmp") for _ in range(2)]
    vst = [A.alloc([8, 512], BF16) for _ in range(2)]
    t_vst = P.toks(2, "vst")
    gst = [A.alloc([TB], F32) for _ in range(2)]
    t_gst = P.toks(2, "gst")
    t_qd = P.tok("QTd"); t_kd = P.tok("KTd"); t_vd = P.tok("Vd"); C.t_gd = P.tok("Gd")
    gq = A.alloc([1], F32); gk = A.alloc([1], F32)
    for half in range(2):
        P.op("sync", lambda e, half=half: e.dma_start(out=gq[half * 64:(half + 1) * 64, :], in_=W["a_qk_g"][0, 0, :].rearrange("(d o) -> d o", o=1)), writes=[C.t_lconst], dma=True, join=True)
        P.op("sync", lambda e, half=half: e.dma_start(out=gk[half * 64:(half + 1) * 64, :], in_=W["a_qk_g"][0, 1, :].rearrange("(d o) -> d o", o=1)), writes=[C.t_lconst], dma=True, join=True)
    P.op("dve", lambda e: e.tensor_scalar(out=gq, in0=gq, scalar1=0.125, scalar2=None, op0=ALU.mult), reads=[C.t_lconst], writes=[C.t_lconst])
    wcnt = 0; qcnt = 0; tcnt = 0; pcnt = 0; vcnt = 0; gcnt = 0
    pf = WPrefetch(C, wt, t_wt, [(lambda d, t, cb=cb: load_w_block(C, d, t, W["a_w_in"][0], cb * 512, 512)) for _tb in range(NTB) for cb in range(16)])
    for tb in range(NTB):
        tok0 = tb * TB
        phase_norm_T(C, x_src, W["norm_g"][C.layer, :], tok0, hT, hT_toks)
        for cb in range(16):
            ws = pf.next()
            kind = cb // 4
            if kind < 2:
                for mm_ in range(4):
                    h = (cb % 4) * 4 + mm_
                    qs = qcnt % 2; qcnt += 1
                    for tq in range(TB // 512):
                        pb = pcnt % 4; pcnt += 1
                        ps, tps = C.psf[pb], C.t_psf[pb]
                        for k in range(16):
                            P.op("pe", lambda e, ps=ps, k=k, ws=ws, mm_=mm_, tq=tq: e.matmul(ps, lhsT=wt[ws][:, k, mm_ * 128:(mm_ + 1) * 128], rhs=hT[:, k, tq * 512:(tq + 1) * 512], start=(k == 0), stop=(k == 15)),
                                 reads=[t_wt[ws]] + hT_toks[tq * 4:(tq + 1) * 4], writes=[tps], join=(k > 0))
                        ts_ = tcnt % 2; tcnt += 1
                        qk_norm_epilogue(C, ps, tps, gq if kind == 0 else gk, qst[qs][:, tq * 512:(tq + 1) * 512], t_qst[qs], tmp[ts_], t_tmp[ts_], 64)
                    dd, td = (C.QT_d, t_qd) if kind == 0 else (C.KT_d, t_kd)
                    defer(C, 2, lambda dd=dd, td=td, h=h, qs=qs, tok0=tok0: P.op("sync", lambda e: e.dma_start(out=dd[h, :, tok0:tok0 + TB], in_=qst[qs]), reads=[t_qst[qs]], writes=[td], dma=True, join=True))
                flush(C)
            elif kind == 3:
                pcnt, gcnt = gT_block(C, hT, hT_toks, wt[ws], t_wt[ws], tok0, (cb % 4) * 4, TB, pcnt, gst, t_gst, gcnt)
            else:
                c0 = (cb % 4) * 512
                for i in range(TPB):
                    pb = pcnt % 4; pcnt += 1
                    ps, tps = C.psf[pb], C.t_psf[pb]
                    for k in range(16):
                        P.op("pe", lambda e, ps=ps, k=k, ws=ws, i=i: e.matmul(ps, lhsT=hT[:, k, i * 128:(i + 1) * 128], rhs=wt[ws][:, k, :], start=(k == 0), stop=(k == 15)),
                             reads=[t_wt[ws], hT_toks[i]], writes=[tps], join=(k > 0))
                    r0 = tok0 + i * 128
                    if True:
                        g8 = i % 8
                        if g8 == 0:
                            vs = vcnt % 2; vcnt += 1
                        P.op("dve", lambda e, ps=ps, vs=vs, g8=g8: e.tensor_copy(out=vst[vs][:, g8, :], in_=ps), reads=[tps], writes=[t_vst[vs]], join=True)
                        if g8 == 7:
                            rr = r0 - 7 * 128
                            P.op("sync", lambda e, vs=vs, rr=rr, c0=c0: e.dma_start(out=C.V_d[rr:rr + 1024, c0:c0 + 512].rearrange("(a p) c -> p a c", p=128), in_=vst[vs]),
                                 reads=[t_vst[vs]], writes=[t_vd], dma=True, join=True)
    A.release(m)
    P.barrier()


def attn_phase(C, W, mode):
    P, A = C.P, C.A
    m = A.mark()
    H = 16
    diff = (mode == "diff")
    nsub = 2 if diff else 1
    lam_init = 0.8 - 0.6 * math.exp(-0.3 * C.layer)
    QT = [A.alloc([S], BF16) for _ in range(2)]
    KT = [A.alloc([S], BF16) for _ in range(2)]
    Vh = [A.alloc([NT, 128], BF16) for _ in range(2)]
    Gh = [A.alloc([S], F32) for _ in range(2)]
    yT = [A.alloc([S], BF16) for _ in range(2)]
    LA = 4 if diff else 3
    NPT = LA + 2
    PT = [A.alloc([512], BF16) for _ in range(NPT)]
    onesb = A.alloc([128], BF16)
    t_QT = P.toks(2, "QT"); t_KT = P.toks(2, "KT"); t_Vh = P.toks(2, "Vh"); t_Gh = P.toks(2, "Gh"); t_yT = P.toks(2, "yT"); t_PT = P.toks(NPT, "PT")
    t_yd = P.tok("YTd")
    Osb = [[A.alloc([512], F32) for _ in range(2)] for _ in range(2)]; Dsb = [[A.alloc([512], F32) for _ in range(2)] for _ in range(2)]
    sqb = [A.alloc([512], F32) for _ in range(2)]; rsb = [A.alloc([512], F32) for _ in range(2)]
    t_Osb = [P.toks(2, "Osb") for _ in range(2)]; t_Dsb = [P.toks(2, "Dsb") for _ in range(2)]; t_sqb = P.toks(2, "sqb"); t_rsb = P.toks(2, "rsb")
    tl = C.t_lconst
    P.op("pool", lambda e: e.memset(onesb, 1.0), writes=[tl])
    if diff:
        biasT = A.alloc([H, 2, 128], F32)
        b15 = A.alloc([H], F32)
        lam4 = A.alloc([4, 64], F32)
        lamc = A.alloc([4], F32)
        sgcol = A.alloc([1], F32)
        P.op("sync", lambda e: e.dma_start(out=biasT, in_=C.biasT_in), writes=[tl], dma=True, join=True)
        P.op("sync", lambda e: e.dma_start(out=b15, in_=W["rel_bias"][15, :].partition_broadcast(128)), writes=[tl], dma=True, join=True)
        P.op("sync", lambda e: e.dma_start(out=lam4, in_=W["a_lambda"][0].rearrange("a d -> (a d)").partition_broadcast(128).rearrange("p (a d) -> p a d", a=4)), writes=[tl], dma=True, join=True)
        load_col(C, sgcol, W["a_subln_g"][0, :], 128, 1.0 - lam_init)
        biasB = A.alloc([H, 2, 128], BF16)
        for hh in range(H):
            P.op("dve", lambda e, hh=hh: e.tensor_scalar(out=biasT[:, hh], in0=biasT[:, hh], scalar1=b15[:, hh:hh + 1], scalar2=None, op0=ALU.subtract), reads=[tl], writes=[tl])
        P.op("dve", lambda e: e.tensor_copy(out=biasB, in_=biasT), reads=[tl], writes=[tl])
        P.op("dve", lambda e: e.tensor_tensor(out=lam4[:, 0, :], in0=lam4[:, 0, :], in1=lam4[:, 1, :], op=ALU.mult), reads=[tl], writes=[tl])
        P.op("dve", lambda e: e.tensor_tensor(out=lam4[:, 2, :], in0=lam4[:, 2, :], in1=lam4[:, 3, :], op=ALU.mult), reads=[tl], writes=[tl])
        P.op("dve", lambda e: e.reduce_sum(out=lamc[:, 0:1], in_=lam4[:, 0, :], axis=AX.X), reads=[tl], writes=[tl])
        P.op("dve", lambda e: e.reduce_sum(out=lamc[:, 1:2], in_=lam4[:, 2, :], axis=AX.X), reads=[tl], writes=[tl])
        P.op("act", lambda e: e.activation(out=lamc[:, 0:2], in_=lamc[:, 0:2], func=AF.Exp), reads=[tl], writes=[tl])
        P.op("dve", lambda e: e.scalar_tensor_tensor(out=lamc[:, 0:1], in0=lamc[:, 1:2], scalar=-lam_init, in1=lamc[:, 0:1], op0=ALU.add, op1=ALU.subtract), reads=[tl], writes=[tl])
    else:
        maskT = A.alloc([128], F32)
        QP = [A.alloc([S], BF16) for _ in range(2)]
        KP = A.alloc([S], BF16)
        t_QP = P.toks(2, "QP"); t_KP = P.tok("KP")
        P.op("sync", lambda e: e.dma_start(out=maskT, in_=C.maskT_d), writes=[tl], dma=True)
        maskB = A.alloc([128], BF16)
        P.op("dve", lambda e: e.tensor_copy(out=maskB, in_=maskT), reads=[tl], writes=[tl])
        P.op("sync", lambda e: e.dma_start(out=KP[0:64, :], in_=C.KPE_d), writes=[t_KP], dma=True)
    SB = [0, 1, 2, 3, 7] if diff else [0, 1, 6, 7]
    NSB = len(SB)

    def banks(qg, t):
        b0 = 4 if diff else 2 + 2 * (qg % 2)
        return b0, b0 + 1

    def emit_loads(h):
        hs = h % 2
        P.op("sync", lambda e: e.dma_start(out=QT[hs], in_=C.QT_d[h]), writes=[t_QT[hs]], dma=True)
        P.op("sync", lambda e: e.dma_start(out=KT[hs], in_=C.KT_d[h]), writes=[t_KT[hs]], dma=True)
        if not diff:
            P.op("sync", lambda e: e.dma_start(out=QP[hs][0:64, :], in_=C.QPE_d[h]), writes=[t_QP[hs]], dma=True)
        P.op("sync", lambda e: e.dma_start(out=Vh[hs], in_=C.V_d[:, h * 128:(h + 1) * 128].rearrange("(a p) c -> p a c", p=128)), writes=[t_Vh[hs]], dma=True)
        P.op("sync", lambda e: e.dma_start(out=Gh[hs], in_=C.GT_d[h]), writes=[t_Gh[hs]], dma=True)

    def emit_S(n, h, qg, t, i):
        hs = h % 2
        jmin = max(0, i - 4 * qg)
        Sp, tSp = C.psf[SB[n % NSB]], C.t_psf[SB[n % NSB]]
        q0 = (4 * qg + jmin) * 128
        ncol = (4 - jmin) * 128
        c0 = jmin * 128
        near = []
        for rel in ((0, 1) if diff else (0,)):
            jj = i - 4 * qg + rel
            if 0 <= jj <= 3 and jj >= jmin:
                near.append((jj, biasB[:, h, rel, :] if diff else maskB))
        nn = len(near)
        if diff:
            P.op("pe", lambda e: e.matmul(Sp[:, c0:c0 + ncol], lhsT=KT[hs][t * 64:(t + 1) * 64, i * 128:(i + 1) * 128], rhs=QT[hs][t * 64:(t + 1) * 64, q0:q0 + ncol], start=True, stop=(nn == 0), skip_group_check=True),
                 reads=[t_KT[hs], t_QT[hs]], writes=[tSp])
        else:
            P.op("pe", lambda e: e.matmul(Sp[:, c0:c0 + ncol], lhsT=KT[hs][:, i * 128:(i + 1) * 128], rhs=QT[hs][:, q0:q0 + ncol], start=True, stop=False, skip_group_check=True),
                 reads=[t_KT[hs], t_QT[hs]], writes=[tSp])
            P.op("pe", lambda e: e.matmul(Sp[:, c0:c0 + ncol], lhsT=KP[0:64, i * 128:(i + 1) * 128], rhs=QP[hs][0:64, q0:q0 + ncol], start=False, stop=(nn == 0), skip_group_check=True),
                 reads=[t_KP, t_QP[hs]], writes=[tSp], join=True)
        for bi, (jj, btile) in enumerate(near):
            P.op("pe", lambda e, jj=jj, btile=btile, bi=bi: e.matmul(Sp[:, jj * 128:(jj + 1) * 128], lhsT=C.ident, rhs=btile, start=False, stop=(bi == nn - 1), skip_group_check=True),
                 reads=[tl, C.t_const], writes=[tSp], join=True)
        pp = n % NPT
        ebias = 0.0
        P.op("act", lambda e: e.activation(out=PT[pp][:, c0:c0 + ncol], in_=Sp[:, c0:c0 + ncol], func=AF.Exp, bias=ebias, scale=1.0),
             reads=[tSp, tl], writes=[t_PT[pp]])

    def emit_PV(n, h, qg, t, i):
        hs = h % 2
        jmin = max(0, i - 4 * qg)
        c0 = jmin * 128
        pp = n % NPT
        bo, bd = banks(qg, t)
        last = (i == 4 * qg + 3)
        P.op("pe", lambda e: e.matmul(C.psf[bo][:, c0:512], lhsT=Vh[hs][:, i, :], rhs=PT[pp][:, c0:512], start=(i == 0), stop=last),
             reads=[t_PT[pp], t_Vh[hs]], writes=[C.t_psf[bo]], join=(i > 0))
        P.op("pe", lambda e: e.matmul(C.psf[bd][:, c0:512], lhsT=onesb, rhs=PT[pp][:, c0:512], start=(i == 0), stop=last),
             reads=[t_PT[pp], tl], writes=[C.t_psf[bd]], join=(i > 0))

    pending = []
    deferred = []
    cur_n = [0]
    gser = [0]

    def emit_evac(qg, t):
        par = qg % 2
        bo, bd = banks(qg, t)
        P.op("dve", lambda e: e.tensor_copy(out=Osb[par][t], in_=C.psf[bo]), reads=[C.t_psf[bo]], writes=[t_Osb[par][t]])
        P.op("dve", lambda e: e.tensor_copy(out=Dsb[par][t], in_=C.psf[bd]), reads=[C.t_psf[bd]], writes=[t_Dsb[par][t]])

    def emit_epilogue(n, h, qg):
        hs = h % 2
        par = qg % 2
        ysl = yT[hs][:, qg * 512:(qg + 1) * 512]
        gsl = Gh[hs][:, qg * 512:(qg + 1) * 512]
        O_, D_, tO, tD = Osb[par], Dsb[par], t_Osb[par], t_Dsb[par]
        ts = (0, 1) if diff else (0,)
        last = (qg == NT // 4 - 1)
        deferred.append(lambda: emit_epilogue2(n, h, qg))

    def emit_epilogue2(n, h, qg):
        hs = h % 2
        par = qg % 2
        ysl = yT[hs][:, qg * 512:(qg + 1) * 512]
        gsl = Gh[hs][:, qg * 512:(qg + 1) * 512]
        O_, D_, tO, tD = Osb[par], Dsb[par], t_Osb[par], t_Dsb[par]
        ts = (0, 1) if diff else (0,)
        last = (qg == NT // 4 - 1)
        for t in ts:
            P.op("dve", lambda e, t=t: e.reciprocal(out=D_[t], in_=D_[t]), reads=[tD[t]], writes=[tD[t]])
        if diff:
            for t in (1, 0):
                P.op("dve", lambda e, t=t: e.tensor_tensor(out=O_[t], in0=O_[t], in1=D_[t], op=ALU.mult), reads=[tO[t], tD[t]], writes=[tO[t]])
            P.op("dve", lambda e: e.scalar_tensor_tensor(out=O_[0], in0=O_[1], scalar=lamc[:, 0:1], in1=O_[0], op0=ALU.mult, op1=ALU.add), reads=[tO[0], tO[1], tl], writes=[tO[0]])
            P.op("pool", lambda e: e.tensor_tensor(out=sqb[par], in0=O_[0], in1=O_[0], op=ALU.mult), reads=[tO[0]], writes=[t_sqb[par]])

            def stage2():
                P.op("pe", lambda e: e.matmul(C.psf[6], lhsT=C.ones128, rhs=sqb[par], start=True, stop=True), reads=[t_sqb[par], C.t_const], writes=[C.t_psf[6]])
                P.op("act", lambda e: e.activation(out=rsb[par], in_=C.psf[6], func=AF.Ln, bias=C.epscol, scale=1.0 / 128), reads=[C.t_psf[6], C.t_const], writes=[t_rsb[par]])
                P.op("act", lambda e: e.activation(out=rsb[par], in_=rsb[par], func=AF.Exp, scale=-0.5), reads=[t_rsb[par]], writes=[t_rsb[par]])
                P.op("dve", lambda e: e.scalar_tensor_tensor(out=O_[0], in0=O_[0], scalar=sgcol, in1=rsb[par], op0=ALU.mult, op1=ALU.mult), reads=[tO[0], t_rsb[par], tl], writes=[tO[0]])
                P.op("pool", lambda e: e.tensor_tensor(out=ysl, in0=O_[0], in1=gsl, op=ALU.mult), reads=[tO[0], t_Gh[hs]], writes=[t_yT[hs]], join=True)
                if last:
                    P.op("sync", lambda e: e.dma_start(out=C.YT_d[h], in_=yT[hs]), reads=[t_yT[hs]], writes=[t_yd], dma=True, join=True)
            pending.append((cur_n[0] + 16, stage2, gser[0] - 1))
        else:
            P.op("pool", lambda e: e.tensor_tensor(out=O_[0], in0=O_[0], in1=D_[0], op=ALU.mult), reads=[tO[0], tD[0]], writes=[tO[0]])
            P.op("pool", lambda e: e.tensor_tensor(out=ysl, in0=O_[0], in1=gsl, op=ALU.mult), reads=[tO[0], t_Gh[hs]], writes=[t_yT[hs]], join=True)
            if last:
                P.op("sync", lambda e: e.dma_start(out=C.YT_d[h], in_=yT[hs]), reads=[t_yT[hs]], writes=[t_yd], dma=True, join=True)

    tiles = [(h, qg, t, i) for h in range(H) for qg in range(NT // 4) for t in range(nsub) for i in range(4 * qg + 4)]

    def emit_S_at(n):
        h_, qg_, t_, i_ = tiles[n]
        if qg_ == 0 and t_ == 0 and i_ == 0:
            emit_loads(h_)
        emit_S(n, h_, qg_, t_, i_)

    for n in range(min(LA, len(tiles))):
        emit_S_at(n)
    for n, (h, qg, t, i) in enumerate(tiles):
        if n + LA < len(tiles):
            emit_S_at(n + LA)
        emit_PV(n, h, qg, t, i)
        while pending and pending[0][0] <= n:
            pending.pop(0)[1]()
        cur_n[0] = n
        if i == 4 * qg + 3:
            if t == 0:
                while pending and pending[0][2] <= gser[0] - 2:
                    pending.pop(0)[1]()
            emit_evac(qg, t)
            while deferred:
                deferred.pop(0)()
            if t == nsub - 1:
                emit_epilogue(n, h, qg)
                gser[0] += 1
    while deferred:
        deferred.pop(0)()
    while pending:
        pending.pop(0)[1]()
    A.release(m)
    P.barrier()


def load_col(C, dst, src_vec, n, scale=None):
    P = C.P
    P.op("sync", lambda e: e.dma_start(out=dst[0:n, :], in_=src_vec.rearrange("(d o) -> d o", o=1)), writes=[C.t_lconst], dma=True, join=True)
    if scale is not None:
        P.op("dve", lambda e: e.tensor_scalar(out=dst[0:n, :], in0=dst[0:n, :], scalar1=float(scale), scalar2=None, op0=ALU.mult), reads=[C.t_lconst], writes=[C.t_lconst])


def rope_epilogue(C, ps, tps, gcol, cs, sn, t_cs, dst, t_dst, tmp, t_tmp, delay=2):
    P = C.P
    xf, sq, rs = tmp
    tick(C)
    P.op("dve", lambda e: e.tensor_copy(out=xf[0:64, :], in_=ps[0:64, :]), reads=[tps], writes=[t_tmp[0]])
    P.op("pool", lambda e: e.tensor_tensor(out=sq[0:64, :], in0=xf[0:64, :], in1=xf[0:64, :], op=ALU.mult), reads=[t_tmp[0]], writes=[t_tmp[1]])

    def stage2():
        pss, tpss = C.psf[4 + C.auxcnt % 2], C.t_psf[4 + C.auxcnt % 2]
        C.auxcnt += 1
        P.op("pe", lambda e: e.matmul(pss[0:64, :], lhsT=C.ones64[0:64, 0:64], rhs=sq[0:64, :], start=True, stop=True), reads=[t_tmp[1], C.t_const], writes=[tpss])
        P.op("act", lambda e: e.activation(out=rs[0:64, :], in_=pss[0:64, :], func=AF.Ln, bias=C.epscol[0:64, :], scale=1.0 / 64), reads=[tpss, C.t_const], writes=[t_tmp[2]])
        P.op("act", lambda e: e.activation(out=rs[0:64, :], in_=rs[0:64, :], func=AF.Exp, scale=-0.5), reads=[t_tmp[2]], writes=[t_tmp[2]])
        P.op("dve", lambda e: e.scalar_tensor_tensor(out=xf[0:64, :], in0=xf[0:64, :], scalar=gcol[0:64, :], in1=rs[0:64, :], op0=ALU.mult, op1=ALU.mult),
             reads=[t_tmp[0], t_tmp[2], C.t_lconst], writes=[t_tmp[0]])

    def stage3():
        pr, tpr = C.psf[4 + C.auxcnt % 2], C.t_psf[4 + C.auxcnt % 2]
        C.auxcnt += 1
        P.op("pe", lambda e: e.matmul(pr[0:64, :], lhsT=C.rotm[0:64, 0:64], rhs=xf[0:64, :], start=True, stop=True), reads=[t_tmp[0], C.t_const], writes=[tpr])
        P.op("dve", lambda e: e.tensor_tensor(out=sq[0:64, :], in0=pr[0:64, :], in1=sn, op=ALU.mult), reads=[tpr, t_cs], writes=[t_tmp[1]])
        P.op("pool", lambda e: e.tensor_tensor(out=xf[0:64, :], in0=xf[0:64, :], in1=cs, op=ALU.mult), reads=[t_tmp[0], t_cs], writes=[t_tmp[0]])
        P.op("pool", lambda e: e.tensor_tensor(out=dst, in0=xf[0:64, :], in1=sq[0:64, :], op=ALU.add), reads=[t_tmp[0], t_tmp[1]], writes=[t_dst], join=True)
    defer(C, delay, stage2)
    defer(C, 2 * delay, stage3)


def g_block(C, hT, hT_toks, wt_s, t_wt_s, tok0, c0, pcnt, gst, t_gst, gcnt, tpb=TPB):
    P = C.P
    t_gd = C.t_gd
    for i in range(tpb):
        pb = pcnt % 4; pcnt += 1
        ps, tps = C.psf[pb], C.t_psf[pb]
        for k in range(16):
            P.op("pe", lambda e, ps=ps, k=k, i=i: e.matmul(ps, lhsT=hT[:, k, i * 128:(i + 1) * 128], rhs=wt_s[:, k, :], start=(k == 0), stop=(k == 15)),
                 reads=[t_wt_s, hT_toks[i]], writes=[tps], join=(k > 0))
        r0 = tok0 + i * 128
        g4 = i % 4
        if g4 == 0:
            gs = gcnt % 2; gcnt += 1
        P.op("act", lambda e, ps=ps, gs=gs, g4=g4: e.activation(out=gst[gs][:, g4, :], in_=ps, func=AF.Silu), reads=[tps], writes=[t_gst[gs]], join=True)
        if g4 == 3:
            rr = r0 - 3 * 128
            P.op("sync", lambda e, gs=gs, rr=rr, c0=c0: e.dma_start(out=C.G_d[rr:rr + 512, c0:c0 + 512].rearrange("(a p) c -> p a c", p=128), in_=gst[gs]),
                 reads=[t_gst[gs]], writes=[t_gd], dma=True, join=True)
    return pcnt, gcnt


def layer3_proj1(C, x_src, W):
    P, A = C.P, C.A
    m = A.mark()
    TB = 1024; TPB = TB // 128; NTB = S // TB
    hT = A.alloc([16, TB], BF16); hT_toks = P.toks(TPB, "hT")
    wt = [A.alloc([16, 512], BF16) for _ in range(2)]; t_wt = P.toks(2, "wt")
    wkp = A.alloc([16, 64], BF16); t_wkp = P.tok("wkp")
    cst = [A.alloc([4, TB], BF16) for _ in range(2)]; t_cst = P.toks(2, "cst")
    cf = [A.alloc([512], F32) for _ in range(4)]; t_cf = P.toks(4, "cf")
    sq = [A.alloc([512], F32) for _ in range(2)]; t_sq = P.toks(2, "sq")
    rs = A.alloc([512], F32); t_rs = P.tok("rs")
    tmp = [A.alloc([512], F32) for _ in range(3)]; t_tmp = P.toks(3, "tmp")
    kpst = A.alloc([TB], BF16); t_kpst = P.tok("kpst")
    cs = A.alloc([TB], F32); sn = A.alloc([TB], F32); t_cs = P.tok("cs")
    gst = [A.alloc([TB], F32) for _ in range(2)]; t_gst = P.toks(2, "gst")
    glat = A.alloc([2, 4], F32); gkp = A.alloc([1], F32)
    C.t_gd = P.tok("Gd"); t_cd = P.tok("CQd"); t_kd = P.tok("KPEd")
    tl = C.t_lconst
    for mm_ in range(4):
        load_col(C, glat[:, 0, mm_:mm_ + 1], W["d_q_lat_g"][0, mm_ * 128:(mm_ + 1) * 128], 128)
        load_col(C, glat[:, 1, mm_:mm_ + 1], W["d_kv_lat_g"][0, mm_ * 128:(mm_ + 1) * 128], 128)
    load_col(C, gkp, W["d_qk_g"][0, 1, 128:192], 64)
    wcnt = 0; pcnt = 0; gcnt = 0; scnt = 0
    cols3 = [0, 512, 1088, 1600, 2112, 2624]
    pf = WPrefetch(C, wt, t_wt, [(lambda d, t, c0=c0: load_w_block(C, d, t, W["d_w_in"][0], c0, 512)) for _tb in range(NTB) for c0 in cols3])
    for tb in range(NTB):
        tok0 = tb * TB
        phase_norm_T(C, x_src, W["norm_g"][C.layer, :], tok0, hT, hT_toks, TB)
        P.op("sync", lambda e, tok0=tok0: e.dma_start(out=cs[0:64, :], in_=C.rope_in[0, :, tok0:tok0 + TB]), writes=[t_cs], dma=True)
        P.op("sync", lambda e, tok0=tok0: e.dma_start(out=sn[0:64, :], in_=C.rope_in[1, :, tok0:tok0 + TB]), writes=[t_cs], dma=True, join=True)
        for kind in range(2):
            ws = pf.next()
            for tq in range(TB // 512):
                pss, tpss = C.psf[4 + C.auxcnt % 2], C.t_psf[4 + C.auxcnt % 2]
                C.auxcnt += 1
                for mm_ in range(4):
                    pb = pcnt % 4; pcnt += 1
                    ps, tps = C.psf[pb], C.t_psf[pb]
                    for k in range(16):
                        P.op("pe", lambda e, ps=ps, k=k, ws=ws, mm_=mm_, tq=tq: e.matmul(ps, lhsT=wt[ws][:, k, mm_ * 128:(mm_ + 1) * 128], rhs=hT[:, k, tq * 512:(tq + 1) * 512], start=(k == 0), stop=(k == 15)),
                             reads=[t_wt[ws]] + hT_toks[tq * 4:(tq + 1) * 4], writes=[tps], join=(k > 0))
                    P.op("act", lambda e, ps=ps, mm_=mm_: e.activation(out=cf[mm_], in_=ps, func=AF.Copy), reads=[tps], writes=[t_cf[mm_]])
                    ss = scnt % 2; scnt += 1
                    P.op("act", lambda e, ps=ps, ss=ss: e.activation(out=sq[ss], in_=ps, func=AF.Square), reads=[tps], writes=[t_sq[ss]])
                    P.op("pe", lambda e, pss=pss, ss=ss, mm_=mm_: e.matmul(pss, lhsT=C.ones128, rhs=sq[ss], start=(mm_ == 0), stop=(mm_ == 3)), reads=[t_sq[ss], C.t_const], writes=[tpss], join=(mm_ > 0))
                P.op("act", lambda e, pss=pss: e.activation(out=rs, in_=pss, func=AF.Ln, bias=C.epscol, scale=1.0 / 512), reads=[tpss, C.t_const], writes=[t_rs])
                P.op("act", lambda e: e.activation(out=rs, in_=rs, func=AF.Exp, scale=-0.5), reads=[t_rs], writes=[t_rs])
                for mm_ in range(4):
                    P.op("dve", lambda e, mm_=mm_, kind=kind, tq=tq: e.scalar_tensor_tensor(out=cst[kind][:, mm_, tq * 512:(tq + 1) * 512], in0=cf[mm_], scalar=glat[:, kind, mm_:mm_ + 1], in1=rs, op0=ALU.mult, op1=ALU.mult),
                         reads=[t_cf[mm_], t_rs, tl], writes=[t_cst[kind]], join=True)
            dd = C.CQ_d if kind == 0 else C.CKV_d
            for mm_ in range(4):
                P.op("sync", lambda e, dd=dd, mm_=mm_, kind=kind, tok0=tok0: e.dma_start(out=dd[mm_, :, tok0:tok0 + TB], in_=cst[kind][:, mm_, :]), reads=[t_cst[kind]], writes=[t_cd], dma=True, join=True)
        load_w_block(C, wkp, t_wkp, W["d_w_in"][0], 1024, 64)
        for tq in range(TB // 512):
            pb = pcnt % 4; pcnt += 1
            ps, tps = C.psf[pb], C.t_psf[pb]
            for k in range(16):
                P.op("pe", lambda e, ps=ps, k=k, tq=tq: e.matmul(ps[0:64, :], lhsT=wkp[:, k, 0:64], rhs=hT[:, k, tq * 512:(tq + 1) * 512], start=(k == 0), stop=(k == 15)),
                     reads=[t_wkp] + hT_toks[tq * 4:(tq + 1) * 4], writes=[tps], join=(k > 0))
            rope_epilogue(C, ps, tps, gkp, cs[0:64, tq * 512:(tq + 1) * 512], sn[0:64, tq * 512:(tq + 1) * 512], t_cs, kpst[0:64, tq * 512:(tq + 1) * 512], t_kpst, tmp, t_tmp, delay=0)
        P.op("sync", lambda e, tok0=tok0: e.dma_start(out=C.KPE_d[:, tok0:tok0 + TB], in_=kpst[0:64, :]), reads=[t_kpst], writes=[t_kd], dma=True, join=True)
        for cb in range(4):
            ws = pf.next()
            pcnt, gcnt = gT_block(C, hT, hT_toks, wt[ws], t_wt[ws], tok0, cb * 4, TB, pcnt, gst, t_gst, gcnt)
    A.release(m)
    P.barrier()


def layer3_proj2(C, W):
    P, A = C.P, C.A
    m = A.mark()
    H = 16
    cq = A.alloc([4, TB], BF16); ckv = A.alloc([4, TB], BF16); t_cq = P.tok("cq"); t_ckv = P.tok("ckv")
    wuq = A.alloc([4, 3072], BF16); wkn = A.alloc([4, 2048], BF16); wv = A.alloc([4, 2048], BF16); t_w = P.tok("wup")
    qst = [A.alloc([TB], BF16) for _ in range(2)]; t_qst = P.toks(2, "qst")
    kst = [A.alloc([TB], BF16) for _ in range(2)]; t_kst = P.toks(2, "kst")
    qpst = [A.alloc([TB], BF16) for _ in range(2)]; t_qpst = P.toks(2, "qpst")
    NTMP = 4
    tmp = [[A.alloc([512], F32) for _ in range(3)] for _ in range(NTMP)]; t_tmp = [P.toks(3, "tmp") for _ in range(NTMP)]
    cs = A.alloc([TB], F32); sn = A.alloc([TB], F32); t_cs = P.tok("cs")
    vst = [A.alloc([8, 512], BF16) for _ in range(2)]; t_vst = P.toks(2, "vst")
    gqn = A.alloc([1], F32); gkn = A.alloc([1], F32); gqp = A.alloc([1], F32)
    t_qd = P.tok("QTd"); t_kd = P.tok("KTd"); t_qpd = P.tok("QPEd"); t_vd = P.tok("Vd")
    sc = 192.0 ** -0.5
    load_col(C, gqn, W["d_qk_g"][0, 0, 0:128], 128, sc)
    load_col(C, gkn, W["d_qk_g"][0, 1, 0:128], 128)
    load_col(C, gqp, W["d_qk_g"][0, 0, 128:192], 64, sc)
    wq_v = W["d_w_uq"][0].rearrange("(k p) c -> p k c", p=128)
    wkv_v = W["d_w_ukv"][0].rearrange("(k p) (h c) -> p k h c", p=128, c=256)
    for k in range(4):
        P.op("pool", lambda e, k=k: e.dma_start(out=wuq[:, k, :], in_=wq_v[:, k, :]), writes=[t_w], dma=True, join=True)
        P.op("pool", lambda e, k=k: e.dma_start(out=wkn[:, k, :].rearrange("p (h c) -> p h c", c=128), in_=wkv_v[:, k, :, 0:128]), writes=[t_w], dma=True, join=True)
        P.op("pool", lambda e, k=k: e.dma_start(out=wv[:, k, :].rearrange("p (h c) -> p h c", c=128), in_=wkv_v[:, k, :, 128:256]), writes=[t_w], dma=True, join=True)
    pcnt = 0; tcnt = 0; vcnt = 0
    for tb in range(NTB):
        tok0 = tb * TB
        for k in range(4):
            P.op("sync", lambda e, k=k, tok0=tok0: e.dma_start(out=cq[:, k, :], in_=C.CQ_d[k, :, tok0:tok0 + TB]), writes=[t_cq], dma=True, join=(k > 0))
            P.op("sync", lambda e, k=k, tok0=tok0: e.dma_start(out=ckv[:, k, :], in_=C.CKV_d[k, :, tok0:tok0 + TB]), writes=[t_ckv], dma=True, join=(k > 0))
        P.op("sync", lambda e, tok0=tok0: e.dma_start(out=cs[0:64, :], in_=C.rope_in[0, :, tok0:tok0 + TB]), writes=[t_cs], dma=True)
        P.op("sync", lambda e, tok0=tok0: e.dma_start(out=sn[0:64, :], in_=C.rope_in[1, :, tok0:tok0 + TB]), writes=[t_cs], dma=True, join=True)
        for h in range(H):
            hs = h % 2
            for which in range(3):
                for tq in range(TB // 512):
                    pb = pcnt % 4; pcnt += 1
                    ps, tps = C.psf[pb], C.t_psf[pb]
                    for k in range(4):
                        if which == 0:
                            lhs, rhs_, rt, M = wuq[:, k, h * 192:h * 192 + 128], cq[:, k, tq * 512:(tq + 1) * 512], t_cq, 128
                        elif which == 1:
                            lhs, rhs_, rt, M = wkn[:, k, h * 128:(h + 1) * 128], ckv[:, k, tq * 512:(tq + 1) * 512], t_ckv, 128
                        else:
                            lhs, rhs_, rt, M = wuq[:, k, h * 192 + 128:h * 192 + 192], cq[:, k, tq * 512:(tq + 1) * 512], t_cq, 64
                        P.op("pe", lambda e, ps=ps, k=k, lhs=lhs, rhs_=rhs_, M=M: e.matmul(ps[0:M, :], lhsT=lhs, rhs=rhs_, start=(k == 0), stop=(k == 3)),
                             reads=[t_w, rt], writes=[tps], join=(k > 0))
                    ts_ = tcnt % NTMP; tcnt += 1
                    if which == 0:
                        qk_norm_epilogue(C, ps, tps, gqn, qst[hs][:, tq * 512:(tq + 1) * 512], t_qst[hs], tmp[ts_], t_tmp[ts_], 128)
                    elif which == 1:
                        qk_norm_epilogue(C, ps, tps, gkn, kst[hs][:, tq * 512:(tq + 1) * 512], t_kst[hs], tmp[ts_], t_tmp[ts_], 128)
                    else:
                        rope_epilogue(C, ps, tps, gqp, cs[0:64, tq * 512:(tq + 1) * 512], sn[0:64, tq * 512:(tq + 1) * 512], t_cs, qpst[hs][0:64, tq * 512:(tq + 1) * 512], t_qpst[hs], tmp[ts_], t_tmp[ts_])
            def outs(h=h, hs=hs, tok0=tok0):
                P.op("sync", lambda e: e.dma_start(out=C.QT_d[h, :, tok0:tok0 + TB], in_=qst[hs]), reads=[t_qst[hs]], writes=[t_qd], dma=True, join=True)
                P.op("sync", lambda e: e.dma_start(out=C.KT_d[h, :, tok0:tok0 + TB], in_=kst[hs]), reads=[t_kst[hs]], writes=[t_kd], dma=True, join=True)
                P.op("sync", lambda e: e.dma_start(out=C.QPE_d[h, :, tok0:tok0 + TB], in_=qpst[hs][0:64, :]), reads=[t_qpst[hs]], writes=[t_qpd], dma=True, join=True)
            defer(C, 4, outs)
        flush(C)
        for n in range(4):
            for i in range(TPB):
                pb = pcnt % 4; pcnt += 1
                ps, tps = C.psf[pb], C.t_psf[pb]
                for k in range(4):
                    P.op("pe", lambda e, ps=ps, k=k, i=i, n=n: e.matmul(ps, lhsT=ckv[:, k, i * 128:(i + 1) * 128], rhs=wv[:, k, n * 512:(n + 1) * 512], start=(k == 0), stop=(k == 3)),
                         reads=[t_w, t_ckv], writes=[tps], join=(k > 0))
                g8 = i % 8
                if g8 == 0:
                    vs = vcnt % 2; vcnt += 1
                P.op("dve", lambda e, ps=ps, vs=vs, g8=g8: e.tensor_copy(out=vst[vs][:, g8, :], in_=ps), reads=[tps], writes=[t_vst[vs]], join=True)
                if g8 == 7:
                    rr = tok0 + (i - 7) * 128
                    P.op("sync", lambda e, vs=vs, rr=rr, n=n: e.dma_start(out=C.V_d[rr:rr + 1024, n * 512:(n + 1) * 512].rearrange("(a p) c -> p a c", p=128), in_=vst[vs]),
                         reads=[t_vst[vs]], writes=[t_vd], dma=True, join=True)
    A.release(m)
    P.barrier()


def layer2_all(C, x_src, W):
    P, A = C.P, C.A
    m = A.mark()
    TB = 1024; TPB = TB // 128; NTB = S // TB
    hT = A.alloc([16, TB], BF16); hT_toks = P.toks(TPB, "hT")
    wt = [A.alloc([16, 512], BF16) for _ in range(2)]; t_wt = P.toks(2, "wt")
    wrg = A.alloc([8, 2, 256], BF16); wig = A.alloc([8, 2, 256], BF16); t_wg = P.tok("wg")
    stage = A.alloc([128], F32); cols = A.alloc([8, 16], F32)
    halo = A.alloc([16, 3], F32); hprev = A.alloc([16], F32); t_halo = P.tok("halo"); t_hprev = P.tok("hprev")
    ubuf = [A.alloc([TB + 8], F32) for _ in range(2)]; t_ubuf = P.toks(2, "ubuf")
    xc = [A.alloc([TB], F32) for _ in range(2)]; t_xc = P.toks(2, "xc")
    xcb = [A.alloc([TB], BF16) for _ in range(2)]; t_xcb = P.toks(2, "xcb")
    sg = [A.alloc([TB], F32) for _ in range(2)]; t_sg = P.toks(2, "sg")
    rg = [A.alloc([TB], F32) for _ in range(2)]; t_rg = P.toks(2, "rg")
    ig = [A.alloc([TB], F32) for _ in range(2)]; t_ig = P.toks(2, "ig")
    abuf = A.alloc([TB], F32); a2buf = A.alloc([TB], F32); xinb = A.alloc([TB], F32); hh = A.alloc([TB], F32)
    t_a = P.tok("a"); t_a2 = P.tok("a2"); t_xin = P.tok("xin"); t_hh = P.tok("hh")
    yst = [A.alloc([TB], BF16) for _ in range(2)]; t_yst = P.toks(2, "yst")
    t_yd = P.tok("YTd")
    tl = C.t_lconst
    vecs = [W["c_conv_w"][0, 0], W["c_conv_w"][0, 1], W["c_conv_w"][0, 2], W["c_conv_w"][0, 3], W["c_conv_b"][0], W["c_b_rgate"][0], W["c_b_igate"][0], W["c_lambda"][0]]
    for v, vec in enumerate(vecs):
        P.op("sync", lambda e, v=v, vec=vec: e.dma_start(out=stage[v * 16:(v + 1) * 16, :], in_=vec.rearrange("(t p) -> t p", p=128)), writes=[tl], dma=True, join=True)
    ps0, tps0 = C.psf[4], C.t_psf[4]
    P.op("pe", lambda e: e.matmul(ps0[:, 0:128], lhsT=stage, rhs=C.identf, start=True, stop=True), reads=[tl, C.t_const], writes=[tps0])
    P.op("dve", lambda e: e.tensor_copy(out=cols, in_=ps0[:, 0:128].rearrange("p (v t) -> p v t", v=8)), reads=[tps0], writes=[tl])
    P.op("act", lambda e: e.activation(out=cols[:, 7, :], in_=cols[:, 7, :], func=AF.Exp, scale=-1.0), reads=[tl], writes=[tl])
    P.op("act", lambda e: e.activation(out=cols[:, 7, :], in_=cols[:, 7, :], func=AF.Ln, bias=1.0, scale=1.0), reads=[tl], writes=[tl])
    P.op("dve", lambda e: e.tensor_scalar(out=cols[:, 7, :], in0=cols[:, 7, :], scalar1=-8.0, scalar2=None, op0=ALU.mult), reads=[tl], writes=[tl])
    for n in range(8):
        P.op("pool", lambda e, n=n: e.dma_start(out=wrg[:, n], in_=W["c_w_rgate"][0, n].rearrange("(c p) e -> p c e", p=128)), writes=[t_wg], dma=True, join=True)
        P.op("pool", lambda e, n=n: e.dma_start(out=wig[:, n], in_=W["c_w_igate"][0, n].rearrange("(c p) e -> p c e", p=128)), writes=[t_wg], dma=True, join=True)
    wcnt = 0; pcnt = 0

    def ld2(d, t, n):
        load_w_block(C, d[:, :, 0:256], t, W["c_w_in"][0], n * 256, 256)
        load_w_block(C, d[:, :, 256:512], t, W["c_w_in"][0], 2048 + n * 256, 256)
    pf = WPrefetch(C, wt, t_wt, [(lambda d, t, n=n: ld2(d, t, n)) for _tb in range(NTB) for n in range(8)])
    for tb in range(NTB):
        tok0 = tb * TB
        phase_norm_T(C, x_src, W["norm_g"][C.layer, :], tok0, hT, hT_toks, TB)
        for n in range(8):
            ws = pf.next()
            for c in range(2):
                tile = n * 2 + c
                if tb == 0:
                    P.op("pool", lambda e, c=c: e.memset(ubuf[c][:, 0:3], 0.0), writes=[t_ubuf[c]])
                else:
                    P.op("pool", lambda e, c=c, tile=tile: e.tensor_copy(out=ubuf[c][:, 0:3], in_=halo[:, tile, :]), reads=[t_halo], writes=[t_ubuf[c]])
                for tq in range(TB // 512):
                    pb = pcnt % 4; pcnt += 1
                    ps, tps = C.psf[pb], C.t_psf[pb]
                    for k in range(16):
                        P.op("pe", lambda e, ps=ps, k=k, ws=ws, c=c, tq=tq: e.matmul(ps, lhsT=wt[ws][:, k, c * 128:(c + 1) * 128], rhs=hT[:, k, tq * 512:(tq + 1) * 512], start=(k == 0), stop=(k == 15)),
                             reads=[t_wt[ws]] + hT_toks[tq * 4:(tq + 1) * 4], writes=[tps], join=(k > 0))
                    P.op("act", lambda e, ps=ps, c=c, tq=tq: e.activation(out=ubuf[c][:, 3 + tq * 512:3 + (tq + 1) * 512], in_=ps, func=AF.Copy), reads=[tps], writes=[t_ubuf[c]], join=True)
                P.op("pool", lambda e, c=c, tile=tile: e.tensor_copy(out=halo[:, tile, :], in_=ubuf[c][:, TB:TB + 3]), reads=[t_ubuf[c]], writes=[t_halo], join=True)
                P.op("dve", lambda e, c=c, tile=tile: e.tensor_scalar(out=xc[c], in0=ubuf[c][:, 3:3 + TB], scalar1=cols[:, 3, tile:tile + 1], scalar2=cols[:, 4, tile:tile + 1], op0=ALU.mult, op1=ALU.add),
                     reads=[t_ubuf[c], tl], writes=[t_xc[c]])
                for tau in (2, 1, 0):
                    P.op("dve", lambda e, c=c, tile=tile, tau=tau: e.scalar_tensor_tensor(out=xc[c], in0=ubuf[c][:, tau:tau + TB], scalar=cols[:, tau, tile:tile + 1], in1=xc[c], op0=ALU.mult, op1=ALU.add),
                         reads=[t_ubuf[c], tl, t_xc[c]], writes=[t_xc[c]])
                P.op("pool", lambda e, c=c: e.tensor_copy(out=xcb[c], in_=xc[c]), reads=[t_xc[c]], writes=[t_xcb[c]])
                for tq in range(TB // 512):
                    pb = pcnt % 4; pcnt += 1
                    ps, tps = C.psf[pb], C.t_psf[pb]
                    for k in range(16):
                        P.op("pe", lambda e, ps=ps, k=k, ws=ws, c=c, tq=tq: e.matmul(ps, lhsT=wt[ws][:, k, 256 + c * 128:256 + (c + 1) * 128], rhs=hT[:, k, tq * 512:(tq + 1) * 512], start=(k == 0), stop=(k == 15)),
                             reads=[t_wt[ws]] + hT_toks[tq * 4:(tq + 1) * 4], writes=[tps], join=(k > 0))
                    P.op("act", lambda e, ps=ps, c=c, tq=tq: e.activation(out=sg[c][:, tq * 512:(tq + 1) * 512], in_=ps, func=AF.Silu), reads=[tps], writes=[t_sg[c]], join=True)
            for ce in range(2):
                tile = n * 2 + ce
                for (wg, bidx, dstb, tdst) in ((wrg, 5, rg, t_rg), (wig, 6, ig, t_ig)):
                    for tq in range(TB // 512):
                        pb = pcnt % 4; pcnt += 1
                        ps, tps = C.psf[pb], C.t_psf[pb]
                        for cc in range(2):
                            P.op("pe", lambda e, ps=ps, wg=wg, cc=cc, ce=ce, tq=tq, n=n: e.matmul(ps, lhsT=wg[:, n, cc, ce * 128:(ce + 1) * 128], rhs=xcb[cc][:, tq * 512:(tq + 1) * 512], start=(cc == 0), stop=(cc == 1)),
                                 reads=[t_wg, t_xcb[cc]], writes=[tps], join=(cc > 0))
                        P.op("act", lambda e, ps=ps, dstb=dstb, ce=ce, tq=tq, bidx=bidx, tile=tile: e.activation(out=dstb[ce][:, tq * 512:(tq + 1) * 512], in_=ps, func=AF.Sigmoid, bias=cols[:, bidx, tile:tile + 1], scale=1.0),
                             reads=[tps, tl], writes=[tdst[ce]], join=True)
                P.op("act", lambda e, ce=ce, tile=tile: e.activation(out=abuf, in_=rg[ce], func=AF.Exp, scale=cols[:, 7, tile:tile + 1]), reads=[t_rg[ce], tl], writes=[t_a])
                P.op("pool", lambda e: e.tensor_tensor(out=a2buf, in0=abuf, in1=abuf, op=ALU.mult), reads=[t_a], writes=[t_a2])
                P.op("act", lambda e: e.activation(out=a2buf, in_=a2buf, func=AF.Sqrt, bias=1.0, scale=-1.0), reads=[t_a2], writes=[t_a2])
                P.op("pool", lambda e, ce=ce: e.tensor_tensor(out=xinb, in0=ig[ce], in1=xc[ce], op=ALU.mult), reads=[t_ig[ce], t_xc[ce]], writes=[t_xin])
                P.op("dve", lambda e: e.tensor_tensor(out=xinb, in0=xinb, in1=a2buf, op=ALU.mult), reads=[t_xin, t_a2], writes=[t_xin])
                init = 0.0 if tb == 0 else hprev[:, tile:tile + 1]
                P.op("dve", lambda e, init=init: e.tensor_tensor_scan(out=hh, data0=abuf, data1=xinb, initial=init, op0=ALU.mult, op1=ALU.add), reads=[t_a, t_xin, t_hprev], writes=[t_hh])
                P.op("pool", lambda e, tile=tile: e.tensor_copy(out=hprev[:, tile:tile + 1], in_=hh[:, TB - 1:TB]), reads=[t_hh], writes=[t_hprev])
                P.op("pool", lambda e, ce=ce: e.tensor_tensor(out=yst[ce], in0=hh, in1=sg[ce], op=ALU.mult), reads=[t_hh, t_sg[ce]], writes=[t_yst[ce]])
                P.op("sync", lambda e, ce=ce, tile=tile, tok0=tok0: e.dma_start(out=C.YT_d[tile, :, tok0:tok0 + TB], in_=yst[ce]), reads=[t_yst[ce]], writes=[t_yd], dma=True, join=True)
    A.release(m)
    P.barrier()


def layer1_all(C, x_src, W):
    P, A = C.P, C.A
    m0 = A.mark()
    dec = A.alloc([8, 64], F32); t_dec = P.tok("dec")
    m = A.mark()
    TB = 1024; TPB = TB // 128; NTB = S // TB
    hT = A.alloc([16, TB], BF16); hT_toks = P.toks(TPB, "hT")
    wt = [A.alloc([16, 512], BF16) for _ in range(2)]; t_wt = P.toks(2, "wt")
    wlr = A.alloc([16, 16], BF16); t_wlr = P.tok("wlr")
    lrT = A.alloc([TB], F32); t_lrT = P.tok("lrT")
    wga = A.alloc([1024], F32)
    TU = A.alloc([128], F32); CI = A.alloc([2], F32)
    qst = [A.alloc([TB], BF16) for _ in range(2)]; t_qst = P.toks(2, "qst")
    ebuf = [A.alloc([512], F32) for _ in range(2)]; t_eb = P.toks(2, "ebuf")
    wbuf = [A.alloc([512], F32) for _ in range(2)]; t_wb = P.toks(2, "wbuf")
    kst = [A.alloc([8, 512], BF16) for _ in range(2)]; t_kst = P.toks(2, "kst")
    vst = [A.alloc([8, 512], BF16) for _ in range(2)]; t_vst = P.toks(2, "vst")
    gst = [A.alloc([4, 512], F32) for _ in range(2)]; t_gst = P.toks(2, "gst")
    C.t_gd = P.tok("Gd"); t_qd = P.tok("QTd"); t_kd = P.tok("KPd"); t_vd = P.tok("Vd")
    tl = C.t_lconst
    Wi = W["b_w_in"][0]
    P.op("pool", lambda e: e.memset(wga[0:32, :], 0.0), writes=[tl])
    P.op("sync", lambda e: e.dma_start(out=wga[0:16, :], in_=W["b_w_gate"][0]), writes=[tl], dma=True)
    P.op("sync", lambda e: e.dma_start(out=wga[16:17, :], in_=W["b_gate_bias"][0:1, :]), writes=[tl], dma=True, join=True)
    P.op("sync", lambda e: e.dma_start(out=TU, in_=C.TU_d), writes=[tl], dma=True, join=True)
    P.op("sync", lambda e: e.dma_start(out=CI, in_=C.CI_d), writes=[tl], dma=True, join=True)
    P.op("pool", lambda e: e.memset(lrT[0:32, :], 1.0), writes=[t_lrT])
    wcnt = 0; pcnt = 0; gcnt = 0; vcnt = 0; qcnt = 0; ecnt = 0; kcnt = 0
    pf = WPrefetch(C, wt, t_wt, [(lambda d, t, cb=cb: load_w_block(C, d, t, Wi, cb * 512, 512)) for _tb in range(NTB) for cb in range(12)])
    for tb in range(NTB):
        tok0 = tb * TB
        phase_norm_T(C, x_src, W["norm_g"][C.layer, :], tok0, hT, hT_toks, TB)
        load_w_block(C, wlr, t_wlr, Wi, 6144, 16)
        for tq in range(TB // 512):
            pb = pcnt % 4; pcnt += 1
            ps, tps = C.psf[pb], C.t_psf[pb]
            for k in range(16):
                P.op("pe", lambda e, ps=ps, k=k, tq=tq: e.matmul(ps[0:16, :], lhsT=wlr[:, k, 0:16], rhs=hT[:, k, tq * 512:(tq + 1) * 512], start=(k == 0), stop=(k == 15)),
                     reads=[t_wlr] + hT_toks[tq * 4:(tq + 1) * 4], writes=[tps], join=(k > 0))
            P.op("act", lambda e, ps=ps, tq=tq: e.activation(out=lrT[0:16, tq * 512:(tq + 1) * 512], in_=ps[0:16, :], func=AF.Copy), reads=[tps], writes=[t_lrT], join=True)
        for cb in range(12):
            ws = pf.next()
            if cb < 2:
                for mm_ in range(4):
                    qt = cb * 4 + mm_
                    qs = qcnt % 2; qcnt += 1
                    for tq in range(TB // 512):
                        pb = pcnt % 4; pcnt += 1
                        ps, tps = C.psf[pb], C.t_psf[pb]
                        for k in range(16):
                            P.op("pe", lambda e, ps=ps, k=k, ws=ws, mm_=mm_, tq=tq: e.matmul(ps, lhsT=wt[ws][:, k, mm_ * 128:(mm_ + 1) * 128], rhs=hT[:, k, tq * 512:(tq + 1) * 512], start=(k == 0), stop=(k == 15)),
                                 reads=[t_wt[ws]] + hT_toks[tq * 4:(tq + 1) * 4], writes=[tps], join=(k > 0))
                        P.op("act", lambda e, ps=ps, qs=qs, tq=tq: e.activation(out=qst[qs][:, tq * 512:(tq + 1) * 512], in_=ps, func=AF.Copy, scale=1.0 / 16.0), reads=[tps], writes=[t_qst[qs]], join=True)
                    P.op("sync", lambda e, qt=qt, qs=qs, tok0=tok0: e.dma_start(out=C.QT_d[qt, :, tok0:tok0 + TB], in_=qst[qs]), reads=[t_qst[qs]], writes=[t_qd], dma=True, join=True)
            elif cb < 4:
                kb = cb - 2
                ks = kcnt % 2; kcnt += 1
                for i in range(TPB):
                    pb = pcnt % 4; pcnt += 1
                    ps, tps = C.psf[pb], C.t_psf[pb]
                    for k in range(16):
                        P.op("pe", lambda e, ps=ps, k=k, ws=ws, i=i: e.matmul(ps, lhsT=hT[:, k, i * 128:(i + 1) * 128], rhs=wt[ws][:, k, :], start=(k == 0), stop=(k == 15)),
                             reads=[t_wt[ws], hT_toks[i]], writes=[tps], join=(k > 0))
                    es = ecnt % 2; ecnt += 1
                    pz, tpz = C.psf[4], C.t_psf[4]
                    P.op("pe", lambda e, pz=pz, i=i, kb=kb: e.matmul(pz, lhsT=lrT[0:32, i * 128:(i + 1) * 128], rhs=wga[0:32, kb * 512:(kb + 1) * 512], start=True, stop=True),
                         reads=[t_lrT, tl], writes=[tpz])
                    P.op("act", lambda e, pz=pz, es=es: e.activation(out=ebuf[es], in_=pz, func=AF.Exp, scale=-1.0), reads=[tpz], writes=[t_eb[es]])
                    P.op("act", lambda e, es=es: e.activation(out=ebuf[es], in_=ebuf[es], func=AF.Ln, bias=1.0, scale=1.0), reads=[t_eb[es]], writes=[t_eb[es]])
                    pr_, tpr = C.psf[5], C.t_psf[5]
                    P.op("pe", lambda e, pr_=pr_, es=es: e.matmul(pr_, lhsT=TU, rhs=ebuf[es], start=True, stop=True), reads=[t_eb[es], tl], writes=[tpr])
                    P.op("act", lambda e, pr_=pr_, es=es: e.activation(out=wbuf[es], in_=pr_, func=AF.Exp), reads=[tpr], writes=[t_wb[es]])
                    P.op("dve", lambda e, ps=ps, es=es, ks=ks, i=i: e.tensor_tensor(out=kst[ks][:, i, :], in0=ps, in1=wbuf[es], op=ALU.mult), reads=[tps, t_wb[es]], writes=[t_kst[ks]], join=True)
                    for dt in range(4):
                        P.op("pe", lambda e, pz=pz, es=es, dt=dt: e.matmul(pz[:, dt * 2:dt * 2 + 2], lhsT=ebuf[es][:, dt * 128:(dt + 1) * 128], rhs=CI, start=(dt == 0), stop=(dt == 3), skip_group_check=True),
                             reads=[t_eb[es], tl], writes=[tpz], join=(dt > 0))
                    ch0 = (tok0 + i * 128) // 64
                    P.op("act", lambda e, pz=pz, kb=kb, ch0=ch0: e.activation(out=dec[:, kb * 4:(kb + 1) * 4, ch0:ch0 + 2], in_=pz[:, 0:8].rearrange("p (a b) -> p a b", a=4), func=AF.Exp), reads=[tpz], writes=[t_dec], join=True)
                P.op("sync", lambda e, ks=ks, tok0=tok0, kb=kb: e.dma_start(out=C.KP_d[tok0:tok0 + TB, kb * 512:(kb + 1) * 512].rearrange("(a p) c -> p a c", p=128), in_=kst[ks]),
                     reads=[t_kst[ks]], writes=[t_kd], dma=True, join=True)
            elif cb < 8:
                c0 = (cb - 4) * 512
                vs = vcnt % 2; vcnt += 1
                for i in range(TPB):
                    pb = pcnt % 4; pcnt += 1
                    ps, tps = C.psf[pb], C.t_psf[pb]
                    for k in range(16):
                        P.op("pe", lambda e, ps=ps, k=k, ws=ws, i=i: e.matmul(ps, lhsT=hT[:, k, i * 128:(i + 1) * 128], rhs=wt[ws][:, k, :], start=(k == 0), stop=(k == 15)),
                             reads=[t_wt[ws], hT_toks[i]], writes=[tps], join=(k > 0))
                    P.op("dve", lambda e, ps=ps, vs=vs, i=i: e.tensor_copy(out=vst[vs][:, i, :], in_=ps), reads=[tps], writes=[t_vst[vs]], join=True)
                P.op("sync", lambda e, vs=vs, tok0=tok0, c0=c0: e.dma_start(out=C.V_d[tok0:tok0 + TB, c0:c0 + 512].rearrange("(a p) c -> p a c", p=128), in_=vst[vs]),
                     reads=[t_vst[vs]], writes=[t_vd], dma=True, join=True)
            else:
                pcnt, gcnt = g_block(C, hT, hT_toks, wt[ws], t_wt[ws], tok0, (cb - 8) * 512, pcnt, gst, t_gst, gcnt, TPB)
    A.release(m)
    P.barrier()
    m = A.mark()
    Kp = A.alloc([NT, 256], BF16); t_Kp = P.tok("Kp")
    Vh = A.alloc([NT, 512], BF16); t_Vh = P.tok("Vh")
    QTt = [A.alloc([S], BF16) for _ in range(2)]; t_QTt = P.tok("QTt")
    Sf = A.alloc([2, 512], F32); t_Sf = P.tok("Sf")
    Sb = [A.alloc([2, 512], BF16) for _ in range(2)]; t_Sb = P.toks(2, "Sb")
    NG = 4
    Gt = [A.alloc([2, 512], F32) for _ in range(NG)]; t_Gt = P.toks(NG, "Gt")
    yT = A.alloc([4, S], BF16); t_yT = P.tok("yT")
    ogb = A.alloc([512], F32)
    NE = 6
    of = [A.alloc([512], F32) for _ in range(NE)]; yb = [A.alloc([512], BF16) for _ in range(NE)]
    ssq = [A.alloc([1], F32) for _ in range(NE)]; junk = A.alloc([512], BF16)
    t_ss = P.toks(NE, "ss"); t_of = P.toks(NE, "of"); t_yb = P.toks(NE, "yb"); t_junk = P.tok("junk"); t_yd = P.tok("YTd")
    P.op("sync", lambda e: e.dma_start(out=ogb, in_=W["b_out_g"][0, :].partition_broadcast(128)), writes=[tl], dma=True)
    NCH = S // 64
    PO = [4, 5, 6]
    tp7, ttp7 = C.tpb[1], C.t_tpb[1]

    def emit_kv(hh, c):
        i, par = c // 2, c % 2
        pr0 = par * 64
        for dh in range(2):
            kv, tkv = C.psf[2 * (c % 2) + dh], C.t_psf[2 * (c % 2) + dh]
            P.op("pe", lambda e, kv=kv, dh=dh: e.matmul(kv, lhsT=Kp[pr0:pr0 + 64, i, dh * 128:(dh + 1) * 128], rhs=Vh[pr0:pr0 + 64, i, :], start=True, stop=True),
                 reads=[t_Kp, t_Vh], writes=[tkv])

    def emit_epiA(hh, c):
        i, par = c // 2, c % 2
        es = c % NE
        gs = i % NG
        po, tpo = C.psf[PO[c % 3]], C.t_psf[PO[c % 3]]
        P.op("act", lambda e: e.activation(out=junk[0:64, :], in_=po[0:64, :], func=AF.Square, accum_out=ssq[es][0:64, :]), reads=[tpo], writes=[t_junk, t_ss[es]])
        P.op("act", lambda e: e.activation(out=ssq[es][0:64, :], in_=ssq[es][0:64, :], func=AF.Ln, bias=C.epscol[0:64, :], scale=1.0 / 512), reads=[t_ss[es], C.t_const], writes=[t_ss[es]])
        P.op("act", lambda e: e.activation(out=ssq[es][0:64, :], in_=ssq[es][0:64, :], func=AF.Exp, scale=-0.5), reads=[t_ss[es]], writes=[t_ss[es]])
        P.op("dve", lambda e: e.scalar_tensor_tensor(out=of[es][0:64, :], in0=po[0:64, :], scalar=ssq[es][0:64, :], in1=ogb[0:64, :], op0=ALU.mult, op1=ALU.mult), reads=[tpo, t_ss[es], tl], writes=[t_of[es]])
        P.op("pool", lambda e: e.tensor_tensor(out=yb[es][0:64, :], in0=of[es][0:64, :], in1=Gt[gs][0:64, par, :], op=ALU.mult), reads=[t_of[es], t_Gt[gs]], writes=[t_yb[es]])

    def emit_epiB(hh, c):
        es = c % NE
        q4 = (c % 4) * 256
        for j in range(4):
            P.op("pe", lambda e, j=j: e.transpose(out=tp7[:, q4 + j * 64:q4 + (j + 1) * 64], in_=yb[es][0:64, j * 128:(j + 1) * 128], identity=C.ident[0:64, 0:64]), reads=[t_yb[es], C.t_const], writes=[ttp7], join=True)
        P.op("dve", lambda e: e.tensor_copy(out=yT[:, :, c * 64:(c + 1) * 64], in_=tp7[:, q4:q4 + 256].rearrange("p (a b) -> p a b", a=4)), reads=[ttp7], writes=[t_yT], join=True)

    DA, DB = 2, 4
    for hh in range(4):
        P.op("sync", lambda e, hh=hh: e.dma_start(out=Kp, in_=C.KP_d[:, hh * 256:(hh + 1) * 256].rearrange("(a p) c -> p a c", p=128)), writes=[t_Kp], dma=True)
        P.op("sync", lambda e, hh=hh: e.dma_start(out=Vh, in_=C.V_d[:, hh * 512:(hh + 1) * 512].rearrange("(a p) c -> p a c", p=128)), writes=[t_Vh], dma=True)
        for dh in range(2):
            P.op("sync", lambda e, hh=hh, dh=dh: e.dma_start(out=QTt[dh], in_=C.QT_d[2 * hh + dh]), writes=[t_QTt], dma=True, join=(dh > 0))
        P.op("pool", lambda e: e.memset(Sf, 0.0), writes=[t_Sf])
        emit_kv(hh, 0)
        for c in range(NCH):
            i, par = c // 2, c % 2
            if par == 0:
                gs = i % NG
                P.op("sync", lambda e, gs=gs, i=i, hh=hh: e.dma_start(out=Gt[gs][0:64, :, :], in_=C.G_d[i * 128:(i + 1) * 128, hh * 512:(hh + 1) * 512].rearrange("(par p) c -> p par c", p=64)), writes=[t_Gt[gs]], dma=True)
            sbs = c % 2
            for dh in range(2):
                kv, tkv = C.psf[2 * (c % 2) + dh], C.t_psf[2 * (c % 2) + dh]
                P.op("dve", lambda e, kv=kv, dh=dh, hh=hh, c=c: e.scalar_tensor_tensor(out=Sf[:, dh, :], in0=Sf[:, dh, :], scalar=dec[:, 2 * hh + dh, c:c + 1], in1=kv, op0=ALU.mult, op1=ALU.add),
                     reads=[t_Sf, t_dec, tkv], writes=[t_Sf])
                P.op("act", lambda e, dh=dh, sbs=sbs: e.activation(out=Sb[sbs][:, dh, :], in_=Sf[:, dh, :], func=AF.Copy), reads=[t_Sf], writes=[t_Sb[sbs]], join=(dh > 0))
            if c + 1 < NCH:
                emit_kv(hh, c + 1)
            po, tpo = C.psf[PO[c % 3]], C.t_psf[PO[c % 3]]
            for dh in range(2):
                P.op("pe", lambda e, po=po, dh=dh, c=c, sbs=sbs: e.matmul(po[0:64, :], lhsT=QTt[dh][:, c * 64:(c + 1) * 64], rhs=Sb[sbs][:, dh, :], start=(dh == 0), stop=(dh == 1)),
                     reads=[t_QTt, t_Sb[sbs]], writes=[tpo], join=(dh > 0))
            if c >= DA:
                emit_epiA(hh, c - DA)
            if c >= DB:
                emit_epiB(hh, c - DB)
        for c in range(NCH - DA, NCH):
            emit_epiA(hh, c)
        for c in range(NCH - DB, NCH):
            emit_epiB(hh, c)
        for j in range(4):
            P.op("sync", lambda e, hh=hh, j=j: e.dma_start(out=C.YT_d[hh * 4 + j], in_=yT[:, j, :]), reads=[t_yT], writes=[t_yd], dma=True, join=True)
    A.release(m0)
    P.barrier()


WSHAPES = {
    "norm_g": (4, 2048), "rel_bias": (32, 16),
    "a_w_in": (1, 2048, 8192), "a_qk_g": (1, 2, 64), "a_lambda": (1, 4, 64), "a_subln_g": (1, 128), "a_w_out": (1, 2048, 2048),
    "b_w_in": (1, 2048, 6160), "b_w_gate": (1, 16, 1024), "b_gate_bias": (1, 1024), "b_out_g": (1, 512), "b_w_out": (1, 2048, 2048),
    "c_w_in": (1, 2048, 4096), "c_conv_w": (1, 4, 2048), "c_conv_b": (1, 2048), "c_w_rgate": (1, 8, 256, 256), "c_b_rgate": (1, 2048),
    "c_w_igate": (1, 8, 256, 256), "c_b_igate": (1, 2048), "c_lambda": (1, 2048), "c_w_out": (1, 2048, 2048),
    "d_w_in": (1, 2048, 3136), "d_q_lat_g": (1, 512), "d_kv_lat_g": (1, 512), "d_w_uq": (1, 512, 3072), "d_w_ukv": (1, 512, 4096),
    "d_qk_g": (1, 2, 192), "d_w_out": (1, 2048, 2048),
}
LAYER_W = {
    0: ["norm_g", "rel_bias", "a_w_in", "a_qk_g", "a_lambda", "a_subln_g", "a_w_out"],
    1: ["norm_g", "b_w_in", "b_w_gate", "b_gate_bias", "b_out_g", "b_w_out"],
    2: ["norm_g", "c_w_in", "c_conv_w", "c_conv_b", "c_w_rgate", "c_b_rgate", "c_w_igate", "c_b_igate", "c_lambda", "c_w_out"],
    3: ["norm_g", "d_w_in", "d_q_lat_g", "d_kv_lat_g", "d_w_uq", "d_w_ukv", "d_qk_g", "d_w_out"],
}


def t5_bucket_np(rel):
    nb = 16; max_exact = 8
    ret = np.where(rel > 0, nb, 0)
    n = np.abs(rel)
    nf = np.maximum(n, 1).astype(np.float32)
    large = max_exact + (np.log(nf / max_exact) / math.log(128 / max_exact) * (nb - max_exact)).astype(np.int32)
    large = np.minimum(large, nb - 1)
    return ret + np.where(n < max_exact, n, large)


def bias_index_tiles():
    k = np.arange(128)[:, None]; q = np.arange(128)[None, :]
    idx = np.zeros((128, 2, 128), np.int64); msk = np.zeros((128, 2, 128), bool)
    idx[:, 0, :] = t5_bucket_np(k - q)
    msk[:, 0, :] = (k // 64) > (q // 64)
    idx[:, 1, :] = t5_bucket_np(k - q - 128)
    return idx, msk


def rope_tables():
    half = 32
    inv = (np.float32(10000.0) ** (-np.arange(half, dtype=np.float32) / np.float32(half))).astype(np.float32)
    ang = (np.arange(S, dtype=np.float32)[:, None] * inv[None, :]).astype(np.float32)
    c = np.cos(ang).astype(np.float32).T; s_ = np.sin(ang).astype(np.float32).T
    return np.ascontiguousarray(np.stack([np.concatenate([c, c], 0), np.concatenate([s_, s_], 0)], 0))


def build_program(layers, debug=False):
    nc = bass.Bass("TRN2", target_bir_lowering=False)
    C = Ctx()
    C.nc = nc
    x_in = dram(nc, "x", [S, D], F32, "ExternalInput")
    out = dram(nc, "out", [S, D], F32, "ExternalOutput")
    names = []
    for l in layers:
        for n in LAYER_W[l]:
            if n not in names:
                names.append(n)
    W = {n: dram(nc, n, WSHAPES[n], F32, "ExternalInput") for n in names}
    if 0 in layers:
        C.biasT_in = dram(nc, "biasT", [128, 16, 2, 128], F32, "ExternalInput")
    ident_d = nc.inline_tensor(np.eye(128, dtype=np.float32), "ident_c").ap()
    o64 = np.zeros((128, 128), np.float32); o64[:64, :64] = 1; o64[64:, 64:] = 1
    ones64_d = nc.inline_tensor(o64, "ones64_c").ap()
    ones128_d = nc.inline_tensor(np.ones((128, 128), np.float32), "ones128_c").ap()
    xs = [dram(nc, f"xs{i}", [S, D], F32) for i in range(2)]
    sk = "ExternalOutput" if debug else "Internal"
    C.QT_d = dram(nc, "QT_d", [16, 128, S], BF16, sk)
    C.KT_d = dram(nc, "KT_d", [16, 128, S], BF16, sk)
    C.V_d = dram(nc, "V_d", [S, D], BF16, sk)
    C.G_d = dram(nc, "G_d", [S, D], F32, sk)
    C.YT_d = dram(nc, "YT_d", [16, 128, S], BF16, sk)
    C.GT_d = dram(nc, "GT_d", [16, 128, S], F32, sk)
    if 3 in layers:
        C.QPE_d = dram(nc, "QPE_d", [16, 64, S], BF16, sk)
        C.KPE_d = dram(nc, "KPE_d", [64, S], BF16, sk)
        C.CQ_d = dram(nc, "CQ_d", [4, 128, S], BF16, sk)
        C.CKV_d = dram(nc, "CKV_d", [4, 128, S], BF16, sk)
        C.rope_in = dram(nc, "rope_cs", [2, 64, S], F32, "ExternalInput")
        kk = np.arange(128)[:, None]; qq = np.arange(128)[None, :]
        C.maskT_d = nc.inline_tensor(np.where((kk // 64) > (qq // 64), -30000.0, 0.0).astype(np.float32), "maskT_c").ap()
    if 1 in layers:
        C.KP_d = dram(nc, "KP_d", [S, 1024], BF16, sk)
        tt = np.arange(128)
        tu = np.where((tt[:, None] > tt[None, :]) & (tt[:, None] // 64 == tt[None, :] // 64), -1.0 / 16.0, 0.0).astype(np.float32)
        ci = np.where(tt[:, None] // 64 == np.arange(2)[None, :], -1.0 / 16.0, 0.0).astype(np.float32)
        C.TU_d = nc.inline_tensor(tu, "TU_c").ap()
        C.CI_d = nc.inline_tensor(ci, "CI_c").ap()
    rot = np.zeros((128, 128), np.float32)
    for i_ in range(32):
        rot[32 + i_, i_] = -1.0; rot[i_, 32 + i_] = 1.0
    rotm_d = nc.inline_tensor(rot, "rotm_c").ap()
    with ExitStack() as ctx:
        P = Prog(nc, ctx)
        C.P = P
        A = Arena(nc, ctx, 200)
        C.A = A
        C.psf = [ctx.enter_context(nc.psum_tensor(f"psf{i}", [128, 512], F32))[:] for i in range(8)]
        C.t_psf = P.toks(8, "psf")
        C.tpb = [C.psf[6].bitcast(BF16), C.psf[7].bitcast(BF16)]
        C.t_tpb = [C.t_psf[6], C.t_psf[7]]
        C.auxcnt = 0
        C.dq = []
        C.tickn = 0
        C.t_const = P.tok("const"); C.t_lconst = P.tok("lconst")
        C.ident = A.alloc([128], BF16); C.ones64 = A.alloc([128], F32); C.ones128 = A.alloc([128], F32); C.epscol = A.alloc([1], F32)
        C.rotm = A.alloc([128], F32); C.identf = A.alloc([128], F32)
        P.op("sync", lambda e: e.dma_start(out=C.identf, in_=ident_d), writes=[C.t_const], dma=True, join=True)
        P.op("sync", lambda e: e.dma_start(out=C.rotm, in_=rotm_d), writes=[C.t_const], dma=True, join=True)
        P.op("pool", lambda e: e.dma_start(out=C.ident, in_=ident_d), writes=[C.t_const], dma=True, join=True)
        P.op("sync", lambda e: e.dma_start(out=C.ones64, in_=ones64_d), writes=[C.t_const], dma=True, join=True)
        P.op("sync", lambda e: e.dma_start(out=C.ones128, in_=ones128_d), writes=[C.t_const], dma=True, join=True)
        P.op("dve", lambda e: e.memset(C.epscol, EPS), writes=[C.t_const], join=True)
        P.barrier()
        cur = x_in
        for li, l in enumerate(layers):
            C.layer = l
            dst = out if li == len(layers) - 1 else xs[li % 2]
            if l == 0:
                layer0_proj(C, cur, W)
                attn_phase(C, W, "diff")
                phase_outproj(C, C.YT_d, W["a_w_out"][0], cur, dst)
            elif l == 1:
                layer1_all(C, cur, W)
                phase_outproj(C, C.YT_d, W["b_w_out"][0], cur, dst)
            elif l == 2:
                layer2_all(C, cur, W)
                phase_outproj(C, C.YT_d, W["c_w_out"][0], cur, dst)
            elif l == 3:
                layer3_proj1(C, cur, W)
                layer3_proj2(C, W)
                attn_phase(C, W, "mla")
                phase_outproj(C, C.YT_d, W["d_w_out"][0], cur, dst)
            else:
                raise NotImplementedError
            cur = dst
        P.emit()
    C.names = names
    return nc, C


_CACHE = {}


def run_layers(layers, x, inputs, n_cores=4):
    key = tuple(layers)
    if key not in _CACHE:
        _CACHE[key] = build_program(layers)
    nc, C = _CACHE[key]
    shared = {n: np.ascontiguousarray(inputs[n], dtype=np.float32) for n in C.names}
    if 0 in layers:
        idx, msk = bias_index_tiles()
        rb = np.asarray(inputs["rel_bias"], np.float32)
        bt = rb[idx]
        bt = np.where(msk[..., None], np.float32(-30000.0), bt)
        shared["biasT"] = np.ascontiguousarray(bt.transpose(0, 3, 1, 2))
    if 3 in layers:
        shared["rope_cs"] = rope_tables()
    in_maps = [dict(shared, x=np.ascontiguousarray(x[b])) for b in range(n_cores)]
    res = run_bass_kernel_spmd(nc, in_maps, core_ids=list(range(n_cores)))
    C.last_res = res
    return np.stack([r["out"] for r in res.results], axis=0)


def kernel(**inputs):
    x = np.asarray(inputs["x"], np.float32)
    return run_layers([0, 1, 2, 3], x, inputs)
```

```python
import math
from contextlib import ExitStack
import numpy as np
import concourse.bass as bass
import concourse.mybir as mybir
from concourse.bass_utils import run_bass_kernel_spmd

F32 = mybir.dt.float32
BF16 = mybir.dt.bfloat16
AF = mybir.ActivationFunctionType
ALU = mybir.AluOpType
AX = mybir.AxisListType

ENGS = ("sync", "act", "pool", "dve", "pe")


class Tok:
    __slots__ = ("name", "w", "r", "pool")

    def __init__(self, name):
        self.name = name
        self.w = {}
        self.r = {}
        self.pool = {}


class Prog:
    def __init__(self, nc, ctx, same_engine_sync=("act", "dve", "pool")):
        self.nc = nc
        self.ctx = ctx
        self.q = {e: [] for e in ENGS}
        self.cnt = {e: 0 for e in ENGS}
        self.waited = {e: {} for e in ENGS}
        self.pend = {e: {} for e in ENGS}
        self.needed = set()
        self.same_sync = set(same_engine_sync)
        self.pool_val = []
        self.pool_free = {"hw": [], "sw": []}
        self.live = []
        self.all_toks = []

    def tok(self, name="t"):
        t = Tok(name)
        self.all_toks.append(t)
        return t

    def toks(self, n, name="t"):
        return [self.tok(f"{name}{i}") for i in range(n)]

    def op(self, eng, fn, reads=(), writes=(), dma=False, join=False):
        deps = dict(self.pend[eng])
        self.pend[eng] = {}

        def add(k, v):
            if deps.get(k, 0) < v:
                deps[k] = v

        for t in reads:
            for k, v in t.w.items():
                add(k, v)
        for t in writes:
            if not (join and not t.r):
                for k, v in t.w.items():
                    add(k, v)
            elif dma:
                for k, v in t.w.items():
                    if k[0] == "e":
                        add(k, v)
            for k, v in t.r.items():
                add(k, v)
        waits = []
        wd = self.waited[eng]
        for k, v in deps.items():
            if k == ("e", eng) and (eng not in self.same_sync or self.cnt[eng] + 1 - v >= 3):
                continue
            if wd.get(k, 0) < v:
                wd[k] = v
                waits.append((k, v))
                self.needed.add((k, v))
        if dma:
            assert len(writes) == 1
            t = reads[0] if reads else writes[0]
            kind = "sw" if eng == "pool" else "hw"
            if kind not in t.pool:
                if self.pool_free[kind]:
                    t.pool[kind] = self.pool_free[kind].pop()
                else:
                    self.pool_val.append(0)
                    t.pool[kind] = len(self.pool_val) - 1
                self.live.append((t, kind))
            pi = t.pool[kind]
            self.pool_val[pi] += 16
            h = (("s", pi), self.pool_val[pi])
        else:
            self.cnt[eng] += 1
            h = (("e", eng), self.cnt[eng])
        k, v = h
        for t in reads:
            if t.r.get(k, 0) < v:
                t.r[k] = v
        for t in writes:
            if join and not t.r:
                t.w[k] = v
            else:
                t.w = {k: v}
                t.r = {}
        self.q[eng].append((waits, fn, h, dma))
        return h

    def barrier(self):
        hs = {}
        for e in ENGS:
            if self.cnt[e] > 0:
                hs[("e", e)] = self.cnt[e]
        for i, v in enumerate(self.pool_val):
            if v > 0:
                hs[("s", i)] = v
        for e in ENGS:
            for k, v in hs.items():
                if self.pend[e].get(k, 0) < v:
                    self.pend[e][k] = v
        for t, kind in self.live:
            self.pool_free[kind].append(t.pool[kind])
        self.live = []
        for t in self.all_toks:
            t.w = {}
            t.r = {}
            t.pool = {}

    def emit(self):
        nc = self.nc
        self.barrier()
        fw = []
        wd = self.waited["sync"]
        for k, v in self.pend["sync"].items():
            if wd.get(k, 0) < v:
                fw.append((k, v))
                self.needed.add((k, v))
        esem = {e: self.ctx.enter_context(nc.semaphore(f"sem_{e}")) for e in ENGS}
        psem = [self.ctx.enter_context(nc.semaphore(f"dsem{i}")) for i in range(len(self.pool_val))]
        rank = {}
        for e in ENGS:
            r = 0
            for (_w, _f, h, dma) in self.q[e]:
                if not dma and h in self.needed:
                    r += 1
                    rank[h] = r

        def resolve(h):
            k, v = h
            if k[0] == "e":
                return esem[k[1]], rank[h]
            return psem[k[1]], v

        block = self.ctx.enter_context(nc.Block())
        names = {"sync": "sync", "act": "scalar", "pool": "gpsimd", "dve": "vector", "pe": "tensor"}
        for e in ENGS:
            ops = self.q[e]
            if not ops and e != "sync":
                continue

            def body(eng, ops=ops, e=e):
                for (waits, fn, h, dma) in ops:
                    for w in waits:
                        s, v = resolve(w)
                        eng.wait_ge(s, v)
                    ins = fn(eng)
                    if dma:
                        s, v = resolve(h)
                        ins.then_inc(s, 16)
                    elif h in self.needed:
                        s, v = resolve(h)
                        ins.then_inc(s, 1)
                if e == "sync":
                    for w in fw:
                        s, v = resolve(w)
                        eng.wait_ge(s, v)

            getattr(block, names[e])(body)
        self.n_sems = len(psem) + 5


class Arena:
    def __init__(self, nc, ctx, kib):
        self.n32 = kib * 256
        self.t = ctx.enter_context(nc.sbuf_tensor("arena", [128, self.n32], F32))
        self.off = 0

    def mark(self):
        return self.off

    def release(self, m):
        self.off = m

    def alloc(self, shape, dt, parts=128):
        n = int(np.prod(shape))
        n32 = n if dt == F32 else (n + 1) // 2
        n32 = (n32 + 7) // 8 * 8
        assert self.off + n32 <= self.n32, f"arena overflow {self.off + n32} > {self.n32}"
        v = self.t[0:parts, self.off:self.off + n32]
        self.off += n32
        if dt != F32:
            v = v.bitcast(dt)
        v = v[:, 0:n]
        if len(shape) == 2:
            v = v.rearrange("p (a b) -> p a b", a=shape[0])
        elif len(shape) == 3:
            v = v.rearrange("p (a b c) -> p a b c", a=shape[0], b=shape[1])
        return v


S = 4096
D = 2048
EPS = 1e-6
NT = S // 128
TB = 2048
NTB = S // TB
TPB = TB // 128


class Ctx:
    pass


def dram(nc, name, shape, dt, kind="Internal"):
    return nc.dram_tensor(name, list(shape), dt, kind=kind).ap()


def load_w_block(C, dst, dtok, wsrc, c0, ncols, kc=16, rows0=0):
    P = C.P
    wv = wsrc[rows0:rows0 + kc * 128, :].rearrange("(k p) c -> p k c", p=128)
    step = 4 if kc >= 4 else kc
    for k0 in range(0, kc, step):
        k1 = min(kc, k0 + step)
        P.op("pool", lambda e, k0=k0, k1=k1: e.dma_start(out=dst[:, k0:k1, 0:ncols], in_=wv[:, k0:k1, c0:c0 + ncols]),
             writes=[dtok], dma=True, join=True)


class WPrefetch:
    def __init__(self, C, wt, t_wt, loaders):
        self.C, self.wt, self.t_wt, self.loaders = C, wt, t_wt, loaders
        self.k = 0
        self._issue(0)

    def _issue(self, k):
        if k < len(self.loaders):
            self.loaders[k](self.wt[k % 2], self.t_wt[k % 2])

    def next(self):
        ws = self.k % 2
        self._issue(self.k + 1)
        self.k += 1
        return ws


def phase_norm_T(C, x_src, g_row, tok0, hT, hT_toks, tbsz=TB):
    P, A = C.P, C.A
    m = A.mark()
    xin = [A.alloc([D], F32) for _ in range(2)]
    hb = [A.alloc([D], BF16) for _ in range(2)]
    gb = A.alloc([D], F32)
    ssq = [A.alloc([1], F32) for _ in range(2)]
    t_xin = P.toks(2, "xin"); t_hb = P.toks(2, "hb"); t_gb = P.tok("gb"); t_ssq = P.toks(2, "ssq")
    P.op("sync", lambda e: e.dma_start(out=gb, in_=g_row.partition_broadcast(128)), writes=[t_gb], dma=True)
    for i in range(tbsz // 128):
        s = i % 2
        r0 = tok0 + i * 128
        P.op("sync", lambda e, s=s, r0=r0: e.dma_start(out=xin[s], in_=x_src[r0:r0 + 128, :]), writes=[t_xin[s]], dma=True)
        P.op("act", lambda e, s=s: e.activation(out=hb[s], in_=xin[s], func=AF.Square, accum_out=ssq[s]),
             reads=[t_xin[s]], writes=[t_hb[s], t_ssq[s]])
        P.op("act", lambda e, s=s: e.activation(out=ssq[s], in_=ssq[s], func=AF.Ln, bias=C.epscol, scale=1.0 / D),
             reads=[t_ssq[s], C.t_const], writes=[t_ssq[s]])
        P.op("act", lambda e, s=s: e.activation(out=ssq[s], in_=ssq[s], func=AF.Exp, scale=-0.5), reads=[t_ssq[s]], writes=[t_ssq[s]])
        P.op("dve", lambda e, s=s: e.scalar_tensor_tensor(out=hb[s], in0=xin[s], scalar=ssq[s], in1=gb, op0=ALU.mult, op1=ALU.mult),
             reads=[t_xin[s], t_ssq[s], t_gb], writes=[t_hb[s]])
        for half in range(2):
            tp, ttp = C.tpb[half], C.t_tpb[half]
            for j in range(8):
                k = half * 8 + j
                P.op("pe", lambda e, s=s, j=j, k=k, tp=tp: e.transpose(out=tp[:, j * 128:(j + 1) * 128], in_=hb[s][:, k * 128:(k + 1) * 128], identity=C.ident),
                     reads=[t_hb[s], C.t_const], writes=[ttp], join=True)
            eng = "act" if half == 0 else "dve"
            dst = hT[:, half * 8:(half + 1) * 8, i * 128:(i + 1) * 128]
            srcv = tp.rearrange("p (a b) -> p a b", a=8)
            if eng == "act":
                P.op("act", lambda e, dst=dst, srcv=srcv: e.activation(out=dst, in_=srcv, func=AF.Copy), reads=[ttp], writes=[hT_toks[i]], join=True)
            else:
                P.op("dve", lambda e, dst=dst, srcv=srcv: e.tensor_copy(out=dst, in_=srcv), reads=[ttp], writes=[hT_toks[i]], join=True)
    A.release(m)


def phase_outproj(C, yT_d, w_out, x_src, x_dst):
    P, A = C.P, C.A
    m = A.mark()
    wo = A.alloc([16, D], BF16)
    yT = A.alloc([16, TB], BF16)
    xin = [A.alloc([D], F32) for _ in range(2)]
    xo = [A.alloc([D], F32) for _ in range(2)]
    t_wo = P.tok("wo"); t_yT = P.toks(16, "yT"); t_xin = P.toks(2, "xin"); t_xo = P.toks(2, "xo"); t_dst = P.tok("xdst")
    for n in range(4):
        load_w_block(C, wo[:, :, n * 512:(n + 1) * 512], t_wo, w_out, n * 512, 512)
    cnt = 0
    for tb in range(NTB):
        tok0 = tb * TB
        for k in range(16):
            P.op("sync", lambda e, k=k, tok0=tok0: e.dma_start(out=yT[:, k, :], in_=yT_d[k, :, tok0:tok0 + TB]), writes=[t_yT[k]], dma=True)
        for i in range(TPB):
            s = i % 2
            r0 = tok0 + i * 128
            P.op("sync", lambda e, s=s, r0=r0: e.dma_start(out=xin[s], in_=x_src[r0:r0 + 128, :]), writes=[t_xin[s]], dma=True)
            for n in range(4):
                pb = cnt % 4; cnt += 1
                ps, tps = C.psf[pb], C.t_psf[pb]
                for k in range(16):
                    P.op("pe", lambda e, ps=ps, k=k, i=i, n=n: e.matmul(ps, lhsT=yT[:, k, i * 128:(i + 1) * 128], rhs=wo[:, k, n * 512:(n + 1) * 512], start=(k == 0), stop=(k == 15)),
                         reads=[t_yT[k], t_wo], writes=[tps], join=(k > 0))
                P.op("dve", lambda e, ps=ps, s=s, n=n: e.tensor_tensor(out=xo[s][:, n * 512:(n + 1) * 512], in0=ps, in1=xin[s][:, n * 512:(n + 1) * 512], op=ALU.add),
                     reads=[tps, t_xin[s]], writes=[t_xo[s]], join=(n > 0))
            P.op("sync", lambda e, s=s, r0=r0: e.dma_start(out=x_dst[r0:r0 + 128, :], in_=xo[s]), reads=[t_xo[s]], writes=[t_dst], dma=True)
    A.release(m)
    P.barrier()


def defer(C, delay, fn):
    due = C.tickn + delay
    if C.dq and C.dq[-1][0] > due:
        due = C.dq[-1][0]
    if delay <= 0 and not C.dq:
        fn()
    else:
        C.dq.append((due, fn))


def tick(C):
    C.tickn += 1
    while C.dq and C.dq[0][0] <= C.tickn:
        C.dq.pop(0)[1]()


def flush(C):
    while C.dq:
        C.dq.pop(0)[1]()


def qk_norm_epilogue(C, ps, tps, gcol, dst, t_dst, tmp, t_tmp, grp, delay=2):
    P = C.P
    qf, sq, rs = tmp
    ones = C.ones64 if grp == 64 else C.ones128
    tick(C)
    P.op("dve", lambda e: e.tensor_copy(out=qf, in_=ps), reads=[tps], writes=[t_tmp[0]])
    P.op("pool", lambda e: e.tensor_tensor(out=sq, in0=qf, in1=qf, op=ALU.mult), reads=[t_tmp[0]], writes=[t_tmp[1]])

    def stage2():
        pss, tpss = C.psf[4 + C.auxcnt % 2], C.t_psf[4 + C.auxcnt % 2]
        C.auxcnt += 1
        P.op("pe", lambda e: e.matmul(pss, lhsT=ones, rhs=sq, start=True, stop=True), reads=[t_tmp[1], C.t_const], writes=[tpss])
        P.op("act", lambda e: e.activation(out=rs, in_=pss, func=AF.Ln, bias=C.epscol, scale=1.0 / grp), reads=[tpss, C.t_const], writes=[t_tmp[2]])
        P.op("act", lambda e: e.activation(out=rs, in_=rs, func=AF.Exp, scale=-0.5), reads=[t_tmp[2]], writes=[t_tmp[2]])
        P.op("dve", lambda e: e.scalar_tensor_tensor(out=dst, in0=qf, scalar=gcol, in1=rs, op0=ALU.mult, op1=ALU.mult),
             reads=[t_tmp[0], t_tmp[2], C.t_lconst], writes=[t_dst], join=True)
    defer(C, delay, stage2)


def gT_block(C, hT, hT_toks, wt_s, t_wt_s, tok0, h0, tbsz, pcnt, gst, t_gst, gcnt):
    P = C.P
    for mm_ in range(4):
        gs = gcnt % 2; gcnt += 1
        for tq in range(tbsz // 512):
            pb = pcnt % 4; pcnt += 1
            ps, tps = C.psf[pb], C.t_psf[pb]
            for k in range(16):
                P.op("pe", lambda e, ps=ps, k=k, mm_=mm_, tq=tq: e.matmul(ps, lhsT=wt_s[:, k, mm_ * 128:(mm_ + 1) * 128], rhs=hT[:, k, tq * 512:(tq + 1) * 512], start=(k == 0), stop=(k == 15)),
                     reads=[t_wt_s] + hT_toks[tq * 4:(tq + 1) * 4], writes=[tps], join=(k > 0))
            P.op("act", lambda e, ps=ps, gs=gs, tq=tq: e.activation(out=gst[gs][:, tq * 512:(tq + 1) * 512], in_=ps, func=AF.Silu), reads=[tps], writes=[t_gst[gs]], join=True)
        P.op("sync", lambda e, gs=gs, mm_=mm_: e.dma_start(out=C.GT_d[h0 + mm_, :, tok0:tok0 + tbsz], in_=gst[gs][:, 0:tbsz]), reads=[t_gst[gs]], writes=[C.t_gd], dma=True, join=True)
    return pcnt, gcnt


def layer0_proj(C, x_src, W):
    P, A = C.P, C.A
    m = A.mark()
    hT = A.alloc([16, TB], BF16)
    hT_toks = P.toks(TPB, "hT")
    wt = [A.alloc([16, 512], BF16) for _ in range(2)]
    t_wt = P.toks(2, "wt")
    qst = [A.alloc([TB], BF16) for _ in range(2)]
    t_qst = P.toks(2, "qst")
    tmp = [[A.alloc([512], F32) for _ in range(3)] for _ in range(2)]
    t_tmp = [P.toks(3, "tmp") for _ in range(2)]
    vst = [A.alloc([8, 512], BF16) for _ in range(2)]
    t_vst = P.toks(2, "vst")
    gst = [A.alloc([TB], F32) for _ in range(2)]
    t_gst = P.toks(2, "gst")
    t_qd = P.tok("QTd"); t_kd = P.tok("KTd"); t_vd = P.tok("Vd"); C.t_gd = P.tok("Gd")
    gq = A.alloc([1], F32); gk = A.alloc([1], F32)
    for half in range(2):
        P.op("sync", lambda e, half=half: e.dma_start(out=gq[half * 64:(half + 1) * 64, :], in_=W["a_qk_g"][0, 0, :].rearrange("(d o) -> d o", o=1)), writes=[C.t_lconst], dma=True, join=True)
        P.op("sync", lambda e, half=half: e.dma_start(out=gk[half * 64:(half + 1) * 64, :], in_=W["a_qk_g"][0, 1, :].rearrange("(d o) -> d o", o=1)), writes=[C.t_lconst], dma=True, join=True)
    P.op("dve", lambda e: e.tensor_scalar(out=gq, in0=gq, scalar1=0.125, scalar2=None, op0=ALU.mult), reads=[C.t_lconst], writes=[C.t_lconst])
    wcnt = 0; qcnt = 0; tcnt = 0; pcnt = 0; vcnt = 0; gcnt = 0
    pf = WPrefetch(C, wt, t_wt, [(lambda d, t, cb=cb: load_w_block(C, d, t, W["a_w_in"][0], cb * 512, 512)) for _tb in range(NTB) for cb in range(16)])
    for tb in range(NTB):
        tok0 = tb * TB
        phase_norm_T(C, x_src, W["norm_g"][C.layer, :], tok0, hT, hT_toks)
        for cb in range(16):
            ws = pf.next()
            kind = cb // 4
            if kind < 2:
                for mm_ in range(4):
                    h = (cb % 4) * 4 + mm_
                    qs = qcnt % 2; qcnt += 1
                    for tq in range(TB // 512):
                        pb = pcnt % 4; pcnt += 1
                        ps, tps = C.psf[pb], C.t_psf[pb]
                        for k in range(16):
                            P.op("pe", lambda e, ps=ps, k=k, ws=ws, mm_=mm_, tq=tq: e.matmul(ps, lhsT=wt[ws][:, k, mm_ * 128:(mm_ + 1) * 128], rhs=hT[:, k, tq * 512:(tq + 1) * 512], start=(k == 0), stop=(k == 15)),
                                 reads=[t_wt[ws]] + hT_toks[tq * 4:(tq + 1) * 4], writes=[tps], join=(k > 0))
                        ts_ = tcnt % 2; tcnt += 1
                        qk_norm_epilogue(C, ps, tps, gq if kind == 0 else gk, qst[qs][:, tq * 512:(tq + 1) * 512], t_qst[qs], tmp[ts_], t_tmp[ts_], 64)
                    dd, td = (C.QT_d, t_qd) if kind == 0 else (C.KT_d, t_kd)
                    defer(C, 2, lambda dd=dd, td=td, h=h, qs=qs, tok0=tok0: P.op("sync", lambda e: e.dma_start(out=dd[h, :, tok0:tok0 + TB], in_=qst[qs]), reads=[t_qst[qs]], writes=[td], dma=True, join=True))
                flush(C)
            elif kind == 3:
                pcnt, gcnt = gT_block(C, hT, hT_toks, wt[ws], t_wt[ws], tok0, (cb % 4) * 4, TB, pcnt, gst, t_gst, gcnt)
            else:
                c0 = (cb % 4) * 512
                for i in range(TPB):
                    pb = pcnt % 4; pcnt += 1
                    ps, tps = C.psf[pb], C.t_psf[pb]
                    for k in range(16):
                        P.op("pe", lambda e, ps=ps, k=k, ws=ws, i=i: e.matmul(ps, lhsT=hT[:, k, i * 128:(i + 1) * 128], rhs=wt[ws][:, k, :], start=(k == 0), stop=(k == 15)),
                             reads=[t_wt[ws], hT_toks[i]], writes=[tps], join=(k > 0))
                    r0 = tok0 + i * 128
                    if True:
                        g8 = i % 8
                        if g8 == 0:
                            vs = vcnt % 2; vcnt += 1
                        P.op("dve", lambda e, ps=ps, vs=vs, g8=g8: e.tensor_copy(out=vst[vs][:, g8, :], in_=ps), reads=[tps], writes=[t_vst[vs]], join=True)
                        if g8 == 7:
                            rr = r0 - 7 * 128
                            P.op("sync", lambda e, vs=vs, rr=rr, c0=c0: e.dma_start(out=C.V_d[rr:rr + 1024, c0:c0 + 512].rearrange("(a p) c -> p a c", p=128), in_=vst[vs]),
                                 reads=[t_vst[vs]], writes=[t_vd], dma=True, join=True)
    A.release(m)
    P.barrier()


def attn_phase(C, W, mode):
    P, A = C.P, C.A
    m = A.mark()
    H = 16
    diff = (mode == "diff")
    nsub = 2 if diff else 1
    lam_init = 0.8 - 0.6 * math.exp(-0.3 * C.layer)
    QT = [A.alloc([S], BF16) for _ in range(2)]
    KT = [A.alloc([S], BF16) for _ in range(2)]
    Vh = [A.alloc([NT, 128], BF16) for _ in range(2)]
    Gh = [A.alloc([S], F32) for _ in range(2)]
    yT = [A.alloc([S], BF16) for _ in range(2)]
    LA = 4 if diff else 3
    NPT = LA + 2
    PT = [A.alloc([512], BF16) for _ in range(NPT)]
    onesb = A.alloc([128], BF16)
    t_QT = P.toks(2, "QT"); t_KT = P.toks(2, "KT"); t_Vh = P.toks(2, "Vh"); t_Gh = P.toks(2, "Gh"); t_yT = P.toks(2, "yT"); t_PT = P.toks(NPT, "PT")
    t_yd = P.tok("YTd")
    Osb = [[A.alloc([512], F32) for _ in range(2)] for _ in range(2)]; Dsb = [[A.alloc([512], F32) for _ in range(2)] for _ in range(2)]
    sqb = [A.alloc([512], F32) for _ in range(2)]; rsb = [A.alloc([512], F32) for _ in range(2)]
    t_Osb = [P.toks(2, "Osb") for _ in range(2)]; t_Dsb = [P.toks(2, "Dsb") for _ in range(2)]; t_sqb = P.toks(2, "sqb"); t_rsb = P.toks(2, "rsb")
    tl = C.t_lconst
    P.op("pool", lambda e: e.memset(onesb, 1.0), writes=[tl])
    if diff:
        biasT = A.alloc([H, 2, 128], F32)
        b15 = A.alloc([H], F32)
        lam4 = A.alloc([4, 64], F32)
        lamc = A.alloc([4], F32)
        sgcol = A.alloc([1], F32)
        P.op("sync", lambda e: e.dma_start(out=biasT, in_=C.biasT_in), writes=[tl], dma=True, join=True)
        P.op("sync", lambda e: e.dma_start(out=b15, in_=W["rel_bias"][15, :].partition_broadcast(128)), writes=[tl], dma=True, join=True)
        P.op("sync", lambda e: e.dma_start(out=lam4, in_=W["a_lambda"][0].rearrange("a d -> (a d)").partition_broadcast(128).rearrange("p (a d) -> p a d", a=4)), writes=[tl], dma=True, join=True)
        load_col(C, sgcol, W["a_subln_g"][0, :], 128, 1.0 - lam_init)
        biasB = A.alloc([H, 2, 128], BF16)
        for hh in range(H):
            P.op("dve", lambda e, hh=hh: e.tensor_scalar(out=biasT[:, hh], in0=biasT[:, hh], scalar1=b15[:, hh:hh + 1], scalar2=None, op0=ALU.subtract), reads=[tl], writes=[tl])
        P.op("dve", lambda e: e.tensor_copy(out=biasB, in_=biasT), reads=[tl], writes=[tl])
        P.op("dve", lambda e: e.tensor_tensor(out=lam4[:, 0, :], in0=lam4[:, 0, :], in1=lam4[:, 1, :], op=ALU.mult), reads=[tl], writes=[tl])
        P.op("dve", lambda e: e.tensor_tensor(out=lam4[:, 2, :], in0=lam4[:, 2, :], in1=lam4[:, 3, :], op=ALU.mult), reads=[tl], writes=[tl])
        P.op("dve", lambda e: e.reduce_sum(out=lamc[:, 0:1], in_=lam4[:, 0, :], axis=AX.X), reads=[tl], writes=[tl])
        P.op("dve", lambda e: e.reduce_sum(out=lamc[:, 1:2], in_=lam4[:, 2, :], axis=AX.X), reads=[tl], writes=[tl])
        P.op("act", lambda e: e.activation(out=lamc[:, 0:2], in_=lamc[:, 0:2], func=AF.Exp), reads=[tl], writes=[tl])
        P.op("dve", lambda e: e.scalar_tensor_tensor(out=lamc[:, 0:1], in0=lamc[:, 1:2], scalar=-lam_init, in1=lamc[:, 0:1], op0=ALU.add, op1=ALU.subtract), reads=[tl], writes=[tl])
    else:
        maskT = A.alloc([128], F32)
        QP = [A.alloc([S], BF16) for _ in range(2)]
        KP = A.alloc([S], BF16)
        t_QP = P.toks(2, "QP"); t_KP = P.tok("KP")
        P.op("sync", lambda e: e.dma_start(out=maskT, in_=C.maskT_d), writes=[tl], dma=True)
        maskB = A.alloc([128], BF16)
        P.op("dve", lambda e: e.tensor_copy(out=maskB, in_=maskT), reads=[tl], writes=[tl])
        P.op("sync", lambda e: e.dma_start(out=KP[0:64, :], in_=C.KPE_d), writes=[t_KP], dma=True)
    SB = [0, 1, 2, 3, 7] if diff else [0, 1, 6, 7]
    NSB = len(SB)

    def banks(qg, t):
        b0 = 4 if diff else 2 + 2 * (qg % 2)
        return b0, b0 + 1

    def emit_loads(h):
        hs = h % 2
        P.op("sync", lambda e: e.dma_start(out=QT[hs], in_=C.QT_d[h]), writes=[t_QT[hs]], dma=True)
        P.op("sync", lambda e: e.dma_start(out=KT[hs], in_=C.KT_d[h]), writes=[t_KT[hs]], dma=True)
        if not diff:
            P.op("sync", lambda e: e.dma_start(out=QP[hs][0:64, :], in_=C.QPE_d[h]), writes=[t_QP[hs]], dma=True)
        P.op("sync", lambda e: e.dma_start(out=Vh[hs], in_=C.V_d[:, h * 128:(h + 1) * 128].rearrange("(a p) c -> p a c", p=128)), writes=[t_Vh[hs]], dma=True)
        P.op("sync", lambda e: e.dma_start(out=Gh[hs], in_=C.GT_d[h]), writes=[t_Gh[hs]], dma=True)

    def emit_S(n, h, qg, t, i):
        hs = h % 2
        jmin = max(0, i - 4 * qg)
        Sp, tSp = C.psf[SB[n % NSB]], C.t_psf[SB[n % NSB]]
        q0 = (4 * qg + jmin) * 128
        ncol = (4 - jmin) * 128
        c0 = jmin * 128
        near = []
        for rel in ((0, 1) if diff else (0,)):
            jj = i - 4 * qg + rel
            if 0 <= jj <= 3 and jj >= jmin:
                near.append((jj, biasB[:, h, rel, :] if diff else maskB))
        nn = len(near)
        if diff:
            P.op("pe", lambda e: e.matmul(Sp[:, c0:c0 + ncol], lhsT=KT[hs][t * 64:(t + 1) * 64, i * 128:(i + 1) * 128], rhs=QT[hs][t * 64:(t + 1) * 64, q0:q0 + ncol], start=True, stop=(nn == 0), skip_group_check=True),
                 reads=[t_KT[hs], t_QT[hs]], writes=[tSp])
        else:
            P.op("pe", lambda e: e.matmul(Sp[:, c0:c0 + ncol], lhsT=KT[hs][:, i * 128:(i + 1) * 128], rhs=QT[hs][:, q0:q0 + ncol], start=True, stop=False, skip_group_check=True),
                 reads=[t_KT[hs], t_QT[hs]], writes=[tSp])
            P.op("pe", lambda e: e.matmul(Sp[:, c0:c0 + ncol], lhsT=KP[0:64, i * 128:(i + 1) * 128], rhs=QP[hs][0:64, q0:q0 + ncol], start=False, stop=(nn == 0), skip_group_check=True),
                 reads=[t_KP, t_QP[hs]], writes=[tSp], join=True)
        for bi, (jj, btile) in enumerate(near):
            P.op("pe", lambda e, jj=jj, btile=btile, bi=bi: e.matmul(Sp[:, jj * 128:(jj + 1) * 128], lhsT=C.ident, rhs=btile, start=False, stop=(bi == nn - 1), skip_group_check=True),
                 reads=[tl, C.t_const], writes=[tSp], join=True)
        pp = n % NPT
        ebias = 0.0
        P.op("act", lambda e: e.activation(out=PT[pp][:, c0:c0 + ncol], in_=Sp[:, c0:c0 + ncol], func=AF.Exp, bias=ebias, scale=1.0),
             reads=[tSp, tl], writes=[t_PT[pp]])

    def emit_PV(n, h, qg, t, i):
        hs = h % 2
        jmin = max(0, i - 4 * qg)
        c0 = jmin * 128
        pp = n % NPT
        bo, bd = banks(qg, t)
        last = (i == 4 * qg + 3)
        P.op("pe", lambda e: e.matmul(C.psf[bo][:, c0:512], lhsT=Vh[hs][:, i, :], rhs=PT[pp][:, c0:512], start=(i == 0), stop=last),
             reads=[t_PT[pp], t_Vh[hs]], writes=[C.t_psf[bo]], join=(i > 0))
        P.op("pe", lambda e: e.matmul(C.psf[bd][:, c0:512], lhsT=onesb, rhs=PT[pp][:, c0:512], start=(i == 0), stop=last),
             reads=[t_PT[pp], tl], writes=[C.t_psf[bd]], join=(i > 0))

    pending = []
    deferred = []
    cur_n = [0]
    gser = [0]

    def emit_evac(qg, t):
        par = qg % 2
        bo, bd = banks(qg, t)
        P.op("dve", lambda e: e.tensor_copy(out=Osb[par][t], in_=C.psf[bo]), reads=[C.t_psf[bo]], writes=[t_Osb[par][t]])
        P.op("dve", lambda e: e.tensor_copy(out=Dsb[par][t], in_=C.psf[bd]), reads=[C.t_psf[bd]], writes=[t_Dsb[par][t]])

    def emit_epilogue(n, h, qg):
        hs = h % 2
        par = qg % 2
        ysl = yT[hs][:, qg * 512:(qg + 1) * 512]
        gsl = Gh[hs][:, qg * 512:(qg + 1) * 512]
        O_, D_, tO, tD = Osb[par], Dsb[par], t_Osb[par], t_Dsb[par]
        ts = (0, 1) if diff else (0,)
        last = (qg == NT // 4 - 1)
        deferred.append(lambda: emit_epilogue2(n, h, qg))

    def emit_epilogue2(n, h, qg):
        hs = h % 2
        par = qg % 2
        ysl = yT[hs][:, qg * 512:(qg + 1) * 512]
        gsl = Gh[hs][:, qg * 512:(qg + 1) * 512]
        O_, D_, tO, tD = Osb[par], Dsb[par], t_Osb[par], t_Dsb[par]
        ts = (0, 1) if diff else (0,)
        last = (qg == NT // 4 - 1)
        for t in ts:
            P.op("dve", lambda e, t=t: e.reciprocal(out=D_[t], in_=D_[t]), reads=[tD[t]], writes=[tD[t]])
        if diff:
            for t in (1, 0):
                P.op("dve", lambda e, t=t: e.tensor_tensor(out=O_[t], in0=O_[t], in1=D_[t], op=ALU.mult), reads=[tO[t], tD[t]], writes=[tO[t]])
            P.op("dve", lambda e: e.scalar_tensor_tensor(out=O_[0], in0=O_[1], scalar=lamc[:, 0:1], in1=O_[0], op0=ALU.mult, op1=ALU.add), reads=[tO[0], tO[1], tl], writes=[tO[0]])
            P.op("pool", lambda e: e.tensor_tensor(out=sqb[par], in0=O_[0], in1=O_[0], op=ALU.mult), reads=[tO[0]], writes=[t_sqb[par]])

            def stage2():
                P.op("pe", lambda e: e.matmul(C.psf[6], lhsT=C.ones128, rhs=sqb[par], start=True, stop=True), reads=[t_sqb[par], C.t_const], writes=[C.t_psf[6]])
                P.op("act", lambda e: e.activation(out=rsb[par], in_=C.psf[6], func=AF.Ln, bias=C.epscol, scale=1.0 / 128), reads=[C.t_psf[6], C.t_const], writes=[t_rsb[par]])
                P.op("act", lambda e: e.activation(out=rsb[par], in_=rsb[par], func=AF.Exp, scale=-0.5), reads=[t_rsb[par]], writes=[t_rsb[par]])
                P.op("dve", lambda e: e.scalar_tensor_tensor(out=O_[0], in0=O_[0], scalar=sgcol, in1=rsb[par], op0=ALU.mult, op1=ALU.mult), reads=[tO[0], t_rsb[par], tl], writes=[tO[0]])
                P.op("pool", lambda e: e.tensor_tensor(out=ysl, in0=O_[0], in1=gsl, op=ALU.mult), reads=[tO[0], t_Gh[hs]], writes=[t_yT[hs]], join=True)
                if last:
                    P.op("sync", lambda e: e.dma_start(out=C.YT_d[h], in_=yT[hs]), reads=[t_yT[hs]], writes=[t_yd], dma=True, join=True)
            pending.append((cur_n[0] + 16, stage2, gser[0] - 1))
        else:
            P.op("pool", lambda e: e.tensor_tensor(out=O_[0], in0=O_[0], in1=D_[0], op=ALU.mult), reads=[tO[0], tD[0]], writes=[tO[0]])
            P.op("pool", lambda e: e.tensor_tensor(out=ysl, in0=O_[0], in1=gsl, op=ALU.mult), reads=[tO[0], t_Gh[hs]], writes=[t_yT[hs]], join=True)
            if last:
                P.op("sync", lambda e: e.dma_start(out=C.YT_d[h], in_=yT[hs]), reads=[t_yT[hs]], writes=[t_yd], dma=True, join=True)

    tiles = [(h, qg, t, i) for h in range(H) for qg in range(NT // 4) for t in range(nsub) for i in range(4 * qg + 4)]

    def emit_S_at(n):
        h_, qg_, t_, i_ = tiles[n]
        if qg_ == 0 and t_ == 0 and i_ == 0:
            emit_loads(h_)
        emit_S(n, h_, qg_, t_, i_)

    for n in range(min(LA, len(tiles))):
        emit_S_at(n)
    for n, (h, qg, t, i) in enumerate(tiles):
        if n + LA < len(tiles):
            emit_S_at(n + LA)
        emit_PV(n, h, qg, t, i)
        while pending and pending[0][0] <= n:
            pending.pop(0)[1]()
        cur_n[0] = n
        if i == 4 * qg + 3:
            if t == 0:
                while pending and pending[0][2] <= gser[0] - 2:
                    pending.pop(0)[1]()
            emit_evac(qg, t)
            while deferred:
                deferred.pop(0)()
            if t == nsub - 1:
                emit_epilogue(n, h, qg)
                gser[0] += 1
    while deferred:
        deferred.pop(0)()
    while pending:
        pending.pop(0)[1]()
    A.release(m)
    P.barrier()


def load_col(C, dst, src_vec, n, scale=None):
    P = C.P
    P.op("sync", lambda e: e.dma_start(out=dst[0:n, :], in_=src_vec.rearrange("(d o) -> d o", o=1)), writes=[C.t_lconst], dma=True, join=True)
    if scale is not None:
        P.op("dve", lambda e: e.tensor_scalar(out=dst[0:n, :], in0=dst[0:n, :], scalar1=float(scale), scalar2=None, op0=ALU.mult), reads=[C.t_lconst], writes=[C.t_lconst])


def rope_epilogue(C, ps, tps, gcol, cs, sn, t_cs, dst, t_dst, tmp, t_tmp, delay=2):
    P = C.P
    xf, sq, rs = tmp
    tick(C)
    P.op("dve", lambda e: e.tensor_copy(out=xf[0:64, :], in_=ps[0:64, :]), reads=[tps], writes=[t_tmp[0]])
    P.op("pool", lambda e: e.tensor_tensor(out=sq[0:64, :], in0=xf[0:64, :], in1=xf[0:64, :], op=ALU.mult), reads=[t_tmp[0]], writes=[t_tmp[1]])

    def stage2():
        pss, tpss = C.psf[4 + C.auxcnt % 2], C.t_psf[4 + C.auxcnt % 2]
        C.auxcnt += 1
        P.op("pe", lambda e: e.matmul(pss[0:64, :], lhsT=C.ones64[0:64, 0:64], rhs=sq[0:64, :], start=True, stop=True), reads=[t_tmp[1], C.t_const], writes=[tpss])
        P.op("act", lambda e: e.activation(out=rs[0:64, :], in_=pss[0:64, :], func=AF.Ln, bias=C.epscol[0:64, :], scale=1.0 / 64), reads=[tpss, C.t_const], writes=[t_tmp[2]])
        P.op("act", lambda e: e.activation(out=rs[0:64, :], in_=rs[0:64, :], func=AF.Exp, scale=-0.5), reads=[t_tmp[2]], writes=[t_tmp[2]])
        P.op("dve", lambda e: e.scalar_tensor_tensor(out=xf[0:64, :], in0=xf[0:64, :], scalar=gcol[0:64, :], in1=rs[0:64, :], op0=ALU.mult, op1=ALU.mult),
             reads=[t_tmp[0], t_tmp[2], C.t_lconst], writes=[t_tmp[0]])

    def stage3():
        pr, tpr = C.psf[4 + C.auxcnt % 2], C.t_psf[4 + C.auxcnt % 2]
        C.auxcnt += 1
        P.op("pe", lambda e: e.matmul(pr[0:64, :], lhsT=C.rotm[0:64, 0:64], rhs=xf[0:64, :], start=True, stop=True), reads=[t_tmp[0], C.t_const], writes=[tpr])
        P.op("dve", lambda e: e.tensor_tensor(out=sq[0:64, :], in0=pr[0:64, :], in1=sn, op=ALU.mult), reads=[tpr, t_cs], writes=[t_tmp[1]])
        P.op("pool", lambda e: e.tensor_tensor(out=xf[0:64, :], in0=xf[0:64, :], in1=cs, op=ALU.mult), reads=[t_tmp[0], t_cs], writes=[t_tmp[0]])
        P.op("pool", lambda e: e.tensor_tensor(out=dst, in0=xf[0:64, :], in1=sq[0:64, :], op=ALU.add), reads=[t_tmp[0], t_tmp[1]], writes=[t_dst], join=True)
    defer(C, delay, stage2)
    defer(C, 2 * delay, stage3)


def g_block(C, hT, hT_toks, wt_s, t_wt_s, tok0, c0, pcnt, gst, t_gst, gcnt, tpb=TPB):
    P = C.P
    t_gd = C.t_gd
    for i in range(tpb):
        pb = pcnt % 4; pcnt += 1
        ps, tps = C.psf[pb], C.t_psf[pb]
        for k in range(16):
            P.op("pe", lambda e, ps=ps, k=k, i=i: e.matmul(ps, lhsT=hT[:, k, i * 128:(i + 1) * 128], rhs=wt_s[:, k, :], start=(k == 0), stop=(k == 15)),
                 reads=[t_wt_s, hT_toks[i]], writes=[tps], join=(k > 0))
        r0 = tok0 + i * 128
        g4 = i % 4
        if g4 == 0:
            gs = gcnt % 2; gcnt += 1
        P.op("act", lambda e, ps=ps, gs=gs, g4=g4: e.activation(out=gst[gs][:, g4, :], in_=ps, func=AF.Silu), reads=[tps], writes=[t_gst[gs]], join=True)
        if g4 == 3:
            rr = r0 - 3 * 128
            P.op("sync", lambda e, gs=gs, rr=rr, c0=c0: e.dma_start(out=C.G_d[rr:rr + 512, c0:c0 + 512].rearrange("(a p) c -> p a c", p=128), in_=gst[gs]),
                 reads=[t_gst[gs]], writes=[t_gd], dma=True, join=True)
    return pcnt, gcnt


def layer3_proj1(C, x_src, W):
    P, A = C.P, C.A
    m = A.mark()
    TB = 1024; TPB = TB // 128; NTB = S // TB
    hT = A.alloc([16, TB], BF16); hT_toks = P.toks(TPB, "hT")
    wt = [A.alloc([16, 512], BF16) for _ in range(2)]; t_wt = P.toks(2, "wt")
    wkp = A.alloc([16, 64], BF16); t_wkp = P.tok("wkp")
    cst = [A.alloc([4, TB], BF16) for _ in range(2)]; t_cst = P.toks(2, "cst")
    cf = [A.alloc([512], F32) for _ in range(4)]; t_cf = P.toks(4, "cf")
    sq = [A.alloc([512], F32) for _ in range(2)]; t_sq = P.toks(2, "sq")
    rs = A.alloc([512], F32); t_rs = P.tok("rs")
    tmp = [A.alloc([512], F32) for _ in range(3)]; t_tmp = P.toks(3, "tmp")
    kpst = A.alloc([TB], BF16); t_kpst = P.tok("kpst")
    cs = A.alloc([TB], F32); sn = A.alloc([TB], F32); t_cs = P.tok("cs")
    gst = [A.alloc([TB], F32) for _ in range(2)]; t_gst = P.toks(2, "gst")
    glat = A.alloc([2, 4], F32); gkp = A.alloc([1], F32)
    C.t_gd = P.tok("Gd"); t_cd = P.tok("CQd"); t_kd = P.tok("KPEd")
    tl = C.t_lconst
    for mm_ in range(4):
        load_col(C, glat[:, 0, mm_:mm_ + 1], W["d_q_lat_g"][0, mm_ * 128:(mm_ + 1) * 128], 128)
        load_col(C, glat[:, 1, mm_:mm_ + 1], W["d_kv_lat_g"][0, mm_ * 128:(mm_ + 1) * 128], 128)
    load_col(C, gkp, W["d_qk_g"][0, 1, 128:192], 64)
    wcnt = 0; pcnt = 0; gcnt = 0; scnt = 0
    cols3 = [0, 512, 1088, 1600, 2112, 2624]
    pf = WPrefetch(C, wt, t_wt, [(lambda d, t, c0=c0: load_w_block(C, d, t, W["d_w_in"][0], c0, 512)) for _tb in range(NTB) for c0 in cols3])
    for tb in range(NTB):
        tok0 = tb * TB
        phase_norm_T(C, x_src, W["norm_g"][C.layer, :], tok0, hT, hT_toks, TB)
        P.op("sync", lambda e, tok0=tok0: e.dma_start(out=cs[0:64, :], in_=C.rope_in[0, :, tok0:tok0 + TB]), writes=[t_cs], dma=True)
        P.op("sync", lambda e, tok0=tok0: e.dma_start(out=sn[0:64, :], in_=C.rope_in[1, :, tok0:tok0 + TB]), writes=[t_cs], dma=True, join=True)
        for kind in range(2):
            ws = pf.next()
            for tq in range(TB // 512):
                pss, tpss = C.psf[4 + C.auxcnt % 2], C.t_psf[4 + C.auxcnt % 2]
                C.auxcnt += 1
                for mm_ in range(4):
                    pb = pcnt % 4; pcnt += 1
                    ps, tps = C.psf[pb], C.t_psf[pb]
                    for k in range(16):
                        P.op("pe", lambda e, ps=ps, k=k, ws=ws, mm_=mm_, tq=tq: e.matmul(ps, lhsT=wt[ws][:, k, mm_ * 128:(mm_ + 1) * 128], rhs=hT[:, k, tq * 512:(tq + 1) * 512], start=(k == 0), stop=(k == 15)),
                             reads=[t_wt[ws]] + hT_toks[tq * 4:(tq + 1) * 4], writes=[tps], join=(k > 0))
                    P.op("act", lambda e, ps=ps, mm_=mm_: e.activation(out=cf[mm_], in_=ps, func=AF.Copy), reads=[tps], writes=[t_cf[mm_]])
                    ss = scnt % 2; scnt += 1
                    P.op("act", lambda e, ps=ps, ss=ss: e.activation(out=sq[ss], in_=ps, func=AF.Square), reads=[tps], writes=[t_sq[ss]])
                    P.op("pe", lambda e, pss=pss, ss=ss, mm_=mm_: e.matmul(pss, lhsT=C.ones128, rhs=sq[ss], start=(mm_ == 0), stop=(mm_ == 3)), reads=[t_sq[ss], C.t_const], writes=[tpss], join=(mm_ > 0))
                P.op("act", lambda e, pss=pss: e.activation(out=rs, in_=pss, func=AF.Ln, bias=C.epscol, scale=1.0 / 512), reads=[tpss, C.t_const], writes=[t_rs])
                P.op("act", lambda e: e.activation(out=rs, in_=rs, func=AF.Exp, scale=-0.5), reads=[t_rs], writes=[t_rs])
                for mm_ in range(4):
                    P.op("dve", lambda e, mm_=mm_, kind=kind, tq=tq: e.scalar_tensor_tensor(out=cst[kind][:, mm_, tq * 512:(tq + 1) * 512], in0=cf[mm_], scalar=glat[:, kind, mm_:mm_ + 1], in1=rs, op0=ALU.mult, op1=ALU.mult),
                         reads=[t_cf[mm_], t_rs, tl], writes=[t_cst[kind]], join=True)
            dd = C.CQ_d if kind == 0 else C.CKV_d
            for mm_ in range(4):
                P.op("sync", lambda e, dd=dd, mm_=mm_, kind=kind, tok0=tok0: e.dma_start(out=dd[mm_, :, tok0:tok0 + TB], in_=cst[kind][:, mm_, :]), reads=[t_cst[kind]], writes=[t_cd], dma=True, join=True)
        load_w_block(C, wkp, t_wkp, W["d_w_in"][0], 1024, 64)
        for tq in range(TB // 512):
            pb = pcnt % 4; pcnt += 1
            ps, tps = C.psf[pb], C.t_psf[pb]
            for k in range(16):
                P.op("pe", lambda e, ps=ps, k=k, tq=tq: e.matmul(ps[0:64, :], lhsT=wkp[:, k, 0:64], rhs=hT[:, k, tq * 512:(tq + 1) * 512], start=(k == 0), stop=(k == 15)),
                     reads=[t_wkp] + hT_toks[tq * 4:(tq + 1) * 4], writes=[tps], join=(k > 0))
            rope_epilogue(C, ps, tps, gkp, cs[0:64, tq * 512:(tq + 1) * 512], sn[0:64, tq * 512:(tq + 1) * 512], t_cs, kpst[0:64, tq * 512:(tq + 1) * 512], t_kpst, tmp, t_tmp, delay=0)
        P.op("sync", lambda e, tok0=tok0: e.dma_start(out=C.KPE_d[:, tok0:tok0 + TB], in_=kpst[0:64, :]), reads=[t_kpst], writes=[t_kd], dma=True, join=True)
        for cb in range(4):
            ws = pf.next()
            pcnt, gcnt = gT_block(C, hT, hT_toks, wt[ws], t_wt[ws], tok0, cb * 4, TB, pcnt, gst, t_gst, gcnt)
    A.release(m)
    P.barrier()


def layer3_proj2(C, W):
    P, A = C.P, C.A
    m = A.mark()
    H = 16
    cq = A.alloc([4, TB], BF16); ckv = A.alloc([4, TB], BF16); t_cq = P.tok("cq"); t_ckv = P.tok("ckv")
    wuq = A.alloc([4, 3072], BF16); wkn = A.alloc([4, 2048], BF16); wv = A.alloc([4, 2048], BF16); t_w = P.tok("wup")
    qst = [A.alloc([TB], BF16) for _ in range(2)]; t_qst = P.toks(2, "qst")
    kst = [A.alloc([TB], BF16) for _ in range(2)]; t_kst = P.toks(2, "kst")
    qpst = [A.alloc([TB], BF16) for _ in range(2)]; t_qpst = P.toks(2, "qpst")
    NTMP = 4
    tmp = [[A.alloc([512], F32) for _ in range(3)] for _ in range(NTMP)]; t_tmp = [P.toks(3, "tmp") for _ in range(NTMP)]
    cs = A.alloc([TB], F32); sn = A.alloc([TB], F32); t_cs = P.tok("cs")
    vst = [A.alloc([8, 512], BF16) for _ in range(2)]; t_vst = P.toks(2, "vst")
    gqn = A.alloc([1], F32); gkn = A.alloc([1], F32); gqp = A.alloc([1], F32)
    t_qd = P.tok("QTd"); t_kd = P.tok("KTd"); t_qpd = P.tok("QPEd"); t_vd = P.tok("Vd")
    sc = 192.0 ** -0.5
    load_col(C, gqn, W["d_qk_g"][0, 0, 0:128], 128, sc)
    load_col(C, gkn, W["d_qk_g"][0, 1, 0:128], 128)
    load_col(C, gqp, W["d_qk_g"][0, 0, 128:192], 64, sc)
    wq_v = W["d_w_uq"][0].rearrange("(k p) c -> p k c", p=128)
    wkv_v = W["d_w_ukv"][0].rearrange("(k p) (h c) -> p k h c", p=128, c=256)
    for k in range(4):
        P.op("pool", lambda e, k=k: e.dma_start(out=wuq[:, k, :], in_=wq_v[:, k, :]), writes=[t_w], dma=True, join=True)
        P.op("pool", lambda e, k=k: e.dma_start(out=wkn[:, k, :].rearrange("p (h c) -> p h c", c=128), in_=wkv_v[:, k, :, 0:128]), writes=[t_w], dma=True, join=True)
        P.op("pool", lambda e, k=k: e.dma_start(out=wv[:, k, :].rearrange("p (h c) -> p h c", c=128), in_=wkv_v[:, k, :, 128:256]), writes=[t_w], dma=True, join=True)
    pcnt = 0; tcnt = 0; vcnt = 0
    for tb in range(NTB):
        tok0 = tb * TB
        for k in range(4):
            P.op("sync", lambda e, k=k, tok0=tok0: e.dma_start(out=cq[:, k, :], in_=C.CQ_d[k, :, tok0:tok0 + TB]), writes=[t_cq], dma=True, join=(k > 0))
            P.op("sync", lambda e, k=k, tok0=tok0: e.dma_start(out=ckv[:, k, :], in_=C.CKV_d[k, :, tok0:tok0 + TB]), writes=[t_ckv], dma=True, join=(k > 0))
        P.op("sync", lambda e, tok0=tok0: e.dma_start(out=cs[0:64, :], in_=C.rope_in[0, :, tok0:tok0 + TB]), writes=[t_cs], dma=True)
        P.op("sync", lambda e, tok0=tok0: e.dma_start(out=sn[0:64, :], in_=C.rope_in[1, :, tok0:tok0 + TB]), writes=[t_cs], dma=True, join=True)
        for h in range(H):
            hs = h % 2
            for which in range(3):
                for tq in range(TB // 512):
                    pb = pcnt % 4; pcnt += 1
                    ps, tps = C.psf[pb], C.t_psf[pb]
                    for k in range(4):
                        if which == 0:
                            lhs, rhs_, rt, M = wuq[:, k, h * 192:h * 192 + 128], cq[:, k, tq * 512:(tq + 1) * 512], t_cq, 128
                        elif which == 1:
                            lhs, rhs_, rt, M = wkn[:, k, h * 128:(h + 1) * 128], ckv[:, k, tq * 512:(tq + 1) * 512], t_ckv, 128
                        else:
                            lhs, rhs_, rt, M = wuq[:, k, h * 192 + 128:h * 192 + 192], cq[:, k, tq * 512:(tq + 1) * 512], t_cq, 64
                        P.op("pe", lambda e, ps=ps, k=k, lhs=lhs, rhs_=rhs_, M=M: e.matmul(ps[0:M, :], lhsT=lhs, rhs=rhs_, start=(k == 0), stop=(k == 3)),
                             reads=[t_w, rt], writes=[tps], join=(k > 0))
                    ts_ = tcnt % NTMP; tcnt += 1
                    if which == 0:
                        qk_norm_epilogue(C, ps, tps, gqn, qst[hs][:, tq * 512:(tq + 1) * 512], t_qst[hs], tmp[ts_], t_tmp[ts_], 128)
                    elif which == 1:
                        qk_norm_epilogue(C, ps, tps, gkn, kst[hs][:, tq * 512:(tq + 1) * 512], t_kst[hs], tmp[ts_], t_tmp[ts_], 128)
                    else:
                        rope_epilogue(C, ps, tps, gqp, cs[0:64, tq * 512:(tq + 1) * 512], sn[0:64, tq * 512:(tq + 1) * 512], t_cs, qpst[hs][0:64, tq * 512:(tq + 1) * 512], t_qpst[hs], tmp[ts_], t_tmp[ts_])
            def outs(h=h, hs=hs, tok0=tok0):
                P.op("sync", lambda e: e.dma_start(out=C.QT_d[h, :, tok0:tok0 + TB], in_=qst[hs]), reads=[t_qst[hs]], writes=[t_qd], dma=True, join=True)
                P.op("sync", lambda e: e.dma_start(out=C.KT_d[h, :, tok0:tok0 + TB], in_=kst[hs]), reads=[t_kst[hs]], writes=[t_kd], dma=True, join=True)
                P.op("sync", lambda e: e.dma_start(out=C.QPE_d[h, :, tok0:tok0 + TB], in_=qpst[hs][0:64, :]), reads=[t_qpst[hs]], writes=[t_qpd], dma=True, join=True)
            defer(C, 4, outs)
        flush(C)
        for n in range(4):
            for i in range(TPB):
                pb = pcnt % 4; pcnt += 1
                ps, tps = C.psf[pb], C.t_psf[pb]
                for k in range(4):
                    P.op("pe", lambda e, ps=ps, k=k, i=i, n=n: e.matmul(ps, lhsT=ckv[:, k, i * 128:(i + 1) * 128], rhs=wv[:, k, n * 512:(n + 1) * 512], start=(k == 0), stop=(k == 3)),
                         reads=[t_w, t_ckv], writes=[tps], join=(k > 0))
                g8 = i % 8
                if g8 == 0:
                    vs = vcnt % 2; vcnt += 1
                P.op("dve", lambda e, ps=ps, vs=vs, g8=g8: e.tensor_copy(out=vst[vs][:, g8, :], in_=ps), reads=[tps], writes=[t_vst[vs]], join=True)
                if g8 == 7:
                    rr = tok0 + (i - 7) * 128
                    P.op("sync", lambda e, vs=vs, rr=rr, n=n: e.dma_start(out=C.V_d[rr:rr + 1024, n * 512:(n + 1) * 512].rearrange("(a p) c -> p a c", p=128), in_=vst[vs]),
                         reads=[t_vst[vs]], writes=[t_vd], dma=True, join=True)
    A.release(m)
    P.barrier()


def layer2_all(C, x_src, W):
    P, A = C.P, C.A
    m = A.mark()
    TB = 1024; TPB = TB // 128; NTB = S // TB
    hT = A.alloc([16, TB], BF16); hT_toks = P.toks(TPB, "hT")
    wt = [A.alloc([16, 512], BF16) for _ in range(2)]; t_wt = P.toks(2, "wt")
    wrg = A.alloc([8, 2, 256], BF16); wig = A.alloc([8, 2, 256], BF16); t_wg = P.tok("wg")
    stage = A.alloc([128], F32); cols = A.alloc([8, 16], F32)
    halo = A.alloc([16, 3], F32); hprev = A.alloc([16], F32); t_halo = P.tok("halo"); t_hprev = P.tok("hprev")
    ubuf = [A.alloc([TB + 8], F32) for _ in range(2)]; t_ubuf = P.toks(2, "ubuf")
    xc = [A.alloc([TB], F32) for _ in range(2)]; t_xc = P.toks(2, "xc")
    xcb = [A.alloc([TB], BF16) for _ in range(2)]; t_xcb = P.toks(2, "xcb")
    sg = [A.alloc([TB], F32) for _ in range(2)]; t_sg = P.toks(2, "sg")
    rg = [A.alloc([TB], F32) for _ in range(2)]; t_rg = P.toks(2, "rg")
    ig = [A.alloc([TB], F32) for _ in range(2)]; t_ig = P.toks(2, "ig")
    abuf = A.alloc([TB], F32); a2buf = A.alloc([TB], F32); xinb = A.alloc([TB], F32); hh = A.alloc([TB], F32)
    t_a = P.tok("a"); t_a2 = P.tok("a2"); t_xin = P.tok("xin"); t_hh = P.tok("hh")
    yst = [A.alloc([TB], BF16) for _ in range(2)]; t_yst = P.toks(2, "yst")
    t_yd = P.tok("YTd")
    tl = C.t_lconst
    vecs = [W["c_conv_w"][0, 0], W["c_conv_w"][0, 1], W["c_conv_w"][0, 2], W["c_conv_w"][0, 3], W["c_conv_b"][0], W["c_b_rgate"][0], W["c_b_igate"][0], W["c_lambda"][0]]
    for v, vec in enumerate(vecs):
        P.op("sync", lambda e, v=v, vec=vec: e.dma_start(out=stage[v * 16:(v + 1) * 16, :], in_=vec.rearrange("(t p) -> t p", p=128)), writes=[tl], dma=True, join=True)
    ps0, tps0 = C.psf[4], C.t_psf[4]
    P.op("pe", lambda e: e.matmul(ps0[:, 0:128], lhsT=stage, rhs=C.identf, start=True, stop=True), reads=[tl, C.t_const], writes=[tps0])
    P.op("dve", lambda e: e.tensor_copy(out=cols, in_=ps0[:, 0:128].rearrange("p (v t) -> p v t", v=8)), reads=[tps0], writes=[tl])
    P.op("act", lambda e: e.activation(out=cols[:, 7, :], in_=cols[:, 7, :], func=AF.Exp, scale=-1.0), reads=[tl], writes=[tl])
    P.op("act", lambda e: e.activation(out=cols[:, 7, :], in_=cols[:, 7, :], func=AF.Ln, bias=1.0, scale=1.0), reads=[tl], writes=[tl])
    P.op("dve", lambda e: e.tensor_scalar(out=cols[:, 7, :], in0=cols[:, 7, :], scalar1=-8.0, scalar2=None, op0=ALU.mult), reads=[tl], writes=[tl])
    for n in range(8):
        P.op("pool", lambda e, n=n: e.dma_start(out=wrg[:, n], in_=W["c_w_rgate"][0, n].rearrange("(c p) e -> p c e", p=128)), writes=[t_wg], dma=True, join=True)
        P.op("pool", lambda e, n=n: e.dma_start(out=wig[:, n], in_=W["c_w_igate"][0, n].rearrange("(c p) e -> p c e", p=128)), writes=[t_wg], dma=True, join=True)
    wcnt = 0; pcnt = 0

    def ld2(d, t, n):
        load_w_block(C, d[:, :, 0:256], t, W["c_w_in"][0], n * 256, 256)
        load_w_block(C, d[:, :, 256:512], t, W["c_w_in"][0], 2048 + n * 256, 256)
    pf = WPrefetch(C, wt, t_wt, [(lambda d, t, n=n: ld2(d, t, n)) for _tb in range(NTB) for n in range(8)])
    for tb in range(NTB):
        tok0 = tb * TB
        phase_norm_T(C, x_src, W["norm_g"][C.layer, :], tok0, hT, hT_toks, TB)
        for n in range(8):
            ws = pf.next()
            for c in range(2):
                tile = n * 2 + c
                if tb == 0:
                    P.op("pool", lambda e, c=c: e.memset(ubuf[c][:, 0:3], 0.0), writes=[t_ubuf[c]])
                else:
                    P.op("pool", lambda e, c=c, tile=tile: e.tensor_copy(out=ubuf[c][:, 0:3], in_=halo[:, tile, :]), reads=[t_halo], writes=[t_ubuf[c]])
                for tq in range(TB // 512):
                    pb = pcnt % 4; pcnt += 1
                    ps, tps = C.psf[pb], C.t_psf[pb]
                    for k in range(16):
                        P.op("pe", lambda e, ps=ps, k=k, ws=ws, c=c, tq=tq: e.matmul(ps, lhsT=wt[ws][:, k, c * 128:(c + 1) * 128], rhs=hT[:, k, tq * 512:(tq + 1) * 512], start=(k == 0), stop=(k == 15)),
                             reads=[t_wt[ws]] + hT_toks[tq * 4:(tq + 1) * 4], writes=[tps], join=(k > 0))
                    P.op("act", lambda e, ps=ps, c=c, tq=tq: e.activation(out=ubuf[c][:, 3 + tq * 512:3 + (tq + 1) * 512], in_=ps, func=AF.Copy), reads=[tps], writes=[t_ubuf[c]], join=True)
                P.op("pool", lambda e, c=c, tile=tile: e.tensor_copy(out=halo[:, tile, :], in_=ubuf[c][:, TB:TB + 3]), reads=[t_ubuf[c]], writes=[t_halo], join=True)
                P.op("dve", lambda e, c=c, tile=tile: e.tensor_scalar(out=xc[c], in0=ubuf[c][:, 3:3 + TB], scalar1=cols[:, 3, tile:tile + 1], scalar2=cols[:, 4, tile:tile + 1], op0=ALU.mult, op1=ALU.add),
                     reads=[t_ubuf[c], tl], writes=[t_xc[c]])
                for tau in (2, 1, 0):
                    P.op("dve", lambda e, c=c, tile=tile, tau=tau: e.scalar_tensor_tensor(out=xc[c], in0=ubuf[c][:, tau:tau + TB], scalar=cols[:, tau, tile:tile + 1], in1=xc[c], op0=ALU.mult, op1=ALU.add),
                         reads=[t_ubuf[c], tl, t_xc[c]], writes=[t_xc[c]])
                P.op("pool", lambda e, c=c: e.tensor_copy(out=xcb[c], in_=xc[c]), reads=[t_xc[c]], writes=[t_xcb[c]])
                for tq in range(TB // 512):
                    pb = pcnt % 4; pcnt += 1
                    ps, tps = C.psf[pb], C.t_psf[pb]
                    for k in range(16):
                        P.op("pe", lambda e, ps=ps, k=k, ws=ws, c=c, tq=tq: e.matmul(ps, lhsT=wt[ws][:, k, 256 + c * 128:256 + (c + 1) * 128], rhs=hT[:, k, tq * 512:(tq + 1) * 512], start=(k == 0), stop=(k == 15)),
                             reads=[t_wt[ws]] + hT_toks[tq * 4:(tq + 1) * 4], writes=[tps], join=(k > 0))
                    P.op("act", lambda e, ps=ps, c=c, tq=tq: e.activation(out=sg[c][:, tq * 512:(tq + 1) * 512], in_=ps, func=AF.Silu), reads=[tps], writes=[t_sg[c]], join=True)
            for ce in range(2):
                tile = n * 2 + ce
                for (wg, bidx, dstb, tdst) in ((wrg, 5, rg, t_rg), (wig, 6, ig, t_ig)):
                    for tq in range(TB // 512):
                        pb = pcnt % 4; pcnt += 1
                        ps, tps = C.psf[pb], C.t_psf[pb]
                        for cc in range(2):
                            P.op("pe", lambda e, ps=ps, wg=wg, cc=cc, ce=ce, tq=tq, n=n: e.matmul(ps, lhsT=wg[:, n, cc, ce * 128:(ce + 1) * 128], rhs=xcb[cc][:, tq * 512:(tq + 1) * 512], start=(cc == 0), stop=(cc == 1)),
                                 reads=[t_wg, t_xcb[cc]], writes=[tps], join=(cc > 0))
                        P.op("act", lambda e, ps=ps, dstb=dstb, ce=ce, tq=tq, bidx=bidx, tile=tile: e.activation(out=dstb[ce][:, tq * 512:(tq + 1) * 512], in_=ps, func=AF.Sigmoid, bias=cols[:, bidx, tile:tile + 1], scale=1.0),
                             reads=[tps, tl], writes=[tdst[ce]], join=True)
                P.op("act", lambda e, ce=ce, tile=tile: e.activation(out=abuf, in_=rg[ce], func=AF.Exp, scale=cols[:, 7, tile:tile + 1]), reads=[t_rg[ce], tl], writes=[t_a])
                P.op("pool", lambda e: e.tensor_tensor(out=a2buf, in0=abuf, in1=abuf, op=ALU.mult), reads=[t_a], writes=[t_a2])
                P.op("act", lambda e: e.activation(out=a2buf, in_=a2buf, func=AF.Sqrt, bias=1.0, scale=-1.0), reads=[t_a2], writes=[t_a2])
                P.op("pool", lambda e, ce=ce: e.tensor_tensor(out=xinb, in0=ig[ce], in1=xc[ce], op=ALU.mult), reads=[t_ig[ce], t_xc[ce]], writes=[t_xin])
                P.op("dve", lambda e: e.tensor_tensor(out=xinb, in0=xinb, in1=a2buf, op=ALU.mult), reads=[t_xin, t_a2], writes=[t_xin])
                init = 0.0 if tb == 0 else hprev[:, tile:tile + 1]
                P.op("dve", lambda e, init=init: e.tensor_tensor_scan(out=hh, data0=abuf, data1=xinb, initial=init, op0=ALU.mult, op1=ALU.add), reads=[t_a, t_xin, t_hprev], writes=[t_hh])
                P.op("pool", lambda e, tile=tile: e.tensor_copy(out=hprev[:, tile:tile + 1], in_=hh[:, TB - 1:TB]), reads=[t_hh], writes=[t_hprev])
                P.op("pool", lambda e, ce=ce: e.tensor_tensor(out=yst[ce], in0=hh, in1=sg[ce], op=ALU.mult), reads=[t_hh, t_sg[ce]], writes=[t_yst[ce]])
                P.op("sync", lambda e, ce=ce, tile=tile, tok0=tok0: e.dma_start(out=C.YT_d[tile, :, tok0:tok0 + TB], in_=yst[ce]), reads=[t_yst[ce]], writes=[t_yd], dma=True, join=True)
    A.release(m)
    P.barrier()


def layer1_all(C, x_src, W):
    P, A = C.P, C.A
    m0 = A.mark()
    dec = A.alloc([8, 64], F32); t_dec = P.tok("dec")
    m = A.mark()
    TB = 1024; TPB = TB // 128; NTB = S // TB
    hT = A.alloc([16, TB], BF16); hT_toks = P.toks(TPB, "hT")
    wt = [A.alloc([16, 512], BF16) for _ in range(2)]; t_wt = P.toks(2, "wt")
    wlr = A.alloc([16, 16], BF16); t_wlr = P.tok("wlr")
    lrT = A.alloc([TB], F32); t_lrT = P.tok("lrT")
    wga = A.alloc([1024], F32)
    TU = A.alloc([128], F32); CI = A.alloc([2], F32)
    qst = [A.alloc([TB], BF16) for _ in range(2)]; t_qst = P.toks(2, "qst")
    ebuf = [A.alloc([512], F32) for _ in range(2)]; t_eb = P.toks(2, "ebuf")
    wbuf = [A.alloc([512], F32) for _ in range(2)]; t_wb = P.toks(2, "wbuf")
    kst = [A.alloc([8, 512], BF16) for _ in range(2)]; t_kst = P.toks(2, "kst")
    vst = [A.alloc([8, 512], BF16) for _ in range(2)]; t_vst = P.toks(2, "vst")
    gst = [A.alloc([4, 512], F32) for _ in range(2)]; t_gst = P.toks(2, "gst")
    C.t_gd = P.tok("Gd"); t_qd = P.tok("QTd"); t_kd = P.tok("KPd"); t_vd = P.tok("Vd")
    tl = C.t_lconst
    Wi = W["b_w_in"][0]
    P.op("pool", lambda e: e.memset(wga[0:32, :], 0.0), writes=[tl])
    P.op("sync", lambda e: e.dma_start(out=wga[0:16, :], in_=W["b_w_gate"][0]), writes=[tl], dma=True)
    P.op("sync", lambda e: e.dma_start(out=wga[16:17, :], in_=W["b_gate_bias"][0:1, :]), writes=[tl], dma=True, join=True)
    P.op("sync", lambda e: e.dma_start(out=TU, in_=C.TU_d), writes=[tl], dma=True, join=True)
    P.op("sync", lambda e: e.dma_start(out=CI, in_=C.CI_d), writes=[tl], dma=True, join=True)
    P.op("pool", lambda e: e.memset(lrT[0:32, :], 1.0), writes=[t_lrT])
    wcnt = 0; pcnt = 0; gcnt = 0; vcnt = 0; qcnt = 0; ecnt = 0; kcnt = 0
    pf = WPrefetch(C, wt, t_wt, [(lambda d, t, cb=cb: load_w_block(C, d, t, Wi, cb * 512, 512)) for _tb in range(NTB) for cb in range(12)])
    for tb in range(NTB):
        tok0 = tb * TB
        phase_norm_T(C, x_src, W["norm_g"][C.layer, :], tok0, hT, hT_toks, TB)
        load_w_block(C, wlr, t_wlr, Wi, 6144, 16)
        for tq in range(TB // 512):
            pb = pcnt % 4; pcnt += 1
            ps, tps = C.psf[pb], C.t_psf[pb]
            for k in range(16):
                P.op("pe", lambda e, ps=ps, k=k, tq=tq: e.matmul(ps[0:16, :], lhsT=wlr[:, k, 0:16], rhs=hT[:, k, tq * 512:(tq + 1) * 512], start=(k == 0), stop=(k == 15)),
                     reads=[t_wlr] + hT_toks[tq * 4:(tq + 1) * 4], writes=[tps], join=(k > 0))
            P.op("act", lambda e, ps=ps, tq=tq: e.activation(out=lrT[0:16, tq * 512:(tq + 1) * 512], in_=ps[0:16, :], func=AF.Copy), reads=[tps], writes=[t_lrT], join=True)
        for cb in range(12):
            ws = pf.next()
            if cb < 2:
                for mm_ in range(4):
                    qt = cb * 4 + mm_
                    qs = qcnt % 2; qcnt += 1
                    for tq in range(TB // 512):
                        pb = pcnt % 4; pcnt += 1
                        ps, tps = C.psf[pb], C.t_psf[pb]
                        for k in range(16):
                            P.op("pe", lambda e, ps=ps, k=k, ws=ws, mm_=mm_, tq=tq: e.matmul(ps, lhsT=wt[ws][:, k, mm_ * 128:(mm_ + 1) * 128], rhs=hT[:, k, tq * 512:(tq + 1) * 512], start=(k == 0), stop=(k == 15)),
                                 reads=[t_wt[ws]] + hT_toks[tq * 4:(tq + 1) * 4], writes=[tps], join=(k > 0))
                        P.op("act", lambda e, ps=ps, qs=qs, tq=tq: e.activation(out=qst[qs][:, tq * 512:(tq + 1) * 512], in_=ps, func=AF.Copy, scale=1.0 / 16.0), reads=[tps], writes=[t_qst[qs]], join=True)
                    P.op("sync", lambda e, qt=qt, qs=qs, tok0=tok0: e.dma_start(out=C.QT_d[qt, :, tok0:tok0 + TB], in_=qst[qs]), reads=[t_qst[qs]], writes=[t_qd], dma=True, join=True)
            elif cb < 4:
                kb = cb - 2
                ks = kcnt % 2; kcnt += 1
                for i in range(TPB):
                    pb = pcnt % 4; pcnt += 1
                    ps, tps = C.psf[pb], C.t_psf[pb]
                    for k in range(16):
                        P.op("pe", lambda e, ps=ps, k=k, ws=ws, i=i: e.matmul(ps, lhsT=hT[:, k, i * 128:(i + 1) * 128], rhs=wt[ws][:, k, :], start=(k == 0), stop=(k == 15)),
                             reads=[t_wt[ws], hT_toks[i]], writes=[tps], join=(k > 0))
                    es = ecnt % 2; ecnt += 1
                    pz, tpz = C.psf[4], C.t_psf[4]
                    P.op("pe", lambda e, pz=pz, i=i, kb=kb: e.matmul(pz, lhsT=lrT[0:32, i * 128:(i + 1) * 128], rhs=wga[0:32, kb * 512:(kb + 1) * 512], start=True, stop=True),
                         reads=[t_lrT, tl], writes=[tpz])
                    P.op("act", lambda e, pz=pz, es=es: e.activation(out=ebuf[es], in_=pz, func=AF.Exp, scale=-1.0), reads=[tpz], writes=[t_eb[es]])
                    P.op("act", lambda e, es=es: e.activation(out=ebuf[es], in_=ebuf[es], func=AF.Ln, bias=1.0, scale=1.0), reads=[t_eb[es]], writes=[t_eb[es]])
                    pr_, tpr = C.psf[5], C.t_psf[5]
                    P.op("pe", lambda e, pr_=pr_, es=es: e.matmul(pr_, lhsT=TU, rhs=ebuf[es], start=True, stop=True), reads=[t_eb[es], tl], writes=[tpr])
                    P.op("act", lambda e, pr_=pr_, es=es: e.activation(out=wbuf[es], in_=pr_, func=AF.Exp), reads=[tpr], writes=[t_wb[es]])
                    P.op("dve", lambda e, ps=ps, es=es, ks=ks, i=i: e.tensor_tensor(out=kst[ks][:, i, :], in0=ps, in1=wbuf[es], op=ALU.mult), reads=[tps, t_wb[es]], writes=[t_kst[ks]], join=True)
                    for dt in range(4):
                        P.op("pe", lambda e, pz=pz, es=es, dt=dt: e.matmul(pz[:, dt * 2:dt * 2 + 2], lhsT=ebuf[es][:, dt * 128:(dt + 1) * 128], rhs=CI, start=(dt == 0), stop=(dt == 3), skip_group_check=True),
                             reads=[t_eb[es], tl], writes=[tpz], join=(dt > 0))
                    ch0 = (tok0 + i * 128) // 64
                    P.op("act", lambda e, pz=pz, kb=kb, ch0=ch0: e.activation(out=dec[:, kb * 4:(kb + 1) * 4, ch0:ch0 + 2], in_=pz[:, 0:8].rearrange("p (a b) -> p a b", a=4), func=AF.Exp), reads=[tpz], writes=[t_dec], join=True)
                P.op("sync", lambda e, ks=ks, tok0=tok0, kb=kb: e.dma_start(out=C.KP_d[tok0:tok0 + TB, kb * 512:(kb + 1) * 512].rearrange("(a p) c -> p a c", p=128), in_=kst[ks]),
                     reads=[t_kst[ks]], writes=[t_kd], dma=True, join=True)
            elif cb < 8:
                c0 = (cb - 4) * 512
                vs = vcnt % 2; vcnt += 1
                for i in range(TPB):
                    pb = pcnt % 4; pcnt += 1
                    ps, tps = C.psf[pb], C.t_psf[pb]
                    for k in range(16):
                        P.op("pe", lambda e, ps=ps, k=k, ws=ws, i=i: e.matmul(ps, lhsT=hT[:, k, i * 128:(i + 1) * 128], rhs=wt[ws][:, k, :], start=(k == 0), stop=(k == 15)),
                             reads=[t_wt[ws], hT_toks[i]], writes=[tps], join=(k > 0))
                    P.op("dve", lambda e, ps=ps, vs=vs, i=i: e.tensor_copy(out=vst[vs][:, i, :], in_=ps), reads=[tps], writes=[t_vst[vs]], join=True)
                P.op("sync", lambda e, vs=vs, tok0=tok0, c0=c0: e.dma_start(out=C.V_d[tok0:tok0 + TB, c0:c0 + 512].rearrange("(a p) c -> p a c", p=128), in_=vst[vs]),
                     reads=[t_vst[vs]], writes=[t_vd], dma=True, join=True)
            else:
                pcnt, gcnt = g_block(C, hT, hT_toks, wt[ws], t_wt[ws], tok0, (cb - 8) * 512, pcnt, gst, t_gst, gcnt, TPB)
    A.release(m)
    P.barrier()
    m = A.mark()
    Kp = A.alloc([NT, 256], BF16); t_Kp = P.tok("Kp")
    Vh = A.alloc([NT, 512], BF16); t_Vh = P.tok("Vh")
    QTt = [A.alloc([S], BF16) for _ in range(2)]; t_QTt = P.tok("QTt")
    Sf = A.alloc([2, 512], F32); t_Sf = P.tok("Sf")
    Sb = [A.alloc([2, 512], BF16) for _ in range(2)]; t_Sb = P.toks(2, "Sb")
    NG = 4
    Gt = [A.alloc([2, 512], F32) for _ in range(NG)]; t_Gt = P.toks(NG, "Gt")
    yT = A.alloc([4, S], BF16); t_yT = P.tok("yT")
    ogb = A.alloc([512], F32)
    NE = 6
    of = [A.alloc([512], F32) for _ in range(NE)]; yb = [A.alloc([512], BF16) for _ in range(NE)]
    ssq = [A.alloc([1], F32) for _ in range(NE)]; junk = A.alloc([512], BF16)
    t_ss = P.toks(NE, "ss"); t_of = P.toks(NE, "of"); t_yb = P.toks(NE, "yb"); t_junk = P.tok("junk"); t_yd = P.tok("YTd")
    P.op("sync", lambda e: e.dma_start(out=ogb, in_=W["b_out_g"][0, :].partition_broadcast(128)), writes=[tl], dma=True)
    NCH = S // 64
    PO = [4, 5, 6]
    tp7, ttp7 = C.tpb[1], C.t_tpb[1]

    def emit_kv(hh, c):
        i, par = c // 2, c % 2
        pr0 = par * 64
        for dh in range(2):
            kv, tkv = C.psf[2 * (c % 2) + dh], C.t_psf[2 * (c % 2) + dh]
            P.op("pe", lambda e, kv=kv, dh=dh: e.matmul(kv, lhsT=Kp[pr0:pr0 + 64, i, dh * 128:(dh + 1) * 128], rhs=Vh[pr0:pr0 + 64, i, :], start=True, stop=True),
                 reads=[t_Kp, t_Vh], writes=[tkv])

    def emit_epiA(hh, c):
        i, par = c // 2, c % 2
        es = c % NE
        gs = i % NG
        po, tpo = C.psf[PO[c % 3]], C.t_psf[PO[c % 3]]
        P.op("act", lambda e: e.activation(out=junk[0:64, :], in_=po[0:64, :], func=AF.Square, accum_out=ssq[es][0:64, :]), reads=[tpo], writes=[t_junk, t_ss[es]])
        P.op("act", lambda e: e.activation(out=ssq[es][0:64, :], in_=ssq[es][0:64, :], func=AF.Ln, bias=C.epscol[0:64, :], scale=1.0 / 512), reads=[t_ss[es], C.t_const], writes=[t_ss[es]])
        P.op("act", lambda e: e.activation(out=ssq[es][0:64, :], in_=ssq[es][0:64, :], func=AF.Exp, scale=-0.5), reads=[t_ss[es]], writes=[t_ss[es]])
        P.op("dve", lambda e: e.scalar_tensor_tensor(out=of[es][0:64, :], in0=po[0:64, :], scalar=ssq[es][0:64, :], in1=ogb[0:64, :], op0=ALU.mult, op1=ALU.mult), reads=[tpo, t_ss[es], tl], writes=[t_of[es]])
        P.op("pool", lambda e: e.tensor_tensor(out=yb[es][0:64, :], in0=of[es][0:64, :], in1=Gt[gs][0:64, par, :], op=ALU.mult), reads=[t_of[es], t_Gt[gs]], writes=[t_yb[es]])

    def emit_epiB(hh, c):
        es = c % NE
        q4 = (c % 4) * 256
        for j in range(4):
            P.op("pe", lambda e, j=j: e.transpose(out=tp7[:, q4 + j * 64:q4 + (j + 1) * 64], in_=yb[es][0:64, j * 128:(j + 1) * 128], identity=C.ident[0:64, 0:64]), reads=[t_yb[es], C.t_const], writes=[ttp7], join=True)
        P.op("dve", lambda e: e.tensor_copy(out=yT[:, :, c * 64:(c + 1) * 64], in_=tp7[:, q4:q4 + 256].rearrange("p (a b) -> p a b", a=4)), reads=[ttp7], writes=[t_yT], join=True)

    DA, DB = 2, 4
    for hh in range(4):
        P.op("sync", lambda e, hh=hh: e.dma_start(out=Kp, in_=C.KP_d[:, hh * 256:(hh + 1) * 256].rearrange("(a p) c -> p a c", p=128)), writes=[t_Kp], dma=True)
        P.op("sync", lambda e, hh=hh: e.dma_start(out=Vh, in_=C.V_d[:, hh * 512:(hh + 1) * 512].rearrange("(a p) c -> p a c", p=128)), writes=[t_Vh], dma=True)
        for dh in range(2):
            P.op("sync", lambda e, hh=hh, dh=dh: e.dma_start(out=QTt[dh], in_=C.QT_d[2 * hh + dh]), writes=[t_QTt], dma=True, join=(dh > 0))
        P.op("pool", lambda e: e.memset(Sf, 0.0), writes=[t_Sf])
        emit_kv(hh, 0)
        for c in range(NCH):
            i, par = c // 2, c % 2
            if par == 0:
                gs = i % NG
                P.op("sync", lambda e, gs=gs, i=i, hh=hh: e.dma_start(out=Gt[gs][0:64, :, :], in_=C.G_d[i * 128:(i + 1) * 128, hh * 512:(hh + 1) * 512].rearrange("(par p) c -> p par c", p=64)), writes=[t_Gt[gs]], dma=True)
            sbs = c % 2
            for dh in range(2):
                kv, tkv = C.psf[2 * (c % 2) + dh], C.t_psf[2 * (c % 2) + dh]
                P.op("dve", lambda e, kv=kv, dh=dh, hh=hh, c=c: e.scalar_tensor_tensor(out=Sf[:, dh, :], in0=Sf[:, dh, :], scalar=dec[:, 2 * hh + dh, c:c + 1], in1=kv, op0=ALU.mult, op1=ALU.add),
                     reads=[t_Sf, t_dec, tkv], writes=[t_Sf])
                P.op("act", lambda e, dh=dh, sbs=sbs: e.activation(out=Sb[sbs][:, dh, :], in_=Sf[:, dh, :], func=AF.Copy), reads=[t_Sf], writes=[t_Sb[sbs]], join=(dh > 0))
            if c + 1 < NCH:
                emit_kv(hh, c + 1)
            po, tpo = C.psf[PO[c % 3]], C.t_psf[PO[c % 3]]
            for dh in range(2):
                P.op("pe", lambda e, po=po, dh=dh, c=c, sbs=sbs: e.matmul(po[0:64, :], lhsT=QTt[dh][:, c * 64:(c + 1) * 64], rhs=Sb[sbs][:, dh, :], start=(dh == 0), stop=(dh == 1)),
                     reads=[t_QTt, t_Sb[sbs]], writes=[tpo], join=(dh > 0))
            if c >= DA:
                emit_epiA(hh, c - DA)
            if c >= DB:
                emit_epiB(hh, c - DB)
        for c in range(NCH - DA, NCH):
            emit_epiA(hh, c)
        for c in range(NCH - DB, NCH):
            emit_epiB(hh, c)
        for j in range(4):
            P.op("sync", lambda e, hh=hh, j=j: e.dma_start(out=C.YT_d[hh * 4 + j], in_=yT[:, j, :]), reads=[t_yT], writes=[t_yd], dma=True, join=True)
    A.release(m0)
    P.barrier()


WSHAPES = {
    "norm_g": (4, 2048), "rel_bias": (32, 16),
    "a_w_in": (1, 2048, 8192), "a_qk_g": (1, 2, 64), "a_lambda": (1, 4, 64), "a_subln_g": (1, 128), "a_w_out": (1, 2048, 2048),
    "b_w_in": (1, 2048, 6160), "b_w_gate": (1, 16, 1024), "b_gate_bias": (1, 1024), "b_out_g": (1, 512), "b_w_out": (1, 2048, 2048),
    "c_w_in": (1, 2048, 4096), "c_conv_w": (1, 4, 2048), "c_conv_b": (1, 2048), "c_w_rgate": (1, 8, 256, 256), "c_b_rgate": (1, 2048),
    "c_w_igate": (1, 8, 256, 256), "c_b_igate": (1, 2048), "c_lambda": (1, 2048), "c_w_out": (1, 2048, 2048),
    "d_w_in": (1, 2048, 3136), "d_q_lat_g": (1, 512), "d_kv_lat_g": (1, 512), "d_w_uq": (1, 512, 3072), "d_w_ukv": (1, 512, 4096),
    "d_qk_g": (1, 2, 192), "d_w_out": (1, 2048, 2048),
}
LAYER_W = {
    0: ["norm_g", "rel_bias", "a_w_in", "a_qk_g", "a_lambda", "a_subln_g", "a_w_out"],
    1: ["norm_g", "b_w_in", "b_w_gate", "b_gate_bias", "b_out_g", "b_w_out"],
    2: ["norm_g", "c_w_in", "c_conv_w", "c_conv_b", "c_w_rgate", "c_b_rgate", "c_w_igate", "c_b_igate", "c_lambda", "c_w_out"],
    3: ["norm_g", "d_w_in", "d_q_lat_g", "d_kv_lat_g", "d_w_uq", "d_w_ukv", "d_qk_g", "d_w_out"],
}


def t5_bucket_np(rel):
    nb = 16; max_exact = 8
    ret = np.where(rel > 0, nb, 0)
    n = np.abs(rel)
    nf = np.maximum(n, 1).astype(np.float32)
    large = max_exact + (np.log(nf / max_exact) / math.log(128 / max_exact) * (nb - max_exact)).astype(np.int32)
    large = np.minimum(large, nb - 1)
    return ret + np.where(n < max_exact, n, large)


def bias_index_tiles():
    k = np.arange(128)[:, None]; q = np.arange(128)[None, :]
    idx = np.zeros((128, 2, 128), np.int64); msk = np.zeros((128, 2, 128), bool)
    idx[:, 0, :] = t5_bucket_np(k - q)
    msk[:, 0, :] = (k // 64) > (q // 64)
    idx[:, 1, :] = t5_bucket_np(k - q - 128)
    return idx, msk


def rope_tables():
    half = 32
    inv = (np.float32(10000.0) ** (-np.arange(half, dtype=np.float32) / np.float32(half))).astype(np.float32)
    ang = (np.arange(S, dtype=np.float32)[:, None] * inv[None, :]).astype(np.float32)
    c = np.cos(ang).astype(np.float32).T; s_ = np.sin(ang).astype(np.float32).T
    return np.ascontiguousarray(np.stack([np.concatenate([c, c], 0), np.concatenate([s_, s_], 0)], 0))


def build_program(layers, debug=False):
    nc = bass.Bass("TRN2", target_bir_lowering=False)
    C = Ctx()
    C.nc = nc
    x_in = dram(nc, "x", [S, D], F32, "ExternalInput")
    out = dram(nc, "out", [S, D], F32, "ExternalOutput")
    names = []
    for l in layers:
        for n in LAYER_W[l]:
            if n not in names:
                names.append(n)
    W = {n: dram(nc, n, WSHAPES[n], F32, "ExternalInput") for n in names}
    if 0 in layers:
        C.biasT_in = dram(nc, "biasT", [128, 16, 2, 128], F32, "ExternalInput")
    ident_d = nc.inline_tensor(np.eye(128, dtype=np.float32), "ident_c").ap()
    o64 = np.zeros((128, 128), np.float32); o64[:64, :64] = 1; o64[64:, 64:] = 1
    ones64_d = nc.inline_tensor(o64, "ones64_c").ap()
    ones128_d = nc.inline_tensor(np.ones((128, 128), np.float32), "ones128_c").ap()
    xs = [dram(nc, f"xs{i}", [S, D], F32) for i in range(2)]
    sk = "ExternalOutput" if debug else "Internal"
    C.QT_d = dram(nc, "QT_d", [16, 128, S], BF16, sk)
    C.KT_d = dram(nc, "KT_d", [16, 128, S], BF16, sk)
    C.V_d = dram(nc, "V_d", [S, D], BF16, sk)
    C.G_d = dram(nc, "G_d", [S, D], F32, sk)
    C.YT_d = dram(nc, "YT_d", [16, 128, S], BF16, sk)
    C.GT_d = dram(nc, "GT_d", [16, 128, S], F32, sk)
    if 3 in layers:
        C.QPE_d = dram(nc, "QPE_d", [16, 64, S], BF16, sk)
        C.KPE_d = dram(nc, "KPE_d", [64, S], BF16, sk)
        C.CQ_d = dram(nc, "CQ_d", [4, 128, S], BF16, sk)
        C.CKV_d = dram(nc, "CKV_d", [4, 128, S], BF16, sk)
        C.rope_in = dram(nc, "rope_cs", [2, 64, S], F32, "ExternalInput")
        kk = np.arange(128)[:, None]; qq = np.arange(128)[None, :]
        C.maskT_d = nc.inline_tensor(np.where((kk // 64) > (qq // 64), -30000.0, 0.0).astype(np.float32), "maskT_c").ap()
    if 1 in layers:
        C.KP_d = dram(nc, "KP_d", [S, 1024], BF16, sk)
        tt = np.arange(128)
        tu = np.where((tt[:, None] > tt[None, :]) & (tt[:, None] // 64 == tt[None, :] // 64), -1.0 / 16.0, 0.0).astype(np.float32)
        ci = np.where(tt[:, None] // 64 == np.arange(2)[None, :], -1.0 / 16.0, 0.0).astype(np.float32)
        C.TU_d = nc.inline_tensor(tu, "TU_c").ap()
        C.CI_d = nc.inline_tensor(ci, "CI_c").ap()
    rot = np.zeros((128, 128), np.float32)
    for i_ in range(32):
        rot[32 + i_, i_] = -1.0; rot[i_, 32 + i_] = 1.0
    rotm_d = nc.inline_tensor(rot, "rotm_c").ap()
    with ExitStack() as ctx:
        P = Prog(nc, ctx)
        C.P = P
        A = Arena(nc, ctx, 200)
        C.A = A
        C.psf = [ctx.enter_context(nc.psum_tensor(f"psf{i}", [128, 512], F32))[:] for i in range(8)]
        C.t_psf = P.toks(8, "psf")
        C.tpb = [C.psf[6].bitcast(BF16), C.psf[7].bitcast(BF16)]
        C.t_tpb = [C.t_psf[6], C.t_psf[7]]
        C.auxcnt = 0
        C.dq = []
        C.tickn = 0
        C.t_const = P.tok("const"); C.t_lconst = P.tok("lconst")
        C.ident = A.alloc([128], BF16); C.ones64 = A.alloc([128], F32); C.ones128 = A.alloc([128], F32); C.epscol = A.alloc([1], F32)
        C.rotm = A.alloc([128], F32); C.identf = A.alloc([128], F32)
        P.op("sync", lambda e: e.dma_start(out=C.identf, in_=ident_d), writes=[C.t_const], dma=True, join=True)
        P.op("sync", lambda e: e.dma_start(out=C.rotm, in_=rotm_d), writes=[C.t_const], dma=True, join=True)
        P.op("pool", lambda e: e.dma_start(out=C.ident, in_=ident_d), writes=[C.t_const], dma=True, join=True)
        P.op("sync", lambda e: e.dma_start(out=C.ones64, in_=ones64_d), writes=[C.t_const], dma=True, join=True)
        P.op("sync", lambda e: e.dma_start(out=C.ones128, in_=ones128_d), writes=[C.t_const], dma=True, join=True)
        P.op("dve", lambda e: e.memset(C.epscol, EPS), writes=[C.t_const], join=True)
        P.barrier()
        cur = x_in
        for li, l in enumerate(layers):
            C.layer = l
            dst = out if li == len(layers) - 1 else xs[li % 2]
            if l == 0:
                layer0_proj(C, cur, W)
                attn_phase(C, W, "diff")
                phase_outproj(C, C.YT_d, W["a_w_out"][0], cur, dst)
            elif l == 1:
                layer1_all(C, cur, W)
                phase_outproj(C, C.YT_d, W["b_w_out"][0], cur, dst)
            elif l == 2:
                layer2_all(C, cur, W)
                phase_outproj(C, C.YT_d, W["c_w_out"][0], cur, dst)
            elif l == 3:
                layer3_proj1(C, cur, W)
                layer3_proj2(C, W)
                attn_phase(C, W, "mla")
                phase_outproj(C, C.YT_d, W["d_w_out"][0], cur, dst)
            else:
                raise NotImplementedError
            cur = dst
        P.emit()
    C.names = names
    return nc, C


_CACHE = {}


def run_layers(layers, x, inputs, n_cores=4):
    key = tuple(layers)
    if key not in _CACHE:
        _CACHE[key] = build_program(layers)
    nc, C = _CACHE[key]
    shared = {n: np.ascontiguousarray(inputs[n], dtype=np.float32) for n in C.names}
    if 0 in layers:
        idx, msk = bias_index_tiles()
        rb = np.asarray(inputs["rel_bias"], np.float32)
        bt = rb[idx]
        bt = np.where(msk[..., None], np.float32(-30000.0), bt)
        shared["biasT"] = np.ascontiguousarray(bt.transpose(0, 3, 1, 2))
    if 3 in layers:
        shared["rope_cs"] = rope_tables()
    in_maps = [dict(shared, x=np.ascontiguousarray(x[b])) for b in range(n_cores)]
    res = run_bass_kernel_spmd(nc, in_maps, core_ids=list(range(n_cores)))
    C.last_res = res
    return np.stack([r["out"] for r in res.results], axis=0)


def kernel(**inputs):
    x = np.asarray(inputs["x"], np.float32)
    return run_layers([0, 1, 2, 3], x, inputs)
```

```python
import math
from contextlib import ExitStack
import numpy as np
import concourse.bass as bass
import concourse.mybir as mybir
from concourse.bass_utils import run_bass_kernel_spmd

F32 = mybir.dt.float32
BF16 = mybir.dt.bfloat16
AF = mybir.ActivationFunctionType
ALU = mybir.AluOpType
AX = mybir.AxisListType

ENGS = ("sync", "act", "pool", "dve", "pe")


class Tok:
    __slots__ = ("name", "w", "r", "pool")

    def __init__(self, name):
        self.name = name
        self.w = {}
        self.r = {}
        self.pool = {}


class Prog:
    def __init__(self, nc, ctx, same_engine_sync=("act", "dve", "pool")):
        self.nc = nc
        self.ctx = ctx
        self.q = {e: [] for e in ENGS}
        self.cnt = {e: 0 for e in ENGS}
        self.waited = {e: {} for e in ENGS}
        self.pend = {e: {} for e in ENGS}
        self.needed = set()
        self.same_sync = set(same_engine_sync)
        self.pool_val = []
        self.pool_free = {"hw": [], "sw": []}
        self.live = []
        self.all_toks = []

    def tok(self, name="t"):
        t = Tok(name)
        self.all_toks.append(t)
        return t

    def toks(self, n, name="t"):
        return [self.tok(f"{name}{i}") for i in range(n)]

    def op(self, eng, fn, reads=(), writes=(), dma=False, join=False):
        deps = dict(self.pend[eng])
        self.pend[eng] = {}

        def add(k, v):
            if deps.get(k, 0) < v:
                deps[k] = v

        for t in reads:
            for k, v in t.w.items():
                add(k, v)
        for t in writes:
            if not (join and not t.r):
                for k, v in t.w.items():
                    add(k, v)
            elif dma:
                for k, v in t.w.items():
                    if k[0] == "e":
                        add(k, v)
            for k, v in t.r.items():
                add(k, v)
        waits = []
        wd = self.waited[eng]
        for k, v in deps.items():
            if k == ("e", eng) and (eng not in self.same_sync or self.cnt[eng] + 1 - v >= 3):
                continue
            if wd.get(k, 0) < v:
                wd[k] = v
                waits.append((k, v))
                self.needed.add((k, v))
        if dma:
            assert len(writes) == 1
            t = reads[0] if reads else writes[0]
            kind = "sw" if eng == "pool" else "hw"
            if kind not in t.pool:
                if self.pool_free[kind]:
                    t.pool[kind] = self.pool_free[kind].pop()
                else:
                    self.pool_val.append(0)
                    t.pool[kind] = len(self.pool_val) - 1
                self.live.append((t, kind))
            pi = t.pool[kind]
            self.pool_val[pi] += 16
            h = (("s", pi), self.pool_val[pi])
        else:
            self.cnt[eng] += 1
            h = (("e", eng), self.cnt[eng])
        k, v = h
        for t in reads:
            if t.r.get(k, 0) < v:
                t.r[k] = v
        for t in writes:
            if join and not t.r:
                t.w[k] = v
            else:
                t.w = {k: v}
                t.r = {}
        self.q[eng].append((waits, fn, h, dma))
        return h

    def barrier(self):
        hs = {}
        for e in ENGS:
            if self.cnt[e] > 0:
                hs[("e", e)] = self.cnt[e]
        for i, v in enumerate(self.pool_val):
            if v > 0:
                hs[("s", i)] = v
        for e in ENGS:
            for k, v in hs.items():
                if self.pend[e].get(k, 0) < v:
                    self.pend[e][k] = v
        for t, kind in self.live:
            self.pool_free[kind].append(t.pool[kind])
        self.live = []
        for t in self.all_toks:
            t.w = {}
            t.r = {}
            t.pool = {}

    def emit(self):
        nc = self.nc
        self.barrier()
        fw = []
        wd = self.waited["sync"]
        for k, v in self.pend["sync"].items():
            if wd.get(k, 0) < v:
                fw.append((k, v))
                self.needed.add((k, v))
        esem = {e: self.ctx.enter_context(nc.semaphore(f"sem_{e}")) for e in ENGS}
        psem = [self.ctx.enter_context(nc.semaphore(f"dsem{i}")) for i in range(len(self.pool_val))]
        rank = {}
        for e in ENGS:
            r = 0
            for (_w, _f, h, dma) in self.q[e]:
                if not dma and h in self.needed:
                    r += 1
                    rank[h] = r

        def resolve(h):
            k, v = h
            if k[0] == "e":
                return esem[k[1]], rank[h]
            return psem[k[1]], v

        block = self.ctx.enter_context(nc.Block())
        names = {"sync": "sync", "act": "scalar", "pool": "gpsimd", "dve": "vector", "pe": "tensor"}
        for e in ENGS:
            ops = self.q[e]
            if not ops and e != "sync":
                continue

            def body(eng, ops=ops, e=e):
                for (waits, fn, h, dma) in ops:
                    for w in waits:
                        s, v = resolve(w)
                        eng.wait_ge(s, v)
                    ins = fn(eng)
                    if dma:
                        s, v = resolve(h)
                        ins.then_inc(s, 16)
                    elif h in self.needed:
                        s, v = resolve(h)
                        ins.then_inc(s, 1)
                if e == "sync":
                    for w in fw:
                        s, v = resolve(w)
                        eng.wait_ge(s, v)

            getattr(block, names[e])(body)
        self.n_sems = len(psem) + 5


class Arena:
    def __init__(self, nc, ctx, kib):
        self.n32 = kib * 256
        self.t = ctx.enter_context(nc.sbuf_tensor("arena", [128, self.n32], F32))
        self.off = 0

    def mark(self):
        return self.off

    def release(self, m):
        self.off = m

    def alloc(self, shape, dt, parts=128):
        n = int(np.prod(shape))
        n32 = n if dt == F32 else (n + 1) // 2
        n32 = (n32 + 7) // 8 * 8
        assert self.off + n32 <= self.n32, f"arena overflow {self.off + n32} > {self.n32}"
        v = self.t[0:parts, self.off:self.off + n32]
        self.off += n32
        if dt != F32:
            v = v.bitcast(dt)
        v = v[:, 0:n]
        if len(shape) == 2:
            v = v.rearrange("p (a b) -> p a b", a=shape[0])
        elif len(shape) == 3:
            v = v.rearrange("p (a b c) -> p a b c", a=shape[0], b=shape[1])
        return v


S = 4096
D = 2048
EPS = 1e-6
NT = S // 128
TB = 2048
NTB = S // TB
TPB = TB // 128


class Ctx:
    pass


def dram(nc, name, shape, dt, kind="Internal"):
    return nc.dram_tensor(name, list(shape), dt, kind=kind).ap()


def load_w_block(C, dst, dtok, wsrc, c0, ncols, kc=16, rows0=0):
    P = C.P
    wv = wsrc[rows0:rows0 + kc * 128, :].rearrange("(k p) c -> p k c", p=128)
    step = 4 if kc >= 4 else kc
    for k0 in range(0, kc, step):
        k1 = min(kc, k0 + step)
        P.op("pool", lambda e, k0=k0, k1=k1: e.dma_start(out=dst[:, k0:k1, 0:ncols], in_=wv[:, k0:k1, c0:c0 + ncols]),
             writes=[dtok], dma=True, join=True)


class WPrefetch:
    def __init__(self, C, wt, t_wt, loaders):
        self.C, self.wt, self.t_wt, self.loaders = C, wt, t_wt, loaders
        self.k = 0
        self._issue(0)

    def _issue(self, k):
        if k < len(self.loaders):
            self.loaders[k](self.wt[k % 2], self.t_wt[k % 2])

    def next(self):
        ws = self.k % 2
        self._issue(self.k + 1)
        self.k += 1
        return ws


def phase_norm_T(C, x_src, g_row, tok0, hT, hT_toks, tbsz=TB):
    P, A = C.P, C.A
    m = A.mark()
    xin = [A.alloc([D], F32) for _ in range(2)]
    hb = [A.alloc([D], BF16) for _ in range(2)]
    gb = A.alloc([D], F32)
    ssq = [A.alloc([1], F32) for _ in range(2)]
    t_xin = P.toks(2, "xin"); t_hb = P.toks(2, "hb"); t_gb = P.tok("gb"); t_ssq = P.toks(2, "ssq")
    P.op("sync", lambda e: e.dma_start(out=gb, in_=g_row.partition_broadcast(128)), writes=[t_gb], dma=True)
    for i in range(tbsz // 128):
        s = i % 2
        r0 = tok0 + i * 128
        P.op("sync", lambda e, s=s, r0=r0: e.dma_start(out=xin[s], in_=x_src[r0:r0 + 128, :]), writes=[t_xin[s]], dma=True)
        P.op("act", lambda e, s=s: e.activation(out=hb[s], in_=xin[s], func=AF.Square, accum_out=ssq[s]),
             reads=[t_xin[s]], writes=[t_hb[s], t_ssq[s]])
        P.op("act", lambda e, s=s: e.activation(out=ssq[s], in_=ssq[s], func=AF.Ln, bias=C.epscol, scale=1.0 / D),
             reads=[t_ssq[s], C.t_const], writes=[t_ssq[s]])
        P.op("act", lambda e, s=s: e.activation(out=ssq[s], in_=ssq[s], func=AF.Exp, scale=-0.5), reads=[t_ssq[s]], writes=[t_ssq[s]])
        P.op("dve", lambda e, s=s: e.scalar_tensor_tensor(out=hb[s], in0=xin[s], scalar=ssq[s], in1=gb, op0=ALU.mult, op1=ALU.mult),
             reads=[t_xin[s], t_ssq[s], t_gb], writes=[t_hb[s]])
        for half in range(2):
            tp, ttp = C.tpb[half], C.t_tpb[half]
            for j in range(8):
                k = half * 8 + j
                P.op("pe", lambda e, s=s, j=j, k=k, tp=tp: e.transpose(out=tp[:, j * 128:(j + 1) * 128], in_=hb[s][:, k * 128:(k + 1) * 128], identity=C.ident),
                     reads=[t_hb[s], C.t_const], writes=[ttp], join=True)
            eng = "act" if half == 0 else "dve"
            dst = hT[:, half * 8:(half + 1) * 8, i * 128:(i + 1) * 128]
            srcv = tp.rearrange("p (a b) -> p a b", a=8)
            if eng == "act":
                P.op("act", lambda e, dst=dst, srcv=srcv: e.activation(out=dst, in_=srcv, func=AF.Copy), reads=[ttp], writes=[hT_toks[i]], join=True)
            else:
                P.op("dve", lambda e, dst=dst, srcv=srcv: e.tensor_copy(out=dst, in_=srcv), reads=[ttp], writes=[hT_toks[i]], join=True)
    A.release(m)


def phase_outproj(C, yT_d, w_out, x_src, x_dst):
    P, A = C.P, C.A
    m = A.mark()
    wo = A.alloc([16, D], BF16)
    yT = A.alloc([16, TB], BF16)
    xin = [A.alloc([D], F32) for _ in range(2)]
    xo = [A.alloc([D], F32) for _ in range(2)]
    t_wo = P.tok("wo"); t_yT = P.toks(16, "yT"); t_xin = P.toks(2, "xin"); t_xo = P.toks(2, "xo"); t_dst = P.tok("xdst")
    for n in range(4):
        load_w_block(C, wo[:, :, n * 512:(n + 1) * 512], t_wo, w_out, n * 512, 512)
    cnt = 0
    for tb in range(NTB):
        tok0 = tb * TB
        for k in range(16):
            P.op("sync", lambda e, k=k, tok0=tok0: e.dma_start(out=yT[:, k, :], in_=yT_d[k, :, tok0:tok0 + TB]), writes=[t_yT[k]], dma=True)
        for i in range(TPB):
            s = i % 2
            r0 = tok0 + i * 128
            P.op("sync", lambda e, s=s, r0=r0: e.dma_start(out=xin[s], in_=x_src[r0:r0 + 128, :]), writes=[t_xin[s]], dma=True)
            for n in range(4):
                pb = cnt % 4; cnt += 1
                ps, tps = C.psf[pb], C.t_psf[pb]
                for k in range(16):
                    P.op("pe", lambda e, ps=ps, k=k, i=i, n=n: e.matmul(ps, lhsT=yT[:, k, i * 128:(i + 1) * 128], rhs=wo[:, k, n * 512:(n + 1) * 512], start=(k == 0), stop=(k == 15)),
                         reads=[t_yT[k], t_wo], writes=[tps], join=(k > 0))
                P.op("dve", lambda e, ps=ps, s=s, n=n: e.tensor_tensor(out=xo[s][:, n * 512:(n + 1) * 512], in0=ps, in1=xin[s][:, n * 512:(n + 1) * 512], op=ALU.add),
                     reads=[tps, t_xin[s]], writes=[t_xo[s]], join=(n > 0))
            P.op("sync", lambda e, s=s, r0=r0: e.dma_start(out=x_dst[r0:r0 + 128, :], in_=xo[s]), reads=[t_xo[s]], writes=[t_dst], dma=True)
    A.release(m)
    P.barrier()


def defer(C, delay, fn):
    due = C.tickn + delay
    if C.dq and C.dq[-1][0] > due:
        due = C.dq[-1][0]
    if delay <= 0 and not C.dq:
        fn()
    else:
        C.dq.append((due, fn))


def tick(C):
    C.tickn += 1
    while C.dq and C.dq[0][0] <= C.tickn:
        C.dq.pop(0)[1]()


def flush(C):
    while C.dq:
        C.dq.pop(0)[1]()


def qk_norm_epilogue(C, ps, tps, gcol, dst, t_dst, tmp, t_tmp, grp, delay=2):
    P = C.P
    qf, sq, rs = tmp
    ones = C.ones64 if grp == 64 else C.ones128
    tick(C)
    P.op("dve", lambda e: e.tensor_copy(out=qf, in_=ps), reads=[tps], writes=[t_tmp[0]])
    P.op("pool", lambda e: e.tensor_tensor(out=sq, in0=qf, in1=qf, op=ALU.mult), reads=[t_tmp[0]], writes=[t_tmp[1]])

    def stage2():
        pss, tpss = C.psf[4 + C.auxcnt % 2], C.t_psf[4 + C.auxcnt % 2]
        C.auxcnt += 1
        P.op("pe", lambda e: e.matmul(pss, lhsT=ones, rhs=sq, start=True, stop=True), reads=[t_tmp[1], C.t_const], writes=[tpss])
        P.op("act", lambda e: e.activation(out=rs, in_=pss, func=AF.Ln, bias=C.epscol, scale=1.0 / grp), reads=[tpss, C.t_const], writes=[t_tmp[2]])
        P.op("act", lambda e: e.activation(out=rs, in_=rs, func=AF.Exp, scale=-0.5), reads=[t_tmp[2]], writes=[t_tmp[2]])
        P.op("dve", lambda e: e.scalar_tensor_tensor(out=dst, in0=qf, scalar=gcol, in1=rs, op0=ALU.mult, op1=ALU.mult),
             reads=[t_tmp[0], t_tmp[2], C.t_lconst], writes=[t_dst], join=True)
    defer(C, delay, stage2)


def gT_block(C, hT, hT_toks, wt_s, t_wt_s, tok0, h0, tbsz, pcnt, gst, t_gst, gcnt):
    P = C.P
    for mm_ in range(4):
        gs = gcnt % 2; gcnt += 1
        for tq in range(tbsz // 512):
            pb = pcnt % 4; pcnt += 1
            ps, tps = C.psf[pb], C.t_psf[pb]
            for k in range(16):
                P.op("pe", lambda e, ps=ps, k=k, mm_=mm_, tq=tq: e.matmul(ps, lhsT=wt_s[:, k, mm_ * 128:(mm_ + 1) * 128], rhs=hT[:, k, tq * 512:(tq + 1) * 512], start=(k == 0), stop=(k == 15)),
                     reads=[t_wt_s] + hT_toks[tq * 4:(tq + 1) * 4], writes=[tps], join=(k > 0))
            P.op("act", lambda e, ps=ps, gs=gs, tq=tq: e.activation(out=gst[gs][:, tq * 512:(tq + 1) * 512], in_=ps, func=AF.Silu), reads=[tps], writes=[t_gst[gs]], join=True)
        P.op("sync", lambda e, gs=gs, mm_=mm_: e.dma_start(out=C.GT_d[h0 + mm_, :, tok0:tok0 + tbsz], in_=gst[gs][:, 0:tbsz]), reads=[t_gst[gs]], writes=[C.t_gd], dma=True, join=True)
    return pcnt, gcnt


def layer0_proj(C, x_src, W):
    P, A = C.P, C.A
    m = A.mark()
    hT = A.alloc([16, TB], BF16)
    hT_toks = P.toks(TPB, "hT")
    wt = [A.alloc([16, 512], BF16) for _ in range(2)]
    t_wt = P.toks(2, "wt")
    qst = [A.alloc([TB], BF16) for _ in range(2)]
    t_qst = P.toks(2, "qst")
    tmp = [[A.alloc([512], F32) for _ in range(3)] for _ in range(2)]
    t_tmp = [P.toks(3, "tmp") for _ in range(2)]
    vst = [A.alloc([8, 512], BF16) for _ in range(2)]
    t_vst = P.toks(2, "vst")
    gst = [A.alloc([TB], F32) for _ in range(2)]
    t_gst = P.toks(2, "gst")
    t_qd = P.tok("QTd"); t_kd = P.tok("KTd"); t_vd = P.tok("Vd"); C.t_gd = P.tok("Gd")
    gq = A.alloc([1], F32); gk = A.alloc([1], F32)
    for half in range(2):
        P.op("sync", lambda e, half=half: e.dma_start(out=gq[half * 64:(half + 1) * 64, :], in_=W["a_qk_g"][0, 0, :].rearrange("(d o) -> d o", o=1)), writes=[C.t_lconst], dma=True, join=True)
        P.op("sync", lambda e, half=half: e.dma_start(out=gk[half * 64:(half + 1) * 64, :], in_=W["a_qk_g"][0, 1, :].rearrange("(d o) -> d o", o=1)), writes=[C.t_lconst], dma=True, join=True)
    P.op("dve", lambda e: e.tensor_scalar(out=gq, in0=gq, scalar1=0.125, scalar2=None, op0=ALU.mult), reads=[C.t_lconst], writes=[C.t_lconst])
    wcnt = 0; qcnt = 0; tcnt = 0; pcnt = 0; vcnt = 0; gcnt = 0
    pf = WPrefetch(C, wt, t_wt, [(lambda d, t, cb=cb: load_w_block(C, d, t, W["a_w_in"][0], cb * 512, 512)) for _tb in range(NTB) for cb in range(16)])
    for tb in range(NTB):
        tok0 = tb * TB
        phase_norm_T(C, x_src, W["norm_g"][C.layer, :], tok0, hT, hT_toks)
        for cb in range(16):
            ws = pf.next()
            kind = cb // 4
            if kind < 2:
                for mm_ in range(4):
                    h = (cb % 4) * 4 + mm_
                    qs = qcnt % 2; qcnt += 1
                    for tq in range(TB // 512):
                        pb = pcnt % 4; pcnt += 1
                        ps, tps = C.psf[pb], C.t_psf[pb]
                        for k in range(16):
                            P.op("pe", lambda e, ps=ps, k=k, ws=ws, mm_=mm_, tq=tq: e.matmul(ps, lhsT=wt[ws][:, k, mm_ * 128:(mm_ + 1) * 128], rhs=hT[:, k, tq * 512:(tq + 1) * 512], start=(k == 0), stop=(k == 15)),
                                 reads=[t_wt[ws]] + hT_toks[tq * 4:(tq + 1) * 4], writes=[tps], join=(k > 0))
                        ts_ = tcnt % 2; tcnt += 1
                        qk_norm_epilogue(C, ps, tps, gq if kind == 0 else gk, qst[qs][:, tq * 512:(tq + 1) * 512], t_qst[qs], tmp[ts_], t_tmp[ts_], 64)
                    dd, td = (C.QT_d, t_qd) if kind == 0 else (C.KT_d, t_kd)
                    defer(C, 2, lambda dd=dd, td=td, h=h, qs=qs, tok0=tok0: P.op("sync", lambda e: e.dma_start(out=dd[h, :, tok0:tok0 + TB], in_=qst[qs]), reads=[t_qst[qs]], writes=[td], dma=True, join=True))
                flush(C)
            elif kind == 3:
                pcnt, gcnt = gT_block(C, hT, hT_toks, wt[ws], t_wt[ws], tok0, (cb % 4) * 4, TB, pcnt, gst, t_gst, gcnt)
            else:
                c0 = (cb % 4) * 512
                for i in range(TPB):
                    pb = pcnt % 4; pcnt += 1
                    ps, tps = C.psf[pb], C.t_psf[pb]
                    for k in range(16):
                        P.op("pe", lambda e, ps=ps, k=k, ws=ws, i=i: e.matmul(ps, lhsT=hT[:, k, i * 128:(i + 1) * 128], rhs=wt[ws][:, k, :], start=(k == 0), stop=(k == 15)),
                             reads=[t_wt[ws], hT_toks[i]], writes=[tps], join=(k > 0))
                    r0 = tok0 + i * 128
                    if True:
                        g8 = i % 8
                        if g8 == 0:
                            vs = vcnt % 2; vcnt += 1
                        P.op("dve", lambda e, ps=ps, vs=vs, g8=g8: e.tensor_copy(out=vst[vs][:, g8, :], in_=ps), reads=[tps], writes=[t_vst[vs]], join=True)
                        if g8 == 7:
                            rr = r0 - 7 * 128
                            P.op("sync", lambda e, vs=vs, rr=rr, c0=c0: e.dma_start(out=C.V_d[rr:rr + 1024, c0:c0 + 512].rearrange("(a p) c -> p a c", p=128), in_=vst[vs]),
                                 reads=[t_vst[vs]], writes=[t_vd], dma=True, join=True)
    A.release(m)
    P.barrier()


def attn_phase(C, W, mode):
    P, A = C.P, C.A
    m = A.mark()
    H = 16
    diff = (mode == "diff")
    nsub = 2 if diff else 1
    lam_init = 0.8 - 0.6 * math.exp(-0.3 * C.layer)
    QT = [A.alloc([S], BF16) for _ in range(2)]
    KT = [A.alloc([S], BF16) for _ in range(2)]
    if diff:
        QT1 = [A.alloc([S], BF16) for _ in range(2)]
        t_QT1 = P.toks(2, "QT1")
    Vh = [A.alloc([NT, 128], BF16) for _ in range(2)]
    Gh = [A.alloc([S], F32) for _ in range(2)]
    yT = [A.alloc([S], BF16) for _ in range(2)]
    LA = 4 if diff else 3
    NPT = LA + 2
    PT = [A.alloc([512], BF16) for _ in range(NPT)]
    onesb = A.alloc([128], BF16)
    t_QT = P.toks(2, "QT"); t_KT = P.toks(2, "KT"); t_Vh = P.toks(2, "Vh"); t_Gh = P.toks(2, "Gh"); t_yT = P.toks(2, "yT"); t_PT = P.toks(NPT, "PT")
    t_yd = P.tok("YTd")
    if diff:
        for s_ in range(2):
            P.op("pool", lambda e, s_=s_: e.memset(QT[s_][64:128, :], 0.0), writes=[t_QT[s_]])
            P.op("pool", lambda e, s_=s_: e.memset(QT1[s_][0:64, :], 0.0), writes=[t_QT1[s_]])
    Osb = [[A.alloc([512], F32) for _ in range(2)] for _ in range(2)]; Dsb = [[A.alloc([512], F32) for _ in range(2)] for _ in range(2)]
    sqb = [A.alloc([512], F32) for _ in range(2)]; rsb = [A.alloc([512], F32) for _ in range(2)]
    t_Osb = [P.toks(2, "Osb") for _ in range(2)]; t_Dsb = [P.toks(2, "Dsb") for _ in range(2)]; t_sqb = P.toks(2, "sqb"); t_rsb = P.toks(2, "rsb")
    tl = C.t_lconst
    P.op("pool", lambda e: e.memset(onesb, 1.0), writes=[tl])
    if diff:
        biasT = A.alloc([H, 2, 128], F32)
        b15 = A.alloc([H], F32)
        lam4 = A.alloc([4, 64], F32)
        lamc = A.alloc([4], F32)
        sgcol = A.alloc([1], F32)
        P.op("sync", lambda e: e.dma_start(out=biasT, in_=C.biasT_in), writes=[tl], dma=True, join=True)
        P.op("sync", lambda e: e.dma_start(out=b15, in_=W["rel_bias"][15, :].partition_broadcast(128)), writes=[tl], dma=True, join=True)
        P.op("sync", lambda e: e.dma_start(out=lam4, in_=W["a_lambda"][0].rearrange("a d -> (a d)").partition_broadcast(128).rearrange("p (a d) -> p a d", a=4)), writes=[tl], dma=True, join=True)
        load_col(C, sgcol, W["a_subln_g"][0, :], 128, 1.0 - lam_init)
        biasB = A.alloc([H, 2, 128], BF16)
        for hh in range(H):
            P.op("dve", lambda e, hh=hh: e.tensor_scalar(out=biasT[:, hh], in0=biasT[:, hh], scalar1=b15[:, hh:hh + 1], scalar2=None, op0=ALU.subtract), reads=[tl], writes=[tl])
        P.op("dve", lambda e: e.tensor_copy(out=biasB, in_=biasT), reads=[tl], writes=[tl])
        P.op("dve", lambda e: e.tensor_tensor(out=lam4[:, 0, :], in0=lam4[:, 0, :], in1=lam4[:, 1, :], op=ALU.mult), reads=[tl], writes=[tl])
        P.op("dve", lambda e: e.tensor_tensor(out=lam4[:, 2, :], in0=lam4[:, 2, :], in1=lam4[:, 3, :], op=ALU.mult), reads=[tl], writes=[tl])
        P.op("dve", lambda e: e.reduce_sum(out=lamc[:, 0:1], in_=lam4[:, 0, :], axis=AX.X), reads=[tl], writes=[tl])
        P.op("dve", lambda e: e.reduce_sum(out=lamc[:, 1:2], in_=lam4[:, 2, :], axis=AX.X), reads=[tl], writes=[tl])
        P.op("act", lambda e: e.activation(out=lamc[:, 0:2], in_=lamc[:, 0:2], func=AF.Exp), reads=[tl], writes=[tl])
        P.op("dve", lambda e: e.scalar_tensor_tensor(out=lamc[:, 0:1], in0=lamc[:, 1:2], scalar=-lam_init, in1=lamc[:, 0:1], op0=ALU.add, op1=ALU.subtract), reads=[tl], writes=[tl])
    else:
        maskT = A.alloc([128], F32)
        QP = [A.alloc([S], BF16) for _ in range(2)]
        KP = A.alloc([S], BF16)
        t_QP = P.toks(2, "QP"); t_KP = P.tok("KP")
        P.op("sync", lambda e: e.dma_start(out=maskT, in_=C.maskT_d), writes=[tl], dma=True)
        maskB = A.alloc([128], BF16)
        P.op("dve", lambda e: e.tensor_copy(out=maskB, in_=maskT), reads=[tl], writes=[tl])
        P.op("sync", lambda e: e.dma_start(out=KP[0:64, :], in_=C.KPE_d), writes=[t_KP], dma=True)
    SB = [0, 1, 2, 3, 7] if diff else [0, 1, 6, 7]
    NSB = len(SB)

    def banks(qg, t):
        b0 = 4 if diff else 2 + 2 * (qg % 2)
        return b0, b0 + 1

    def emit_loads(h):
        hs = h % 2
        if diff:
            P.op("sync", lambda e: e.dma_start(out=QT[hs][0:64, :], in_=C.QT_d[h, 0:64, :]), writes=[t_QT[hs]], dma=True, join=True)
            P.op("sync", lambda e: e.dma_start(out=QT1[hs][64:128, :], in_=C.QT_d[h, 64:128, :]), writes=[t_QT1[hs]], dma=True, join=True)
        else:
            P.op("sync", lambda e: e.dma_start(out=QT[hs], in_=C.QT_d[h]), writes=[t_QT[hs]], dma=True)
        P.op("sync", lambda e: e.dma_start(out=KT[hs], in_=C.KT_d[h]), writes=[t_KT[hs]], dma=True)
        if not diff:
            P.op("sync", lambda e: e.dma_start(out=QP[hs][0:64, :], in_=C.QPE_d[h]), writes=[t_QP[hs]], dma=True)
        P.op("sync", lambda e: e.dma_start(out=Vh[hs], in_=C.V_d[:, h * 128:(h + 1) * 128].rearrange("(a p) c -> p a c", p=128)), writes=[t_Vh[hs]], dma=True)
        P.op("sync", lambda e: e.dma_start(out=Gh[hs], in_=C.GT_d[h]), writes=[t_Gh[hs]], dma=True)

    def emit_S(n, h, qg, t, i):
        hs = h % 2
        jmin = max(0, i - 4 * qg)
        Sp, tSp = C.psf[SB[n % NSB]], C.t_psf[SB[n % NSB]]
        q0 = (4 * qg + jmin) * 128
        ncol = (4 - jmin) * 128
        c0 = jmin * 128
        near = []
        for rel in ((0, 1) if diff else (0,)):
            jj = i - 4 * qg + rel
            if 0 <= jj <= 3 and jj >= jmin:
                near.append((jj, biasB[:, h, rel, :] if diff else maskB))
        nn = len(near)
        if diff:
            Qz, tQz = (QT[hs], t_QT[hs]) if t == 0 else (QT1[hs], t_QT1[hs])
            P.op("pe", lambda e: e.matmul(Sp[:, c0:c0 + ncol], lhsT=KT[hs][:, i * 128:(i + 1) * 128], rhs=Qz[:, q0:q0 + ncol], start=True, stop=(nn == 0), skip_group_check=True),
                 reads=[t_KT[hs], tQz], writes=[tSp])
        else:
            P.op("pe", lambda e: e.matmul(Sp[:, c0:c0 + ncol], lhsT=KT[hs][:, i * 128:(i + 1) * 128], rhs=QT[hs][:, q0:q0 + ncol], start=True, stop=False, skip_group_check=True),
                 reads=[t_KT[hs], t_QT[hs]], writes=[tSp])
            P.op("pe", lambda e: e.matmul(Sp[:, c0:c0 + ncol], lhsT=KP[0:64, i * 128:(i + 1) * 128], rhs=QP[hs][0:64, q0:q0 + ncol], start=False, stop=(nn == 0), skip_group_check=True),
                 reads=[t_KP, t_QP[hs]], writes=[tSp], join=True)
        for bi, (jj, btile) in enumerate(near):
            P.op("pe", lambda e, jj=jj, btile=btile, bi=bi: e.matmul(Sp[:, jj * 128:(jj + 1) * 128], lhsT=C.ident, rhs=btile, start=False, stop=(bi == nn - 1), skip_group_check=True),
                 reads=[tl, C.t_const], writes=[tSp], join=True)
        pp = n % NPT
        ebias = 0.0
        P.op("act", lambda e: e.activation(out=PT[pp][:, c0:c0 + ncol], in_=Sp[:, c0:c0 + ncol], func=AF.Exp, bias=ebias, scale=1.0),
             reads=[tSp, tl], writes=[t_PT[pp]])

    def emit_PV(n, h, qg, t, i):
        hs = h % 2
        jmin = max(0, i - 4 * qg)
        c0 = jmin * 128
        pp = n % NPT
        bo, bd = banks(qg, t)
        last = (i == 4 * qg + 3)
        P.op("pe", lambda e: e.matmul(C.psf[bo][:, c0:512], lhsT=Vh[hs][:, i, :], rhs=PT[pp][:, c0:512], start=(i == 0), stop=last),
             reads=[t_PT[pp], t_Vh[hs]], writes=[C.t_psf[bo]], join=(i > 0))
        P.op("pe", lambda e: e.matmul(C.psf[bd][:, c0:512], lhsT=onesb, rhs=PT[pp][:, c0:512], start=(i == 0), stop=last),
             reads=[t_PT[pp], tl], writes=[C.t_psf[bd]], join=(i > 0))

    pending = []
    deferred = []
    cur_n = [0]
    gser = [0]

    def emit_evac(qg, t):
        par = qg % 2
        bo, bd = banks(qg, t)
        P.op("dve", lambda e: e.tensor_copy(out=Osb[par][t], in_=C.psf[bo]), reads=[C.t_psf[bo]], writes=[t_Osb[par][t]])
        P.op("dve", lambda e: e.tensor_copy(out=Dsb[par][t], in_=C.psf[bd]), reads=[C.t_psf[bd]], writes=[t_Dsb[par][t]])

    def emit_epilogue(n, h, qg):
        hs = h % 2
        par = qg % 2
        ysl = yT[hs][:, qg * 512:(qg + 1) * 512]
        gsl = Gh[hs][:, qg * 512:(qg + 1) * 512]
        O_, D_, tO, tD = Osb[par], Dsb[par], t_Osb[par], t_Dsb[par]
        ts = (0, 1) if diff else (0,)
        last = (qg == NT // 4 - 1)
        deferred.append(lambda: emit_epilogue2(n, h, qg))

    def emit_epilogue2(n, h, qg):
        hs = h % 2
        par = qg % 2
        ysl = yT[hs][:, qg * 512:(qg + 1) * 512]
        gsl = Gh[hs][:, qg * 512:(qg + 1) * 512]
        O_, D_, tO, tD = Osb[par], Dsb[par], t_Osb[par], t_Dsb[par]
        ts = (0, 1) if diff else (0,)
        last = (qg == NT // 4 - 1)
        for t in ts:
            P.op("dve", lambda e, t=t: e.reciprocal(out=D_[t], in_=D_[t]), reads=[tD[t]], writes=[tD[t]])
        if diff:
            for t in (1, 0):
                P.op("dve", lambda e, t=t: e.tensor_tensor(out=O_[t], in0=O_[t], in1=D_[t], op=ALU.mult), reads=[tO[t], tD[t]], writes=[tO[t]])
            P.op("dve", lambda e: e.scalar_tensor_tensor(out=O_[0], in0=O_[1], scalar=lamc[:, 0:1], in1=O_[0], op0=ALU.mult, op1=ALU.add), reads=[tO[0], tO[1], tl], writes=[tO[0]])
            P.op("pool", lambda e: e.tensor_tensor(out=sqb[par], in0=O_[0], in1=O_[0], op=ALU.mult), reads=[tO[0]], writes=[t_sqb[par]])

            def stage2():
                P.op("pe", lambda e: e.matmul(C.psf[6], lhsT=C.ones128, rhs=sqb[par], start=True, stop=True), reads=[t_sqb[par], C.t_const], writes=[C.t_psf[6]])
                P.op("act", lambda e: e.activation(out=rsb[par], in_=C.psf[6], func=AF.Ln, bias=C.epscol, scale=1.0 / 128), reads=[C.t_psf[6], C.t_const], writes=[t_rsb[par]])
                P.op("act", lambda e: e.activation(out=rsb[par], in_=rsb[par], func=AF.Exp, scale=-0.5), reads=[t_rsb[par]], writes=[t_rsb[par]])
                P.op("dve", lambda e: e.scalar_tensor_tensor(out=O_[0], in0=O_[0], scalar=sgcol, in1=rsb[par], op0=ALU.mult, op1=ALU.mult), reads=[tO[0], t_rsb[par], tl], writes=[tO[0]])
                P.op("pool", lambda e: e.tensor_tensor(out=ysl, in0=O_[0], in1=gsl, op=ALU.mult), reads=[tO[0], t_Gh[hs]], writes=[t_yT[hs]], join=True)
                if last:
                    P.op("sync", lambda e: e.dma_start(out=C.YT_d[h], in_=yT[hs]), reads=[t_yT[hs]], writes=[t_yd], dma=True, join=True)
            pending.append((cur_n[0] + 16, stage2, gser[0] - 1))
        else:
            P.op("pool", lambda e: e.tensor_tensor(out=O_[0], in0=O_[0], in1=D_[0], op=ALU.mult), reads=[tO[0], tD[0]], writes=[tO[0]])
            P.op("pool", lambda e: e.tensor_tensor(out=ysl, in0=O_[0], in1=gsl, op=ALU.mult), reads=[tO[0], t_Gh[hs]], writes=[t_yT[hs]], join=True)
            if last:
                P.op("sync", lambda e: e.dma_start(out=C.YT_d[h], in_=yT[hs]), reads=[t_yT[hs]], writes=[t_yd], dma=True, join=True)

    tiles = [(h, qg, t, i) for h in range(H) for qg in range(NT // 4) for t in range(nsub) for i in range(4 * qg + 4)]

    def emit_S_at(n):
        h_, qg_, t_, i_ = tiles[n]
        if qg_ == 0 and t_ == 0 and i_ == 0:
            emit_loads(h_)
        emit_S(n, h_, qg_, t_, i_)

    for n in range(min(LA, len(tiles))):
        emit_S_at(n)
    for n, (h, qg, t, i) in enumerate(tiles):
        if n + LA < len(tiles):
            emit_S_at(n + LA)
        emit_PV(n, h, qg, t, i)
        while pending and pending[0][0] <= n:
            pending.pop(0)[1]()
        cur_n[0] = n
        if i == 4 * qg + 3:
            if t == 0:
                while pending and pending[0][2] <= gser[0] - 2:
                    pending.pop(0)[1]()
            emit_evac(qg, t)
            while deferred:
                deferred.pop(0)()
            if t == nsub - 1:
                emit_epilogue(n, h, qg)
                gser[0] += 1
    while deferred:
        deferred.pop(0)()
    while pending:
        pending.pop(0)[1]()
    A.release(m)
    P.barrier()


def load_col(C, dst, src_vec, n, scale=None):
    P = C.P
    P.op("sync", lambda e: e.dma_start(out=dst[0:n, :], in_=src_vec.rearrange("(d o) -> d o", o=1)), writes=[C.t_lconst], dma=True, join=True)
    if scale is not None:
        P.op("dve", lambda e: e.tensor_scalar(out=dst[0:n, :], in0=dst[0:n, :], scalar1=float(scale), scalar2=None, op0=ALU.mult), reads=[C.t_lconst], writes=[C.t_lconst])


def rope_epilogue(C, ps, tps, gcol, cs, sn, t_cs, dst, t_dst, tmp, t_tmp, delay=2):
    P = C.P
    xf, sq, rs = tmp
    tick(C)
    P.op("dve", lambda e: e.tensor_copy(out=xf[0:64, :], in_=ps[0:64, :]), reads=[tps], writes=[t_tmp[0]])
    P.op("pool", lambda e: e.tensor_tensor(out=sq[0:64, :], in0=xf[0:64, :], in1=xf[0:64, :], op=ALU.mult), reads=[t_tmp[0]], writes=[t_tmp[1]])

    def stage2():
        pss, tpss = C.psf[4 + C.auxcnt % 2], C.t_psf[4 + C.auxcnt % 2]
        C.auxcnt += 1
        P.op("pe", lambda e: e.matmul(pss[0:64, :], lhsT=C.ones64[0:64, 0:64], rhs=sq[0:64, :], start=True, stop=True), reads=[t_tmp[1], C.t_const], writes=[tpss])
        P.op("act", lambda e: e.activation(out=rs[0:64, :], in_=pss[0:64, :], func=AF.Ln, bias=C.epscol[0:64, :], scale=1.0 / 64), reads=[tpss, C.t_const], writes=[t_tmp[2]])
        P.op("act", lambda e: e.activation(out=rs[0:64, :], in_=rs[0:64, :], func=AF.Exp, scale=-0.5), reads=[t_tmp[2]], writes=[t_tmp[2]])
        P.op("dve", lambda e: e.scalar_tensor_tensor(out=xf[0:64, :], in0=xf[0:64, :], scalar=gcol[0:64, :], in1=rs[0:64, :], op0=ALU.mult, op1=ALU.mult),
             reads=[t_tmp[0], t_tmp[2], C.t_lconst], writes=[t_tmp[0]])

    def stage3():
        pr, tpr = C.psf[4 + C.auxcnt % 2], C.t_psf[4 + C.auxcnt % 2]
        C.auxcnt += 1
        P.op("pe", lambda e: e.matmul(pr[0:64, :], lhsT=C.rotm[0:64, 0:64], rhs=xf[0:64, :], start=True, stop=True), reads=[t_tmp[0], C.t_const], writes=[tpr])
        P.op("dve", lambda e: e.tensor_tensor(out=sq[0:64, :], in0=pr[0:64, :], in1=sn, op=ALU.mult), reads=[tpr, t_cs], writes=[t_tmp[1]])
        P.op("pool", lambda e: e.tensor_tensor(out=xf[0:64, :], in0=xf[0:64, :], in1=cs, op=ALU.mult), reads=[t_tmp[0], t_cs], writes=[t_tmp[0]])
        P.op("pool", lambda e: e.tensor_tensor(out=dst, in0=xf[0:64, :], in1=sq[0:64, :], op=ALU.add), reads=[t_tmp[0], t_tmp[1]], writes=[t_dst], join=True)
    defer(C, delay, stage2)
    defer(C, 2 * delay, stage3)


def g_block(C, hT, hT_toks, wt_s, t_wt_s, tok0, c0, pcnt, gst, t_gst, gcnt, tpb=TPB):
    P = C.P
    t_gd = C.t_gd
    for i in range(tpb):
        pb = pcnt % 4; pcnt += 1
        ps, tps = C.psf[pb], C.t_psf[pb]
        for k in range(16):
            P.op("pe", lambda e, ps=ps, k=k, i=i: e.matmul(ps, lhsT=hT[:, k, i * 128:(i + 1) * 128], rhs=wt_s[:, k, :], start=(k == 0), stop=(k == 15)),
                 reads=[t_wt_s, hT_toks[i]], writes=[tps], join=(k > 0))
        r0 = tok0 + i * 128
        g4 = i % 4
        if g4 == 0:
            gs = gcnt % 2; gcnt += 1
        P.op("act", lambda e, ps=ps, gs=gs, g4=g4: e.activation(out=gst[gs][:, g4, :], in_=ps, func=AF.Silu), reads=[tps], writes=[t_gst[gs]], join=True)
        if g4 == 3:
            rr = r0 - 3 * 128
            P.op("sync", lambda e, gs=gs, rr=rr, c0=c0: e.dma_start(out=C.G_d[rr:rr + 512, c0:c0 + 512].rearrange("(a p) c -> p a c", p=128), in_=gst[gs]),
                 reads=[t_gst[gs]], writes=[t_gd], dma=True, join=True)
    return pcnt, gcnt


def layer3_proj1(C, x_src, W):
    P, A = C.P, C.A
    m = A.mark()
    TB = 1024; TPB = TB // 128; NTB = S // TB
    hT = A.alloc([16, TB], BF16); hT_toks = P.toks(TPB, "hT")
    wt = [A.alloc([16, 512], BF16) for _ in range(2)]; t_wt = P.toks(2, "wt")
    wkp = A.alloc([16, 64], BF16); t_wkp = P.tok("wkp")
    cst = [A.alloc([4, TB], BF16) for _ in range(2)]; t_cst = P.toks(2, "cst")
    cf = [A.alloc([512], F32) for _ in range(4)]; t_cf = P.toks(4, "cf")
    sq = [A.alloc([512], F32) for _ in range(2)]; t_sq = P.toks(2, "sq")
    rs = A.alloc([512], F32); t_rs = P.tok("rs")
    tmp = [A.alloc([512], F32) for _ in range(3)]; t_tmp = P.toks(3, "tmp")
    kpst = A.alloc([TB], BF16); t_kpst = P.tok("kpst")
    cs = A.alloc([TB], F32); sn = A.alloc([TB], F32); t_cs = P.tok("cs")
    gst = [A.alloc([TB], F32) for _ in range(2)]; t_gst = P.toks(2, "gst")
    glat = A.alloc([2, 4], F32); gkp = A.alloc([1], F32)
    C.t_gd = P.tok("Gd"); t_cd = P.tok("CQd"); t_kd = P.tok("KPEd")
    tl = C.t_lconst
    for mm_ in range(4):
        load_col(C, glat[:, 0, mm_:mm_ + 1], W["d_q_lat_g"][0, mm_ * 128:(mm_ + 1) * 128], 128)
        load_col(C, glat[:, 1, mm_:mm_ + 1], W["d_kv_lat_g"][0, mm_ * 128:(mm_ + 1) * 128], 128)
    load_col(C, gkp, W["d_qk_g"][0, 1, 128:192], 64)
    wcnt = 0; pcnt = 0; gcnt = 0; scnt = 0
    cols3 = [0, 512, 1088, 1600, 2112, 2624]
    pf = WPrefetch(C, wt, t_wt, [(lambda d, t, c0=c0: load_w_block(C, d, t, W["d_w_in"][0], c0, 512)) for _tb in range(NTB) for c0 in cols3])
    for tb in range(NTB):
        tok0 = tb * TB
        phase_norm_T(C, x_src, W["norm_g"][C.layer, :], tok0, hT, hT_toks, TB)
        P.op("sync", lambda e, tok0=tok0: e.dma_start(out=cs[0:64, :], in_=C.rope_in[0, :, tok0:tok0 + TB]), writes=[t_cs], dma=True)
        P.op("sync", lambda e, tok0=tok0: e.dma_start(out=sn[0:64, :], in_=C.rope_in[1, :, tok0:tok0 + TB]), writes=[t_cs], dma=True, join=True)
        for kind in range(2):
            ws = pf.next()
            for tq in range(TB // 512):
                pss, tpss = C.psf[4 + C.auxcnt % 2], C.t_psf[4 + C.auxcnt % 2]
                C.auxcnt += 1
                for mm_ in range(4):
                    pb = pcnt % 4; pcnt += 1
                    ps, tps = C.psf[pb], C.t_psf[pb]
                    for k in range(16):
                        P.op("pe", lambda e, ps=ps, k=k, ws=ws, mm_=mm_, tq=tq: e.matmul(ps, lhsT=wt[ws][:, k, mm_ * 128:(mm_ + 1) * 128], rhs=hT[:, k, tq * 512:(tq + 1) * 512], start=(k == 0), stop=(k == 15)),
                             reads=[t_wt[ws]] + hT_toks[tq * 4:(tq + 1) * 4], writes=[tps], join=(k > 0))
                    P.op("act", lambda e, ps=ps, mm_=mm_: e.activation(out=cf[mm_], in_=ps, func=AF.Copy), reads=[tps], writes=[t_cf[mm_]])
                    ss = scnt % 2; scnt += 1
                    P.op("act", lambda e, ps=ps, ss=ss: e.activation(out=sq[ss], in_=ps, func=AF.Square), reads=[tps], writes=[t_sq[ss]])
                    P.op("pe", lambda e, pss=pss, ss=ss, mm_=mm_: e.matmul(pss, lhsT=C.ones128, rhs=sq[ss], start=(mm_ == 0), stop=(mm_ == 3)), reads=[t_sq[ss], C.t_const], writes=[tpss], join=(mm_ > 0))
                P.op("act", lambda e, pss=pss: e.activation(out=rs, in_=pss, func=AF.Ln, bias=C.epscol, scale=1.0 / 512), reads=[tpss, C.t_const], writes=[t_rs])
                P.op("act", lambda e: e.activation(out=rs, in_=rs, func=AF.Exp, scale=-0.5), reads=[t_rs], writes=[t_rs])
                for mm_ in range(4):
                    P.op("dve", lambda e, mm_=mm_, kind=kind, tq=tq: e.scalar_tensor_tensor(out=cst[kind][:, mm_, tq * 512:(tq + 1) * 512], in0=cf[mm_], scalar=glat[:, kind, mm_:mm_ + 1], in1=rs, op0=ALU.mult, op1=ALU.mult),
                         reads=[t_cf[mm_], t_rs, tl], writes=[t_cst[kind]], join=True)
            dd = C.CQ_d if kind == 0 else C.CKV_d
            for mm_ in range(4):
                P.op("sync", lambda e, dd=dd, mm_=mm_, kind=kind, tok0=tok0: e.dma_start(out=dd[mm_, :, tok0:tok0 + TB], in_=cst[kind][:, mm_, :]), reads=[t_cst[kind]], writes=[t_cd], dma=True, join=True)
        load_w_block(C, wkp, t_wkp, W["d_w_in"][0], 1024, 64)
        for tq in range(TB // 512):
            pb = pcnt % 4; pcnt += 1
            ps, tps = C.psf[pb], C.t_psf[pb]
            for k in range(16):
                P.op("pe", lambda e, ps=ps, k=k, tq=tq: e.matmul(ps[0:64, :], lhsT=wkp[:, k, 0:64], rhs=hT[:, k, tq * 512:(tq + 1) * 512], start=(k == 0), stop=(k == 15)),
                     reads=[t_wkp] + hT_toks[tq * 4:(tq + 1) * 4], writes=[tps], join=(k > 0))
            rope_epilogue(C, ps, tps, gkp, cs[0:64, tq * 512:(tq + 1) * 512], sn[0:64, tq * 512:(tq + 1) * 512], t_cs, kpst[0:64, tq * 512:(tq + 1) * 512], t_kpst, tmp, t_tmp, delay=0)
        P.op("sync", lambda e, tok0=tok0: e.dma_start(out=C.KPE_d[:, tok0:tok0 + TB], in_=kpst[0:64, :]), reads=[t_kpst], writes=[t_kd], dma=True, join=True)
        for cb in range(4):
            ws = pf.next()
            pcnt, gcnt = gT_block(C, hT, hT_toks, wt[ws], t_wt[ws], tok0, cb * 4, TB, pcnt, gst, t_gst, gcnt)
    A.release(m)
    P.barrier()


def layer3_proj2(C, W):
    P, A = C.P, C.A
    m = A.mark()
    H = 16
    cq = A.alloc([4, TB], BF16); ckv = A.alloc([4, TB], BF16); t_cq = P.tok("cq"); t_ckv = P.tok("ckv")
    wuq = A.alloc([4, 3072], BF16); wkn = A.alloc([4, 2048], BF16); wv = A.alloc([4, 2048], BF16); t_w = P.tok("wup")
    qst = [A.alloc([TB], BF16) for _ in range(2)]; t_qst = P.toks(2, "qst")
    kst = [A.alloc([TB], BF16) for _ in range(2)]; t_kst = P.toks(2, "kst")
    qpst = [A.alloc([TB], BF16) for _ in range(2)]; t_qpst = P.toks(2, "qpst")
    NTMP = 4
    tmp = [[A.alloc([512], F32) for _ in range(3)] for _ in range(NTMP)]; t_tmp = [P.toks(3, "tmp") for _ in range(NTMP)]
    cs = A.alloc([TB], F32); sn = A.alloc([TB], F32); t_cs = P.tok("cs")
    vst = [A.alloc([8, 512], BF16) for _ in range(2)]; t_vst = P.toks(2, "vst")
    gqn = A.alloc([1], F32); gkn = A.alloc([1], F32); gqp = A.alloc([1], F32)
    t_qd = P.tok("QTd"); t_kd = P.tok("KTd"); t_qpd = P.tok("QPEd"); t_vd = P.tok("Vd")
    sc = 192.0 ** -0.5
    load_col(C, gqn, W["d_qk_g"][0, 0, 0:128], 128, sc)
    load_col(C, gkn, W["d_qk_g"][0, 1, 0:128], 128)
    load_col(C, gqp, W["d_qk_g"][0, 0, 128:192], 64, sc)
    wq_v = W["d_w_uq"][0].rearrange("(k p) c -> p k c", p=128)
    wkv_v = W["d_w_ukv"][0].rearrange("(k p) (h c) -> p k h c", p=128, c=256)
    for k in range(4):
        P.op("pool", lambda e, k=k: e.dma_start(out=wuq[:, k, :], in_=wq_v[:, k, :]), writes=[t_w], dma=True, join=True)
        P.op("pool", lambda e, k=k: e.dma_start(out=wkn[:, k, :].rearrange("p (h c) -> p h c", c=128), in_=wkv_v[:, k, :, 0:128]), writes=[t_w], dma=True, join=True)
        P.op("pool", lambda e, k=k: e.dma_start(out=wv[:, k, :].rearrange("p (h c) -> p h c", c=128), in_=wkv_v[:, k, :, 128:256]), writes=[t_w], dma=True, join=True)
    pcnt = 0; tcnt = 0; vcnt = 0
    for tb in range(NTB):
        tok0 = tb * TB
        for k in range(4):
            P.op("sync", lambda e, k=k, tok0=tok0: e.dma_start(out=cq[:, k, :], in_=C.CQ_d[k, :, tok0:tok0 + TB]), writes=[t_cq], dma=True, join=(k > 0))
            P.op("sync", lambda e, k=k, tok0=tok0: e.dma_start(out=ckv[:, k, :], in_=C.CKV_d[k, :, tok0:tok0 + TB]), writes=[t_ckv], dma=True, join=(k > 0))
        P.op("sync", lambda e, tok0=tok0: e.dma_start(out=cs[0:64, :], in_=C.rope_in[0, :, tok0:tok0 + TB]), writes=[t_cs], dma=True)
        P.op("sync", lambda e, tok0=tok0: e.dma_start(out=sn[0:64, :], in_=C.rope_in[1, :, tok0:tok0 + TB]), writes=[t_cs], dma=True, join=True)
        for h in range(H):
            hs = h % 2
            for which in range(3):
                for tq in range(TB // 512):
                    pb = pcnt % 4; pcnt += 1
                    ps, tps = C.psf[pb], C.t_psf[pb]
                    for k in range(4):
                        if which == 0:
                            lhs, rhs_, rt, M = wuq[:, k, h * 192:h * 192 + 128], cq[:, k, tq * 512:(tq + 1) * 512], t_cq, 128
                        elif which == 1:
                            lhs, rhs_, rt, M = wkn[:, k, h * 128:(h + 1) * 128], ckv[:, k, tq * 512:(tq + 1) * 512], t_ckv, 128
                        else:
                            lhs, rhs_, rt, M = wuq[:, k, h * 192 + 128:h * 192 + 192], cq[:, k, tq * 512:(tq + 1) * 512], t_cq, 64
                        P.op("pe", lambda e, ps=ps, k=k, lhs=lhs, rhs_=rhs_, M=M: e.matmul(ps[0:M, :], lhsT=lhs, rhs=rhs_, start=(k == 0), stop=(k == 3)),
                             reads=[t_w, rt], writes=[tps], join=(k > 0))
                    ts_ = tcnt % NTMP; tcnt += 1
                    if which == 0:
                        qk_norm_epilogue(C, ps, tps, gqn, qst[hs][:, tq * 512:(tq + 1) * 512], t_qst[hs], tmp[ts_], t_tmp[ts_], 128)
                    elif which == 1:
                        qk_norm_epilogue(C, ps, tps, gkn, kst[hs][:, tq * 512:(tq + 1) * 512], t_kst[hs], tmp[ts_], t_tmp[ts_], 128)
                    else:
                        rope_epilogue(C, ps, tps, gqp, cs[0:64, tq * 512:(tq + 1) * 512], sn[0:64, tq * 512:(tq + 1) * 512], t_cs, qpst[hs][0:64, tq * 512:(tq + 1) * 512], t_qpst[hs], tmp[ts_], t_tmp[ts_])
            def outs(h=h, hs=hs, tok0=tok0):
                P.op("sync", lambda e: e.dma_start(out=C.QT_d[h, :, tok0:tok0 + TB], in_=qst[hs]), reads=[t_qst[hs]], writes=[t_qd], dma=True, join=True)
                P.op("sync", lambda e: e.dma_start(out=C.KT_d[h, :, tok0:tok0 + TB], in_=kst[hs]), reads=[t_kst[hs]], writes=[t_kd], dma=True, join=True)
                P.op("sync", lambda e: e.dma_start(out=C.QPE_d[h, :, tok0:tok0 + TB], in_=qpst[hs][0:64, :]), reads=[t_qpst[hs]], writes=[t_qpd], dma=True, join=True)
            defer(C, 4, outs)
        flush(C)
        for n in range(4):
            for i in range(TPB):
                pb = pcnt % 4; pcnt += 1
                ps, tps = C.psf[pb], C.t_psf[pb]
                for k in range(4):
                    P.op("pe", lambda e, ps=ps, k=k, i=i, n=n: e.matmul(ps, lhsT=ckv[:, k, i * 128:(i + 1) * 128], rhs=wv[:, k, n * 512:(n + 1) * 512], start=(k == 0), stop=(k == 3)),
                         reads=[t_w, t_ckv], writes=[tps], join=(k > 0))
                g8 = i % 8
                if g8 == 0:
                    vs = vcnt % 2; vcnt += 1
                P.op("dve", lambda e, ps=ps, vs=vs, g8=g8: e.tensor_copy(out=vst[vs][:, g8, :], in_=ps), reads=[tps], writes=[t_vst[vs]], join=True)
                if g8 == 7:
                    rr = tok0 + (i - 7) * 128
                    P.op("sync", lambda e, vs=vs, rr=rr, n=n: e.dma_start(out=C.V_d[rr:rr + 1024, n * 512:(n + 1) * 512].rearrange("(a p) c -> p a c", p=128), in_=vst[vs]),
                         reads=[t_vst[vs]], writes=[t_vd], dma=True, join=True)
    A.release(m)
    P.barrier()


def layer2_all(C, x_src, W):
    P, A = C.P, C.A
    m = A.mark()
    TB = 1024; TPB = TB // 128; NTB = S // TB
    hT = A.alloc([16, TB], BF16); hT_toks = P.toks(TPB, "hT")
    wt = [A.alloc([16, 512], BF16) for _ in range(2)]; t_wt = P.toks(2, "wt")
    wrg = A.alloc([8, 2, 256], BF16); wig = A.alloc([8, 2, 256], BF16); t_wg = P.tok("wg")
    stage = A.alloc([128], F32); cols = A.alloc([8, 16], F32)
    halo = A.alloc([16, 3], F32); hprev = A.alloc([16], F32); t_halo = P.tok("halo"); t_hprev = P.tok("hprev")
    ubuf = [A.alloc([TB + 8], F32) for _ in range(2)]; t_ubuf = P.toks(2, "ubuf")
    xc = [A.alloc([TB], F32) for _ in range(2)]; t_xc = P.toks(2, "xc")
    xcb = [A.alloc([TB], BF16) for _ in range(2)]; t_xcb = P.toks(2, "xcb")
    sg = [A.alloc([TB], F32) for _ in range(2)]; t_sg = P.toks(2, "sg")
    rg = [A.alloc([TB], F32) for _ in range(2)]; t_rg = P.toks(2, "rg")
    ig = [A.alloc([TB], F32) for _ in range(2)]; t_ig = P.toks(2, "ig")
    abuf = A.alloc([TB], F32); a2buf = A.alloc([TB], F32); xinb = A.alloc([TB], F32); hh = A.alloc([TB], F32)
    t_a = P.tok("a"); t_a2 = P.tok("a2"); t_xin = P.tok("xin"); t_hh = P.tok("hh")
    yst = [A.alloc([TB], BF16) for _ in range(2)]; t_yst = P.toks(2, "yst")
    t_yd = P.tok("YTd")
    tl = C.t_lconst
    vecs = [W["c_conv_w"][0, 0], W["c_conv_w"][0, 1], W["c_conv_w"][0, 2], W["c_conv_w"][0, 3], W["c_conv_b"][0], W["c_b_rgate"][0], W["c_b_igate"][0], W["c_lambda"][0]]
    for v, vec in enumerate(vecs):
        P.op("sync", lambda e, v=v, vec=vec: e.dma_start(out=stage[v * 16:(v + 1) * 16, :], in_=vec.rearrange("(t p) -> t p", p=128)), writes=[tl], dma=True, join=True)
    ps0, tps0 = C.psf[4], C.t_psf[4]
    P.op("pe", lambda e: e.matmul(ps0[:, 0:128], lhsT=stage, rhs=C.identf, start=True, stop=True), reads=[tl, C.t_const], writes=[tps0])
    P.op("dve", lambda e: e.tensor_copy(out=cols, in_=ps0[:, 0:128].rearrange("p (v t) -> p v t", v=8)), reads=[tps0], writes=[tl])
    P.op("act", lambda e: e.activation(out=cols[:, 7, :], in_=cols[:, 7, :], func=AF.Exp, scale=-1.0), reads=[tl], writes=[tl])
    P.op("act", lambda e: e.activation(out=cols[:, 7, :], in_=cols[:, 7, :], func=AF.Ln, bias=1.0, scale=1.0), reads=[tl], writes=[tl])
    P.op("dve", lambda e: e.tensor_scalar(out=cols[:, 7, :], in0=cols[:, 7, :], scalar1=-8.0, scalar2=None, op0=ALU.mult), reads=[tl], writes=[tl])
    for n in range(8):
        P.op("pool", lambda e, n=n: e.dma_start(out=wrg[:, n], in_=W["c_w_rgate"][0, n].rearrange("(c p) e -> p c e", p=128)), writes=[t_wg], dma=True, join=True)
        P.op("pool", lambda e, n=n: e.dma_start(out=wig[:, n], in_=W["c_w_igate"][0, n].rearrange("(c p) e -> p c e", p=128)), writes=[t_wg], dma=True, join=True)
    wcnt = 0; pcnt = 0

    def ld2(d, t, n):
        load_w_block(C, d[:, :, 0:256], t, W["c_w_in"][0], n * 256, 256)
        load_w_block(C, d[:, :, 256:512], t, W["c_w_in"][0], 2048 + n * 256, 256)
    pf = WPrefetch(C, wt, t_wt, [(lambda d, t, n=n: ld2(d, t, n)) for _tb in range(NTB) for n in range(8)])
    for tb in range(NTB):
        tok0 = tb * TB
        phase_norm_T(C, x_src, W["norm_g"][C.layer, :], tok0, hT, hT_toks, TB)
        for n in range(8):
            ws = pf.next()
            for c in range(2):
                tile = n * 2 + c
                if tb == 0:
                    P.op("pool", lambda e, c=c: e.memset(ubuf[c][:, 0:3], 0.0), writes=[t_ubuf[c]])
                else:
                    P.op("pool", lambda e, c=c, tile=tile: e.tensor_copy(out=ubuf[c][:, 0:3], in_=halo[:, tile, :]), reads=[t_halo], writes=[t_ubuf[c]])
                for tq in range(TB // 512):
                    pb = pcnt % 4; pcnt += 1
                    ps, tps = C.psf[pb], C.t_psf[pb]
                    for k in range(16):
                        P.op("pe", lambda e, ps=ps, k=k, ws=ws, c=c, tq=tq: e.matmul(ps, lhsT=wt[ws][:, k, c * 128:(c + 1) * 128], rhs=hT[:, k, tq * 512:(tq + 1) * 512], start=(k == 0), stop=(k == 15)),
                             reads=[t_wt[ws]] + hT_toks[tq * 4:(tq + 1) * 4], writes=[tps], join=(k > 0))
                    P.op("act", lambda e, ps=ps, c=c, tq=tq: e.activation(out=ubuf[c][:, 3 + tq * 512:3 + (tq + 1) * 512], in_=ps, func=AF.Copy), reads=[tps], writes=[t_ubuf[c]], join=True)
                P.op("pool", lambda e, c=c, tile=tile: e.tensor_copy(out=halo[:, tile, :], in_=ubuf[c][:, TB:TB + 3]), reads=[t_ubuf[c]], writes=[t_halo], join=True)
                P.op("dve", lambda e, c=c, tile=tile: e.tensor_scalar(out=xc[c], in0=ubuf[c][:, 3:3 + TB], scalar1=cols[:, 3, tile:tile + 1], scalar2=cols[:, 4, tile:tile + 1], op0=ALU.mult, op1=ALU.add),
                     reads=[t_ubuf[c], tl], writes=[t_xc[c]])
                for tau in (2, 1, 0):
                    P.op("dve", lambda e, c=c, tile=tile, tau=tau: e.scalar_tensor_tensor(out=xc[c], in0=ubuf[c][:, tau:tau + TB], scalar=cols[:, tau, tile:tile + 1], in1=xc[c], op0=ALU.mult, op1=ALU.add),
                         reads=[t_ubuf[c], tl, t_xc[c]], writes=[t_xc[c]])
                P.op("pool", lambda e, c=c: e.tensor_copy(out=xcb[c], in_=xc[c]), reads=[t_xc[c]], writes=[t_xcb[c]])
                for tq in range(TB // 512):
                    pb = pcnt % 4; pcnt += 1
                    ps, tps = C.psf[pb], C.t_psf[pb]
                    for k in range(16):
                        P.op("pe", lambda e, ps=ps, k=k, ws=ws, c=c, tq=tq: e.matmul(ps, lhsT=wt[ws][:, k, 256 + c * 128:256 + (c + 1) * 128], rhs=hT[:, k, tq * 512:(tq + 1) * 512], start=(k == 0), stop=(k == 15)),
                             reads=[t_wt[ws]] + hT_toks[tq * 4:(tq + 1) * 4], writes=[tps], join=(k > 0))
                    P.op("act", lambda e, ps=ps, c=c, tq=tq: e.activation(out=sg[c][:, tq * 512:(tq + 1) * 512], in_=ps, func=AF.Silu), reads=[tps], writes=[t_sg[c]], join=True)
            for ce in range(2):
                tile = n * 2 + ce
                for (wg, bidx, dstb, tdst) in ((wrg, 5, rg, t_rg), (wig, 6, ig, t_ig)):
                    for tq in range(TB // 512):
                        pb = pcnt % 4; pcnt += 1
                        ps, tps = C.psf[pb], C.t_psf[pb]
                        for cc in range(2):
                            P.op("pe", lambda e, ps=ps, wg=wg, cc=cc, ce=ce, tq=tq, n=n: e.matmul(ps, lhsT=wg[:, n, cc, ce * 128:(ce + 1) * 128], rhs=xcb[cc][:, tq * 512:(tq + 1) * 512], start=(cc == 0), stop=(cc == 1)),
                                 reads=[t_wg, t_xcb[cc]], writes=[tps], join=(cc > 0))
                        P.op("act", lambda e, ps=ps, dstb=dstb, ce=ce, tq=tq, bidx=bidx, tile=tile: e.activation(out=dstb[ce][:, tq * 512:(tq + 1) * 512], in_=ps, func=AF.Sigmoid, bias=cols[:, bidx, tile:tile + 1], scale=1.0),
                             reads=[tps, tl], writes=[tdst[ce]], join=True)
                P.op("act", lambda e, ce=ce, tile=tile: e.activation(out=abuf, in_=rg[ce], func=AF.Exp, scale=cols[:, 7, tile:tile + 1]), reads=[t_rg[ce], tl], writes=[t_a])
                P.op("pool", lambda e: e.tensor_tensor(out=a2buf, in0=abuf, in1=abuf, op=ALU.mult), reads=[t_a], writes=[t_a2])
                P.op("act", lambda e: e.activation(out=a2buf, in_=a2buf, func=AF.Sqrt, bias=1.0, scale=-1.0), reads=[t_a2], writes=[t_a2])
                P.op("pool", lambda e, ce=ce: e.tensor_tensor(out=xinb, in0=ig[ce], in1=xc[ce], op=ALU.mult), reads=[t_ig[ce], t_xc[ce]], writes=[t_xin])
                P.op("dve", lambda e: e.tensor_tensor(out=xinb, in0=xinb, in1=a2buf, op=ALU.mult), reads=[t_xin, t_a2], writes=[t_xin])
                init = 0.0 if tb == 0 else hprev[:, tile:tile + 1]
                P.op("dve", lambda e, init=init: e.tensor_tensor_scan(out=hh, data0=abuf, data1=xinb, initial=init, op0=ALU.mult, op1=ALU.add), reads=[t_a, t_xin, t_hprev], writes=[t_hh])
                P.op("pool", lambda e, tile=tile: e.tensor_copy(out=hprev[:, tile:tile + 1], in_=hh[:, TB - 1:TB]), reads=[t_hh], writes=[t_hprev])
                P.op("pool", lambda e, ce=ce: e.tensor_tensor(out=yst[ce], in0=hh, in1=sg[ce], op=ALU.mult), reads=[t_hh, t_sg[ce]], writes=[t_yst[ce]])
                P.op("sync", lambda e, ce=ce, tile=tile, tok0=tok0: e.dma_start(out=C.YT_d[tile, :, tok0:tok0 + TB], in_=yst[ce]), reads=[t_yst[ce]], writes=[t_yd], dma=True, join=True)
    A.release(m)
    P.barrier()


def layer1_all(C, x_src, W):
    P, A = C.P, C.A
    m0 = A.mark()
    dec = A.alloc([8, 64], F32); t_dec = P.tok("dec")
    m = A.mark()
    TB = 1024; TPB = TB // 128; NTB = S // TB
    hT = A.alloc([16, TB], BF16); hT_toks = P.toks(TPB, "hT")
    wt = [A.alloc([16, 512], BF16) for _ in range(2)]; t_wt = P.toks(2, "wt")
    wlr = A.alloc([16, 16], BF16); t_wlr = P.tok("wlr")
    lrT = A.alloc([TB], F32); t_lrT = P.tok("lrT")
    wga = A.alloc([1024], F32)
    TU = A.alloc([128], F32); CI = A.alloc([2], F32)
    qst = [A.alloc([TB], BF16) for _ in range(2)]; t_qst = P.toks(2, "qst")
    ebuf = [A.alloc([512], F32) for _ in range(2)]; t_eb = P.toks(2, "ebuf")
    wbuf = [A.alloc([512], F32) for _ in range(2)]; t_wb = P.toks(2, "wbuf")
    kst = [A.alloc([8, 512], BF16) for _ in range(2)]; t_kst = P.toks(2, "kst")
    vst = [A.alloc([8, 512], BF16) for _ in range(2)]; t_vst = P.toks(2, "vst")
    gst = [A.alloc([4, 512], F32) for _ in range(2)]; t_gst = P.toks(2, "gst")
    C.t_gd = P.tok("Gd"); t_qd = P.tok("QTd"); t_kd = P.tok("KPd"); t_vd = P.tok("Vd")
    tl = C.t_lconst
    Wi = W["b_w_in"][0]
    P.op("pool", lambda e: e.memset(wga[0:32, :], 0.0), writes=[tl])
    P.op("sync", lambda e: e.dma_start(out=wga[0:16, :], in_=W["b_w_gate"][0]), writes=[tl], dma=True)
    P.op("sync", lambda e: e.dma_start(out=wga[16:17, :], in_=W["b_gate_bias"][0:1, :]), writes=[tl], dma=True, join=True)
    P.op("sync", lambda e: e.dma_start(out=TU, in_=C.TU_d), writes=[tl], dma=True, join=True)
    P.op("sync", lambda e: e.dma_start(out=CI, in_=C.CI_d), writes=[tl], dma=True, join=True)
    P.op("pool", lambda e: e.memset(lrT[0:32, :], 1.0), writes=[t_lrT])
    wcnt = 0; pcnt = 0; gcnt = 0; vcnt = 0; qcnt = 0; ecnt = 0; kcnt = 0
    pf = WPrefetch(C, wt, t_wt, [(lambda d, t, cb=cb: load_w_block(C, d, t, Wi, cb * 512, 512)) for _tb in range(NTB) for cb in range(12)])
    for tb in range(NTB):
        tok0 = tb * TB
        phase_norm_T(C, x_src, W["norm_g"][C.layer, :], tok0, hT, hT_toks, TB)
        load_w_block(C, wlr, t_wlr, Wi, 6144, 16)
        for tq in range(TB // 512):
            pb = pcnt % 4; pcnt += 1
            ps, tps = C.psf[pb], C.t_psf[pb]
            for k in range(16):
                P.op("pe", lambda e, ps=ps, k=k, tq=tq: e.matmul(ps[0:16, :], lhsT=wlr[:, k, 0:16], rhs=hT[:, k, tq * 512:(tq + 1) * 512], start=(k == 0), stop=(k == 15)),
                     reads=[t_wlr] + hT_toks[tq * 4:(tq + 1) * 4], writes=[tps], join=(k > 0))
            P.op("act", lambda e, ps=ps, tq=tq: e.activation(out=lrT[0:16, tq * 512:(tq + 1) * 512], in_=ps[0:16, :], func=AF.Copy), reads=[tps], writes=[t_lrT], join=True)
        for cb in range(12):
            ws = pf.next()
            if cb < 2:
                for mm_ in range(4):
                    qt = cb * 4 + mm_
                    qs = qcnt % 2; qcnt += 1
                    for tq in range(TB // 512):
                        pb = pcnt % 4; pcnt += 1
                        ps, tps = C.psf[pb], C.t_psf[pb]
                        for k in range(16):
                            P.op("pe", lambda e, ps=ps, k=k, ws=ws, mm_=mm_, tq=tq: e.matmul(ps, lhsT=wt[ws][:, k, mm_ * 128:(mm_ + 1) * 128], rhs=hT[:, k, tq * 512:(tq + 1) * 512], start=(k == 0), stop=(k == 15)),
                                 reads=[t_wt[ws]] + hT_toks[tq * 4:(tq + 1) * 4], writes=[tps], join=(k > 0))
                        P.op("act", lambda e, ps=ps, qs=qs, tq=tq: e.activation(out=qst[qs][:, tq * 512:(tq + 1) * 512], in_=ps, func=AF.Copy, scale=1.0 / 16.0), reads=[tps], writes=[t_qst[qs]], join=True)
                    P.op("sync", lambda e, qt=qt, qs=qs, tok0=tok0: e.dma_start(out=C.QT_d[qt, :, tok0:tok0 + TB], in_=qst[qs]), reads=[t_qst[qs]], writes=[t_qd], dma=True, join=True)
            elif cb < 4:
                kb = cb - 2
                ks = kcnt % 2; kcnt += 1
                for i in range(TPB):
                    pb = pcnt % 4; pcnt += 1
                    ps, tps = C.psf[pb], C.t_psf[pb]
                    for k in range(16):
                        P.op("pe", lambda e, ps=ps, k=k, ws=ws, i=i: e.matmul(ps, lhsT=hT[:, k, i * 128:(i + 1) * 128], rhs=wt[ws][:, k, :], start=(k == 0), stop=(k == 15)),
                             reads=[t_wt[ws], hT_toks[i]], writes=[tps], join=(k > 0))
                    es = ecnt % 2; ecnt += 1
                    pz, tpz = C.psf[4], C.t_psf[4]
                    P.op("pe", lambda e, pz=pz, i=i, kb=kb: e.matmul(pz, lhsT=lrT[0:32, i * 128:(i + 1) * 128], rhs=wga[0:32, kb * 512:(kb + 1) * 512], start=True, stop=True),
                         reads=[t_lrT, tl], writes=[tpz])
                    P.op("act", lambda e, pz=pz, es=es: e.activation(out=ebuf[es], in_=pz, func=AF.Exp, scale=-1.0), reads=[tpz], writes=[t_eb[es]])
                    P.op("act", lambda e, es=es: e.activation(out=ebuf[es], in_=ebuf[es], func=AF.Ln, bias=1.0, scale=1.0), reads=[t_eb[es]], writes=[t_eb[es]])
                    pr_, tpr = C.psf[5], C.t_psf[5]
                    P.op("pe", lambda e, pr_=pr_, es=es: e.matmul(pr_, lhsT=TU, rhs=ebuf[es], start=True, stop=True), reads=[t_eb[es], tl], writes=[tpr])
                    P.op("act", lambda e, pr_=pr_, es=es: e.activation(out=wbuf[es], in_=pr_, func=AF.Exp), reads=[tpr], writes=[t_wb[es]])
                    P.op("dve", lambda e, ps=ps, es=es, ks=ks, i=i: e.tensor_tensor(out=kst[ks][:, i, :], in0=ps, in1=wbuf[es], op=ALU.mult), reads=[tps, t_wb[es]], writes=[t_kst[ks]], join=True)
                    for dt in range(4):
                        P.op("pe", lambda e, pz=pz, es=es, dt=dt: e.matmul(pz[:, dt * 2:dt * 2 + 2], lhsT=ebuf[es][:, dt * 128:(dt + 1) * 128], rhs=CI, start=(dt == 0), stop=(dt == 3), skip_group_check=True),
                             reads=[t_eb[es], tl], writes=[tpz], join=(dt > 0))
                    ch0 = (tok0 + i * 128) // 64
                    P.op("act", lambda e, pz=pz, kb=kb, ch0=ch0: e.activation(out=dec[:, kb * 4:(kb + 1) * 4, ch0:ch0 + 2], in_=pz[:, 0:8].rearrange("p (a b) -> p a b", a=4), func=AF.Exp), reads=[tpz], writes=[t_dec], join=True)
                P.op("sync", lambda e, ks=ks, tok0=tok0, kb=kb: e.dma_start(out=C.KP_d[tok0:tok0 + TB, kb * 512:(kb + 1) * 512].rearrange("(a p) c -> p a c", p=128), in_=kst[ks]),
                     reads=[t_kst[ks]], writes=[t_kd], dma=True, join=True)
            elif cb < 8:
                c0 = (cb - 4) * 512
                vs = vcnt % 2; vcnt += 1
                for i in range(TPB):
                    pb = pcnt % 4; pcnt += 1
                    ps, tps = C.psf[pb], C.t_psf[pb]
                    for k in range(16):
                        P.op("pe", lambda e, ps=ps, k=k, ws=ws, i=i: e.matmul(ps, lhsT=hT[:, k, i * 128:(i + 1) * 128], rhs=wt[ws][:, k, :], start=(k == 0), stop=(k == 15)),
                             reads=[t_wt[ws], hT_toks[i]], writes=[tps], join=(k > 0))
                    P.op("dve", lambda e, ps=ps, vs=vs, i=i: e.tensor_copy(out=vst[vs][:, i, :], in_=ps), reads=[tps], writes=[t_vst[vs]], join=True)
                P.op("sync", lambda e, vs=vs, tok0=tok0, c0=c0: e.dma_start(out=C.V_d[tok0:tok0 + TB, c0:c0 + 512].rearrange("(a p) c -> p a c", p=128), in_=vst[vs]),
                     reads=[t_vst[vs]], writes=[t_vd], dma=True, join=True)
            else:
                pcnt, gcnt = g_block(C, hT, hT_toks, wt[ws], t_wt[ws], tok0, (cb - 8) * 512, pcnt, gst, t_gst, gcnt, TPB)
    A.release(m)
    P.barrier()
    m = A.mark()
    Kp = A.alloc([NT, 256], BF16); t_Kp = P.tok("Kp")
    Vh = A.alloc([NT, 512], BF16); t_Vh = P.tok("Vh")
    QTt = [A.alloc([S], BF16) for _ in range(2)]; t_QTt = P.tok("QTt")
    Sf = A.alloc([2, 512], F32); t_Sf = P.tok("Sf")
    Sb = [A.alloc([2, 512], BF16) for _ in range(2)]; t_Sb = P.toks(2, "Sb")
    NG = 4
    Gt = [A.alloc([2, 512], F32) for _ in range(NG)]; t_Gt = P.toks(NG, "Gt")
    yT = A.alloc([4, S], BF16); t_yT = P.tok("yT")
    ogb = A.alloc([512], F32)
    NE = 6
    of = [A.alloc([512], F32) for _ in range(NE)]; yb = [A.alloc([512], BF16) for _ in range(NE)]
    ssq = [A.alloc([1], F32) for _ in range(NE)]; junk = A.alloc([512], BF16)
    t_ss = P.toks(NE, "ss"); t_of = P.toks(NE, "of"); t_yb = P.toks(NE, "yb"); t_junk = P.tok("junk"); t_yd = P.tok("YTd")
    P.op("sync", lambda e: e.dma_start(out=ogb, in_=W["b_out_g"][0, :].partition_broadcast(128)), writes=[tl], dma=True)
    NCH = S // 64
    PO = [4, 5, 6]
    tp7, ttp7 = C.tpb[1], C.t_tpb[1]

    def emit_kv(hh, c):
        i, par = c // 2, c % 2
        pr0 = par * 64
        for dh in range(2):
            kv, tkv = C.psf[2 * (c % 2) + dh], C.t_psf[2 * (c % 2) + dh]
            P.op("pe", lambda e, kv=kv, dh=dh: e.matmul(kv, lhsT=Kp[pr0:pr0 + 64, i, dh * 128:(dh + 1) * 128], rhs=Vh[pr0:pr0 + 64, i, :], start=True, stop=True),
                 reads=[t_Kp, t_Vh], writes=[tkv])

    def emit_epiA(hh, c):
        i, par = c // 2, c % 2
        es = c % NE
        gs = i % NG
        po, tpo = C.psf[PO[c % 3]], C.t_psf[PO[c % 3]]
        P.op("act", lambda e: e.activation(out=junk[0:64, :], in_=po[0:64, :], func=AF.Square, accum_out=ssq[es][0:64, :]), reads=[tpo], writes=[t_junk, t_ss[es]])
        P.op("act", lambda e: e.activation(out=ssq[es][0:64, :], in_=ssq[es][0:64, :], func=AF.Ln, bias=C.epscol[0:64, :], scale=1.0 / 512), reads=[t_ss[es], C.t_const], writes=[t_ss[es]])
        P.op("act", lambda e: e.activation(out=ssq[es][0:64, :], in_=ssq[es][0:64, :], func=AF.Exp, scale=-0.5), reads=[t_ss[es]], writes=[t_ss[es]])
        P.op("dve", lambda e: e.scalar_tensor_tensor(out=of[es][0:64, :], in0=po[0:64, :], scalar=ssq[es][0:64, :], in1=ogb[0:64, :], op0=ALU.mult, op1=ALU.mult), reads=[tpo, t_ss[es], tl], writes=[t_of[es]])
        P.op("pool", lambda e: e.tensor_tensor(out=yb[es][0:64, :], in0=of[es][0:64, :], in1=Gt[gs][0:64, par, :], op=ALU.mult), reads=[t_of[es], t_Gt[gs]], writes=[t_yb[es]])

    def emit_epiB(hh, c):
        es = c % NE
        q4 = (c % 4) * 256
        for j in range(4):
            P.op("pe", lambda e, j=j: e.transpose(out=tp7[:, q4 + j * 64:q4 + (j + 1) * 64], in_=yb[es][0:64, j * 128:(j + 1) * 128], identity=C.ident[0:64, 0:64]), reads=[t_yb[es], C.t_const], writes=[ttp7], join=True)
        P.op("dve", lambda e: e.tensor_copy(out=yT[:, :, c * 64:(c + 1) * 64], in_=tp7[:, q4:q4 + 256].rearrange("p (a b) -> p a b", a=4)), reads=[ttp7], writes=[t_yT], join=True)

    DA, DB = 2, 4
    for hh in range(4):
        P.op("sync", lambda e, hh=hh: e.dma_start(out=Kp, in_=C.KP_d[:, hh * 256:(hh + 1) * 256].rearrange("(a p) c -> p a c", p=128)), writes=[t_Kp], dma=True)
        P.op("sync", lambda e, hh=hh: e.dma_start(out=Vh, in_=C.V_d[:, hh * 512:(hh + 1) * 512].rearrange("(a p) c -> p a c", p=128)), writes=[t_Vh], dma=True)
        for dh in range(2):
            P.op("sync", lambda e, hh=hh, dh=dh: e.dma_start(out=QTt[dh], in_=C.QT_d[2 * hh + dh]), writes=[t_QTt], dma=True, join=(dh > 0))
        P.op("pool", lambda e: e.memset(Sf, 0.0), writes=[t_Sf])
        emit_kv(hh, 0)
        for c in range(NCH):
            i, par = c // 2, c % 2
            if par == 0:
                gs = i % NG
                P.op("sync", lambda e, gs=gs, i=i, hh=hh: e.dma_start(out=Gt[gs][0:64, :, :], in_=C.G_d[i * 128:(i + 1) * 128, hh * 512:(hh + 1) * 512].rearrange("(par p) c -> p par c", p=64)), writes=[t_Gt[gs]], dma=True)
            sbs = c % 2
            for dh in range(2):
                kv, tkv = C.psf[2 * (c % 2) + dh], C.t_psf[2 * (c % 2) + dh]
                P.op("dve", lambda e, kv=kv, dh=dh, hh=hh, c=c: e.scalar_tensor_tensor(out=Sf[:, dh, :], in0=Sf[:, dh, :], scalar=dec[:, 2 * hh + dh, c:c + 1], in1=kv, op0=ALU.mult, op1=ALU.add),
                     reads=[t_Sf, t_dec, tkv], writes=[t_Sf])
                P.op("act", lambda e, dh=dh, sbs=sbs: e.activation(out=Sb[sbs][:, dh, :], in_=Sf[:, dh, :], func=AF.Copy), reads=[t_Sf], writes=[t_Sb[sbs]], join=(dh > 0))
            if c + 1 < NCH:
                emit_kv(hh, c + 1)
            po, tpo = C.psf[PO[c % 3]], C.t_psf[PO[c % 3]]
            for dh in range(2):
                P.op("pe", lambda e, po=po, dh=dh, c=c, sbs=sbs: e.matmul(po[0:64, :], lhsT=QTt[dh][:, c * 64:(c + 1) * 64], rhs=Sb[sbs][:, dh, :], start=(dh == 0), stop=(dh == 1)),
                     reads=[t_QTt, t_Sb[sbs]], writes=[tpo], join=(dh > 0))
            if c >= DA:
                emit_epiA(hh, c - DA)
            if c >= DB:
                emit_epiB(hh, c - DB)
        for c in range(NCH - DA, NCH):
            emit_epiA(hh, c)
        for c in range(NCH - DB, NCH):
            emit_epiB(hh, c)
        for j in range(4):
            P.op("sync", lambda e, hh=hh, j=j: e.dma_start(out=C.YT_d[hh * 4 + j], in_=yT[:, j, :]), reads=[t_yT], writes=[t_yd], dma=True, join=True)
    A.release(m0)
    P.barrier()


WSHAPES = {
    "norm_g": (4, 2048), "rel_bias": (32, 16),
    "a_w_in": (1, 2048, 8192), "a_qk_g": (1, 2, 64), "a_lambda": (1, 4, 64), "a_subln_g": (1, 128), "a_w_out": (1, 2048, 2048),
    "b_w_in": (1, 2048, 6160), "b_w_gate": (1, 16, 1024), "b_gate_bias": (1, 1024), "b_out_g": (1, 512), "b_w_out": (1, 2048, 2048),
    "c_w_in": (1, 2048, 4096), "c_conv_w": (1, 4, 2048), "c_conv_b": (1, 2048), "c_w_rgate": (1, 8, 256, 256), "c_b_rgate": (1, 2048),
    "c_w_igate": (1, 8, 256, 256), "c_b_igate": (1, 2048), "c_lambda": (1, 2048), "c_w_out": (1, 2048, 2048),
    "d_w_in": (1, 2048, 3136), "d_q_lat_g": (1, 512), "d_kv_lat_g": (1, 512), "d_w_uq": (1, 512, 3072), "d_w_ukv": (1, 512, 4096),
    "d_qk_g": (1, 2, 192), "d_w_out": (1, 2048, 2048),
}
LAYER_W = {
    0: ["norm_g", "rel_bias", "a_w_in", "a_qk_g", "a_lambda", "a_subln_g", "a_w_out"],
    1: ["norm_g", "b_w_in", "b_w_gate", "b_gate_bias", "b_out_g", "b_w_out"],
    2: ["norm_g", "c_w_in", "c_conv_w", "c_conv_b", "c_w_rgate", "c_b_rgate", "c_w_igate", "c_b_igate", "c_lambda", "c_w_out"],
    3: ["norm_g", "d_w_in", "d_q_lat_g", "d_kv_lat_g", "d_w_uq", "d_w_ukv", "d_qk_g", "d_w_out"],
}


def t5_bucket_np(rel):
    nb = 16; max_exact = 8
    ret = np.where(rel > 0, nb, 0)
    n = np.abs(rel)
    nf = np.maximum(n, 1).astype(np.float32)
    large = max_exact + (np.log(nf / max_exact) / math.log(128 / max_exact) * (nb - max_exact)).astype(np.int32)
    large = np.minimum(large, nb - 1)
    return ret + np.where(n < max_exact, n, large)


def bias_index_tiles():
    k = np.arange(128)[:, None]; q = np.arange(128)[None, :]
    idx = np.zeros((128, 2, 128), np.int64); msk = np.zeros((128, 2, 128), bool)
    idx[:, 0, :] = t5_bucket_np(k - q)
    msk[:, 0, :] = (k // 64) > (q // 64)
    idx[:, 1, :] = t5_bucket_np(k - q - 128)
    return idx, msk


def rope_tables():
    half = 32
    inv = (np.float32(10000.0) ** (-np.arange(half, dtype=np.float32) / np.float32(half))).astype(np.float32)
    ang = (np.arange(S, dtype=np.float32)[:, None] * inv[None, :]).astype(np.float32)
    c = np.cos(ang).astype(np.float32).T; s_ = np.sin(ang).astype(np.float32).T
    return np.ascontiguousarray(np.stack([np.concatenate([c, c], 0), np.concatenate([s_, s_], 0)], 0))


def build_program(layers, debug=False):
    nc = bass.Bass("TRN2", target_bir_lowering=False)
    C = Ctx()
    C.nc = nc
    x_in = dram(nc, "x", [S, D], F32, "ExternalInput")
    out = dram(nc, "out", [S, D], F32, "ExternalOutput")
    names = []
    for l in layers:
        for n in LAYER_W[l]:
            if n not in names:
                names.append(n)
    W = {n: dram(nc, n, WSHAPES[n], F32, "ExternalInput") for n in names}
    if 0 in layers:
        C.biasT_in = dram(nc, "biasT", [128, 16, 2, 128], F32, "ExternalInput")
    ident_d = nc.inline_tensor(np.eye(128, dtype=np.float32), "ident_c").ap()
    o64 = np.zeros((128, 128), np.float32); o64[:64, :64] = 1; o64[64:, 64:] = 1
    ones64_d = nc.inline_tensor(o64, "ones64_c").ap()
    ones128_d = nc.inline_tensor(np.ones((128, 128), np.float32), "ones128_c").ap()
    xs = [dram(nc, f"xs{i}", [S, D], F32) for i in range(2)]
    sk = "ExternalOutput" if debug else "Internal"
    C.QT_d = dram(nc, "QT_d", [16, 128, S], BF16, sk)
    C.KT_d = dram(nc, "KT_d", [16, 128, S], BF16, sk)
    C.V_d = dram(nc, "V_d", [S, D], BF16, sk)
    C.G_d = dram(nc, "G_d", [S, D], F32, sk)
    C.YT_d = dram(nc, "YT_d", [16, 128, S], BF16, sk)
    C.GT_d = dram(nc, "GT_d", [16, 128, S], F32, sk)
    if 3 in layers:
        C.QPE_d = dram(nc, "QPE_d", [16, 64, S], BF16, sk)
        C.KPE_d = dram(nc, "KPE_d", [64, S], BF16, sk)
        C.CQ_d = dram(nc, "CQ_d", [4, 128, S], BF16, sk)
        C.CKV_d = dram(nc, "CKV_d", [4, 128, S], BF16, sk)
        C.rope_in = dram(nc, "rope_cs", [2, 64, S], F32, "ExternalInput")
        kk = np.arange(128)[:, None]; qq = np.arange(128)[None, :]
        C.maskT_d = nc.inline_tensor(np.where((kk // 64) > (qq // 64), -30000.0, 0.0).astype(np.float32), "maskT_c").ap()
    if 1 in layers:
        C.KP_d = dram(nc, "KP_d", [S, 1024], BF16, sk)
        tt = np.arange(128)
        tu = np.where((tt[:, None] > tt[None, :]) & (tt[:, None] // 64 == tt[None, :] // 64), -1.0 / 16.0, 0.0).astype(np.float32)
        ci = np.where(tt[:, None] // 64 == np.arange(2)[None, :], -1.0 / 16.0, 0.0).astype(np.float32)
        C.TU_d = nc.inline_tensor(tu, "TU_c").ap()
        C.CI_d = nc.inline_tensor(ci, "CI_c").ap()
    rot = np.zeros((128, 128), np.float32)
    for i_ in range(32):
        rot[32 + i_, i_] = -1.0; rot[i_, 32 + i_] = 1.0
    rotm_d = nc.inline_tensor(rot, "rotm_c").ap()
    with ExitStack() as ctx:
        P = Prog(nc, ctx)
        C.P = P
        A = Arena(nc, ctx, 200)
        C.A = A
        C.psf = [ctx.enter_context(nc.psum_tensor(f"psf{i}", [128, 512], F32))[:] for i in range(8)]
        C.t_psf = P.toks(8, "psf")
        C.tpb = [C.psf[6].bitcast(BF16), C.psf[7].bitcast(BF16)]
        C.t_tpb = [C.t_psf[6], C.t_psf[7]]
        C.auxcnt = 0
        C.dq = []
        C.tickn = 0
        C.t_const = P.tok("const"); C.t_lconst = P.tok("lconst")
        C.ident = A.alloc([128], BF16); C.ones64 = A.alloc([128], F32); C.ones128 = A.alloc([128], F32); C.epscol = A.alloc([1], F32)
        C.rotm = A.alloc([128], F32); C.identf = A.alloc([128], F32)
        P.op("sync", lambda e: e.dma_start(out=C.identf, in_=ident_d), writes=[C.t_const], dma=True, join=True)
        P.op("sync", lambda e: e.dma_start(out=C.rotm, in_=rotm_d), writes=[C.t_const], dma=True, join=True)
        P.op("pool", lambda e: e.dma_start(out=C.ident, in_=ident_d), writes=[C.t_const], dma=True, join=True)
        P.op("sync", lambda e: e.dma_start(out=C.ones64, in_=ones64_d), writes=[C.t_const], dma=True, join=True)
        P.op("sync", lambda e: e.dma_start(out=C.ones128, in_=ones128_d), writes=[C.t_const], dma=True, join=True)
        P.op("dve", lambda e: e.memset(C.epscol, EPS), writes=[C.t_const], join=True)
        P.barrier()
        cur = x_in
        for li, l in enumerate(layers):
            C.layer = l
            dst = out if li == len(layers) - 1 else xs[li % 2]
            if l == 0:
                layer0_proj(C, cur, W)
                attn_phase(C, W, "diff")
                phase_outproj(C, C.YT_d, W["a_w_out"][0], cur, dst)
            elif l == 1:
                layer1_all(C, cur, W)
                phase_outproj(C, C.YT_d, W["b_w_out"][0], cur, dst)
            elif l == 2:
                layer2_all(C, cur, W)
                phase_outproj(C, C.YT_d, W["c_w_out"][0], cur, dst)
            elif l == 3:
                layer3_proj1(C, cur, W)
                layer3_proj2(C, W)
                attn_phase(C, W, "mla")
                phase_outproj(C, C.YT_d, W["d_w_out"][0], cur, dst)
            else:
                raise NotImplementedError
            cur = dst
        P.emit()
    C.names = names
    return nc, C


_CACHE = {}


def run_layers(layers, x, inputs, n_cores=4):
    key = tuple(layers)
    if key not in _CACHE:
        _CACHE[key] = build_program(layers)
    nc, C = _CACHE[key]
    shared = {n: np.ascontiguousarray(inputs[n], dtype=np.float32) for n in C.names}
    if 0 in layers:
        idx, msk = bias_index_tiles()
        rb = np.asarray(inputs["rel_bias"], np.float32)
        bt = rb[idx]
        bt = np.where(msk[..., None], np.float32(-30000.0), bt)
        shared["biasT"] = np.ascontiguousarray(bt.transpose(0, 3, 1, 2))
    if 3 in layers:
        shared["rope_cs"] = rope_tables()
    in_maps = [dict(shared, x=np.ascontiguousarray(x[b])) for b in range(n_cores)]
    res = run_bass_kernel_spmd(nc, in_maps, core_ids=list(range(n_cores)))
    C.last_res = res
    return np.stack([r["out"] for r in res.results], axis=0)


def kernel(**inputs):
    x = np.asarray(inputs["x"], np.float32)
    return run_layers([0, 1, 2, 3], x, inputs)
```
